# Optimizing a Trainium2 kernel written in Bass

```python
import math
import jax, jax.numpy as jnp
from jax import lax
import numpy as np

D_MODEL = 1024
BATCH = 4
SEQ = 8192
DEPTH = 1

GRID_W = 64
CTX_LEN = 256
D_MIX = 2 * D_MODEL
SSD_WIDTH = D_MIX // 2
SSD_HEAD_DIM = 64
SSD_HEADS = SSD_WIDTH // SSD_HEAD_DIM
SSD_GROUPS = 2
SSD_HPG = SSD_HEADS // SSD_GROUPS
SSD_STATE = 128
SSD_CONV_CH = SSD_WIDTH + 2 * SSD_GROUPS * SSD_STATE
GDN_WIDTH = D_MIX - SSD_WIDTH
GDN_HEAD_DIM = 128
GDN_HEADS = GDN_WIDTH // GDN_HEAD_DIM
GDN_CONV_CH = 3 * GDN_WIDTH
N_DIR = 2
CONV_K = 5
CHUNK = 64
D_IN_PROJ = SSD_WIDTH + SSD_CONV_CH + N_DIR * SSD_HEADS + GDN_CONV_CH + GDN_WIDTH + 2 * N_DIR * GDN_HEADS
D_FF = ((8 * D_MODEL // 3 + 255) // 256) * 256
DEEPNORM_ALPHA = (2.0 * DEPTH) ** 0.25
DEEPNORM_BETA = (8.0 * DEPTH) ** -0.25
DT_MIN = 1e-3
DT_MAX = 1e-1
LN_EPS = 1e-5
RMS_EPS = 1e-6

kernel_name = "hymba_ssd_gdn_deepnorm_prefix_dit"

f32 = jnp.float32


def layer_norm(x, g, b):
    xf = x.astype(f32)
    mu = jnp.mean(xf, -1, keepdims=True)
    var = jnp.mean(jnp.square(xf - mu), -1, keepdims=True)
    return ((xf - mu) * lax.rsqrt(var + LN_EPS) * g + b).astype(x.dtype)


def rms_norm(x):
    return x * lax.rsqrt(jnp.mean(x * x, -1, keepdims=True) + RMS_EPS)


def l2_norm(x):
    return x * lax.rsqrt(jnp.sum(x * x, -1, keepdims=True) + RMS_EPS)


def masked_exp(diff, mask):
    return jnp.exp(jnp.where(mask, diff, -jnp.inf))


def dwconv(u, w):
    return lax.conv_general_dilated(
        u, w[:, None, :].astype(u.dtype), (1,), [(CONV_K // 2, CONV_K // 2)],
        dimension_numbers=("NWC", "WIO", "NWC"), feature_group_count=u.shape[-1])


def conv_grid(u, w, rows):
    b, t, ch = u.shape
    return dwconv(u.reshape(b * rows, GRID_W, ch), w).reshape(b, t, ch)


def ssd_chunk_scan(xdt, la, bm, cm, s0):
    b, t = la.shape[:2]
    nc = t // CHUNK
    xdt, la, bm, cm = (a.reshape(b, nc, CHUNK, *a.shape[2:]) for a in (xdt, la, bm, cm))
    acs = jnp.cumsum(la, axis=2)
    tril = jnp.tril(jnp.ones((CHUNK, CHUNK), bool))[:, :, None, None]
    decay = masked_exp(acs[:, :, :, None] - acs[:, :, None, :], tril)
    att = jnp.einsum("bclgn,bcsgn->bclsg", cm, bm)[..., None] * decay
    y = jnp.einsum("bclsgj,bcsgjp->bclgjp", att, xdt)
    states = jnp.einsum("bclgn,bclgjp->bcgjpn", bm,
                        xdt * jnp.exp(acs[:, :, -1:] - acs)[..., None])

    def step(s, inp):
        dec, st = inp
        return s * dec[..., None, None] + st, s

    s_fin, s_in = lax.scan(step, s0, (jnp.exp(acs[:, :, -1]).swapaxes(0, 1), states.swapaxes(0, 1)))
    y = y + jnp.einsum("bclgn,cbgjpn->bclgjp", cm, s_in) * jnp.exp(acs)[..., None]
    return y.reshape(b, t, *y.shape[3:]), s_fin


def gdn_chunk_scan(q, k, v, log_a, beta, s0):
    b, t, h, dv = v.shape
    nc = t // CHUNK
    q, k, v, log_a, beta = (a.reshape(b, nc, CHUNK, *a.shape[2:]) for a in (q, k, v, log_a, beta))
    g = jnp.cumsum(log_a, axis=2)
    diff = g[:, :, :, None] - g[:, :, None, :]
    idx = jnp.arange(CHUNK)
    strict = (idx[:, None] > idx[None, :])[:, :, None]
    incl = (idx[:, None] >= idx[None, :])[:, :, None]
    a_mat = jnp.einsum("bclhd,bcshd->bclsh", k, k) * beta[:, :, :, None] * masked_exp(diff, strict)
    lhs = jnp.moveaxis(a_mat, -1, 2) + jnp.eye(CHUNK, dtype=a_mat.dtype)
    rhs = jnp.concatenate([v * beta[..., None], k * (beta * jnp.exp(g))[..., None]], -1)
    sol = lax.linalg.triangular_solve(lhs, jnp.moveaxis(rhs, 3, 2), left_side=True,
                                      lower=True, unit_diagonal=True)
    u, w = sol[..., :dv], sol[..., dv:]
    qk = jnp.einsum("bclhd,bcshd->bchls", q, k) * jnp.moveaxis(masked_exp(diff, incl), -1, 2)
    qg = jnp.moveaxis(q * jnp.exp(g)[..., None], 3, 2)
    kd = jnp.moveaxis(k * jnp.exp(g[:, :, -1:] - g)[..., None], 3, 2)
    dec = jnp.exp(g[:, :, -1])

    def step(s, inp):
        u_c, w_c, qk_c, qg_c, kd_c, dec_c = inp
        v_new = u_c - jnp.einsum("bhlk,bhkv->bhlv", w_c, s)
        o = jnp.einsum("bhlk,bhkv->bhlv", qg_c, s) + jnp.einsum("bhls,bhsv->bhlv", qk_c, v_new)
        s = s * dec_c[..., None, None] + jnp.einsum("bhlk,bhlv->bhkv", kd_c, v_new)
        return s, o

    s_fin, o = lax.scan(step, s0, tuple(jnp.moveaxis(a, 1, 0) for a in (u, w, qk, qg, kd, dec)))
    return jnp.transpose(o, (1, 0, 3, 2, 4)).reshape(b, t, h, dv), s_fin


def prefix_scan(scan_fn, ctx_in, lat_in, s0, reverse):
    flip = (lambda a: jnp.flip(a, axis=1)) if reverse else (lambda a: a)
    y_c, s_c = scan_fn(*map(flip, ctx_in), s0)
    y_l, _ = scan_fn(*map(flip, lat_in), s_c)
    return flip(y_c), flip(y_l)


def token_mixers(h_ctx, h_lat, rows, w_in, conv_w_ssd, conv_b_ssd, dt_bias_ssd, a_log_ssd,
                 d_skip_ssd, norm_w_ssd, conv_w_gdn, dt_bias_gdn, a_log_gdn, norm_w_gdn):
    splits = [int(s) for s in np.cumsum([SSD_WIDTH, SSD_CONV_CH, N_DIR * SSD_HEADS,
                                          GDN_CONV_CH, GDN_WIDTH, N_DIR * GDN_HEADS])]

    def project(h, conv):
        bsz, t = h.shape[:2]
        z, xbc, dt_raw, qkv, gate, b_raw, a_raw = jnp.split(h @ w_in, splits, axis=-1)
        xbc = jax.nn.silu(conv(xbc, conv_w_ssd) + conv_b_ssd).astype(f32)
        qkv = jax.nn.silu(conv(qkv, conv_w_gdn)).astype(f32)
        xs, bm, cm = jnp.split(xbc, [SSD_WIDTH, SSD_WIDTH + SSD_GROUPS * SSD_STATE], -1)
        dt = jax.nn.softplus(dt_raw.astype(f32).reshape(bsz, t, N_DIR, SSD_HEADS) + dt_bias_ssd)
        q, k, v = (a.reshape(bsz, t, GDN_HEADS, GDN_HEAD_DIM) for a in jnp.split(qkv, 3, -1))
        return {
            "z": z.astype(f32).reshape(bsz, t, SSD_GROUPS, SSD_WIDTH // SSD_GROUPS),
            "xs": xs.reshape(bsz, t, SSD_GROUPS, SSD_HPG, SSD_HEAD_DIM),
            "bm": bm.reshape(bsz, t, SSD_GROUPS, SSD_STATE),
            "cm": cm.reshape(bsz, t, SSD_GROUPS, SSD_STATE),
            "dt": dt.reshape(bsz, t, N_DIR, SSD_GROUPS, SSD_HPG),
            "q": l2_norm(q) * GDN_HEAD_DIM ** -0.5,
            "k": l2_norm(k),
            "v": v,
            "gate": gate.astype(f32).reshape(bsz, t, GDN_HEADS, GDN_HEAD_DIM),
            "beta": jax.nn.sigmoid(b_raw.astype(f32).reshape(bsz, t, N_DIR, GDN_HEADS)),
            "log_a": -jnp.exp(a_log_gdn.astype(f32)) * jax.nn.softplus(
                a_raw.astype(f32).reshape(bsz, t, N_DIR, GDN_HEADS) + dt_bias_gdn),
        }

    pc = project(h_ctx, dwconv)
    pl = project(h_lat, lambda u, w: conv_grid(u, w, rows))
    a_ssd = -jnp.exp(a_log_ssd.astype(f32)).reshape(N_DIR, SSD_GROUPS, SSD_HPG)

    def ssd_in(p, d):
        dt_d = p["dt"][:, :, d]
        return (p["xs"] * dt_d[..., None], dt_d * a_ssd[d], p["bm"], p["cm"])

    def gdn_in(p, d):
        return (p["q"], p["k"], p["v"], p["log_a"][:, :, d], p["beta"][:, :, d])

    bsz = h_lat.shape[0]
    s0_ssd = jnp.zeros((bsz, SSD_GROUPS, SSD_HPG, SSD_HEAD_DIM, SSD_STATE), f32)
    s0_gdn = jnp.zeros((bsz, GDN_HEADS, GDN_HEAD_DIM, GDN_HEAD_DIM), f32)
    d_skip = d_skip_ssd.astype(f32).reshape(SSD_GROUPS, SSD_HPG, 1)
    y_ssd_c, y_ssd_l = d_skip * pc["xs"], d_skip * pl["xs"]
    o_gdn_c, o_gdn_l = 0.0, 0.0
    for d, rev in enumerate((False, True)):
        yc, yl = prefix_scan(ssd_chunk_scan, ssd_in(pc, d), ssd_in(pl, d), s0_ssd, rev)
        y_ssd_c, y_ssd_l = y_ssd_c + yc, y_ssd_l + yl
        oc, ol = prefix_scan(gdn_chunk_scan, gdn_in(pc, d), gdn_in(pl, d), s0_gdn, rev)
        o_gdn_c, o_gdn_l = o_gdn_c + oc, o_gdn_l + ol

    def finish(p, y_ssd, o_gdn, dtype):
        bsz, t = y_ssd.shape[:2]
        y = rms_norm(y_ssd.reshape(p["z"].shape) * jax.nn.silu(p["z"])).reshape(bsz, t, SSD_WIDTH) * norm_w_ssd
        o = (rms_norm(o_gdn) * norm_w_gdn * jax.nn.silu(p["gate"])).reshape(bsz, t, GDN_WIDTH)
        return jnp.concatenate([y, o], -1).astype(dtype)

    return (finish(pc, y_ssd_c, o_gdn_c, h_ctx.dtype), finish(pl, y_ssd_l, o_gdn_l, h_lat.dtype))


def swiglu(h, w_gate, w_up, w_down):
    return (jax.nn.silu(h @ w_gate) * (h @ w_up)) @ w_down


def setup_inputs(seed: int = 0) -> dict:
    key = jax.random.key(seed)
    ks = iter(jax.random.split(key, 32))

    def nrm(shape, scale):
        return jax.random.normal(next(ks), shape, f32) * scale

    def dt_bias(shape):
        u = jax.random.uniform(next(ks), shape, f32)
        dt = jnp.exp(u * (math.log(DT_MAX) - math.log(DT_MIN)) + math.log(DT_MIN))
        return dt + jnp.log(-jnp.expm1(-dt))

    def a_log(shape):
        return jnp.log(jax.random.uniform(next(ks), shape, f32, 1.0, 16.0))

    L = DEPTH
    return {
        "x": nrm((BATCH, SEQ, D_MODEL), 1.0),
        "c": nrm((BATCH, D_MODEL), 1.0),
        "ctx": nrm((BATCH, CTX_LEN, D_MODEL), 1.0),
        "c_ctx": nrm((D_MODEL,), 1.0),
        "w_ada": nrm((L, D_MODEL, 6 * D_MODEL), D_MODEL ** -0.5),
        "b_ada": nrm((L, 6 * D_MODEL), 0.02),
        "w_in": nrm((L, D_MODEL, D_IN_PROJ), D_MODEL ** -0.5),
        "conv_w_ssd": nrm((L, CONV_K, SSD_CONV_CH), CONV_K ** -0.5),
        "conv_b_ssd": nrm((L, SSD_CONV_CH), 0.02),
        "dt_bias_ssd": dt_bias((L, N_DIR, SSD_HEADS)),
        "a_log_ssd": a_log((L, N_DIR, SSD_HEADS)),
        "d_skip_ssd": 1.0 + nrm((L, SSD_HEADS), 0.1),
        "norm_w_ssd": 1.0 + nrm((L, SSD_WIDTH), 0.1),
        "conv_w_gdn": nrm((L, CONV_K, GDN_CONV_CH), CONV_K ** -0.5),
        "dt_bias_gdn": dt_bias((L, N_DIR, GDN_HEADS)),
        "a_log_gdn": a_log((L, N_DIR, GDN_HEADS)),
        "norm_w_gdn": 1.0 + nrm((L, GDN_HEAD_DIM), 0.1),
        "w_out": nrm((L, D_MIX, D_MODEL), D_MIX ** -0.5 * DEEPNORM_BETA),
        "ln1_g": 1.0 + nrm((L, D_MODEL), 0.1),
        "ln1_b": nrm((L, D_MODEL), 0.02),
        "w_ffn_gate": nrm((L, D_MODEL, D_FF), D_MODEL ** -0.5),
        "w_ffn_up": nrm((L, D_MODEL, D_FF), D_MODEL ** -0.5),
        "w_ffn_down": nrm((L, D_FF, D_MODEL), D_FF ** -0.5 * DEEPNORM_BETA),
        "ln2_g": 1.0 + nrm((L, D_MODEL), 0.1),
        "ln2_b": nrm((L, D_MODEL), 0.02),
    }


def reference(x, c, ctx, c_ctx, w_ada, b_ada, w_in, conv_w_ssd, conv_b_ssd, dt_bias_ssd, a_log_ssd,
              d_skip_ssd, norm_w_ssd, conv_w_gdn, dt_bias_gdn, a_log_gdn, norm_w_gdn, w_out,
              ln1_g, ln1_b, w_ffn_gate, w_ffn_up, w_ffn_down, ln2_g, ln2_b):
    rows = x.shape[1] // GRID_W
    bsz = x.shape[0]
    for l in range(DEPTH):
        mod_l = (jax.nn.silu(c) @ w_ada[l] + b_ada[l]).reshape(bsz, 6, 1, D_MODEL)
        mod_c = (jax.nn.silu(c_ctx) @ w_ada[l] + b_ada[l]).reshape(6, D_MODEL)
        sh1, sc1, g1, sh2, sc2, g2 = (mod_l[:, i] for i in range(6))
        csh1, csc1, cg1, csh2, csc2, cg2 = (mod_c[i] for i in range(6))
        y_c, y_l = token_mixers(
            ctx * (1.0 + csc1) + csh1, x * (1.0 + sc1) + sh1, rows, w_in[l],
            conv_w_ssd[l], conv_b_ssd[l], dt_bias_ssd[l], a_log_ssd[l], d_skip_ssd[l], norm_w_ssd[l],
            conv_w_gdn[l], dt_bias_gdn[l], a_log_gdn[l], norm_w_gdn[l])
        x = layer_norm(DEEPNORM_ALPHA * x + g1 * (y_l @ w_out[l]), ln1_g[l], ln1_b[l])
        x = layer_norm(DEEPNORM_ALPHA * x + g2 * swiglu(x * (1.0 + sc2) + sh2, w_ffn_gate[l],
                                                        w_ffn_up[l], w_ffn_down[l]), ln2_g[l], ln2_b[l])
        if l + 1 < DEPTH:
            ctx = layer_norm(DEEPNORM_ALPHA * ctx + cg1 * (y_c @ w_out[l]), ln1_g[l], ln1_b[l])
            ctx = layer_norm(DEEPNORM_ALPHA * ctx + cg2 * swiglu(ctx * (1.0 + csc2) + csh2, w_ffn_gate[l],
                                                                w_ffn_up[l], w_ffn_down[l]), ln2_g[l], ln2_b[l])
    return x
```

```python
import numpy as np
from contextlib import ExitStack
import concourse.bass as bass
import concourse.mybir as mybir
from concourse.bass_utils import run_bass_kernel_spmd

F32 = mybir.dt.float32
BF16 = mybir.dt.bfloat16
AF = mybir.ActivationFunctionType
ALU = mybir.AluOpType
AX = mybir.AxisListType

D = 1024
CTX = 256
GRID_W = 64
DFF = 2816
NCM = 18
NTMC = 1056
TMW = 2688
SMW = 64
ALPHA = 2.0 ** 0.25
LN_EPS = 1e-5
RMS_EPS = 1e-6
EPOCH = 30000
NCST = 13
PARTS = 3
DBG_NCH = 10 ** 6


class T:
    __slots__ = ("name", "ap", "last_w", "readers", "dma_readers", "sem", "cnt", "last_dma", "excl")

    def __init__(self, name, ap):
        self.excl = False
        self.name = name
        self.ap = ap
        self.last_w = None
        self.readers = {}
        self.dma_readers = []
        self.sem = None
        self.cnt = 0
        self.last_dma = None

    def __getitem__(self, k):
        return self.ap[k]


class Op:
    __slots__ = ("eng", "fn", "deps", "needs_inc", "inc_val", "epoch", "is_dma", "sem", "val", "amt")

    def __init__(self, eng, fn, is_dma=False):
        self.eng = eng
        self.fn = fn
        self.deps = []
        self.needs_inc = False
        self.inc_val = 0
        self.epoch = 0
        self.is_dma = is_dma
        self.sem = None
        self.val = 0
        self.amt = 16


class Prog:
    ENGS = ("sp", "act", "pool", "dve", "pe")

    def __init__(self, nc, stack):
        self.nc = nc
        self.stack = stack
        self.ops = {e: [] for e in self.ENGS}
        self.same_engine_sync = {"sp": False, "act": True, "pool": True, "dve": True, "pe": False}
        self.nsem = 0
        self.dma_open = []
        self.pending = {e: [] for e in self.ENGS}
        self.sem_pool = {}

    def sb(self, st, name, shape, dt):
        return T(name, st.enter_context(self.nc.sbuf_tensor(name, list(shape), dt)))

    def ps(self, st, name, shape, dt):
        t = T(name, st.enter_context(self.nc.psum_tensor(name, list(shape), dt)))
        t.excl = True
        return t

    def new_sem(self, name):
        self.nsem += 1
        return self.stack.enter_context(self.nc.semaphore(name))

    def _track(self, op, R, W, extra=()):
        deps = list(extra)
        for t in R:
            if t.last_w is not None:
                deps.append(t.last_w)
            if t.excl:
                deps.extend(o for en, o in t.readers.items() if en != op.eng)
        for t in W:
            if t.last_w is not None:
                deps.append(t.last_w)
            deps.extend(t.readers.values())
            deps.extend(t.dma_readers)
        deps.extend(self.pending[op.eng])
        self.pending[op.eng] = []
        seen = set()
        for d in deps:
            if d is op or id(d) in seen:
                continue
            seen.add(id(d))
            if (not d.is_dma) and d.eng == op.eng and not op.is_dma and not self.same_engine_sync[op.eng]:
                continue
            if not d.is_dma:
                d.needs_inc = True
            op.deps.append(d)
        for t in R:
            if op.is_dma:
                t.dma_readers.append(op)
            else:
                t.readers[op.eng] = op
        for t in W:
            t.last_w = op
            t.readers = {}
            t.dma_readers = []

    def op(self, eng, fn, R=(), W=()):
        o = Op(eng, fn)
        self._track(o, R, W)
        self.ops[eng].append(o)
        return o

    def pe(self, fn, R=(), W=()):
        return self.op("pe", fn, R, W)

    def act(self, fn, R=(), W=()):
        return self.op("act", fn, R, W)

    def dve(self, fn, R=(), W=()):
        return self.op("dve", fn, R, W)

    def pool(self, fn, R=(), W=()):
        return self.op("pool", fn, R, W)

    def dma(self, eng, out_ap=None, in_ap=None, R=(), W=(), semt=None, fn=None, amt=16, **kw):
        if fn is None:
            fn = lambda e: e.dma_start(out=out_ap, in_=in_ap, **kw)
        o = Op(eng, fn, is_dma=True)
        o.amt = amt
        if semt is None:
            semt = (list(W) + list(R))[0]
        if semt.sem is None:
            key = semt.name
            if key not in self.sem_pool:
                self.sem_pool[key] = [self.new_sem("d_" + key), 0]
            semt.sem = self.sem_pool[key]
        semt.sem[1] += amt
        o.sem = semt.sem[0]
        o.val = semt.sem[1]
        extra = [semt.last_dma] if semt.last_dma is not None else []
        semt.last_dma = o
        self._track(o, R, W, extra)
        self.ops[eng].append(o)
        self.dma_open.append(o)
        return o

    def barrier(self):
        deps = list(self.dma_open)
        for e in self.ENGS:
            for o in reversed(self.ops[e]):
                if not o.is_dma:
                    deps.append(o)
                    break
        self.dma_open = []
        for e in self.ENGS:
            self.pending[e] = list(deps)

    def emit(self, final_ops=()):
        nc = self.nc
        esems = {}
        for e in self.ENGS:
            n = 0
            for o in self.ops[e]:
                if o.is_dma or not o.needs_inc:
                    continue
                o.epoch = n // EPOCH
                o.inc_val = n % EPOCH + 1
                n += 1
            nep = (n + EPOCH - 1) // EPOCH
            esems[e] = [self.new_sem(f"s_{e}{i}") for i in range(max(nep, 1))]
        block = self.stack.enter_context(nc.Block())
        deco = {"sp": block.sync, "act": block.scalar, "pool": block.gpsimd, "dve": block.vector, "pe": block.tensor}
        nwaits = {e: 0 for e in self.ENGS}

        def make(eng):
            def body(e):
                seen = {}
                maxep = {}

                def wait_all(deps):
                    need = {}
                    for d in deps:
                        if d.is_dma:
                            key, val, sem = ("d", id(d.sem)), d.val, d.sem
                        else:
                            key, val, sem = (d.eng, d.epoch), d.inc_val, esems[d.eng][d.epoch]
                        if seen.get(key, 0) >= val:
                            continue
                        if key not in need or need[key][0] < val:
                            need[key] = (val, sem)
                    for key, (val, sem) in need.items():
                        if key[0] != "d":
                            if any(k[0] == key[0] and k[1] > key[1] for k in list(seen) + list(need) if k[0] != "d"):
                                continue
                        seen[key] = val
                        e.wait_ge(sem, val)
                        nwaits[eng] += 1

                def wait_for(d):
                    wait_all([d])

                for o in self.ops[eng]:
                    wait_all(o.deps)
                    ins = o.fn(e)
                    if o.is_dma:
                        ins.then_inc(o.sem, o.amt)
                    elif o.needs_inc:
                        ins.then_inc(esems[eng][o.epoch], 1)
                if eng == "sp":
                    for d in final_ops:
                        wait_for(d)
                        e.nop()
            return body

        for eng in self.ENGS:
            if self.ops[eng] or eng == "sp":
                deco[eng](make(eng))
        self.nwaits = nwaits
        return block


def bc(ap, shape, axis):
    return ap.unsqueeze(axis).to_broadcast(list(shape))


def build_program(SEQ, debug=False, stop_after=99):
    TT = CTX + SEQ
    NTT = TT // 128
    NCT = CTX // 128
    NLT = SEQ // 128
    HALF = SEQ // 2
    NTH = HALF // 128
    nc = bass.Bass("TRN2", target_bir_lowering=False)
    dt = nc.dram_tensor
    xin = dt("xin", [TT, D], F32, kind="ExternalInput").ap()
    xhalf = dt("xhalf", [HALF, D], F32, kind="ExternalInput").ap()
    cvecT = dt("cvecT", [128, 16], F32, kind="ExternalInput").ap()
    w_ada = dt("w_ada", [D, 6 * D], F32, kind="ExternalInput").ap()
    b_ada = dt("b_ada", [1, 6 * D], F32, kind="ExternalInput").ap()
    w_cm = dt("w_cm", [D, NCM * 128], F32, kind="ExternalInput").ap()
    w_tm = dt("w_tm", [D, NTMC], F32, kind="ExternalInput").ap()
    convw = dt("convw", [128, NCM * 5], F32, kind="ExternalInput").ap()
    convb = dt("convb", [128, 6], F32, kind="ExternalInput").ap()
    rowp = dt("rowp", [1, 4792], F32, kind="ExternalInput").ap()
    consts = dt("consts", [128, NCST * 128], F32, kind="ExternalInput").ap()
    w_out = dt("w_out", [2 * D, D], F32, kind="ExternalInput").ap()
    w_gate = dt("w_gate", [D, DFF], F32, kind="ExternalInput").ap()
    w_up = dt("w_up", [D, DFF], F32, kind="ExternalInput").ap()
    w_down = dt("w_down", [DFF, D], F32, kind="ExternalInput").ap()
    yout = dt("yout", [HALF, D], F32, kind="ExternalOutput").ap()
    CMs = dt("CMs", [NTT, 128, 10 * 128], BF16).ap()
    TMs = dt("TMs", [TT, TMW], BF16).ap()
    SMs = dt("SMs", [TT, SMW], F32).ap()
    YFs = dt("YFs", [SEQ, 1024], F32).ap()
    YY = dt("YY", [2, NTH, 128, 1024], BF16)
    TPK = min(NTH, 8)
    NCC = NTH // TPK
    ZO = dt("ZO", [NCC, 2, TPK, 128, 1024], BF16)
    ZIN = dt("ZIN", [NTH, 128, 1024], BF16)
    MINE = dt("MINE", [NTH, 128, 1024], BF16)
    PART = dt("PART", [NTH, 128, 1024], BF16)
    WO = dt("WO", [2, 128, 16 * 512], BF16).ap()
    WGU = dt("WGU", [11, 128, 2 * 8 * 256], BF16).ap()
    WD = dt("WD", [4, 128, 11 * 512], BF16).ap()
    dbg = {}
    if debug:
        dbg["modT"] = dt("dbg_modT", [128, 64], F32, kind="ExternalOutput").ap()
        dbg["CM"] = dt("dbg_CM", [NTT, 128, 10 * 128], BF16, kind="ExternalOutput").ap()
        dbg["TM"] = dt("dbg_TM", [TT, TMW], BF16, kind="ExternalOutput").ap()
        dbg["SM"] = dt("dbg_SM", [TT, SMW], F32, kind="ExternalOutput").ap()
        dbg["YF"] = dt("dbg_YF", [SEQ, 1024], F32, kind="ExternalOutput").ap()
        dbg["YY"] = dt("dbg_YY", [2 * NTH * 128, 1024], BF16, kind="ExternalOutput").ap()
        for nm_, dt_ in (("E", F32), ("Es", F32), ("tks", F32), ("S", F32), ("t3", F32), ("vb0", F32)):
            dbg["s_" + nm_] = dt("dbg_s_" + nm_, [128, 512], dt_, kind="ExternalOutput").ap()
        for nm_ in ("Ei", "Ak0", "Ak1", "Nk0", "Nk1", "Pf0", "qkT0", "kd0", "rv", "vnb", "Sb", "qk", "Pk0", "Pk1", "Afull", "Nfull", "Ym", "Dk0", "Dk1", "Wk0", "Wk1"):
            dbg["s_" + nm_] = dt("dbg_s_" + nm_, [128, 512], BF16, kind="ExternalOutput").ap()
        dbg["s_eg0"] = dt("dbg_s_eg0", [128, 12], F32, kind="ExternalOutput").ap()

    with ExitStack() as top:
        P = Prog(nc, top)
        cst = P.sb(top, "cst", [128, NCST * 128], F32)
        identb = P.sb(top, "identb", [128, 128], BF16)
        modT = P.sb(top, "modT", [128, 64], F32)
        g12row = P.sb(top, "g12row", [128, 2 * D], F32)
        psb = [P.ps(top, f"psb{i}", [128, 512], F32) for i in range(8)]
        epsT = P.sb(top, "epsT", [128, 4], F32)
        P.pool(lambda e: e.memset(epsT[:, 0:1], RMS_EPS), W=[epsT])
        P.pool(lambda e: e.memset(epsT[:, 1:2], LN_EPS), W=[epsT])
        ident = cst[:, 0:128]
        Uincl, Lstrict, Lincl, Ustrict, ones = (cst[:, i * 128:(i + 1) * 128] for i in range(1, 6))
        P.dma("sp", cst[:], consts[:, :], W=[cst])
        P.dve(lambda e: e.tensor_copy(out=identb[:], in_=ident), R=[cst], W=[identb])
        final_ops = []

        with ExitStack() as st:
            ccol = P.sb(st, "ccol", [128, 2, 8], F32)
            csil = P.sb(st, "csil", [128, 2, 8], F32)
            crep = P.sb(st, "crep", [128, 2, 8, 128], F32)
            barow = P.sb(st, "barow", [128, 6 * D], F32)
            modrow = P.sb(st, "modrow", [128, 4 * D], F32)
            cmodrow = P.sb(st, "cmodrow", [128, 2 * D], F32)
            wab = [P.sb(st, f"wab{i}", [128, 8, 512], F32) for i in range(2)]
            P.dma("sp", ccol[:].rearrange("p v k -> p (v k)"), cvecT[:, :], W=[ccol])
            P.dma("sp", barow[:], b_ada.partition_broadcast(128), W=[barow])
            P.act(lambda e: e.activation(out=csil[:], in_=ccol[:], func=AF.Silu), R=[ccol], W=[csil])
            P.dve(lambda e: e.tensor_copy(out=crep[:].rearrange("p v k m -> p (v k) m"),
                                          in_=bc(csil[:].rearrange("p v k -> p (v k)"), [128, 16, 128], 2)),
                  R=[csil], W=[crep])
            w_ada_v = w_ada.rearrange("(kc p) n -> p kc n", p=128)
            for nb in range(12):
                wb = wab[nb % 2]
                P.dma("sp", wb[:], w_ada_v[:, :, nb * 512:(nb + 1) * 512], W=[wb])
                pl, pc = psb[(2 * nb) % 8], psb[(2 * nb + 1) % 8]
                for kc in range(8):
                    P.pe(lambda e, kc=kc, wb=wb, pl=pl: e.matmul(pl[:], lhsT=crep[:, 0, kc, :], rhs=wb[:, kc, :],
                                                                 start=(kc == 0), stop=(kc == 7)), R=[crep, wb], W=[pl])
                if nb < 4:
                    for kc in range(8):
                        P.pe(lambda e, kc=kc, wb=wb, pc=pc: e.matmul(pc[:], lhsT=crep[:, 1, kc, :], rhs=wb[:, kc, :],
                                                                     start=(kc == 0), stop=(kc == 7)), R=[crep, wb], W=[pc])
                seg = nb // 2
                half = nb % 2
                bsl = barow[:, nb * 512:(nb + 1) * 512]
                if seg in (0, 1, 3, 4):
                    mi = {0: 0, 1: 1, 3: 2, 4: 3}[seg]
                    dst = modrow[:, mi * D + half * 512: mi * D + half * 512 + 512]
                    P.dve(lambda e, dst=dst, pl=pl, bsl=bsl: e.tensor_tensor(out=dst, in0=pl[:], in1=bsl, op=ALU.add),
                          R=[pl, barow], W=[modrow])
                    if seg in (1, 4):
                        P.dve(lambda e, dst=dst: e.tensor_scalar_add(out=dst, in0=dst, scalar1=1.0), R=[modrow], W=[modrow])
                else:
                    gi = 0 if seg == 2 else 1
                    dst = g12row[:, gi * D + half * 512: gi * D + half * 512 + 512]
                    P.dve(lambda e, dst=dst, pl=pl, bsl=bsl: e.tensor_tensor(out=dst, in0=pl[:], in1=bsl, op=ALU.add),
                          R=[pl, barow], W=[g12row])
                if nb < 4:
                    dst = cmodrow[:, nb * 512:(nb + 1) * 512]
                    P.dve(lambda e, dst=dst, pc=pc, bsl=bsl: e.tensor_tensor(out=dst, in0=pc[:], in1=bsl, op=ALU.add),
                          R=[pc, barow], W=[cmodrow])
                    if nb >= 2:
                        P.dve(lambda e, dst=dst: e.tensor_scalar_add(out=dst, in0=dst, scalar1=1.0), R=[cmodrow], W=[cmodrow])
            srcs = [(modrow, 0), (modrow, 1), (cmodrow, 0), (cmodrow, 1), (modrow, 2), (modrow, 3)]
            for v, (src, si) in enumerate(srcs):
                for g in range(2):
                    pt = psb[(2 * v + g) % 8]
                    for q in range(4):
                        fc = g * 4 + q
                        P.pe(lambda e, pt=pt, q=q, src=src, off=si * D + fc * 128: e.transpose(
                            out=pt[:, q * 128:(q + 1) * 128], in_=src[:, off:off + 128], identity=ident),
                            R=[src, cst], W=[pt])
                    P.act(lambda e, pt=pt, v=v, g=g: e.copy(
                        out=modT[:, 8 * v + 4 * g: 8 * v + 4 * g + 4],
                        in_=pt[:].rearrange("p (q m) -> p q m", q=4)[:, :, 0]), R=[pt], W=[modT])
            if debug:
                final_ops.append(P.dma("sp", dbg["modT"], modT[:], R=[modT]))

            wst = [P.sb(st, f"wst{i}", [128, 6144], F32) for i in range(2)]
            wsb = [P.sb(st, f"wsb{i}", [128, 6144], BF16) for i in range(2)]
            cnt = [0]

            def convert(src_aps, dst_ap, n):
                i = cnt[0] % 2
                cnt[0] += 1
                s, b = wst[i], wsb[i]
                off = 0
                for sap, shape in src_aps:
                    sz = int(np.prod(shape))
                    view = s[:, off:off + sz]
                    if len(shape) == 2:
                        view = view.rearrange("p (a b) -> p a b", a=shape[0])
                    P.dma("sp", view, sap, W=[s])
                    off += sz
                ei = cnt[0] % 3
                if ei == 2:
                    P.act(lambda e, s=s, b=b: e.copy(out=b[:, 0:n], in_=s[:, 0:n]), R=[s], W=[b])
                else:
                    (P.dve, P.pool)[ei](lambda e, s=s, b=b: e.tensor_copy(out=b[:, 0:n], in_=s[:, 0:n]), R=[s], W=[b])
                P.dma("sp", dst_ap, b[:, 0:n], R=[b])

            if stop_after >= 5:
                wo_v = w_out.rearrange("(kc p) n -> p kc n", p=128)
                for dh in range(2):
                    for kh in range(2):
                        convert([(wo_v[:, kh * 8:(kh + 1) * 8, dh * 512:(dh + 1) * 512], (8, 512))],
                                WO[dh, :, kh * 4096:(kh + 1) * 4096], 4096)
                wg_v = w_gate.rearrange("(kc p) n -> p kc n", p=128)
                wu_v = w_up.rearrange("(kc p) n -> p kc n", p=128)
                for blk in range(11):
                    convert([(wg_v[:, :, blk * 256:(blk + 1) * 256], (8, 256)),
                             (wu_v[:, :, blk * 256:(blk + 1) * 256], (8, 256))], WGU[blk, :, :], 4096)
                wd_v = w_down.rearrange("(j p) n -> p j n", p=128)
                for dh in range(2):
                    for jh in range(2):
                        convert([(wd_v[:, jh * 11:(jh + 1) * 11, dh * 512:(dh + 1) * 512], (11, 512))],
                                WD[dh * 2 + jh, :, :], 5632)
        P.barrier()

        if stop_after >= 1:
            stage1(P, nc, locals())
        P.barrier()
        if stop_after >= 2:
            scan_pass(P, nc, locals(), 0)
            P.barrier()
        if stop_after >= 3:
            scan_pass(P, nc, locals(), 1)
            P.barrier()
        if debug and stop_after >= 1:
            dd = T("dd", None)
            final_ops.append(P.dma("sp", dbg["CM"], CMs, semt=dd))
            final_ops.append(P.dma("sp", dbg["TM"], TMs, semt=dd))
            final_ops.append(P.dma("sp", dbg["SM"], SMs, semt=dd))
            if stop_after >= 2:
                final_ops.append(P.dma("sp", dbg["YF"], YFs, semt=dd))
                final_ops.append(P.dma("sp", dbg["YY"], YY.ap().rearrange("a t p f -> (a t p) f"), semt=dd))
            P.barrier()
        if stop_after >= 4:
            cps = [T(f"cp{i}", None) for i in range(4)]
            CW = 8192
            ccnt = [0]

            def dyn_copy(dst3, src4, sel, fresh):
                nt_ = dst3.shape[0]
                nr = nt_ * 128 * 1024 // CW
                dflat = dst3.rearrange("t p f -> (t p f)").rearrange("(r c) -> r c", c=CW)
                def fn(e, fresh=fresh):
                    if fresh:
                        pid = e.partition_id()
                        P.dyn = {0: e.snap(pid % 2), 1: e.snap(1 - pid % 2)}
                    sflat = src4[bass.ds(P.dyn[sel], 1)].rearrange("a t p f -> (a t p f)").rearrange("(r c) -> r c", c=CW)
                    return e.dma_start(out=dflat, in_=sflat)
                ccnt[0] += 1
                P.dma("sp", semt=cps[ccnt[0] % 4], fn=fn)

            dyn_copy(ZIN.ap(), YY.ap(), 1, True)
            P.barrier()
            cct = T("cc", None)
            for k in range(NCC):
                P.dma("pool", semt=cct, amt=1, fn=lambda e, k=k: e.collective_compute(
                    "AllGather", ALU.bypass, replica_groups=[[0, 1], [2, 3], [4, 5], [6, 7]],
                    ins=[ZIN.ap()[k * TPK:(k + 1) * TPK].rearrange("t p f -> (t p) f").opt()],
                    outs=[ZO.ap()[k].rearrange("a t p f -> (a t p) f").opt()]))
            P.barrier()
            dyn_copy(MINE.ap(), YY.ap(), 0, True)
            for k in range(NCC):
                dyn_copy(PART.ap()[k * TPK:(k + 1) * TPK], ZO.ap()[k], 1, False)
            P.barrier()
        if stop_after >= 5:
            stage5(P, nc, locals())
        else:
            with ExitStack() as st:
                tb = P.sb(st, "tb", [128, D], F32)
                for a in range(HALF // 128):
                    P.dma("sp", tb[:], xhalf[a * 128:(a + 1) * 128, :], W=[tb])
                    final_ops.append(P.dma("sp", yout[a * 128:(a + 1) * 128, :], tb[:], R=[tb]))
        P.emit(final_ops=final_ops + locals().get("_final", []))
        stats = dict(nsem=P.nsem, nops={e: len(P.ops[e]) for e in P.ENGS}, nwaits=P.nwaits)
    return nc, stats


def stage1(P, nc, L):
    TT, NTT, NCT = L["TT"], L["NTT"], L["NCT"]
    xin, w_cm, w_tm, convw, convb, rowp = L["xin"], L["w_cm"], L["w_tm"], L["convw"], L["convb"], L["rowp"]
    CMs, TMs, SMs = L["CMs"], L["TMs"], L["SMs"]
    cst, identb, modT, psb = L["cst"], L["identb"], L["modT"], L["psb"]
    ident = cst[:, 0:128]
    ones = cst[:, 5 * 128:6 * 128]
    with ExitStack() as st:
        wcm = P.sb(st, "wcm", [128, 8, NCM * 128], BF16)
        wtm = P.sb(st, "wtm", [128, 8, NTMC], BF16)
        wld = [P.sb(st, f"wld{i}", [128, 1152], F32) for i in range(2)]
        cw = P.sb(st, "cw", [128, NCM, 5], F32)
        cb = P.sb(st, "cb", [128, 6], F32)
        spb = P.sb(st, "spb", [128, 48], F32)
        amul = P.sb(st, "amul", [128, 24], F32)
        xt = [P.sb(st, f"xt{i}", [128, 4, D], F32) for i in range(2)]
        hT = [P.sb(st, f"hT{i}", [128, 8, 512], BF16) for i in range(2)]
        pad = [P.sb(st, f"pad{i}", [128, 544], F32) for i in range(3)]
        acc = [P.sb(st, f"acc{i}", [128, 512], F32) for i in range(3)]
        ptmp = P.sb(st, "ptmp", [128, 512], F32)
        sv = [P.sb(st, f"sv{i}", [128, 512], F32) for i in range(2)]
        sq = [P.sb(st, f"sq{i}", [128, 512], F32) for i in range(2)]
        rr = [P.sb(st, f"rr{i}", [128, 512], F32) for i in range(2)]
        tbf = [P.sb(st, f"tbf{i}", [128, 512], BF16) for i in range(3)]
        cmst = [P.sb(st, f"cmst{i}", [128, 4, 10, 128], BF16) for i in range(2)]
        tmst = [P.sb(st, f"tmst{i}", [128, 4, TMW], BF16) for i in range(1)]
        smst = [P.sb(st, f"smst{i}", [128, 4, SMW], F32) for i in range(2)]
        smr = [P.sb(st, f"smr{i}", [128, 32], F32) for i in range(2)]
        wcm_v = w_cm.rearrange("(kc p) n -> p kc n", p=128)
        wtm_v = w_tm.rearrange("(kc p) n -> p kc n", p=128)
        for kc in range(8):
            for hf in range(2):
                w = wld[hf]
                P.dma("sp", w[:], wcm_v[:, kc, hf * 1152:(hf + 1) * 1152], W=[w])
                (P.dve if hf == 0 else P.pool)(lambda e, w=w, kc=kc, hf=hf: e.tensor_copy(
                    out=wcm[:, kc, hf * 1152:(hf + 1) * 1152], in_=w[:]), R=[w], W=[wcm])
        for kc in range(8):
            w = wld[kc % 2]
            P.dma("sp", w[:, 0:NTMC], wtm_v[:, kc, :], W=[w])
            (P.dve if kc % 2 == 0 else P.pool)(lambda e, w=w, kc=kc: e.tensor_copy(out=wtm[:, kc, :], in_=w[:, 0:NTMC]), R=[w], W=[wtm])
        P.dma("sp", cw[:].rearrange("p c k -> p (c k)"), convw[:, :], W=[cw])
        P.dma("sp", cb[:], convb[:, :], W=[cb])
        P.dma("sp", spb[:], rowp[:, 0:48].partition_broadcast(128), W=[spb])
        P.act(lambda e: e.activation(out=amul[:], in_=spb[:, 24:48], func=AF.Exp), R=[spb], W=[amul])
        P.dve(lambda e: e.tensor_scalar_mul(out=amul[:], in0=amul[:], scalar1=-1.0), R=[amul], W=[amul])
        for p_ in pad:
            P.pool(lambda e, p_=p_: e.memset(p_[:], 0.0), W=[p_])

        blocks = [(0, CTX, CTX, 2)]
        t0 = CTX
        while t0 < TT:
            blocks.append((t0, 512, GRID_W, 0))
            t0 += 512
        PX, PA, PB, PN, PZ0, PZ1, PSm, PTr = L["psb"]
        ptr_bf = PTr.ap[:].bitcast(BF16)
        for bi, (t0, ntok, rowlen, mv) in enumerate(blocks):
            nt = ntok // 128
            nrow = ntok // rowlen
            x_, h_ = xt[bi % 2], hT[bi % 2]
            if bi == 1:
                for p_ in pad:
                    P.pool(lambda e, p_=p_: e.memset(p_[:], 0.0), W=[p_])
            cm_, tm_, sm_ = cmst[bi % 2], tmst[0], smst[bi % 2]
            P.dma("sp", x_[:, 0:nt, :], xin[t0:t0 + ntok, :].rearrange("(a p) d -> p a d", p=128), W=[x_])
            for fc in range(8):
                for a in range(nt):
                    P.pe(lambda e, a=a, fc=fc, x_=x_: e.transpose(out=PX[:, a * 128:(a + 1) * 128],
                                                                  in_=x_[:, a, fc * 128:(fc + 1) * 128], identity=ident),
                         R=[x_, cst], W=[PX])
                P.act(lambda e, fc=fc, h_=h_, mv=mv, ntok=ntok: e.activation(
                    out=h_[:, fc, 0:ntok], in_=PX[:, 0:ntok], func=AF.Identity,
                    bias=modT[:, 8 * mv + fc: 8 * mv + fc + 1], scale=modT[:, 8 * (mv + 1) + fc: 8 * (mv + 1) + fc + 1]),
                    R=[PX, modT], W=[h_])
            for cc in range(NCM):
                pp = (PA, PB)[cc % 2]
                for kc in range(8):
                    P.pe(lambda e, kc=kc, cc=cc, pp=pp, h_=h_, ntok=ntok: e.matmul(
                        pp[:, 0:ntok], lhsT=wcm[:, kc, cc * 128:(cc + 1) * 128], rhs=h_[:, kc, 0:ntok],
                        start=(kc == 0), stop=(kc == 7)), R=[wcm, h_], W=[pp])
                pd, ac = pad[cc % 3], acc[cc % 3]
                pdv = pd[:, 0:nrow * (rowlen + 4)].rearrange("p (r l) -> p r l", r=nrow)
                P.act(lambda e, pdv=pdv, pp=pp, ntok=ntok, nrow=nrow, rowlen=rowlen: e.copy(
                    out=pdv[:, :, 2:2 + rowlen], in_=pp[:, 0:ntok].rearrange("p (r l) -> p r l", r=nrow)), R=[pp], W=[pd])
                acv = ac[:, 0:ntok].rearrange("p (r l) -> p r l", r=nrow)
                if cc % 3 != 2:
                    P.dve(lambda e, acv=acv, pdv=pdv, cc=cc, rowlen=rowlen: e.tensor_scalar_mul(
                        out=acv, in0=pdv[:, :, 0:rowlen], scalar1=cw[:, cc, 0:1]), R=[pd, cw], W=[ac])
                    for k in range(1, 5):
                        P.dve(lambda e, acv=acv, pdv=pdv, cc=cc, k=k, rowlen=rowlen: e.scalar_tensor_tensor(
                            out=acv, in0=pdv[:, :, k:k + rowlen], scalar=cw[:, cc, k:k + 1], in1=acv,
                            op0=ALU.mult, op1=ALU.add), R=[pd, cw, ac], W=[ac])
                else:
                    ptv = ptmp[:, 0:ntok].rearrange("p (r l) -> p r l", r=nrow)
                    P.pool(lambda e, acv=acv, pdv=pdv, cc=cc, rowlen=rowlen: e.tensor_scalar_mul(
                        out=acv, in0=pdv[:, :, 0:rowlen], scalar1=cw[:, cc, 0:1]), R=[pd, cw], W=[ac])
                    for k in range(1, 5):
                        P.pool(lambda e, ptv=ptv, pdv=pdv, cc=cc, k=k, rowlen=rowlen: e.tensor_scalar_mul(
                            out=ptv, in0=pdv[:, :, k:k + rowlen], scalar1=cw[:, cc, k:k + 1]), R=[pd, cw], W=[ptmp])
                        P.pool(lambda e, acv=acv, ptv=ptv: e.tensor_tensor(out=acv, in0=acv, in1=ptv, op=ALU.add),
                               R=[ac, ptmp], W=[ac])
                kind = ("xs" if cc < 4 else "B" if cc == 4 else "C" if cc == 5 else "q" if cc < 10 else "k" if cc < 14 else "v")
                if kind in ("xs", "v", "B", "C"):
                    tb = tbf[cc % 3]
                    if kind == "C":
                        dst = cm_[:, 0:nt, 0, :]
                    elif kind == "B":
                        dst = cm_[:, 0:nt, 1, :]
                    else:
                        dst = tb[:, 0:ntok].rearrange("p (a l) -> p a l", a=nt)
                    Wt = [cm_] if kind in ("B", "C") else [tb]
                    if cc < 6:
                        P.act(lambda e, dst=dst, ac=ac, cc=cc, ntok=ntok, nt=nt: e.activation(
                            out=dst, in_=ac[:, 0:ntok].rearrange("p (a l) -> p a l", a=nt), func=AF.Silu,
                            bias=cb[:, cc:cc + 1], scale=1.0), R=[ac, cb], W=Wt)
                    else:
                        P.act(lambda e, dst=dst, ac=ac, ntok=ntok, nt=nt: e.activation(
                            out=dst, in_=ac[:, 0:ntok].rearrange("p (a l) -> p a l", a=nt), func=AF.Silu),
                            R=[ac], W=Wt)
                    if kind == "C":
                        continue
                    tmoff = {"xs": cc * 128, "B": 512, "v": 1152 + (cc - 14) * 128}[kind]
                    for a in range(nt):
                        src = cm_[:, a, 1, :] if kind == "B" else tb[:, a * 128:(a + 1) * 128]
                        P.pe(lambda e, a=a, src=src: e.transpose(out=ptr_bf[:, a * 128:(a + 1) * 128], in_=src, identity=identb[:]),
                             R=[cm_ if kind == "B" else tb, identb], W=[PTr])
                    P.dve(lambda e, tm_=tm_, nt=nt, tmoff=tmoff: e.tensor_copy(
                        out=tm_[:, 0:nt, tmoff:tmoff + 128], in_=ptr_bf[:, 0:nt * 128].rearrange("p (a l) -> p a l", a=nt)),
                        R=[PTr], W=[tm_])
                else:
                    s_, q_, r_ = sv[cc % 2], sq[cc % 2], rr[cc % 2]
                    P.act(lambda e, s_=s_, ac=ac, ntok=ntok: e.activation(out=s_[:, 0:ntok], in_=ac[:, 0:ntok], func=AF.Silu),
                          R=[ac], W=[s_])
                    P.pool(lambda e, s_=s_, q_=q_, ntok=ntok: e.tensor_tensor(out=q_[:, 0:ntok], in0=s_[:, 0:ntok], in1=s_[:, 0:ntok], op=ALU.mult),
                           R=[s_], W=[q_])
                    P.pe(lambda e, q_=q_, ntok=ntok: e.matmul(PN[:, 0:ntok], lhsT=ones, rhs=q_[:, 0:ntok], start=True, stop=True),
                         R=[cst, q_], W=[PN])
                    P.act(lambda e, r_=r_, ntok=ntok: e.activation(out=r_[:, 0:ntok], in_=PN[:, 0:ntok], func=AF.Sqrt,
                                                                   bias=cst_eps(L), scale=1.0), R=[PN, L["epsT"]], W=[r_])
                    P.dve(lambda e, r_=r_, ntok=ntok: e.reciprocal(out=r_[:, 0:ntok], in_=r_[:, 0:ntok]), R=[r_], W=[r_])
                    if kind == "q":
                        ci = 6 + (cc - 6)
                        dst = cm_[:, 0:nt, ci, :]
                        P.dve(lambda e, dst=dst, s_=s_, r_=r_, ntok=ntok, nt=nt: e.scalar_tensor_tensor(
                            out=dst, in0=s_[:, 0:ntok].rearrange("p (a l) -> p a l", a=nt), scalar=128.0 ** -0.5,
                            in1=r_[:, 0:ntok].rearrange("p (a l) -> p a l", a=nt), op0=ALU.mult, op1=ALU.mult),
                            R=[s_, r_], W=[cm_])
                    else:
                        ci = 2 + (cc - 10)
                        dst = cm_[:, 0:nt, ci, :]
                        P.dve(lambda e, dst=dst, s_=s_, r_=r_, ntok=ntok, nt=nt: e.tensor_tensor(
                            out=dst, in0=s_[:, 0:ntok].rearrange("p (a l) -> p a l", a=nt),
                            in1=r_[:, 0:ntok].rearrange("p (a l) -> p a l", a=nt), op=ALU.mult), R=[s_, r_], W=[cm_])
                        tmoff = 640 + (cc - 10) * 128
                        for a in range(nt):
                            P.pe(lambda e, a=a, ci=ci, cm_=cm_: e.transpose(out=ptr_bf[:, a * 128:(a + 1) * 128], in_=cm_[:, a, ci, :], identity=identb[:]),
                                 R=[cm_, identb], W=[PTr])
                        P.dve(lambda e, tm_=tm_, nt=nt, tmoff=tmoff: e.tensor_copy(
                            out=tm_[:, 0:nt, tmoff:tmoff + 128], in_=ptr_bf[:, 0:nt * 128].rearrange("p (a l) -> p a l", a=nt)),
                            R=[PTr], W=[tm_])
            for a in range(nt):
                for gi, (n0, nn, pz) in enumerate(((0, 512, PZ0), (512, 512, PZ1), (1024, 32, PSm))):
                    for kc in range(8):
                        P.pe(lambda e, kc=kc, a=a, n0=n0, nn=nn, pz=pz, h_=h_: e.matmul(
                            pz[:, 0:nn], lhsT=h_[:, kc, a * 128:(a + 1) * 128], rhs=wtm[:, kc, n0:n0 + nn],
                            start=(kc == 0), stop=(kc == 7)), R=[h_, wtm], W=[pz])
                    if gi < 2:
                        off = 1664 + gi * 512
                        P.act(lambda e, a=a, off=off, pz=pz, tm_=tm_: e.activation(out=tm_[:, a, off:off + 512], in_=pz[:], func=AF.Silu),
                              R=[pz], W=[tm_])
                    else:
                        r_ = smr[a % 2]
                        smv = sm_[:, a, :]
                        P.dve(lambda e, r_=r_: e.tensor_tensor(out=r_[:, 0:24], in0=PSm[:, 0:24], in1=spb[:, 0:24], op=ALU.add),
                              R=[PSm, spb], W=[r_])
                        P.act(lambda e, r_=r_: e.activation(out=r_[:, 0:24], in_=r_[:, 0:24], func=AF.Exp), R=[r_], W=[r_])
                        P.act(lambda e, r_=r_: e.activation(out=r_[:, 0:24], in_=r_[:, 0:24], func=AF.Ln, bias=1.0, scale=1.0), R=[r_], W=[r_])
                        P.act(lambda e, smv=smv: e.activation(out=smv[:, 40:48], in_=PSm[:, 24:32], func=AF.Sigmoid), R=[PSm], W=[sm_])
                        P.dve(lambda e, smv=smv, r_=r_: e.tensor_copy(out=smv[:, 0:16], in_=r_[:, 0:16]), R=[r_], W=[sm_])
                        P.dve(lambda e, smv=smv, r_=r_: e.tensor_tensor(out=smv[:, 16:40], in0=r_[:, 0:24], in1=amul[:], op=ALU.mult),
                              R=[r_, amul], W=[sm_])
                        P.dve(lambda e, smv=smv: e.tensor_scalar_mul(out=smv[:, 48:56], in0=smv[:, 40:48], scalar1=-1.0), R=[sm_], W=[sm_])
                        P.dve(lambda e, smv=smv: e.memset(smv[:, 56:64], 0.0), W=[sm_])
            ti0 = t0 // 128
            P.dma("sp", CMs[ti0:ti0 + nt].rearrange("t p f -> p t f"), cm_[:, 0:nt].rearrange("p t c l -> p t (c l)"), R=[cm_])
            P.dma("sp", TMs[t0:t0 + ntok, :].rearrange("(a p) f -> p a f", p=128), tm_[:, 0:nt, :], R=[tm_])
            P.dma("sp", SMs[t0:t0 + ntok, :].rearrange("(a p) f -> p a f", p=128), sm_[:, 0:nt, :], R=[sm_])


def cst_eps(L):
    return L["epsT"][:, 0:1]


def scan_pass(P, nc, L, d):
    NTT, NCT, NLT, NTH = L["NTT"], L["NCT"], L["NLT"], L["NTH"]
    CMs, TMs, SMs, YFs, YY, rowp = L["CMs"], L["TMs"], L["SMs"], L["YFs"], L["YY"], L["rowp"]
    cst, identb, psb, epsT = L["cst"], L["identb"], L["psb"], L["epsT"]
    Uincl, Lstrict, Lincl, Ustrict, ones = (cst[:, i * 128:(i + 1) * 128] for i in range(1, 6))
    if d == 0:
        Tm, Um, TmT = Uincl, Lstrict, Lincl
        order = list(range(NCT)) + [NCT + i for i in range(NLT)]
    else:
        Tm, Um, TmT = Lincl, Ustrict, Uincl
        order = list(reversed(range(NCT))) + [NCT + i for i in reversed(range(NLT))]
    prep_banks = psb[0:3]
    out_banks = psb[3:5]
    bKS, bSU, bST = psb[5], psb[6], psb[7]
    cnt = {"p": 0, "o": 0}

    def pb():
        cnt["p"] += 1
        return prep_banks[cnt["p"] % 3]

    def ob():
        cnt["o"] += 1
        return out_banks[cnt["o"] % 2]

    with ExitStack() as st:
        sb = lambda n, shp, dt_: P.sb(st, f"{n}_{d}", shp, dt_)
        cmb = [sb(f"cmb{i}", [128, 10, 128], BF16) for i in range(3)]
        tmb = [sb(f"tmb{i}", [128, TMW], BF16) for i in range(3)]
        smb = [sb(f"smb{i}", [128, SMW], F32) for i in range(3)]
        yo = [sb(f"yo{i}", [128, 1024], F32) for i in range(2)]
        rla = sb("rla", [128, 8, 128], F32)
        ET = sb("ET", [128, 8, 128], BF16)
        gmt = sb("gmt", [128, 128], BF16)
        att = [sb(f"att{i}", [128, 8, 128], BF16) for i in range(2)]
        xdt = [sb(f"xdt{i}", [128, 8, 64], BF16) for i in range(2)]
        xdw = [sb(f"xdw{i}", [128, 8, 64], BF16) for i in range(2)]
        ea = [sb(f"ea{i}", [128, 24], F32) for i in range(2)]
        t1 = sb("t1", [128, 8, 64], F32)
        ST = sb("ST", [128, 8, 64], F32)
        STb = sb("STb", [128, 512], BF16)
        rlu = sb("rlu", [128, 4, 128], F32)
        E = sb("E", [128, 4, 128], F32)
        Es = sb("Es", [128, 4, 128], F32)
        Ei = sb("Ei", [128, 4, 128], BF16)
        eg = [sb(f"eg{i}", [128, 12], F32) for i in range(2)]
        eb = [sb(f"eb{i}", [128, 4], F32) for i in range(2)]
        Ak = [sb(f"Ak{i}", [128, 4, 128], BF16) for i in range(2)]
        Nk = [sb(f"Nk{i}", [128, 4, 128], BF16) for i in range(2)]
        Pk = [sb(f"Pk{i}", [128, 4, 128], BF16) for i in range(2)]
        Pf = [sb(f"Pf{i}", [128, 4, 128], BF16) for i in range(2)]
        Wk = [sb(f"Wk{i}", [128, 4, 128], BF16) for i in range(2)]
        Dk = [sb(f"Dk{i}", [128, 4, 128], BF16) for i in range(2)]
        Afull = sb("Afull", [128, 4, 128], BF16)
        Nfull = sb("Nfull", [128, 4, 128], BF16)
        Ym = sb("Ym", [128, 4, 128], BF16)
        qk = sb("qk", [128, 4, 128], BF16)
        qkT = [sb(f"qkT{i}", [128, 4, 128], BF16) for i in range(2)]
        vb = [sb(f"vb{i}", [128, 4, 128], F32) for i in range(2)]
        kd = [sb(f"kd{i}", [128, 4, 128], BF16) for i in range(2)]
        tks = sb("tks", [128, 4, 128], F32)
        rv = sb("rv", [128, 4, 128], BF16)
        vnb = sb("vnb", [128, 4, 128], BF16)
        S = sb("S", [128, 4, 128], F32)
        Sb = sb("Sb", [128, 4, 128], BF16)
        t3 = sb("t3", [128, 4, 128], F32)
        P.pool(lambda e: e.memset(ST[:], 0.0), W=[ST])
        P.pool(lambda e: e.memset(STb[:], 0.0), W=[STb])
        P.pool(lambda e: e.memset(S[:], 0.0), W=[S])
        P.pool(lambda e: e.memset(Sb[:], 0.0), W=[Sb])
        if d == 1:
            yfb = [sb(f"yfb{i}", [128, 1024], F32) for i in range(3)]
            fin = sb("fin", [128, 648], F32)
            P.dma("sp", fin[:], rowp[:, 48:696].partition_broadcast(128), W=[fin])
            u = sb("u", [128, 1024], F32)
            usq = sb("usq", [128, 1024], F32)
            ss = sb("ss", [128, 8], F32)
            yfin = sb("yfin", [128, 1024], BF16)
            yT = [sb(f"yT{i}", [128, 8, 128], BF16) for i in range(2)]

        def load(ci):
            ti = order[ci]
            cm, tm, sm = cmb[ci % 3], tmb[ci % 3], smb[ci % 3]
            P.dma("sp", cm[:].rearrange("p c l -> p (c l)"), CMs[ti], W=[cm])
            P.dma("sp", tm[:], TMs[ti * 128:(ti + 1) * 128, :], W=[tm])
            P.dma("sp", sm[:], SMs[ti * 128:(ti + 1) * 128, :], W=[sm])
            if d == 1 and ti >= NCT:
                yf = yfb[ci % 3]
                P.dma("sp", yf[:], YFs[(ti - NCT) * 128:(ti - NCT + 1) * 128, :], W=[yf])

        def prep(ci):
            cm, tm, sm = cmb[ci % 3], tmb[ci % 3], smb[ci % 3]
            i2 = ci % 2
            CT, BT = cm[:, 0, :], cm[:, 1, :]
            la = sm[:, 16 + 8 * d:24 + 8 * d]
            dtd = sm[:, 8 * d:8 * d + 8]
            lag = sm[:, 32 + 4 * d:36 + 4 * d]
            beta = sm[:, 40 + 4 * d:44 + 4 * d]
            nbeta = sm[:, 48 + 4 * d:52 + 4 * d]
            if PARTS & 4:
                return
            P.pool(lambda e: e.tensor_tensor(out=rla[:], in0=bc(la, [128, 8, 128], 2), in1=bc(Tm, [128, 8, 128], 1), op=ALU.mult),
                   R=[sm, cst], W=[rla])
            for hf in range(2):
                pd = pb()
                P.pe(lambda e, pd=pd, hf=hf: e.matmul(pd[:], lhsT=Um, rhs=rla[:, 4 * hf:4 * hf + 4, :].rearrange("p h l -> p (h l)"),
                                                      start=True, stop=True), R=[cst, rla], W=[pd])
                P.act(lambda e, pd=pd, hf=hf: e.activation(out=ET[:, 4 * hf:4 * hf + 4, :].rearrange("p h l -> p (h l)"), in_=pd[:], func=AF.Exp),
                      R=[pd], W=[ET])
            pg = pb()
            P.pe(lambda e: e.matmul(pg[:, 0:128], lhsT=BT, rhs=CT, start=True, stop=True), R=[cm], W=[pg])
            P.pe(lambda e: e.matmul(pg[:, 128:136], lhsT=Tm, rhs=la, start=True, stop=True), R=[cst, sm], W=[pg])
            P.pe(lambda e: e.matmul(pg[:, 136:144], lhsT=Um, rhs=la, start=True, stop=True), R=[cst, sm], W=[pg])
            P.pe(lambda e: e.matmul(pg[:, 144:152], lhsT=ones, rhs=la, start=True, stop=True), R=[cst, sm], W=[pg])
            P.dve(lambda e: e.tensor_tensor(out=gmt[:], in0=pg[:, 0:128], in1=Tm, op=ALU.mult), R=[pg, cst], W=[gmt])
            ea_ = ea[i2]
            P.act(lambda e: e.activation(out=ea_[:], in_=pg[:, 128:152], func=AF.Exp), R=[pg], W=[ea_])
            att_, xdt_, xdw_ = att[i2], xdt[i2], xdw[i2]
            P.pool(lambda e: e.tensor_tensor(out=att_[:], in0=ET[:], in1=bc(gmt[:], [128, 8, 128], 1), op=ALU.mult),
                   R=[ET, gmt], W=[att_])
            P.pool(lambda e: e.tensor_tensor(out=xdt_[:], in0=tm[:, 0:512].rearrange("p (h q) -> p h q", h=8),
                                             in1=bc(dtd, [128, 8, 64], 2), op=ALU.mult), R=[tm, sm], W=[xdt_])
            P.pool(lambda e: e.tensor_tensor(out=xdw_[:], in0=xdt_[:], in1=bc(ea_[:, 8:16], [128, 8, 64], 2), op=ALU.mult),
                   R=[xdt_, ea_], W=[xdw_])
            if PARTS & 8:
                return
            P.pool(lambda e: e.tensor_tensor(out=rlu[:], in0=bc(lag, [128, 4, 128], 2), in1=bc(Um, [128, 4, 128], 1), op=ALU.mult),
                   R=[sm, cst], W=[rlu])
            pdg = pb()
            P.pe(lambda e: e.matmul(pdg[:], lhsT=Tm, rhs=rlu[:].rearrange("p h l -> p (h l)"), start=True, stop=True),
                 R=[cst, rlu], W=[pdg])
            P.act(lambda e: e.activation(out=E[:].rearrange("p h l -> p (h l)"), in_=pdg[:], func=AF.Exp), R=[pdg], W=[E])
            ps2 = pb()
            P.pe(lambda e: e.matmul(ps2[:, 0:4], lhsT=Tm, rhs=lag, start=True, stop=True), R=[cst, sm], W=[ps2])
            P.pe(lambda e: e.matmul(ps2[:, 4:8], lhsT=Um, rhs=lag, start=True, stop=True), R=[cst, sm], W=[ps2])
            P.pe(lambda e: e.matmul(ps2[:, 8:12], lhsT=ones, rhs=lag, start=True, stop=True), R=[cst, sm], W=[ps2])
            eg_, eb_ = eg[i2], eb[i2]
            P.act(lambda e: e.activation(out=eg_[:], in_=ps2[:, 0:12], func=AF.Exp), R=[ps2], W=[eg_])
            P.pool(lambda e: e.tensor_tensor(out=Es[:], in0=E[:], in1=bc(Um, [128, 4, 128], 1), op=ALU.mult), R=[E, cst], W=[Es])
            P.pool(lambda e: e.tensor_tensor(out=Es[:], in0=Es[:], in1=bc(nbeta, [128, 4, 128], 2), op=ALU.mult), R=[Es, sm], W=[Es])
            P.pool(lambda e: e.tensor_tensor(out=Ei[:], in0=E[:], in1=bc(TmT, [128, 4, 128], 1), op=ALU.mult), R=[E, cst], W=[Ei])
            pkk, pqk = pb(), pb()
            for h in range(4):
                P.pe(lambda e, h=h: e.matmul(pkk[:, h * 128:(h + 1) * 128], lhsT=cm[:, 2 + h, :], rhs=cm[:, 2 + h, :], start=True, stop=True),
                     R=[cm], W=[pkk])
            for h in range(4):
                P.pe(lambda e, h=h: e.matmul(pqk[:, h * 128:(h + 1) * 128], lhsT=cm[:, 6 + h, :], rhs=cm[:, 2 + h, :], start=True, stop=True),
                     R=[cm], W=[pqk])
            BD16 = cst[:, 6 * 128:7 * 128]
            Mb = [cst[:, (7 + i + 3 * d) * 128:(8 + i + 3 * d) * 128] for i in range(3)]
            P.dve(lambda e: e.tensor_tensor(out=Afull[:].rearrange("p h l -> p (h l)"), in0=pkk[:], in1=Es[:].rearrange("p h l -> p (h l)"), op=ALU.mult),
                  R=[pkk, Es], W=[Afull])
            P.dve(lambda e: e.tensor_tensor(out=qk[:].rearrange("p h l -> p (h l)"), in0=pqk[:], in1=Ei[:].rearrange("p h l -> p (h l)"), op=ALU.mult),
                  R=[pqk, Ei], W=[qk])
            pt = pb()
            ptb = pt.ap[:].bitcast(BF16)
            for h in range(4):
                P.pe(lambda e, h=h: e.transpose(out=ptb[:, h * 128:(h + 1) * 128], in_=Afull[:, h, :], identity=identb[:]), R=[Afull, identb], W=[pt])
            for h in range(4):
                P.pe(lambda e, h=h: e.transpose(out=ptb[:, 512 + h * 128:512 + (h + 1) * 128], in_=qk[:, h, :], identity=identb[:]), R=[qk, identb], W=[pt])
            qkT_ = qkT[i2]
            P.act(lambda e: e.copy(out=Nfull[:].rearrange("p h l -> p (h l)"), in_=ptb[:, 0:512]), R=[pt], W=[Nfull])
            P.dve(lambda e: e.tensor_copy(out=qkT_[:].rearrange("p h l -> p (h l)"), in_=ptb[:, 512:1024]), R=[pt], W=[qkT_])
            A, N, Pc = Ak[0], Nk[0], Pk[0]
            P.pool(lambda e, A=A: e.tensor_tensor(out=A[:], in0=Afull[:], in1=bc(BD16, [128, 4, 128], 1), op=ALU.mult), R=[Afull, cst], W=[A])
            P.pool(lambda e, N=N: e.tensor_tensor(out=N[:], in0=Nfull[:], in1=bc(BD16, [128, 4, 128], 1), op=ALU.mult), R=[Nfull, cst], W=[N])
            P.pool(lambda e, N=N, Pc=Pc: e.tensor_tensor(out=Pc[:], in0=N[:], in1=bc(identb[:], [128, 4, 128], 1), op=ALU.add), R=[N, identb], W=[Pc])
            for k in range(1, 4):
                A2, N2 = Ak[k % 2], Nk[k % 2]
                P2 = Pk[k % 2]
                pa = pb()
                for h in range(4):
                    P.pe(lambda e, h=h, N=N, A=A, pa=pa: e.matmul(pa[:, h * 128:(h + 1) * 128], lhsT=N[:, h, :], rhs=A[:, h, :], start=True, stop=True),
                         R=[N, A], W=[pa])
                P.act(lambda e, A2=A2, pa=pa: e.copy(out=A2[:].rearrange("p h l -> p (h l)"), in_=pa[:]), R=[pa], W=[A2])
                if k <= 2:
                    pn = pb()
                    for h in range(4):
                        P.pe(lambda e, h=h, N=N, A=A, pn=pn: e.matmul(pn[:, h * 128:(h + 1) * 128], lhsT=A[:, h, :], rhs=N[:, h, :], start=True, stop=True),
                             R=[N, A], W=[pn])
                    P.act(lambda e, N2=N2, pn=pn: e.copy(out=N2[:].rearrange("p h l -> p (h l)"), in_=pn[:]), R=[pn], W=[N2])
                pm = pb()
                for h in range(4):
                    P.pe(lambda e, h=h, A2=A2, Pc=Pc, pm=pm: e.matmul(pm[:, h * 128:(h + 1) * 128], lhsT=A2[:, h, :], rhs=Pc[:, h, :], start=True, stop=True),
                         R=[A2, Pc], W=[pm])
                P.dve(lambda e, P2=P2, Pc=Pc, pm=pm: e.tensor_tensor(out=P2[:].rearrange("p h l -> p (h l)"), in0=pm[:],
                                                                     in1=Pc[:].rearrange("p h l -> p (h l)"), op=ALU.add), R=[pm, Pc], W=[P2])
                A, N, Pc = A2, N2, P2
            Wc = Pc
            pt2 = pb()
            pt2b = pt2.ap[:].bitcast(BF16)
            for h in range(4):
                P.pe(lambda e, h=h, Wc=Wc: e.transpose(out=pt2b[:, h * 128:(h + 1) * 128], in_=Wc[:, h, :], identity=identb[:]), R=[Wc, identb], W=[pt2])
            Dc = Dk[0]
            P.act(lambda e, Dc=Dc: e.copy(out=Dc[:].rearrange("p h l -> p (h l)"), in_=pt2b[:, 0:512]), R=[pt2], W=[Dc])
            for li in range(3):
                last = li == 2
                py_ = pb()
                for h in range(4):
                    P.pe(lambda e, h=h, Dc=Dc, py_=py_: e.matmul(py_[:, h * 128:(h + 1) * 128], lhsT=Nfull[:, h, :], rhs=Dc[:, h, :], start=True, stop=True),
                         R=[Nfull, Dc], W=[py_])
                P.dve(lambda e, py_=py_, li=li: e.tensor_tensor(out=Ym[:], in0=py_[:].rearrange("p (h l) -> p h l", h=4), in1=bc(Mb[li], [128, 4, 128], 1), op=ALU.mult),
                      R=[py_, cst], W=[Ym])
                if not last:
                    pz = pb()
                    for h in range(4):
                        P.pe(lambda e, h=h, Wc=Wc, pz=pz: e.matmul(pz[:, h * 128:(h + 1) * 128], lhsT=Wc[:, h, :], rhs=Ym[:, h, :], start=True, stop=True),
                             R=[Wc, Ym], W=[pz])
                    D2 = Dk[(li + 1) % 2]
                    P.dve(lambda e, D2=D2, Dc=Dc, pz=pz: e.tensor_tensor(out=D2[:].rearrange("p h l -> p (h l)"), in0=pz[:],
                                                                         in1=Dc[:].rearrange("p h l -> p (h l)"), op=ALU.add), R=[pz, Dc], W=[D2])
                pzt = pb()
                for h in range(4):
                    P.pe(lambda e, h=h, Wc=Wc, pzt=pzt: e.matmul(pzt[:, h * 128:(h + 1) * 128], lhsT=Ym[:, h, :], rhs=Wc[:, h, :], start=True, stop=True),
                         R=[Wc, Ym], W=[pzt])
                W2 = Pf[i2] if last else Wk[li % 2]
                P.dve(lambda e, W2=W2, Wc=Wc, pzt=pzt: e.tensor_tensor(out=W2[:].rearrange("p h l -> p (h l)"), in0=pzt[:],
                                                                        in1=Wc[:].rearrange("p h l -> p (h l)"), op=ALU.add), R=[pzt, Wc], W=[W2])
                Wc = W2
                if not last:
                    Dc = D2
            vb_, kd_ = vb[i2], kd[i2]
            P.pool(lambda e: e.tensor_tensor(out=vb_[:], in0=tm[:, 1152:1664].rearrange("p (h v) -> p h v", h=4), in1=bc(beta, [128, 4, 128], 2), op=ALU.mult),
                   R=[tm, sm], W=[vb_])
            P.dve(lambda e: e.tensor_tensor(out=eb_[:], in0=eg_[:, 0:4], in1=beta, op=ALU.mult), R=[eg_, sm], W=[eb_])
            P.pool(lambda e: e.tensor_tensor(out=kd_[:], in0=tm[:, 640:1152].rearrange("p (h v) -> p h v", h=4), in1=bc(eg_[:, 4:8], [128, 4, 128], 2), op=ALU.mult),
                   R=[tm, eg_], W=[kd_])

        def seq(ci):
            ti = order[ci]
            lat = ti >= NCT
            cm, tm, sm = cmb[ci % 3], tmb[ci % 3], smb[ci % 3]
            i2 = ci % 2
            CT = cm[:, 0, :]
            ea_, att_, xdt_, xdw_ = ea[i2], att[i2], xdt[i2], xdw[i2]
            eg_, eb_, qkT_, vb_, kd_, Pf_ = eg[i2], eb[i2], qkT[i2], vb[i2], kd[i2], Pf[i2]
            yo_ = yo[ci % 2]
            for h in range(4):
                P.pe(lambda e, h=h: e.matmul(bKS[:, h * 128:(h + 1) * 128], lhsT=cm[:, 2 + h, :], rhs=Sb[:, h, :], start=True, stop=True),
                     R=[cm, Sb], W=[bKS])
            P.dve(lambda e: e.tensor_tensor(out=tks[:], in0=bKS[:].rearrange("p (h v) -> p h v", h=4), in1=bc(eb_[:], [128, 4, 128], 2), op=ALU.mult),
                  R=[bKS, eb_], W=[tks])
            P.pool(lambda e: e.tensor_tensor(out=rv[:], in0=vb_[:], in1=tks[:], op=ALU.subtract), R=[vb_, tks], W=[rv])
            if lat:
                pi = ob()
                P.pe(lambda e: e.matmul(pi[:], lhsT=CT, rhs=STb[:], start=True, stop=True), R=[cm, STb], W=[pi])
                pqs = ob()
                for h in range(4):
                    P.pe(lambda e, h=h: e.matmul(pqs[:, h * 128:(h + 1) * 128], lhsT=cm[:, 6 + h, :], rhs=Sb[:, h, :], start=True, stop=True),
                         R=[cm, Sb], W=[pqs])
            P.pe(lambda e: e.matmul(bST[:], lhsT=tm[:, 512:640], rhs=xdw_[:].rearrange("p h q -> p (h q)"), start=True, stop=True),
                 R=[tm, xdw_], W=[bST])
            if lat:
                P.dve(lambda e: e.tensor_tensor(out=t1[:], in0=pi[:].rearrange("p (h q) -> p h q", h=8), in1=bc(ea_[:, 0:8], [128, 8, 64], 2), op=ALU.mult),
                      R=[pi, ea_], W=[t1])
                P.dve(lambda e: e.tensor_tensor(out=t3[:], in0=pqs[:].rearrange("p (h v) -> p h v", h=4), in1=bc(eg_[:, 0:4], [128, 4, 128], 2), op=ALU.mult),
                      R=[pqs, eg_], W=[t3])
            P.dve(lambda e: e.tensor_tensor(out=ST[:], in0=ST[:], in1=bc(ea_[:, 16:24], [128, 8, 64], 2), op=ALU.mult), R=[ST, ea_], W=[ST])
            P.dve(lambda e: e.tensor_tensor(out=ST[:].rearrange("p h q -> p (h q)"), in0=ST[:].rearrange("p h q -> p (h q)"), in1=bST[:], op=ALU.add),
                  R=[ST, bST], W=[ST])
            P.act(lambda e: e.copy(out=STb[:], in_=ST[:].rearrange("p h q -> p (h q)")), R=[ST], W=[STb])
            for h in range(4):
                P.pe(lambda e, h=h: e.matmul(bKS[:, h * 128:(h + 1) * 128], lhsT=Pf_[:, h, :], rhs=rv[:, h, :], start=True, stop=True),
                     R=[Pf_, rv], W=[bKS])
            P.act(lambda e: e.copy(out=vnb[:].rearrange("p h v -> p (h v)"), in_=bKS[:]), R=[bKS], W=[vnb])
            for h in range(4):
                P.pe(lambda e, h=h: e.matmul(bSU[:, h * 128:(h + 1) * 128], lhsT=kd_[:, h, :], rhs=vnb[:, h, :], start=True, stop=True),
                     R=[kd_, vnb], W=[bSU])
            P.dve(lambda e: e.tensor_tensor(out=S[:], in0=S[:], in1=bc(eg_[:, 8:12], [128, 4, 128], 2), op=ALU.mult), R=[S, eg_], W=[S])
            P.dve(lambda e: e.tensor_tensor(out=S[:].rearrange("p h v -> p (h v)"), in0=S[:].rearrange("p h v -> p (h v)"), in1=bSU[:], op=ALU.add),
                  R=[S, bSU], W=[S])
            P.act(lambda e: e.copy(out=Sb[:].rearrange("p h v -> p (h v)"), in_=S[:].rearrange("p h v -> p (h v)")), R=[S], W=[Sb])
            if not lat:
                return
            py = ob()
            for h in range(8):
                P.pe(lambda e, h=h: e.matmul(py[:, h * 64:(h + 1) * 64], lhsT=att_[:, h, :], rhs=xdt_[:, h, :], start=True, stop=True),
                     R=[att_, xdt_], W=[py])
            P.dve(lambda e: e.tensor_tensor(out=yo_[:, 0:512], in0=t1[:].rearrange("p h q -> p (h q)"), in1=py[:], op=ALU.add), R=[t1, py], W=[yo_])
            pqv = ob()
            for h in range(4):
                P.pe(lambda e, h=h: e.matmul(pqv[:, h * 128:(h + 1) * 128], lhsT=qkT_[:, h, :], rhs=vnb[:, h, :], start=True, stop=True),
                     R=[qkT_, vnb], W=[pqv])
            P.dve(lambda e: e.tensor_tensor(out=yo_[:, 512:1024], in0=t3[:].rearrange("p h v -> p (h v)"), in1=pqv[:], op=ALU.add), R=[t3, pqv], W=[yo_])
            li = ti - NCT
            if d == 0:
                P.dma("sp", YFs[li * 128:(li + 1) * 128, :], yo_[:], R=[yo_])
                return
            yf = yfb[ci % 3]
            dsk, nws, nwg = fin[:, 0:8], fin[:, 8:520], fin[:, 520:648]
            P.pool(lambda e: e.tensor_tensor(out=u[:], in0=yo_[:], in1=yf[:], op=ALU.add), R=[yo_, yf], W=[u])
            P.pool(lambda e: e.tensor_tensor(out=usq[:, 0:512].rearrange("p (h q) -> p h q", h=8), in0=tm[:, 0:512].rearrange("p (h q) -> p h q", h=8),
                                             in1=bc(dsk, [128, 8, 64], 2), op=ALU.mult), R=[tm, fin], W=[usq])
            P.pool(lambda e: e.tensor_tensor(out=u[:, 0:512], in0=u[:, 0:512], in1=usq[:, 0:512], op=ALU.add), R=[u, usq], W=[u])
            P.pool(lambda e: e.tensor_tensor(out=u[:, 0:512], in0=u[:, 0:512], in1=tm[:, 1664:2176], op=ALU.mult), R=[u, tm], W=[u])
            P.pool(lambda e: e.memset(ss[:], 0.0), W=[ss])
            P.act(lambda e: e.activation(out=usq[:, 0:512], in_=u[:, 0:512], func=AF.Square, accum_out=ss[:, 0:1]), R=[u, ss], W=[usq, ss])
            for h in range(4):
                P.act(lambda e, h=h: e.activation(out=usq[:, 512 + h * 128:512 + (h + 1) * 128], in_=u[:, 512 + h * 128:512 + (h + 1) * 128],
                                                  func=AF.Square, accum_out=ss[:, 1 + h:2 + h]), R=[u, ss], W=[usq, ss])
            P.act(lambda e: e.activation(out=ss[:, 0:1], in_=ss[:, 0:1], func=AF.Sqrt, bias=epsT[:, 0:1], scale=1.0 / 512), R=[ss, epsT], W=[ss])
            P.act(lambda e: e.activation(out=ss[:, 1:5], in_=ss[:, 1:5], func=AF.Sqrt, bias=epsT[:, 0:1], scale=1.0 / 128), R=[ss, epsT], W=[ss])
            P.dve(lambda e: e.reciprocal(out=ss[:, 0:5], in_=ss[:, 0:5]), R=[ss], W=[ss])
            P.dve(lambda e: e.scalar_tensor_tensor(out=yfin[:, 0:512], in0=u[:, 0:512], scalar=ss[:, 0:1], in1=nws, op0=ALU.mult, op1=ALU.mult),
                  R=[u, ss, fin], W=[yfin])
            u4 = u[:, 512:1024].rearrange("p (h v) -> p h v", h=4)
            P.pool(lambda e: e.tensor_tensor(out=u4, in0=u4, in1=bc(ss[:, 1:5], [128, 4, 128], 2), op=ALU.mult), R=[u, ss], W=[u])
            P.pool(lambda e: e.tensor_tensor(out=u4, in0=u4, in1=bc(nwg, [128, 4, 128], 1), op=ALU.mult), R=[u, fin], W=[u])
            P.dve(lambda e: e.tensor_tensor(out=yfin[:, 512:1024], in0=u[:, 512:1024], in1=tm[:, 2176:2688], op=ALU.mult), R=[u, tm], W=[yfin])
            pt = ob()
            ptb = pt.ap[:].bitcast(BF16)
            for c8 in range(8):
                P.pe(lambda e, c8=c8: e.transpose(out=ptb[:, c8 * 128:(c8 + 1) * 128], in_=yfin[:, c8 * 128:(c8 + 1) * 128], identity=identb[:]),
                     R=[yfin, identb], W=[pt])
            yT_ = yT[ci % 2]
            P.act(lambda e: e.copy(out=yT_[:].rearrange("p c l -> p (c l)"), in_=ptb[:, :]), R=[pt], W=[yT_])
            hc, tl = li // NTH, li % NTH
            P.dma("sp", YY.ap()[hc, tl], yT_[:].rearrange("p c l -> p (c l)"), R=[yT_])

        n = len(order)
        if L["debug"] and d == 0:
            n = min(DBG_NCH, n)
        load(0)
        if n > 1:
            load(1)
        if PARTS & 1:
            prep(0)
        for ci in range(n):
            if ci + 2 < n:
                load(ci + 2)
            if ci + 1 < n and PARTS & 1:
                prep(ci + 1)
            if PARTS & 2:
                seq(ci)
        if L["debug"] and d == 0:
            loc = locals()
            for nm_, ap_ in L["dbg"].items():
                if not nm_.startswith("s_"):
                    continue
                key = nm_[2:]
                t_ = loc[key] if key in loc else loc[key[:-1]][int(key[-1])]
                src = t_[:] if len(t_.ap.shape) == 2 else t_[:].rearrange("p h l -> p (h l)")
                L["final_ops"].append(P.dma("sp", ap_, src, R=[t_]))


def stage5(P, nc, L):
    HALF, NTH = L["HALF"], L["NTH"]
    xhalf, yout, MINE, PART, WO, WGU, WD, rowp = L["xhalf"], L["yout"], L["MINE"], L["PART"], L["WO"], L["WGU"], L["WD"], L["rowp"]
    g12row, modT, cst, psb, epsT, final_ops = L["g12row"], L["modT"], L["cst"], L["psb"], L["epsT"], L["final_ops"]
    ident = cst[:, 0:128]
    with ExitStack() as st:
        lnp = P.sb(st, "lnp", [128, 4 * D], F32)
        P.dma("sp", lnp[:], rowp[:, 696:4792].partition_broadcast(128), W=[lnp])
        xt = [P.sb(st, f"x5_{i}", [128, 4, D], F32) for i in range(2)]
        yT = P.sb(st, "yT5", [128, 2, 4, 1024], BF16)
        h2T = P.sb(st, "h2T", [128, 8, 512], BF16)
        actT = P.sb(st, "actT", [128, 22, 512], BF16)
        wbuf = [P.sb(st, f"wbuf{i}", [128, 8192], BF16) for i in range(3)]
        tmp = [P.sb(st, f"tmp5_{i}", [128, 512], F32) for i in range(2)]
        sgt = [P.sb(st, f"sgt{i}", [128, 512], F32) for i in range(2)]
        junk = P.sb(st, "junk", [128, D], F32)
        stat = [P.sb(st, f"stat{i}", [128, 8], F32) for i in range(2)]
        wcnt = [0]

        def wload(src, n):
            wb = wbuf[wcnt[0] % 3]
            wcnt[0] += 1
            P.dma("sp", wb[:, 0:n], src, W=[wb])
            return wb

        def resid(ps, x_, a, dh, gi, k):
            t = tmp[k % 2]
            P.dve(lambda e: e.tensor_tensor(out=t[:], in0=ps[:], in1=g12row[:, gi * D + dh * 512: gi * D + dh * 512 + 512], op=ALU.mult),
                  R=[ps, g12row], W=[t])
            xs_ = x_[:, a, dh * 512:(dh + 1) * 512]
            P.dve(lambda e: e.scalar_tensor_tensor(out=xs_, in0=xs_, scalar=ALPHA, in1=t[:], op0=ALU.mult, op1=ALU.add),
                  R=[x_, t], W=[x_])

        def layer_norm(x_, a, li, k):
            sx = stat[k % 2]
            xa = x_[:, a, :]
            g_, b_ = lnp[:, (2 * li) * D:(2 * li + 1) * D], lnp[:, (2 * li + 1) * D:(2 * li + 2) * D]
            P.pool(lambda e: e.memset(sx[:], 0.0), W=[sx])
            P.act(lambda e: e.activation(out=junk[:], in_=xa, func=AF.Identity, accum_out=sx[:, 0:1]), R=[x_, sx], W=[junk, sx])
            P.act(lambda e: e.activation(out=junk[:], in_=xa, func=AF.Square, accum_out=sx[:, 1:2]), R=[x_, sx], W=[junk, sx])
            P.dve(lambda e: e.tensor_scalar_mul(out=sx[:, 2:3], in0=sx[:, 0:1], scalar1=1.0 / D), R=[sx], W=[sx])
            P.dve(lambda e: e.tensor_tensor(out=sx[:, 3:4], in0=sx[:, 2:3], in1=sx[:, 2:3], op=ALU.mult), R=[sx], W=[sx])
            P.dve(lambda e: e.scalar_tensor_tensor(out=sx[:, 4:5], in0=sx[:, 1:2], scalar=1.0 / D, in1=sx[:, 3:4], op0=ALU.mult, op1=ALU.subtract),
                  R=[sx], W=[sx])
            P.act(lambda e: e.activation(out=sx[:, 5:6], in_=sx[:, 4:5], func=AF.Sqrt, bias=epsT[:, 1:2], scale=1.0), R=[sx, epsT], W=[sx])
            P.dve(lambda e: e.reciprocal(out=sx[:, 5:6], in_=sx[:, 5:6]), R=[sx], W=[sx])
            P.dve(lambda e: e.scalar_tensor_tensor(out=sx[:, 6:7], in0=sx[:, 2:3], scalar=-1.0, in1=sx[:, 5:6], op0=ALU.mult, op1=ALU.mult),
                  R=[sx], W=[sx])
            P.act(lambda e: e.activation(out=xa, in_=xa, func=AF.Identity, bias=sx[:, 6:7], scale=sx[:, 5:6]), R=[x_, sx], W=[x_])
            P.pool(lambda e: e.tensor_tensor(out=xa, in0=xa, in1=g_, op=ALU.mult), R=[x_, lnp], W=[x_])
            P.pool(lambda e: e.tensor_tensor(out=xa, in0=xa, in1=b_, op=ALU.add), R=[x_, lnp], W=[x_])

        nblk = HALF // 512
        for blk in range(nblk):
            x_ = xt[blk % 2]
            tl0 = blk * 4
            P.dma("sp", x_[:], xhalf[blk * 512:(blk + 1) * 512, :].rearrange("(a p) d -> p a d", p=128), W=[x_])
            yTa = T("yTa", yT.ap)
            yTb = T("yTb", yT.ap)
            yTa.last_w, yTa.readers = yT.last_w, dict(yT.readers)
            P.dma("sp", yT[:, 0], MINE.ap()[tl0:tl0 + 4].rearrange("t p f -> p t f"), W=[yT], semt=yTa)
            P.dma("sp", yT[:, 1], PART.ap()[tl0:tl0 + 4].rearrange("t p f -> p t f"), W=[yT], semt=yTb)
            k = 0
            for dh in range(2):
                wb = wload(WO[dh], 8192)
                wv = wb[:, 0:8192].rearrange("p (k n) -> p k n", k=16)
                for a in range(4):
                    ps = psb[k % 2]
                    for kc in range(16):
                        src, cc = kc // 8, kc % 8
                        P.pe(lambda e, ps=ps, kc=kc, src=src, cc=cc, a=a, wv=wv: e.matmul(
                            ps[:], lhsT=yT[:, src, a, cc * 128:(cc + 1) * 128], rhs=wv[:, kc, :], start=(kc == 0), stop=(kc == 15)),
                            R=[yT, wb], W=[ps])
                    resid(ps, x_, a, dh, 0, k)
                    k += 1
            for a in range(4):
                layer_norm(x_, a, 0, a)
            PX = psb[2]
            for fc in range(8):
                for a in range(4):
                    P.pe(lambda e, a=a, fc=fc, x_=x_: e.transpose(out=PX[:, a * 128:(a + 1) * 128], in_=x_[:, a, fc * 128:(fc + 1) * 128], identity=ident),
                         R=[x_, cst], W=[PX])
                P.act(lambda e, fc=fc: e.activation(out=h2T[:, fc, :], in_=PX[:], func=AF.Identity,
                                                    bias=modT[:, 32 + fc:33 + fc], scale=modT[:, 40 + fc:41 + fc]), R=[PX, modT], W=[h2T])
            for b11 in range(11):
                wb = wload(WGU[b11], 4096)
                wv = wb[:, 0:4096].rearrange("p (g k n) -> p g k n", g=2, k=8)
                for jj in range(2):
                    j = b11 * 2 + jj
                    pg, pu = psb[4 + j % 2], psb[6 + j % 2]
                    for kc in range(8):
                        P.pe(lambda e, pg=pg, kc=kc, jj=jj, wv=wv: e.matmul(pg[:], lhsT=wv[:, 0, kc, jj * 128:(jj + 1) * 128], rhs=h2T[:, kc, :],
                                                                            start=(kc == 0), stop=(kc == 7)), R=[wb, h2T], W=[pg])
                    for kc in range(8):
                        P.pe(lambda e, pu=pu, kc=kc, jj=jj, wv=wv: e.matmul(pu[:], lhsT=wv[:, 1, kc, jj * 128:(jj + 1) * 128], rhs=h2T[:, kc, :],
                                                                            start=(kc == 0), stop=(kc == 7)), R=[wb, h2T], W=[pu])
                    sg = sgt[j % 2]
                    P.act(lambda e, sg=sg, pg=pg: e.activation(out=sg[:], in_=pg[:], func=AF.Silu), R=[pg], W=[sg])
                    P.dve(lambda e, sg=sg, pu=pu, j=j: e.tensor_tensor(out=actT[:, j, :], in0=sg[:], in1=pu[:], op=ALU.mult), R=[sg, pu], W=[actT])
            for dh in range(2):
                accs = [psb[0], psb[1], psb[2], psb[3]]
                for jh in range(2):
                    wb = wload(WD[dh * 2 + jh], 5632)
                    wv = wb[:, 0:5632].rearrange("p (j n) -> p j n", j=11)
                    for a in range(4):
                        for j11 in range(11):
                            j = jh * 11 + j11
                            P.pe(lambda e, a=a, j=j, j11=j11, wv=wv, accs=accs: e.matmul(
                                accs[a][:], lhsT=actT[:, j, a * 128:(a + 1) * 128], rhs=wv[:, j11, :], start=(j == 0), stop=(j == 21)),
                                R=[actT, wb], W=[accs[a]])
                for a in range(4):
                    resid(accs[a], x_, a, dh, 1, a)
            for a in range(4):
                layer_norm(x_, a, 1, a)
            final_ops.append(P.dma("sp", yout[blk * 512:(blk + 1) * 512, :].rearrange("(a p) d -> p a d", p=128), x_[:], R=[x_]))


def host_consts():
    j = np.arange(128)[:, None]
    l = np.arange(128)[None, :]
    mats = [np.eye(128), (j <= l), (j > l), (j >= l), (j < l), np.ones((128, 128)), (j // 16 == l // 16)]
    for b in (16, 32, 64):
        mats.append((j // (2 * b) == l // (2 * b)) & (j % (2 * b) >= b) & (l % (2 * b) < b))
    for b in (16, 32, 64):
        mats.append(((j // (2 * b) == l // (2 * b)) & (j % (2 * b) >= b) & (l % (2 * b) < b)).T)
    return np.concatenate([m.astype(np.float32) for m in mats], axis=1)


def prep_core(inp, core, SEQ):
    b, h = core // 2, core % 2
    HALF = SEQ // 2
    f = lambda a: np.ascontiguousarray(a, dtype=np.float32)
    x, ctx = inp["x"], inp["ctx"]
    w_in = inp["w_in"][0]
    xs0 = 1024
    B0, C0 = 2048, 2304
    dt0 = 2560
    q0, k0, v0 = 2592, 3616, 4640
    g0 = 5664
    b0, a0 = 6688, 6704
    r = lambda s, n: np.arange(s, s + n)
    cols_cm = np.concatenate([r(xs0 + h * 512, 512), r(B0 + h * 128, 128), r(C0 + h * 128, 128),
                              r(q0 + h * 512, 512), r(k0 + h * 512, 512), r(v0 + h * 512, 512)])
    cols_tm = np.concatenate([r(h * 512, 512), r(g0 + h * 512, 512),
                              r(dt0 + 8 * h, 8), r(dt0 + 16 + 8 * h, 8),
                              r(a0 + 4 * h, 4), r(a0 + 8 + 4 * h, 4),
                              r(b0 + 4 * h, 4), r(b0 + 8 + 4 * h, 4)])
    cws, cwg = inp["conv_w_ssd"][0], inp["conv_w_gdn"][0]
    ssd_cols = np.concatenate([r(h * 512, 512), r(1024 + h * 128, 128), r(1280 + h * 128, 128)])
    gdn_cols = np.concatenate([r(h * 512, 512), r(1024 + h * 512, 512), r(2048 + h * 512, 512)])
    cw = np.concatenate([cws[:, ssd_cols], cwg[:, gdn_cols]], axis=1)
    cwT = cw.T.reshape(NCM, 128, 5).transpose(1, 0, 2).reshape(128, NCM * 5)
    cbT = inp["conv_b_ssd"][0][ssd_cols].reshape(6, 128).T
    rowp = np.concatenate([
        inp["dt_bias_ssd"][0][0, 8 * h:8 * h + 8], inp["dt_bias_ssd"][0][1, 8 * h:8 * h + 8],
        inp["dt_bias_gdn"][0][0, 4 * h:4 * h + 4], inp["dt_bias_gdn"][0][1, 4 * h:4 * h + 4],
        inp["a_log_ssd"][0][0, 8 * h:8 * h + 8], inp["a_log_ssd"][0][1, 8 * h:8 * h + 8],
        inp["a_log_gdn"][0][0, 4 * h:4 * h + 4], inp["a_log_gdn"][0][1, 4 * h:4 * h + 4],
        inp["d_skip_ssd"][0][8 * h:8 * h + 8],
        inp["norm_w_ssd"][0][h * 512:(h + 1) * 512], inp["norm_w_gdn"][0],
        inp["ln1_g"][0], inp["ln1_b"][0], inp["ln2_g"][0], inp["ln2_b"][0]])[None, :]
    cv = np.stack([inp["c"][b], inp["c_ctx"]])
    cvT = cv.reshape(2, 8, 128).transpose(2, 0, 1).reshape(128, 16)
    wo = inp["w_out"][0]
    own = np.concatenate([r(h * 512, 512), r(1024 + h * 512, 512)])
    oth = np.concatenate([r((1 - h) * 512, 512), r(1024 + (1 - h) * 512, 512)])
    return {
        "xin": f(np.concatenate([ctx[b], x[b]], axis=0)),
        "xhalf": f(x[b, h * HALF:(h + 1) * HALF]),
        "cvecT": f(cvT),
        "w_ada": f(inp["w_ada"][0]), "b_ada": f(inp["b_ada"][0][None, :]),
        "w_cm": f(w_in[:, cols_cm]), "w_tm": f(w_in[:, cols_tm]),
        "convw": f(cwT), "convb": f(cbT), "rowp": f(rowp), "consts": host_consts(),
        "w_out": f(wo[np.concatenate([own, oth])]),
        "w_gate": f(inp["w_ffn_gate"][0]), "w_up": f(inp["w_ffn_up"][0]), "w_down": f(inp["w_ffn_down"][0]),
    }


_CACHE = {}


def run(inputs, SEQ, debug=False, stop_after=99, ncores=8):
    key = (SEQ, debug, stop_after)
    if key not in _CACHE:
        _CACHE[key] = build_program(SEQ, debug, stop_after)
    nc, stats = _CACHE[key]
    in_maps = [prep_core(inputs, c, SEQ) for c in range(ncores)]
    res = run_bass_kernel_spmd(nc, in_maps, core_ids=list(range(ncores)))
    return res.results, stats


def kernel(**inputs):
    SEQ = inputs["x"].shape[1]
    B = inputs["x"].shape[0]
    results, _ = run(inputs, SEQ)
    HALF = SEQ // 2
    out = np.empty((B, SEQ, D), np.float32)
    for c in range(8):
        b, h = c // 2, c % 2
        out[b, h * HALF:(h + 1) * HALF] = results[c]["yout"]
    return out
```

```python
import numpy as np
from contextlib import ExitStack
import concourse.bass as bass
import concourse.mybir as mybir
from concourse.bass_utils import run_bass_kernel_spmd

F32 = mybir.dt.float32
BF16 = mybir.dt.bfloat16
AF = mybir.ActivationFunctionType
ALU = mybir.AluOpType
AX = mybir.AxisListType

D = 1024
CTX = 256
GRID_W = 64
DFF = 2816
NCM = 18
NTMC = 1056
TMW = 2688
SMW = 64
ALPHA = 2.0 ** 0.25
LN_EPS = 1e-5
RMS_EPS = 1e-6
EPOCH = 30000
NCST = 13
PARTS = 3
DBG_NCH = 10 ** 6


class T:
    __slots__ = ("name", "ap", "last_w", "readers", "dma_readers", "sem", "cnt", "last_dma", "excl")

    def __init__(self, name, ap):
        self.excl = False
        self.name = name
        self.ap = ap
        self.last_w = None
        self.readers = {}
        self.dma_readers = []
        self.sem = None
        self.cnt = 0
        self.last_dma = None

    def __getitem__(self, k):
        return self.ap[k]


class Op:
    __slots__ = ("eng", "fn", "deps", "needs_inc", "inc_val", "epoch", "is_dma", "sem", "val", "amt")

    def __init__(self, eng, fn, is_dma=False):
        self.eng = eng
        self.fn = fn
        self.deps = []
        self.needs_inc = False
        self.inc_val = 0
        self.epoch = 0
        self.is_dma = is_dma
        self.sem = None
        self.val = 0
        self.amt = 16


class Prog:
    ENGS = ("sp", "act", "pool", "dve", "pe")

    def __init__(self, nc, stack):
        self.nc = nc
        self.stack = stack
        self.ops = {e: [] for e in self.ENGS}
        self.same_engine_sync = {"sp": False, "act": True, "pool": True, "dve": True, "pe": False}
        self.nsem = 0
        self.dma_open = []
        self.pending = {e: [] for e in self.ENGS}
        self.sem_pool = {}

    def sb(self, st, name, shape, dt):
        return T(name, st.enter_context(self.nc.sbuf_tensor(name, list(shape), dt)))

    def ps(self, st, name, shape, dt):
        t = T(name, st.enter_context(self.nc.psum_tensor(name, list(shape), dt)))
        t.excl = True
        return t

    def new_sem(self, name):
        self.nsem += 1
        return self.stack.enter_context(self.nc.semaphore(name))

    def _track(self, op, R, W, extra=()):
        deps = list(extra)
        for t in R:
            if t.last_w is not None:
                deps.append(t.last_w)
            if t.excl:
                deps.extend(o for en, o in t.readers.items() if en != op.eng)
        for t in W:
            if t.last_w is not None:
                deps.append(t.last_w)
            deps.extend(t.readers.values())
            deps.extend(t.dma_readers)
        deps.extend(self.pending[op.eng])
        self.pending[op.eng] = []
        seen = set()
        for d in deps:
            if d is op or id(d) in seen:
                continue
            seen.add(id(d))
            if (not d.is_dma) and d.eng == op.eng and not op.is_dma and not self.same_engine_sync[op.eng]:
                continue
            if not d.is_dma:
                d.needs_inc = True
            op.deps.append(d)
        for t in R:
            if op.is_dma:
                t.dma_readers.append(op)
            else:
                t.readers[op.eng] = op
        for t in W:
            t.last_w = op
            t.readers = {}
            t.dma_readers = []

    def op(self, eng, fn, R=(), W=()):
        o = Op(eng, fn)
        self._track(o, R, W)
        self.ops[eng].append(o)
        return o

    def pe(self, fn, R=(), W=()):
        return self.op("pe", fn, R, W)

    def act(self, fn, R=(), W=()):
        return self.op("act", fn, R, W)

    def dve(self, fn, R=(), W=()):
        return self.op("dve", fn, R, W)

    def pool(self, fn, R=(), W=()):
        return self.op("pool", fn, R, W)

    def dma(self, eng, out_ap=None, in_ap=None, R=(), W=(), semt=None, fn=None, amt=16, **kw):
        if fn is None:
            fn = lambda e: e.dma_start(out=out_ap, in_=in_ap, **kw)
        o = Op(eng, fn, is_dma=True)
        o.amt = amt
        if semt is None:
            semt = (list(W) + list(R))[0]
        if semt.sem is None:
            key = semt.name
            if key not in self.sem_pool:
                self.sem_pool[key] = [self.new_sem("d_" + key), 0]
            semt.sem = self.sem_pool[key]
        semt.sem[1] += amt
        o.sem = semt.sem[0]
        o.val = semt.sem[1]
        extra = [semt.last_dma] if semt.last_dma is not None else []
        semt.last_dma = o
        self._track(o, R, W, extra)
        self.ops[eng].append(o)
        self.dma_open.append(o)
        return o

    def barrier(self):
        deps = list(self.dma_open)
        for e in self.ENGS:
            for o in reversed(self.ops[e]):
                if not o.is_dma:
                    deps.append(o)
                    break
        self.dma_open = []
        for e in self.ENGS:
            self.pending[e] = list(deps)

    def emit(self, final_ops=()):
        nc = self.nc
        esems = {}
        for e in self.ENGS:
            n = 0
            for o in self.ops[e]:
                if o.is_dma or not o.needs_inc:
                    continue
                o.epoch = n // EPOCH
                o.inc_val = n % EPOCH + 1
                n += 1
            nep = (n + EPOCH - 1) // EPOCH
            esems[e] = [self.new_sem(f"s_{e}{i}") for i in range(max(nep, 1))]
        block = self.stack.enter_context(nc.Block())
        deco = {"sp": block.sync, "act": block.scalar, "pool": block.gpsimd, "dve": block.vector, "pe": block.tensor}
        nwaits = {e: 0 for e in self.ENGS}

        def make(eng):
            def body(e):
                seen = {}
                maxep = {}

                def wait_all(deps):
                    need = {}
                    for d in deps:
                        if d.is_dma:
                            key, val, sem = ("d", id(d.sem)), d.val, d.sem
                        else:
                            key, val, sem = (d.eng, d.epoch), d.inc_val, esems[d.eng][d.epoch]
                        if seen.get(key, 0) >= val:
                            continue
                        if key not in need or need[key][0] < val:
                            need[key] = (val, sem)
                    for key, (val, sem) in need.items():
                        if key[0] != "d":
                            if any(k[0] == key[0] and k[1] > key[1] for k in list(seen) + list(need) if k[0] != "d"):
                                continue
                        seen[key] = val
                        e.wait_ge(sem, val)
                        nwaits[eng] += 1

                def wait_for(d):
                    wait_all([d])

                for o in self.ops[eng]:
                    wait_all(o.deps)
                    ins = o.fn(e)
                    if o.is_dma:
                        ins.then_inc(o.sem, o.amt)
                    elif o.needs_inc:
                        ins.then_inc(esems[eng][o.epoch], 1)
                if eng == "sp":
                    for d in final_ops:
                        wait_for(d)
                        e.nop()
            return body

        for eng in self.ENGS:
            if self.ops[eng] or eng == "sp":
                deco[eng](make(eng))
        self.nwaits = nwaits
        return block


def bc(ap, shape, axis):
    return ap.unsqueeze(axis).to_broadcast(list(shape))


def build_program(SEQ, debug=False, stop_after=99):
    TT = CTX + SEQ
    NTT = TT // 128
    NCT = CTX // 128
    NLT = SEQ // 128
    HALF = SEQ // 2
    NTH = HALF // 128
    nc = bass.Bass("TRN2", target_bir_lowering=False)
    dt = nc.dram_tensor
    xin = dt("xin", [TT, D], F32, kind="ExternalInput").ap()
    xhalf = dt("xhalf", [HALF, D], F32, kind="ExternalInput").ap()
    cvecT = dt("cvecT", [128, 16], F32, kind="ExternalInput").ap()
    w_ada = dt("w_ada", [D, 6 * D], F32, kind="ExternalInput").ap()
    b_ada = dt("b_ada", [1, 6 * D], F32, kind="ExternalInput").ap()
    w_cm = dt("w_cm", [D, NCM * 128], F32, kind="ExternalInput").ap()
    w_tm = dt("w_tm", [D, NTMC], F32, kind="ExternalInput").ap()
    convw = dt("convw", [128, NCM * 5], F32, kind="ExternalInput").ap()
    convb = dt("convb", [128, 6], F32, kind="ExternalInput").ap()
    rowp = dt("rowp", [1, 4792], F32, kind="ExternalInput").ap()
    consts = dt("consts", [128, NCST * 128], F32, kind="ExternalInput").ap()
    w_out = dt("w_out", [2 * D, D], F32, kind="ExternalInput").ap()
    w_gate = dt("w_gate", [D, DFF], F32, kind="ExternalInput").ap()
    w_up = dt("w_up", [D, DFF], F32, kind="ExternalInput").ap()
    w_down = dt("w_down", [DFF, D], F32, kind="ExternalInput").ap()
    yout = dt("yout", [HALF, D], F32, kind="ExternalOutput").ap()
    CMs = dt("CMs", [NTT, 128, 10 * 128], BF16).ap()
    TMs = dt("TMs", [TT, TMW], BF16).ap()
    SMs = dt("SMs", [TT, SMW], F32).ap()
    YFs = dt("YFs", [SEQ, 1024], F32).ap()
    YY = dt("YY", [2, NTH, 128, 1024], BF16)
    TPK = min(NTH, 8)
    NCC = NTH // TPK
    ZO = dt("ZO", [NCC, 2, TPK, 128, 1024], BF16)
    ZIN = dt("ZIN", [NTH, 128, 1024], BF16)
    MINE = dt("MINE", [NTH, 128, 1024], BF16)
    PART = dt("PART", [NTH, 128, 1024], BF16)
    WO = dt("WO", [2, 128, 16 * 512], BF16).ap()
    WGU = dt("WGU", [11, 128, 2 * 8 * 256], BF16).ap()
    WD = dt("WD", [4, 128, 11 * 512], BF16).ap()
    dbg = {}
    if debug:
        dbg["modT"] = dt("dbg_modT", [128, 64], F32, kind="ExternalOutput").ap()
        dbg["CM"] = dt("dbg_CM", [NTT, 128, 10 * 128], BF16, kind="ExternalOutput").ap()
        dbg["TM"] = dt("dbg_TM", [TT, TMW], BF16, kind="ExternalOutput").ap()
        dbg["SM"] = dt("dbg_SM", [TT, SMW], F32, kind="ExternalOutput").ap()
        dbg["YF"] = dt("dbg_YF", [SEQ, 1024], F32, kind="ExternalOutput").ap()
        dbg["YY"] = dt("dbg_YY", [2 * NTH * 128, 1024], BF16, kind="ExternalOutput").ap()
        for nm_, dt_ in (("E", F32), ("Es", F32), ("tks", F32), ("S", F32), ("t3", F32), ("vb0", F32)):
            dbg["s_" + nm_] = dt("dbg_s_" + nm_, [128, 512], dt_, kind="ExternalOutput").ap()
        for nm_ in ("Ei", "Ak0", "Ak1", "Nk0", "Nk1", "Pf0", "qkT0", "kd0", "rv", "vnb", "Sb", "qk", "Pk0", "Pk1", "Afull", "Nfull", "Ym", "Dk0", "Dk1", "Wk0", "Wk1"):
            dbg["s_" + nm_] = dt("dbg_s_" + nm_, [128, 512], BF16, kind="ExternalOutput").ap()
        dbg["s_eg0"] = dt("dbg_s_eg0", [128, 12], F32, kind="ExternalOutput").ap()

    with ExitStack() as top:
        P = Prog(nc, top)
        cst = P.sb(top, "cst", [128, NCST * 128], F32)
        identb = P.sb(top, "identb", [128, 128], BF16)
        modT = P.sb(top, "modT", [128, 64], F32)
        g12row = P.sb(top, "g12row", [128, 2 * D], F32)
        psb = [P.ps(top, f"psb{i}", [128, 512], F32) for i in range(8)]
        epsT = P.sb(top, "epsT", [128, 4], F32)
        P.pool(lambda e: e.memset(epsT[:, 0:1], RMS_EPS), W=[epsT])
        P.pool(lambda e: e.memset(epsT[:, 1:2], LN_EPS), W=[epsT])
        ident = cst[:, 0:128]
        Uincl, Lstrict, Lincl, Ustrict, ones = (cst[:, i * 128:(i + 1) * 128] for i in range(1, 6))
        P.dma("sp", cst[:], consts[:, :], W=[cst])
        P.dve(lambda e: e.tensor_copy(out=identb[:], in_=ident), R=[cst], W=[identb])
        final_ops = []

        with ExitStack() as st:
            ccol = P.sb(st, "ccol", [128, 2, 8], F32)
            csil = P.sb(st, "csil", [128, 2, 8], F32)
            crep = P.sb(st, "crep", [128, 2, 8, 128], F32)
            barow = P.sb(st, "barow", [128, 6 * D], F32)
            modrow = P.sb(st, "modrow", [128, 4 * D], F32)
            cmodrow = P.sb(st, "cmodrow", [128, 2 * D], F32)
            wab = [P.sb(st, f"wab{i}", [128, 8, 512], F32) for i in range(2)]
            P.dma("sp", ccol[:].rearrange("p v k -> p (v k)"), cvecT[:, :], W=[ccol])
            P.dma("sp", barow[:], b_ada.partition_broadcast(128), W=[barow])
            P.act(lambda e: e.activation(out=csil[:], in_=ccol[:], func=AF.Silu), R=[ccol], W=[csil])
            P.dve(lambda e: e.tensor_copy(out=crep[:].rearrange("p v k m -> p (v k) m"),
                                          in_=bc(csil[:].rearrange("p v k -> p (v k)"), [128, 16, 128], 2)),
                  R=[csil], W=[crep])
            w_ada_v = w_ada.rearrange("(kc p) n -> p kc n", p=128)
            for nb in range(12):
                wb = wab[nb % 2]
                P.dma("sp", wb[:], w_ada_v[:, :, nb * 512:(nb + 1) * 512], W=[wb])
                pl, pc = psb[(2 * nb) % 8], psb[(2 * nb + 1) % 8]
                for kc in range(8):
                    P.pe(lambda e, kc=kc, wb=wb, pl=pl: e.matmul(pl[:], lhsT=crep[:, 0, kc, :], rhs=wb[:, kc, :],
                                                                 start=(kc == 0), stop=(kc == 7)), R=[crep, wb], W=[pl])
                if nb < 4:
                    for kc in range(8):
                        P.pe(lambda e, kc=kc, wb=wb, pc=pc: e.matmul(pc[:], lhsT=crep[:, 1, kc, :], rhs=wb[:, kc, :],
                                                                     start=(kc == 0), stop=(kc == 7)), R=[crep, wb], W=[pc])
                seg = nb // 2
                half = nb % 2
                bsl = barow[:, nb * 512:(nb + 1) * 512]
                if seg in (0, 1, 3, 4):
                    mi = {0: 0, 1: 1, 3: 2, 4: 3}[seg]
                    dst = modrow[:, mi * D + half * 512: mi * D + half * 512 + 512]
                    P.dve(lambda e, dst=dst, pl=pl, bsl=bsl: e.tensor_tensor(out=dst, in0=pl[:], in1=bsl, op=ALU.add),
                          R=[pl, barow], W=[modrow])
                    if seg in (1, 4):
                        P.dve(lambda e, dst=dst: e.tensor_scalar_add(out=dst, in0=dst, scalar1=1.0), R=[modrow], W=[modrow])
                else:
                    gi = 0 if seg == 2 else 1
                    dst = g12row[:, gi * D + half * 512: gi * D + half * 512 + 512]
                    P.dve(lambda e, dst=dst, pl=pl, bsl=bsl: e.tensor_tensor(out=dst, in0=pl[:], in1=bsl, op=ALU.add),
                          R=[pl, barow], W=[g12row])
                if nb < 4:
                    dst = cmodrow[:, nb * 512:(nb + 1) * 512]
                    P.dve(lambda e, dst=dst, pc=pc, bsl=bsl: e.tensor_tensor(out=dst, in0=pc[:], in1=bsl, op=ALU.add),
                          R=[pc, barow], W=[cmodrow])
                    if nb >= 2:
                        P.dve(lambda e, dst=dst: e.tensor_scalar_add(out=dst, in0=dst, scalar1=1.0), R=[cmodrow], W=[cmodrow])
            srcs = [(modrow, 0), (modrow, 1), (cmodrow, 0), (cmodrow, 1), (modrow, 2), (modrow, 3)]
            for v, (src, si) in enumerate(srcs):
                for g in range(2):
                    pt = psb[(2 * v + g) % 8]
                    for q in range(4):
                        fc = g * 4 + q
                        P.pe(lambda e, pt=pt, q=q, src=src, off=si * D + fc * 128: e.transpose(
                            out=pt[:, q * 128:(q + 1) * 128], in_=src[:, off:off + 128], identity=ident),
                            R=[src, cst], W=[pt])
                    P.act(lambda e, pt=pt, v=v, g=g: e.copy(
                        out=modT[:, 8 * v + 4 * g: 8 * v + 4 * g + 4],
                        in_=pt[:].rearrange("p (q m) -> p q m", q=4)[:, :, 0]), R=[pt], W=[modT])
            if debug:
                final_ops.append(P.dma("sp", dbg["modT"], modT[:], R=[modT]))

            wst = [P.sb(st, f"wst{i}", [128, 6144], F32) for i in range(2)]
            wsb = [P.sb(st, f"wsb{i}", [128, 6144], BF16) for i in range(2)]
            cnt = [0]

            def convert(src_aps, dst_ap, n):
                i = cnt[0] % 2
                cnt[0] += 1
                s, b = wst[i], wsb[i]
                off = 0
                for sap, shape in src_aps:
                    sz = int(np.prod(shape))
                    view = s[:, off:off + sz]
                    if len(shape) == 2:
                        view = view.rearrange("p (a b) -> p a b", a=shape[0])
                    P.dma("sp", view, sap, W=[s])
                    off += sz
                ei = cnt[0] % 3
                if ei == 2:
                    P.act(lambda e, s=s, b=b: e.copy(out=b[:, 0:n], in_=s[:, 0:n]), R=[s], W=[b])
                else:
                    (P.dve, P.pool)[ei](lambda e, s=s, b=b: e.tensor_copy(out=b[:, 0:n], in_=s[:, 0:n]), R=[s], W=[b])
                P.dma("sp", dst_ap, b[:, 0:n], R=[b])

            if stop_after >= 5:
                wo_v = w_out.rearrange("(kc p) n -> p kc n", p=128)
                for dh in range(2):
                    for kh in range(2):
                        convert([(wo_v[:, kh * 8:(kh + 1) * 8, dh * 512:(dh + 1) * 512], (8, 512))],
                                WO[dh, :, kh * 4096:(kh + 1) * 4096], 4096)
                wg_v = w_gate.rearrange("(kc p) n -> p kc n", p=128)
                wu_v = w_up.rearrange("(kc p) n -> p kc n", p=128)
                for blk in range(11):
                    convert([(wg_v[:, :, blk * 256:(blk + 1) * 256], (8, 256)),
                             (wu_v[:, :, blk * 256:(blk + 1) * 256], (8, 256))], WGU[blk, :, :], 4096)
                wd_v = w_down.rearrange("(j p) n -> p j n", p=128)
                for dh in range(2):
                    for jh in range(2):
                        convert([(wd_v[:, jh * 11:(jh + 1) * 11, dh * 512:(dh + 1) * 512], (11, 512))],
                                WD[dh * 2 + jh, :, :], 5632)
        P.barrier()

        if stop_after >= 1:
            stage1(P, nc, locals())
        P.barrier()
        if stop_after >= 2:
            scan_pass(P, nc, locals(), 0)
            P.barrier()
        if stop_after >= 3:
            scan_pass(P, nc, locals(), 1)
            P.barrier()
        if debug and stop_after >= 1:
            dd = T("dd", None)
            final_ops.append(P.dma("sp", dbg["CM"], CMs, semt=dd))
            final_ops.append(P.dma("sp", dbg["TM"], TMs, semt=dd))
            final_ops.append(P.dma("sp", dbg["SM"], SMs, semt=dd))
            if stop_after >= 2:
                final_ops.append(P.dma("sp", dbg["YF"], YFs, semt=dd))
                final_ops.append(P.dma("sp", dbg["YY"], YY.ap().rearrange("a t p f -> (a t p) f"), semt=dd))
            P.barrier()
        if stop_after >= 4:
            cps = [T(f"cp{i}", None) for i in range(4)]
            CW = 8192
            ccnt = [0]

            def dyn_copy(dst3, src4, sel, fresh):
                nt_ = dst3.shape[0]
                nr = nt_ * 128 * 1024 // CW
                dflat = dst3.rearrange("t p f -> (t p f)").rearrange("(r c) -> r c", c=CW)
                def fn(e, fresh=fresh):
                    if fresh:
                        pid = e.partition_id()
                        P.dyn = {0: e.snap(pid % 2), 1: e.snap(1 - pid % 2)}
                    sflat = src4[bass.ds(P.dyn[sel], 1)].rearrange("a t p f -> (a t p f)").rearrange("(r c) -> r c", c=CW)
                    return e.dma_start(out=dflat, in_=sflat)
                ccnt[0] += 1
                P.dma("sp", semt=cps[ccnt[0] % 4], fn=fn)

            dyn_copy(ZIN.ap(), YY.ap(), 1, True)
            P.barrier()
            cct = T("cc", None)
            for k in range(NCC):
                P.dma("pool", semt=cct, amt=1, fn=lambda e, k=k: e.collective_compute(
                    "AllGather", ALU.bypass, replica_groups=[[0, 1], [2, 3], [4, 5], [6, 7]],
                    ins=[ZIN.ap()[k * TPK:(k + 1) * TPK].rearrange("t p f -> (t p) f").opt()],
                    outs=[ZO.ap()[k].rearrange("a t p f -> (a t p) f").opt()]))
            P.barrier()
            dyn_copy(MINE.ap(), YY.ap(), 0, True)
            for k in range(NCC):
                dyn_copy(PART.ap()[k * TPK:(k + 1) * TPK], ZO.ap()[k], 1, False)
            P.barrier()
        if stop_after >= 5:
            stage5(P, nc, locals())
        else:
            with ExitStack() as st:
                tb = P.sb(st, "tb", [128, D], F32)
                for a in range(HALF // 128):
                    P.dma("sp", tb[:], xhalf[a * 128:(a + 1) * 128, :], W=[tb])
                    final_ops.append(P.dma("sp", yout[a * 128:(a + 1) * 128, :], tb[:], R=[tb]))
        P.emit(final_ops=final_ops + locals().get("_final", []))
        stats = dict(nsem=P.nsem, nops={e: len(P.ops[e]) for e in P.ENGS}, nwaits=P.nwaits)
    return nc, stats


def stage1(P, nc, L):
    TT, NTT, NCT = L["TT"], L["NTT"], L["NCT"]
    xin, w_cm, w_tm, convw, convb, rowp = L["xin"], L["w_cm"], L["w_tm"], L["convw"], L["convb"], L["rowp"]
    CMs, TMs, SMs = L["CMs"], L["TMs"], L["SMs"]
    cst, identb, modT, psb = L["cst"], L["identb"], L["modT"], L["psb"]
    ident = cst[:, 0:128]
    ones = cst[:, 5 * 128:6 * 128]
    with ExitStack() as st:
        wcm = P.sb(st, "wcm", [128, 8, NCM * 128], BF16)
        wtm = P.sb(st, "wtm", [128, 8, NTMC], BF16)
        wld = [P.sb(st, f"wld{i}", [128, 1152], F32) for i in range(2)]
        cw = P.sb(st, "cw", [128, NCM, 5], F32)
        cb = P.sb(st, "cb", [128, 6], F32)
        spb = P.sb(st, "spb", [128, 48], F32)
        amul = P.sb(st, "amul", [128, 24], F32)
        xt = [P.sb(st, f"xt{i}", [128, 4, D], F32) for i in range(2)]
        hT = [P.sb(st, f"hT{i}", [128, 8, 512], BF16) for i in range(2)]
        pad = [P.sb(st, f"pad{i}", [128, 544], F32) for i in range(3)]
        acc = [P.sb(st, f"acc{i}", [128, 512], F32) for i in range(3)]
        ptmp = P.sb(st, "ptmp", [128, 512], F32)
        sv = [P.sb(st, f"sv{i}", [128, 512], F32) for i in range(2)]
        sq = [P.sb(st, f"sq{i}", [128, 512], F32) for i in range(2)]
        rr = [P.sb(st, f"rr{i}", [128, 512], F32) for i in range(2)]
        tbf = [P.sb(st, f"tbf{i}", [128, 512], BF16) for i in range(3)]
        cmst = [P.sb(st, f"cmst{i}", [128, 4, 10, 128], BF16) for i in range(2)]
        tmst = [P.sb(st, f"tmst{i}", [128, 4, TMW], BF16) for i in range(1)]
        smst = [P.sb(st, f"smst{i}", [128, 4, SMW], F32) for i in range(2)]
        smr = [P.sb(st, f"smr{i}", [128, 32], F32) for i in range(2)]
        wcm_v = w_cm.rearrange("(kc p) n -> p kc n", p=128)
        wtm_v = w_tm.rearrange("(kc p) n -> p kc n", p=128)
        for kc in range(8):
            for hf in range(2):
                w = wld[hf]
                P.dma("sp", w[:], wcm_v[:, kc, hf * 1152:(hf + 1) * 1152], W=[w])
                (P.dve if hf == 0 else P.pool)(lambda e, w=w, kc=kc, hf=hf: e.tensor_copy(
                    out=wcm[:, kc, hf * 1152:(hf + 1) * 1152], in_=w[:]), R=[w], W=[wcm])
        for kc in range(8):
            w = wld[kc % 2]
            P.dma("sp", w[:, 0:NTMC], wtm_v[:, kc, :], W=[w])
            (P.dve if kc % 2 == 0 else P.pool)(lambda e, w=w, kc=kc: e.tensor_copy(out=wtm[:, kc, :], in_=w[:, 0:NTMC]), R=[w], W=[wtm])
        P.dma("sp", cw[:].rearrange("p c k -> p (c k)"), convw[:, :], W=[cw])
        P.dma("sp", cb[:], convb[:, :], W=[cb])
        P.dma("sp", spb[:], rowp[:, 0:48].partition_broadcast(128), W=[spb])
        P.act(lambda e: e.activation(out=amul[:], in_=spb[:, 24:48], func=AF.Exp), R=[spb], W=[amul])
        P.dve(lambda e: e.tensor_scalar_mul(out=amul[:], in0=amul[:], scalar1=-1.0), R=[amul], W=[amul])
        for p_ in pad:
            P.pool(lambda e, p_=p_: e.memset(p_[:], 0.0), W=[p_])

        blocks = [(0, CTX, CTX, 2)]
        t0 = CTX
        while t0 < TT:
            blocks.append((t0, 512, GRID_W, 0))
            t0 += 512
        PX, PA, PB, PN, PZ0, PZ1, PSm, PTr = L["psb"]
        ptr_bf = PTr.ap[:].bitcast(BF16)
        for bi, (t0, ntok, rowlen, mv) in enumerate(blocks):
            nt = ntok // 128
            nrow = ntok // rowlen
            x_, h_ = xt[bi % 2], hT[bi % 2]
            if bi == 1:
                for p_ in pad:
                    P.pool(lambda e, p_=p_: e.memset(p_[:], 0.0), W=[p_])
            cm_, tm_, sm_ = cmst[bi % 2], tmst[0], smst[bi % 2]
            P.dma("sp", x_[:, 0:nt, :], xin[t0:t0 + ntok, :].rearrange("(a p) d -> p a d", p=128), W=[x_])
            for fc in range(8):
                for a in range(nt):
                    P.pe(lambda e, a=a, fc=fc, x_=x_: e.transpose(out=PX[:, a * 128:(a + 1) * 128],
                                                                  in_=x_[:, a, fc * 128:(fc + 1) * 128], identity=ident),
                         R=[x_, cst], W=[PX])
                P.act(lambda e, fc=fc, h_=h_, mv=mv, ntok=ntok: e.activation(
                    out=h_[:, fc, 0:ntok], in_=PX[:, 0:ntok], func=AF.Identity,
                    bias=modT[:, 8 * mv + fc: 8 * mv + fc + 1], scale=modT[:, 8 * (mv + 1) + fc: 8 * (mv + 1) + fc + 1]),
                    R=[PX, modT], W=[h_])
            for cc in range(NCM):
                pp = (PA, PB)[cc % 2]
                for kc in range(8):
                    P.pe(lambda e, kc=kc, cc=cc, pp=pp, h_=h_, ntok=ntok: e.matmul(
                        pp[:, 0:ntok], lhsT=wcm[:, kc, cc * 128:(cc + 1) * 128], rhs=h_[:, kc, 0:ntok],
                        start=(kc == 0), stop=(kc == 7)), R=[wcm, h_], W=[pp])
                pd, ac = pad[cc % 3], acc[cc % 3]
                pdv = pd[:, 0:nrow * (rowlen + 4)].rearrange("p (r l) -> p r l", r=nrow)
                P.act(lambda e, pdv=pdv, pp=pp, ntok=ntok, nrow=nrow, rowlen=rowlen: e.copy(
                    out=pdv[:, :, 2:2 + rowlen], in_=pp[:, 0:ntok].rearrange("p (r l) -> p r l", r=nrow)), R=[pp], W=[pd])
                acv = ac[:, 0:ntok].rearrange("p (r l) -> p r l", r=nrow)
                if True:
                    P.dve(lambda e, acv=acv, pdv=pdv, cc=cc, rowlen=rowlen: e.tensor_scalar_mul(
                        out=acv, in0=pdv[:, :, 0:rowlen], scalar1=cw[:, cc, 0:1]), R=[pd, cw], W=[ac])
                    for k in range(1, 5):
                        P.dve(lambda e, acv=acv, pdv=pdv, cc=cc, k=k, rowlen=rowlen: e.scalar_tensor_tensor(
                            out=acv, in0=pdv[:, :, k:k + rowlen], scalar=cw[:, cc, k:k + 1], in1=acv,
                            op0=ALU.mult, op1=ALU.add), R=[pd, cw, ac], W=[ac])
                else:
                    ptv = ptmp[:, 0:ntok].rearrange("p (r l) -> p r l", r=nrow)
                    P.pool(lambda e, acv=acv, pdv=pdv, cc=cc, rowlen=rowlen: e.tensor_scalar_mul(
                        out=acv, in0=pdv[:, :, 0:rowlen], scalar1=cw[:, cc, 0:1]), R=[pd, cw], W=[ac])
                    for k in range(1, 5):
                        P.pool(lambda e, ptv=ptv, pdv=pdv, cc=cc, k=k, rowlen=rowlen: e.tensor_scalar_mul(
                            out=ptv, in0=pdv[:, :, k:k + rowlen], scalar1=cw[:, cc, k:k + 1]), R=[pd, cw], W=[ptmp])
                        P.pool(lambda e, acv=acv, ptv=ptv: e.tensor_tensor(out=acv, in0=acv, in1=ptv, op=ALU.add),
                               R=[ac, ptmp], W=[ac])
                kind = ("xs" if cc < 4 else "B" if cc == 4 else "C" if cc == 5 else "q" if cc < 10 else "k" if cc < 14 else "v")
                if kind in ("xs", "v", "B", "C"):
                    tb = tbf[cc % 3]
                    if kind == "C":
                        dst = cm_[:, 0:nt, 0, :]
                    elif kind == "B":
                        dst = cm_[:, 0:nt, 1, :]
                    else:
                        dst = tb[:, 0:ntok].rearrange("p (a l) -> p a l", a=nt)
                    Wt = [cm_] if kind in ("B", "C") else [tb]
                    if cc < 6:
                        P.act(lambda e, dst=dst, ac=ac, cc=cc, ntok=ntok, nt=nt: e.activation(
                            out=dst, in_=ac[:, 0:ntok].rearrange("p (a l) -> p a l", a=nt), func=AF.Silu,
                            bias=cb[:, cc:cc + 1], scale=1.0), R=[ac, cb], W=Wt)
                    else:
                        P.act(lambda e, dst=dst, ac=ac, ntok=ntok, nt=nt: e.activation(
                            out=dst, in_=ac[:, 0:ntok].rearrange("p (a l) -> p a l", a=nt), func=AF.Silu),
                            R=[ac], W=Wt)
                    if kind == "C":
                        continue
                    tmoff = {"xs": cc * 128, "B": 512, "v": 1152 + (cc - 14) * 128}[kind]
                    for a in range(nt):
                        src = cm_[:, a, 1, :] if kind == "B" else tb[:, a * 128:(a + 1) * 128]
                        P.pe(lambda e, a=a, src=src: e.transpose(out=ptr_bf[:, a * 128:(a + 1) * 128], in_=src, identity=identb[:]),
                             R=[cm_ if kind == "B" else tb, identb], W=[PTr])
                    P.dve(lambda e, tm_=tm_, nt=nt, tmoff=tmoff: e.tensor_copy(
                        out=tm_[:, 0:nt, tmoff:tmoff + 128], in_=ptr_bf[:, 0:nt * 128].rearrange("p (a l) -> p a l", a=nt)),
                        R=[PTr], W=[tm_])
                else:
                    s_, q_, r_ = sv[cc % 2], sq[cc % 2], rr[cc % 2]
                    P.act(lambda e, s_=s_, ac=ac, ntok=ntok: e.activation(out=s_[:, 0:ntok], in_=ac[:, 0:ntok], func=AF.Silu),
                          R=[ac], W=[s_])
                    P.pool(lambda e, s_=s_, q_=q_, ntok=ntok: e.tensor_tensor(out=q_[:, 0:ntok], in0=s_[:, 0:ntok], in1=s_[:, 0:ntok], op=ALU.mult),
                           R=[s_], W=[q_])
                    P.pe(lambda e, q_=q_, ntok=ntok: e.matmul(PN[:, 0:ntok], lhsT=ones, rhs=q_[:, 0:ntok], start=True, stop=True),
                         R=[cst, q_], W=[PN])
                    P.act(lambda e, r_=r_, ntok=ntok: e.activation(out=r_[:, 0:ntok], in_=PN[:, 0:ntok], func=AF.Sqrt,
                                                                   bias=cst_eps(L), scale=1.0), R=[PN, L["epsT"]], W=[r_])
                    P.dve(lambda e, r_=r_, ntok=ntok: e.reciprocal(out=r_[:, 0:ntok], in_=r_[:, 0:ntok]), R=[r_], W=[r_])
                    if kind == "q":
                        ci = 6 + (cc - 6)
                        dst = cm_[:, 0:nt, ci, :]
                        P.dve(lambda e, dst=dst, s_=s_, r_=r_, ntok=ntok, nt=nt: e.scalar_tensor_tensor(
                            out=dst, in0=s_[:, 0:ntok].rearrange("p (a l) -> p a l", a=nt), scalar=128.0 ** -0.5,
                            in1=r_[:, 0:ntok].rearrange("p (a l) -> p a l", a=nt), op0=ALU.mult, op1=ALU.mult),
                            R=[s_, r_], W=[cm_])
                    else:
                        ci = 2 + (cc - 10)
                        dst = cm_[:, 0:nt, ci, :]
                        P.dve(lambda e, dst=dst, s_=s_, r_=r_, ntok=ntok, nt=nt: e.tensor_tensor(
                            out=dst, in0=s_[:, 0:ntok].rearrange("p (a l) -> p a l", a=nt),
                            in1=r_[:, 0:ntok].rearrange("p (a l) -> p a l", a=nt), op=ALU.mult), R=[s_, r_], W=[cm_])
                        tmoff = 640 + (cc - 10) * 128
                        for a in range(nt):
                            P.pe(lambda e, a=a, ci=ci, cm_=cm_: e.transpose(out=ptr_bf[:, a * 128:(a + 1) * 128], in_=cm_[:, a, ci, :], identity=identb[:]),
                                 R=[cm_, identb], W=[PTr])
                        P.dve(lambda e, tm_=tm_, nt=nt, tmoff=tmoff: e.tensor_copy(
                            out=tm_[:, 0:nt, tmoff:tmoff + 128], in_=ptr_bf[:, 0:nt * 128].rearrange("p (a l) -> p a l", a=nt)),
                            R=[PTr], W=[tm_])
            for a in range(nt):
                for gi, (n0, nn, pz) in enumerate(((0, 512, PZ0), (512, 512, PZ1), (1024, 32, PSm))):
                    for kc in range(8):
                        P.pe(lambda e, kc=kc, a=a, n0=n0, nn=nn, pz=pz, h_=h_: e.matmul(
                            pz[:, 0:nn], lhsT=h_[:, kc, a * 128:(a + 1) * 128], rhs=wtm[:, kc, n0:n0 + nn],
                            start=(kc == 0), stop=(kc == 7)), R=[h_, wtm], W=[pz])
                    if gi < 2:
                        off = 1664 + gi * 512
                        P.act(lambda e, a=a, off=off, pz=pz, tm_=tm_: e.activation(out=tm_[:, a, off:off + 512], in_=pz[:], func=AF.Silu),
                              R=[pz], W=[tm_])
                    else:
                        r_ = smr[a % 2]
                        smv = sm_[:, a, :]
                        P.dve(lambda e, r_=r_: e.tensor_tensor(out=r_[:, 0:24], in0=PSm[:, 0:24], in1=spb[:, 0:24], op=ALU.add),
                              R=[PSm, spb], W=[r_])
                        P.act(lambda e, r_=r_: e.activation(out=r_[:, 0:24], in_=r_[:, 0:24], func=AF.Exp), R=[r_], W=[r_])
                        P.act(lambda e, r_=r_: e.activation(out=r_[:, 0:24], in_=r_[:, 0:24], func=AF.Ln, bias=1.0, scale=1.0), R=[r_], W=[r_])
                        P.act(lambda e, smv=smv: e.activation(out=smv[:, 40:48], in_=PSm[:, 24:32], func=AF.Sigmoid), R=[PSm], W=[sm_])
                        P.dve(lambda e, smv=smv, r_=r_: e.tensor_copy(out=smv[:, 0:16], in_=r_[:, 0:16]), R=[r_], W=[sm_])
                        P.dve(lambda e, smv=smv, r_=r_: e.tensor_tensor(out=smv[:, 16:40], in0=r_[:, 0:24], in1=amul[:], op=ALU.mult),
                              R=[r_, amul], W=[sm_])
                        P.dve(lambda e, smv=smv: e.tensor_scalar_mul(out=smv[:, 48:56], in0=smv[:, 40:48], scalar1=-1.0), R=[sm_], W=[sm_])
                        P.dve(lambda e, smv=smv: e.memset(smv[:, 56:64], 0.0), W=[sm_])
            ti0 = t0 // 128
            P.dma("sp", CMs[ti0:ti0 + nt].rearrange("t p f -> p t f"), cm_[:, 0:nt].rearrange("p t c l -> p t (c l)"), R=[cm_])
            P.dma("sp", TMs[t0:t0 + ntok, :].rearrange("(a p) f -> p a f", p=128), tm_[:, 0:nt, :], R=[tm_])
            P.dma("sp", SMs[t0:t0 + ntok, :].rearrange("(a p) f -> p a f", p=128), sm_[:, 0:nt, :], R=[sm_])


def cst_eps(L):
    return L["epsT"][:, 0:1]


def scan_pass(P, nc, L, d):
    NTT, NCT, NLT, NTH = L["NTT"], L["NCT"], L["NLT"], L["NTH"]
    CMs, TMs, SMs, YFs, YY, rowp = L["CMs"], L["TMs"], L["SMs"], L["YFs"], L["YY"], L["rowp"]
    cst, identb, psb, epsT = L["cst"], L["identb"], L["psb"], L["epsT"]
    Uincl, Lstrict, Lincl, Ustrict, ones = (cst[:, i * 128:(i + 1) * 128] for i in range(1, 6))
    if d == 0:
        Tm, Um, TmT = Uincl, Lstrict, Lincl
        order = list(range(NCT)) + [NCT + i for i in range(NLT)]
    else:
        Tm, Um, TmT = Lincl, Ustrict, Uincl
        order = list(reversed(range(NCT))) + [NCT + i for i in reversed(range(NLT))]
    prep_banks = psb[0:3]
    out_banks = psb[3:5]
    bKS, bSU, bST = psb[5], psb[6], psb[7]
    cnt = {"p": 0, "o": 0}

    def pb():
        cnt["p"] += 1
        return prep_banks[cnt["p"] % 3]

    def ob():
        cnt["o"] += 1
        return out_banks[cnt["o"] % 2]

    with ExitStack() as st:
        sb = lambda n, shp, dt_: P.sb(st, f"{n}_{d}", shp, dt_)
        cmb = [sb(f"cmb{i}", [128, 10, 128], BF16) for i in range(3)]
        tmb = [sb(f"tmb{i}", [128, TMW], BF16) for i in range(3)]
        smb = [sb(f"smb{i}", [128, SMW], F32) for i in range(3)]
        yo = [sb(f"yo{i}", [128, 1024], F32) for i in range(2)]
        rla = sb("rla", [128, 8, 128], F32)
        ET = sb("ET", [128, 8, 128], BF16)
        gmt = sb("gmt", [128, 128], BF16)
        att = [sb(f"att{i}", [128, 8, 128], BF16) for i in range(2)]
        xdt = [sb(f"xdt{i}", [128, 8, 64], BF16) for i in range(2)]
        xdw = [sb(f"xdw{i}", [128, 8, 64], BF16) for i in range(2)]
        ea = [sb(f"ea{i}", [128, 24], F32) for i in range(2)]
        t1 = sb("t1", [128, 8, 64], F32)
        ST = sb("ST", [128, 8, 64], F32)
        STb = sb("STb", [128, 512], BF16)
        rlu = sb("rlu", [128, 4, 128], F32)
        E = sb("E", [128, 4, 128], F32)
        Es = sb("Es", [128, 4, 128], F32)
        Ei = sb("Ei", [128, 4, 128], BF16)
        eg = [sb(f"eg{i}", [128, 12], F32) for i in range(2)]
        eb = [sb(f"eb{i}", [128, 4], F32) for i in range(2)]
        Ak = [sb(f"Ak{i}", [128, 4, 128], BF16) for i in range(2)]
        Nk = [sb(f"Nk{i}", [128, 4, 128], BF16) for i in range(2)]
        Pk = [sb(f"Pk{i}", [128, 4, 128], BF16) for i in range(2)]
        Pf = [sb(f"Pf{i}", [128, 4, 128], BF16) for i in range(2)]
        Wk = [sb(f"Wk{i}", [128, 4, 128], BF16) for i in range(2)]
        Dk = [sb(f"Dk{i}", [128, 4, 128], BF16) for i in range(2)]
        Afull = sb("Afull", [128, 4, 128], BF16)
        Nfull = sb("Nfull", [128, 4, 128], BF16)
        Ym = sb("Ym", [128, 4, 128], BF16)
        qk = sb("qk", [128, 4, 128], BF16)
        qkT = [sb(f"qkT{i}", [128, 4, 128], BF16) for i in range(2)]
        vb = [sb(f"vb{i}", [128, 4, 128], F32) for i in range(2)]
        kd = [sb(f"kd{i}", [128, 4, 128], BF16) for i in range(2)]
        tks = sb("tks", [128, 4, 128], F32)
        rv = sb("rv", [128, 4, 128], BF16)
        vnb = sb("vnb", [128, 4, 128], BF16)
        S = sb("S", [128, 4, 128], F32)
        Sb = sb("Sb", [128, 4, 128], BF16)
        t3 = sb("t3", [128, 4, 128], F32)
        P.pool(lambda e: e.memset(ST[:], 0.0), W=[ST])
        P.pool(lambda e: e.memset(STb[:], 0.0), W=[STb])
        P.pool(lambda e: e.memset(S[:], 0.0), W=[S])
        P.pool(lambda e: e.memset(Sb[:], 0.0), W=[Sb])
        if d == 1:
            yfb = [sb(f"yfb{i}", [128, 1024], F32) for i in range(3)]
            fin = sb("fin", [128, 648], F32)
            P.dma("sp", fin[:], rowp[:, 48:696].partition_broadcast(128), W=[fin])
            u = sb("u", [128, 1024], F32)
            usq = sb("usq", [128, 1024], F32)
            ss = sb("ss", [128, 8], F32)
            yfin = sb("yfin", [128, 1024], BF16)
            yT = [sb(f"yT{i}", [128, 8, 128], BF16) for i in range(2)]

        def load(ci):
            ti = order[ci]
            cm, tm, sm = cmb[ci % 3], tmb[ci % 3], smb[ci % 3]
            P.dma("sp", cm[:].rearrange("p c l -> p (c l)"), CMs[ti], W=[cm])
            P.dma("sp", tm[:], TMs[ti * 128:(ti + 1) * 128, :], W=[tm])
            P.dma("sp", sm[:], SMs[ti * 128:(ti + 1) * 128, :], W=[sm])
            if d == 1 and ti >= NCT:
                yf = yfb[ci % 3]
                P.dma("sp", yf[:], YFs[(ti - NCT) * 128:(ti - NCT + 1) * 128, :], W=[yf])

        def prep(ci):
            cm, tm, sm = cmb[ci % 3], tmb[ci % 3], smb[ci % 3]
            i2 = ci % 2
            CT, BT = cm[:, 0, :], cm[:, 1, :]
            la = sm[:, 16 + 8 * d:24 + 8 * d]
            dtd = sm[:, 8 * d:8 * d + 8]
            lag = sm[:, 32 + 4 * d:36 + 4 * d]
            beta = sm[:, 40 + 4 * d:44 + 4 * d]
            nbeta = sm[:, 48 + 4 * d:52 + 4 * d]
            if PARTS & 4:
                return
            P.pool(lambda e: e.tensor_tensor(out=rla[:], in0=bc(la, [128, 8, 128], 2), in1=bc(Tm, [128, 8, 128], 1), op=ALU.mult),
                   R=[sm, cst], W=[rla])
            for hf in range(2):
                pd = pb()
                P.pe(lambda e, pd=pd, hf=hf: e.matmul(pd[:], lhsT=Um, rhs=rla[:, 4 * hf:4 * hf + 4, :].rearrange("p h l -> p (h l)"),
                                                      start=True, stop=True), R=[cst, rla], W=[pd])
                P.act(lambda e, pd=pd, hf=hf: e.activation(out=ET[:, 4 * hf:4 * hf + 4, :].rearrange("p h l -> p (h l)"), in_=pd[:], func=AF.Exp),
                      R=[pd], W=[ET])
            pg = pb()
            P.pe(lambda e: e.matmul(pg[:, 0:128], lhsT=BT, rhs=CT, start=True, stop=True), R=[cm], W=[pg])
            P.pe(lambda e: e.matmul(pg[:, 128:136], lhsT=Tm, rhs=la, start=True, stop=True), R=[cst, sm], W=[pg])
            P.pe(lambda e: e.matmul(pg[:, 136:144], lhsT=Um, rhs=la, start=True, stop=True), R=[cst, sm], W=[pg])
            P.pe(lambda e: e.matmul(pg[:, 144:152], lhsT=ones, rhs=la, start=True, stop=True), R=[cst, sm], W=[pg])
            P.dve(lambda e: e.tensor_tensor(out=gmt[:], in0=pg[:, 0:128], in1=Tm, op=ALU.mult), R=[pg, cst], W=[gmt])
            ea_ = ea[i2]
            P.act(lambda e: e.activation(out=ea_[:], in_=pg[:, 128:152], func=AF.Exp), R=[pg], W=[ea_])
            att_, xdt_, xdw_ = att[i2], xdt[i2], xdw[i2]
            P.dve(lambda e: e.tensor_tensor(out=att_[:], in0=ET[:], in1=bc(gmt[:], [128, 8, 128], 1), op=ALU.mult),
                  R=[ET, gmt], W=[att_])
            P.pool(lambda e: e.tensor_tensor(out=xdt_[:], in0=tm[:, 0:512].rearrange("p (h q) -> p h q", h=8),
                                             in1=bc(dtd, [128, 8, 64], 2), op=ALU.mult), R=[tm, sm], W=[xdt_])
            P.dve(lambda e: e.tensor_tensor(out=xdw_[:], in0=xdt_[:], in1=bc(ea_[:, 8:16], [128, 8, 64], 2), op=ALU.mult),
                  R=[xdt_, ea_], W=[xdw_])
            if PARTS & 8:
                return
            P.pool(lambda e: e.tensor_tensor(out=rlu[:], in0=bc(lag, [128, 4, 128], 2), in1=bc(Um, [128, 4, 128], 1), op=ALU.mult),
                   R=[sm, cst], W=[rlu])
            pdg = pb()
            P.pe(lambda e: e.matmul(pdg[:], lhsT=Tm, rhs=rlu[:].rearrange("p h l -> p (h l)"), start=True, stop=True),
                 R=[cst, rlu], W=[pdg])
            P.act(lambda e: e.activation(out=E[:].rearrange("p h l -> p (h l)"), in_=pdg[:], func=AF.Exp), R=[pdg], W=[E])
            ps2 = pb()
            P.pe(lambda e: e.matmul(ps2[:, 0:4], lhsT=Tm, rhs=lag, start=True, stop=True), R=[cst, sm], W=[ps2])
            P.pe(lambda e: e.matmul(ps2[:, 4:8], lhsT=Um, rhs=lag, start=True, stop=True), R=[cst, sm], W=[ps2])
            P.pe(lambda e: e.matmul(ps2[:, 8:12], lhsT=ones, rhs=lag, start=True, stop=True), R=[cst, sm], W=[ps2])
            eg_, eb_ = eg[i2], eb[i2]
            P.act(lambda e: e.activation(out=eg_[:], in_=ps2[:, 0:12], func=AF.Exp), R=[ps2], W=[eg_])
            P.dve(lambda e: e.tensor_tensor(out=Es[:], in0=E[:], in1=bc(Um, [128, 4, 128], 1), op=ALU.mult), R=[E, cst], W=[Es])
            P.dve(lambda e: e.tensor_tensor(out=Es[:], in0=Es[:], in1=bc(nbeta, [128, 4, 128], 2), op=ALU.mult), R=[Es, sm], W=[Es])
            P.pool(lambda e: e.tensor_tensor(out=Ei[:], in0=E[:], in1=bc(TmT, [128, 4, 128], 1), op=ALU.mult), R=[E, cst], W=[Ei])
            pkk, pqk = pb(), pb()
            for h in range(4):
                P.pe(lambda e, h=h: e.matmul(pkk[:, h * 128:(h + 1) * 128], lhsT=cm[:, 2 + h, :], rhs=cm[:, 2 + h, :], start=True, stop=True),
                     R=[cm], W=[pkk])
            for h in range(4):
                P.pe(lambda e, h=h: e.matmul(pqk[:, h * 128:(h + 1) * 128], lhsT=cm[:, 6 + h, :], rhs=cm[:, 2 + h, :], start=True, stop=True),
                     R=[cm], W=[pqk])
            BD16 = cst[:, 6 * 128:7 * 128]
            Mb = [cst[:, (7 + i + 3 * d) * 128:(8 + i + 3 * d) * 128] for i in range(3)]
            P.dve(lambda e: e.tensor_tensor(out=Afull[:].rearrange("p h l -> p (h l)"), in0=pkk[:], in1=Es[:].rearrange("p h l -> p (h l)"), op=ALU.mult),
                  R=[pkk, Es], W=[Afull])
            P.dve(lambda e: e.tensor_tensor(out=qk[:].rearrange("p h l -> p (h l)"), in0=pqk[:], in1=Ei[:].rearrange("p h l -> p (h l)"), op=ALU.mult),
                  R=[pqk, Ei], W=[qk])
            pt = pb()
            ptb = pt.ap[:].bitcast(BF16)
            for h in range(4):
                P.pe(lambda e, h=h: e.transpose(out=ptb[:, h * 128:(h + 1) * 128], in_=Afull[:, h, :], identity=identb[:]), R=[Afull, identb], W=[pt])
            for h in range(4):
                P.pe(lambda e, h=h: e.transpose(out=ptb[:, 512 + h * 128:512 + (h + 1) * 128], in_=qk[:, h, :], identity=identb[:]), R=[qk, identb], W=[pt])
            qkT_ = qkT[i2]
            P.act(lambda e: e.copy(out=Nfull[:].rearrange("p h l -> p (h l)"), in_=ptb[:, 0:512]), R=[pt], W=[Nfull])
            P.dve(lambda e: e.tensor_copy(out=qkT_[:].rearrange("p h l -> p (h l)"), in_=ptb[:, 512:1024]), R=[pt], W=[qkT_])
            A, N, Pc = Ak[0], Nk[0], Pk[0]
            P.dve(lambda e, A=A: e.tensor_tensor(out=A[:], in0=Afull[:], in1=bc(BD16, [128, 4, 128], 1), op=ALU.mult), R=[Afull, cst], W=[A])
            P.dve(lambda e, N=N: e.tensor_tensor(out=N[:], in0=Nfull[:], in1=bc(BD16, [128, 4, 128], 1), op=ALU.mult), R=[Nfull, cst], W=[N])
            P.dve(lambda e, N=N, Pc=Pc: e.tensor_tensor(out=Pc[:], in0=N[:], in1=bc(identb[:], [128, 4, 128], 1), op=ALU.add), R=[N, identb], W=[Pc])
            for k in range(1, 4):
                A2, N2 = Ak[k % 2], Nk[k % 2]
                P2 = Pk[k % 2]
                pa = pb()
                for h in range(4):
                    P.pe(lambda e, h=h, N=N, A=A, pa=pa: e.matmul(pa[:, h * 128:(h + 1) * 128], lhsT=N[:, h, :], rhs=A[:, h, :], start=True, stop=True),
                         R=[N, A], W=[pa])
                P.act(lambda e, A2=A2, pa=pa: e.copy(out=A2[:].rearrange("p h l -> p (h l)"), in_=pa[:]), R=[pa], W=[A2])
                if k <= 2:
                    pn = pb()
                    for h in range(4):
                        P.pe(lambda e, h=h, N=N, A=A, pn=pn: e.matmul(pn[:, h * 128:(h + 1) * 128], lhsT=A[:, h, :], rhs=N[:, h, :], start=True, stop=True),
                             R=[N, A], W=[pn])
                    P.act(lambda e, N2=N2, pn=pn: e.copy(out=N2[:].rearrange("p h l -> p (h l)"), in_=pn[:]), R=[pn], W=[N2])
                pm = pb()
                for h in range(4):
                    P.pe(lambda e, h=h, A2=A2, Pc=Pc, pm=pm: e.matmul(pm[:, h * 128:(h + 1) * 128], lhsT=A2[:, h, :], rhs=Pc[:, h, :], start=True, stop=True),
                         R=[A2, Pc], W=[pm])
                P.dve(lambda e, P2=P2, Pc=Pc, pm=pm: e.tensor_tensor(out=P2[:].rearrange("p h l -> p (h l)"), in0=pm[:],
                                                                     in1=Pc[:].rearrange("p h l -> p (h l)"), op=ALU.add), R=[pm, Pc], W=[P2])
                A, N, Pc = A2, N2, P2
            Wc = Pc
            pt2 = pb()
            pt2b = pt2.ap[:].bitcast(BF16)
            for h in range(4):
                P.pe(lambda e, h=h, Wc=Wc: e.transpose(out=pt2b[:, h * 128:(h + 1) * 128], in_=Wc[:, h, :], identity=identb[:]), R=[Wc, identb], W=[pt2])
            Dc = Dk[0]
            P.act(lambda e, Dc=Dc: e.copy(out=Dc[:].rearrange("p h l -> p (h l)"), in_=pt2b[:, 0:512]), R=[pt2], W=[Dc])
            for li in range(3):
                last = li == 2
                py_ = pb()
                for h in range(4):
                    P.pe(lambda e, h=h, Dc=Dc, py_=py_: e.matmul(py_[:, h * 128:(h + 1) * 128], lhsT=Nfull[:, h, :], rhs=Dc[:, h, :], start=True, stop=True),
                         R=[Nfull, Dc], W=[py_])
                P.dve(lambda e, py_=py_, li=li: e.tensor_tensor(out=Ym[:], in0=py_[:].rearrange("p (h l) -> p h l", h=4), in1=bc(Mb[li], [128, 4, 128], 1), op=ALU.mult),
                      R=[py_, cst], W=[Ym])
                if not last:
                    pz = pb()
                    for h in range(4):
                        P.pe(lambda e, h=h, Wc=Wc, pz=pz: e.matmul(pz[:, h * 128:(h + 1) * 128], lhsT=Wc[:, h, :], rhs=Ym[:, h, :], start=True, stop=True),
                             R=[Wc, Ym], W=[pz])
                    D2 = Dk[(li + 1) % 2]
                    P.dve(lambda e, D2=D2, Dc=Dc, pz=pz: e.tensor_tensor(out=D2[:].rearrange("p h l -> p (h l)"), in0=pz[:],
                                                                         in1=Dc[:].rearrange("p h l -> p (h l)"), op=ALU.add), R=[pz, Dc], W=[D2])
                pzt = pb()
                for h in range(4):
                    P.pe(lambda e, h=h, Wc=Wc, pzt=pzt: e.matmul(pzt[:, h * 128:(h + 1) * 128], lhsT=Ym[:, h, :], rhs=Wc[:, h, :], start=True, stop=True),
                         R=[Wc, Ym], W=[pzt])
                W2 = Pf[i2] if last else Wk[li % 2]
                P.dve(lambda e, W2=W2, Wc=Wc, pzt=pzt: e.tensor_tensor(out=W2[:].rearrange("p h l -> p (h l)"), in0=pzt[:],
                                                                        in1=Wc[:].rearrange("p h l -> p (h l)"), op=ALU.add), R=[pzt, Wc], W=[W2])
                Wc = W2
                if not last:
                    Dc = D2
            vb_, kd_ = vb[i2], kd[i2]
            P.pool(lambda e: e.tensor_tensor(out=vb_[:], in0=tm[:, 1152:1664].rearrange("p (h v) -> p h v", h=4), in1=bc(beta, [128, 4, 128], 2), op=ALU.mult),
                   R=[tm, sm], W=[vb_])
            P.dve(lambda e: e.tensor_tensor(out=eb_[:], in0=eg_[:, 0:4], in1=beta, op=ALU.mult), R=[eg_, sm], W=[eb_])
            P.pool(lambda e: e.tensor_tensor(out=kd_[:], in0=tm[:, 640:1152].rearrange("p (h v) -> p h v", h=4), in1=bc(eg_[:, 4:8], [128, 4, 128], 2), op=ALU.mult),
                   R=[tm, eg_], W=[kd_])

        def seq(ci):
            ti = order[ci]
            lat = ti >= NCT
            cm, tm, sm = cmb[ci % 3], tmb[ci % 3], smb[ci % 3]
            i2 = ci % 2
            CT = cm[:, 0, :]
            ea_, att_, xdt_, xdw_ = ea[i2], att[i2], xdt[i2], xdw[i2]
            eg_, eb_, qkT_, vb_, kd_, Pf_ = eg[i2], eb[i2], qkT[i2], vb[i2], kd[i2], Pf[i2]
            yo_ = yo[ci % 2]
            for h in range(4):
                P.pe(lambda e, h=h: e.matmul(bKS[:, h * 128:(h + 1) * 128], lhsT=cm[:, 2 + h, :], rhs=Sb[:, h, :], start=True, stop=True),
                     R=[cm, Sb], W=[bKS])
            P.dve(lambda e: e.tensor_tensor(out=tks[:], in0=bKS[:].rearrange("p (h v) -> p h v", h=4), in1=bc(eb_[:], [128, 4, 128], 2), op=ALU.mult),
                  R=[bKS, eb_], W=[tks])
            P.dve(lambda e: e.tensor_tensor(out=rv[:], in0=vb_[:], in1=tks[:], op=ALU.subtract), R=[vb_, tks], W=[rv])
            if lat:
                pi = ob()
                P.pe(lambda e: e.matmul(pi[:], lhsT=CT, rhs=STb[:], start=True, stop=True), R=[cm, STb], W=[pi])
                pqs = ob()
                for h in range(4):
                    P.pe(lambda e, h=h: e.matmul(pqs[:, h * 128:(h + 1) * 128], lhsT=cm[:, 6 + h, :], rhs=Sb[:, h, :], start=True, stop=True),
                         R=[cm, Sb], W=[pqs])
            P.pe(lambda e: e.matmul(bST[:], lhsT=tm[:, 512:640], rhs=xdw_[:].rearrange("p h q -> p (h q)"), start=True, stop=True),
                 R=[tm, xdw_], W=[bST])
            if lat:
                P.dve(lambda e: e.tensor_tensor(out=t1[:], in0=pi[:].rearrange("p (h q) -> p h q", h=8), in1=bc(ea_[:, 0:8], [128, 8, 64], 2), op=ALU.mult),
                      R=[pi, ea_], W=[t1])
                P.dve(lambda e: e.tensor_tensor(out=t3[:], in0=pqs[:].rearrange("p (h v) -> p h v", h=4), in1=bc(eg_[:, 0:4], [128, 4, 128], 2), op=ALU.mult),
                      R=[pqs, eg_], W=[t3])
            P.dve(lambda e: e.tensor_tensor(out=ST[:], in0=ST[:], in1=bc(ea_[:, 16:24], [128, 8, 64], 2), op=ALU.mult), R=[ST, ea_], W=[ST])
            P.dve(lambda e: e.tensor_tensor(out=ST[:].rearrange("p h q -> p (h q)"), in0=ST[:].rearrange("p h q -> p (h q)"), in1=bST[:], op=ALU.add),
                  R=[ST, bST], W=[ST])
            P.act(lambda e: e.copy(out=STb[:], in_=ST[:].rearrange("p h q -> p (h q)")), R=[ST], W=[STb])
            for h in range(4):
                P.pe(lambda e, h=h: e.matmul(bKS[:, h * 128:(h + 1) * 128], lhsT=Pf_[:, h, :], rhs=rv[:, h, :], start=True, stop=True),
                     R=[Pf_, rv], W=[bKS])
            P.act(lambda e: e.copy(out=vnb[:].rearrange("p h v -> p (h v)"), in_=bKS[:]), R=[bKS], W=[vnb])
            for h in range(4):
                P.pe(lambda e, h=h: e.matmul(bSU[:, h * 128:(h + 1) * 128], lhsT=kd_[:, h, :], rhs=vnb[:, h, :], start=True, stop=True),
                     R=[kd_, vnb], W=[bSU])
            P.dve(lambda e: e.tensor_tensor(out=S[:], in0=S[:], in1=bc(eg_[:, 8:12], [128, 4, 128], 2), op=ALU.mult), R=[S, eg_], W=[S])
            P.dve(lambda e: e.tensor_tensor(out=S[:].rearrange("p h v -> p (h v)"), in0=S[:].rearrange("p h v -> p (h v)"), in1=bSU[:], op=ALU.add),
                  R=[S, bSU], W=[S])
            P.act(lambda e: e.copy(out=Sb[:].rearrange("p h v -> p (h v)"), in_=S[:].rearrange("p h v -> p (h v)")), R=[S], W=[Sb])
            if not lat:
                return
            py = ob()
            for h in range(8):
                P.pe(lambda e, h=h: e.matmul(py[:, h * 64:(h + 1) * 64], lhsT=att_[:, h, :], rhs=xdt_[:, h, :], start=True, stop=True),
                     R=[att_, xdt_], W=[py])
            P.dve(lambda e: e.tensor_tensor(out=yo_[:, 0:512], in0=t1[:].rearrange("p h q -> p (h q)"), in1=py[:], op=ALU.add), R=[t1, py], W=[yo_])
            pqv = ob()
            for h in range(4):
                P.pe(lambda e, h=h: e.matmul(pqv[:, h * 128:(h + 1) * 128], lhsT=qkT_[:, h, :], rhs=vnb[:, h, :], start=True, stop=True),
                     R=[qkT_, vnb], W=[pqv])
            P.dve(lambda e: e.tensor_tensor(out=yo_[:, 512:1024], in0=t3[:].rearrange("p h v -> p (h v)"), in1=pqv[:], op=ALU.add), R=[t3, pqv], W=[yo_])
            li = ti - NCT
            if d == 0:
                P.dma("sp", YFs[li * 128:(li + 1) * 128, :], yo_[:], R=[yo_])
                return
            yf = yfb[ci % 3]
            dsk, nws, nwg = fin[:, 0:8], fin[:, 8:520], fin[:, 520:648]
            P.dve(lambda e: e.tensor_tensor(out=u[:], in0=yo_[:], in1=yf[:], op=ALU.add), R=[yo_, yf], W=[u])
            P.pool(lambda e: e.tensor_tensor(out=usq[:, 0:512].rearrange("p (h q) -> p h q", h=8), in0=tm[:, 0:512].rearrange("p (h q) -> p h q", h=8),
                                             in1=bc(dsk, [128, 8, 64], 2), op=ALU.mult), R=[tm, fin], W=[usq])
            P.dve(lambda e: e.tensor_tensor(out=u[:, 0:512], in0=u[:, 0:512], in1=usq[:, 0:512], op=ALU.add), R=[u, usq], W=[u])
            P.dve(lambda e: e.tensor_tensor(out=u[:, 0:512], in0=u[:, 0:512], in1=tm[:, 1664:2176], op=ALU.mult), R=[u, tm], W=[u])
            P.pool(lambda e: e.memset(ss[:], 0.0), W=[ss])
            P.act(lambda e: e.activation(out=usq[:, 0:512], in_=u[:, 0:512], func=AF.Square, accum_out=ss[:, 0:1]), R=[u, ss], W=[usq, ss])
            for h in range(4):
                P.act(lambda e, h=h: e.activation(out=usq[:, 512 + h * 128:512 + (h + 1) * 128], in_=u[:, 512 + h * 128:512 + (h + 1) * 128],
                                                  func=AF.Square, accum_out=ss[:, 1 + h:2 + h]), R=[u, ss], W=[usq, ss])
            P.act(lambda e: e.activation(out=ss[:, 0:1], in_=ss[:, 0:1], func=AF.Sqrt, bias=epsT[:, 0:1], scale=1.0 / 512), R=[ss, epsT], W=[ss])
            P.act(lambda e: e.activation(out=ss[:, 1:5], in_=ss[:, 1:5], func=AF.Sqrt, bias=epsT[:, 0:1], scale=1.0 / 128), R=[ss, epsT], W=[ss])
            P.dve(lambda e: e.reciprocal(out=ss[:, 0:5], in_=ss[:, 0:5]), R=[ss], W=[ss])
            P.dve(lambda e: e.scalar_tensor_tensor(out=yfin[:, 0:512], in0=u[:, 0:512], scalar=ss[:, 0:1], in1=nws, op0=ALU.mult, op1=ALU.mult),
                  R=[u, ss, fin], W=[yfin])
            u4 = u[:, 512:1024].rearrange("p (h v) -> p h v", h=4)
            P.dve(lambda e: e.tensor_tensor(out=u4, in0=u4, in1=bc(ss[:, 1:5], [128, 4, 128], 2), op=ALU.mult), R=[u, ss], W=[u])
            P.dve(lambda e: e.tensor_tensor(out=u4, in0=u4, in1=bc(nwg, [128, 4, 128], 1), op=ALU.mult), R=[u, fin], W=[u])
            P.dve(lambda e: e.tensor_tensor(out=yfin[:, 512:1024], in0=u[:, 512:1024], in1=tm[:, 2176:2688], op=ALU.mult), R=[u, tm], W=[yfin])
            pt = ob()
            ptb = pt.ap[:].bitcast(BF16)
            for c8 in range(8):
                P.pe(lambda e, c8=c8: e.transpose(out=ptb[:, c8 * 128:(c8 + 1) * 128], in_=yfin[:, c8 * 128:(c8 + 1) * 128], identity=identb[:]),
                     R=[yfin, identb], W=[pt])
            yT_ = yT[ci % 2]
            P.act(lambda e: e.copy(out=yT_[:].rearrange("p c l -> p (c l)"), in_=ptb[:, :]), R=[pt], W=[yT_])
            hc, tl = li // NTH, li % NTH
            P.dma("sp", YY.ap()[hc, tl], yT_[:].rearrange("p c l -> p (c l)"), R=[yT_])

        n = len(order)
        if L["debug"] and d == 0:
            n = min(DBG_NCH, n)
        load(0)
        if n > 1:
            load(1)
        if PARTS & 1:
            prep(0)
        for ci in range(n):
            if ci + 2 < n:
                load(ci + 2)
            if ci + 1 < n and PARTS & 1:
                prep(ci + 1)
            if PARTS & 2:
                seq(ci)
        if L["debug"] and d == 0:
            loc = locals()
            for nm_, ap_ in L["dbg"].items():
                if not nm_.startswith("s_"):
                    continue
                key = nm_[2:]
                t_ = loc[key] if key in loc else loc[key[:-1]][int(key[-1])]
                src = t_[:] if len(t_.ap.shape) == 2 else t_[:].rearrange("p h l -> p (h l)")
                L["final_ops"].append(P.dma("sp", ap_, src, R=[t_]))


def stage5(P, nc, L):
    HALF, NTH = L["HALF"], L["NTH"]
    xhalf, yout, MINE, PART, WO, WGU, WD, rowp = L["xhalf"], L["yout"], L["MINE"], L["PART"], L["WO"], L["WGU"], L["WD"], L["rowp"]
    g12row, modT, cst, psb, epsT, final_ops = L["g12row"], L["modT"], L["cst"], L["psb"], L["epsT"], L["final_ops"]
    ident = cst[:, 0:128]
    with ExitStack() as st:
        lnp = P.sb(st, "lnp", [128, 4 * D], F32)
        P.dma("sp", lnp[:], rowp[:, 696:4792].partition_broadcast(128), W=[lnp])
        xt = [P.sb(st, f"x5_{i}", [128, 4, D], F32) for i in range(2)]
        yT = P.sb(st, "yT5", [128, 2, 4, 1024], BF16)
        h2T = P.sb(st, "h2T", [128, 8, 512], BF16)
        actT = P.sb(st, "actT", [128, 22, 512], BF16)
        wbuf = [P.sb(st, f"wbuf{i}", [128, 8192], BF16) for i in range(3)]
        tmp = [P.sb(st, f"tmp5_{i}", [128, 512], F32) for i in range(2)]
        sgt = [P.sb(st, f"sgt{i}", [128, 512], F32) for i in range(2)]
        junk = P.sb(st, "junk", [128, D], F32)
        stat = [P.sb(st, f"stat{i}", [128, 8], F32) for i in range(2)]
        wcnt = [0]

        def wload(src, n):
            wb = wbuf[wcnt[0] % 3]
            wcnt[0] += 1
            P.dma("sp", wb[:, 0:n], src, W=[wb])
            return wb

        def resid(ps, x_, a, dh, gi, k):
            t = tmp[k % 2]
            P.dve(lambda e: e.tensor_tensor(out=t[:], in0=ps[:], in1=g12row[:, gi * D + dh * 512: gi * D + dh * 512 + 512], op=ALU.mult),
                  R=[ps, g12row], W=[t])
            xs_ = x_[:, a, dh * 512:(dh + 1) * 512]
            P.dve(lambda e: e.scalar_tensor_tensor(out=xs_, in0=xs_, scalar=ALPHA, in1=t[:], op0=ALU.mult, op1=ALU.add),
                  R=[x_, t], W=[x_])

        def layer_norm(x_, a, li, k):
            sx = stat[k % 2]
            xa = x_[:, a, :]
            g_, b_ = lnp[:, (2 * li) * D:(2 * li + 1) * D], lnp[:, (2 * li + 1) * D:(2 * li + 2) * D]
            P.pool(lambda e: e.memset(sx[:], 0.0), W=[sx])
            P.act(lambda e: e.activation(out=junk[:], in_=xa, func=AF.Identity, accum_out=sx[:, 0:1]), R=[x_, sx], W=[junk, sx])
            P.act(lambda e: e.activation(out=junk[:], in_=xa, func=AF.Square, accum_out=sx[:, 1:2]), R=[x_, sx], W=[junk, sx])
            P.dve(lambda e: e.tensor_scalar_mul(out=sx[:, 2:3], in0=sx[:, 0:1], scalar1=1.0 / D), R=[sx], W=[sx])
            P.dve(lambda e: e.tensor_tensor(out=sx[:, 3:4], in0=sx[:, 2:3], in1=sx[:, 2:3], op=ALU.mult), R=[sx], W=[sx])
            P.dve(lambda e: e.scalar_tensor_tensor(out=sx[:, 4:5], in0=sx[:, 1:2], scalar=1.0 / D, in1=sx[:, 3:4], op0=ALU.mult, op1=ALU.subtract),
                  R=[sx], W=[sx])
            P.act(lambda e: e.activation(out=sx[:, 5:6], in_=sx[:, 4:5], func=AF.Sqrt, bias=epsT[:, 1:2], scale=1.0), R=[sx, epsT], W=[sx])
            P.dve(lambda e: e.reciprocal(out=sx[:, 5:6], in_=sx[:, 5:6]), R=[sx], W=[sx])
            P.dve(lambda e: e.scalar_tensor_tensor(out=sx[:, 6:7], in0=sx[:, 2:3], scalar=-1.0, in1=sx[:, 5:6], op0=ALU.mult, op1=ALU.mult),
                  R=[sx], W=[sx])
            P.act(lambda e: e.activation(out=xa, in_=xa, func=AF.Identity, bias=sx[:, 6:7], scale=sx[:, 5:6]), R=[x_, sx], W=[x_])
            P.dve(lambda e: e.tensor_tensor(out=xa, in0=xa, in1=g_, op=ALU.mult), R=[x_, lnp], W=[x_])
            P.dve(lambda e: e.tensor_tensor(out=xa, in0=xa, in1=b_, op=ALU.add), R=[x_, lnp], W=[x_])

        nblk = HALF // 512
        for blk in range(nblk):
            x_ = xt[blk % 2]
            tl0 = blk * 4
            P.dma("sp", x_[:], xhalf[blk * 512:(blk + 1) * 512, :].rearrange("(a p) d -> p a d", p=128), W=[x_])
            yTa = T("yTa", yT.ap)
            yTb = T("yTb", yT.ap)
            yTa.last_w, yTa.readers = yT.last_w, dict(yT.readers)
            P.dma("sp", yT[:, 0], MINE.ap()[tl0:tl0 + 4].rearrange("t p f -> p t f"), W=[yT], semt=yTa)
            P.dma("sp", yT[:, 1], PART.ap()[tl0:tl0 + 4].rearrange("t p f -> p t f"), W=[yT], semt=yTb)
            k = 0
            for dh in range(2):
                wb = wload(WO[dh], 8192)
                wv = wb[:, 0:8192].rearrange("p (k n) -> p k n", k=16)
                for a in range(4):
                    ps = psb[k % 2]
                    for kc in range(16):
                        src, cc = kc // 8, kc % 8
                        P.pe(lambda e, ps=ps, kc=kc, src=src, cc=cc, a=a, wv=wv: e.matmul(
                            ps[:], lhsT=yT[:, src, a, cc * 128:(cc + 1) * 128], rhs=wv[:, kc, :], start=(kc == 0), stop=(kc == 15)),
                            R=[yT, wb], W=[ps])
                    resid(ps, x_, a, dh, 0, k)
                    k += 1
            for a in range(4):
                layer_norm(x_, a, 0, a)
            PX = psb[2]
            for fc in range(8):
                for a in range(4):
                    P.pe(lambda e, a=a, fc=fc, x_=x_: e.transpose(out=PX[:, a * 128:(a + 1) * 128], in_=x_[:, a, fc * 128:(fc + 1) * 128], identity=ident),
                         R=[x_, cst], W=[PX])
                P.act(lambda e, fc=fc: e.activation(out=h2T[:, fc, :], in_=PX[:], func=AF.Identity,
                                                    bias=modT[:, 32 + fc:33 + fc], scale=modT[:, 40 + fc:41 + fc]), R=[PX, modT], W=[h2T])
            for b11 in range(11):
                wb = wload(WGU[b11], 4096)
                wv = wb[:, 0:4096].rearrange("p (g k n) -> p g k n", g=2, k=8)
                for jj in range(2):
                    j = b11 * 2 + jj
                    pg, pu = psb[4 + j % 2], psb[6 + j % 2]
                    for kc in range(8):
                        P.pe(lambda e, pg=pg, kc=kc, jj=jj, wv=wv: e.matmul(pg[:], lhsT=wv[:, 0, kc, jj * 128:(jj + 1) * 128], rhs=h2T[:, kc, :],
                                                                            start=(kc == 0), stop=(kc == 7)), R=[wb, h2T], W=[pg])
                    for kc in range(8):
                        P.pe(lambda e, pu=pu, kc=kc, jj=jj, wv=wv: e.matmul(pu[:], lhsT=wv[:, 1, kc, jj * 128:(jj + 1) * 128], rhs=h2T[:, kc, :],
                                                                            start=(kc == 0), stop=(kc == 7)), R=[wb, h2T], W=[pu])
                    sg = sgt[j % 2]
                    P.act(lambda e, sg=sg, pg=pg: e.activation(out=sg[:], in_=pg[:], func=AF.Silu), R=[pg], W=[sg])
                    P.dve(lambda e, sg=sg, pu=pu, j=j: e.tensor_tensor(out=actT[:, j, :], in0=sg[:], in1=pu[:], op=ALU.mult), R=[sg, pu], W=[actT])
            for dh in range(2):
                accs = [psb[0], psb[1], psb[2], psb[3]]
                for jh in range(2):
                    wb = wload(WD[dh * 2 + jh], 5632)
                    wv = wb[:, 0:5632].rearrange("p (j n) -> p j n", j=11)
                    for a in range(4):
                        for j11 in range(11):
                            j = jh * 11 + j11
                            P.pe(lambda e, a=a, j=j, j11=j11, wv=wv, accs=accs: e.matmul(
                                accs[a][:], lhsT=actT[:, j, a * 128:(a + 1) * 128], rhs=wv[:, j11, :], start=(j == 0), stop=(j == 21)),
                                R=[actT, wb], W=[accs[a]])
                for a in range(4):
                    resid(accs[a], x_, a, dh, 1, a)
            for a in range(4):
                layer_norm(x_, a, 1, a)
            final_ops.append(P.dma("sp", yout[blk * 512:(blk + 1) * 512, :].rearrange("(a p) d -> p a d", p=128), x_[:], R=[x_]))


def host_consts():
    j = np.arange(128)[:, None]
    l = np.arange(128)[None, :]
    mats = [np.eye(128), (j <= l), (j > l), (j >= l), (j < l), np.ones((128, 128)), (j // 16 == l // 16)]
    for b in (16, 32, 64):
        mats.append((j // (2 * b) == l // (2 * b)) & (j % (2 * b) >= b) & (l % (2 * b) < b))
    for b in (16, 32, 64):
        mats.append(((j // (2 * b) == l // (2 * b)) & (j % (2 * b) >= b) & (l % (2 * b) < b)).T)
    return np.concatenate([m.astype(np.float32) for m in mats], axis=1)


def prep_core(inp, core, SEQ):
    b, h = core // 2, core % 2
    HALF = SEQ // 2
    f = lambda a: np.ascontiguousarray(a, dtype=np.float32)
    x, ctx = inp["x"], inp["ctx"]
    w_in = inp["w_in"][0]
    xs0 = 1024
    B0, C0 = 2048, 2304
    dt0 = 2560
    q0, k0, v0 = 2592, 3616, 4640
    g0 = 5664
    b0, a0 = 6688, 6704
    r = lambda s, n: np.arange(s, s + n)
    cols_cm = np.concatenate([r(xs0 + h * 512, 512), r(B0 + h * 128, 128), r(C0 + h * 128, 128),
                              r(q0 + h * 512, 512), r(k0 + h * 512, 512), r(v0 + h * 512, 512)])
    cols_tm = np.concatenate([r(h * 512, 512), r(g0 + h * 512, 512),
                              r(dt0 + 8 * h, 8), r(dt0 + 16 + 8 * h, 8),
                              r(a0 + 4 * h, 4), r(a0 + 8 + 4 * h, 4),
                              r(b0 + 4 * h, 4), r(b0 + 8 + 4 * h, 4)])
    cws, cwg = inp["conv_w_ssd"][0], inp["conv_w_gdn"][0]
    ssd_cols = np.concatenate([r(h * 512, 512), r(1024 + h * 128, 128), r(1280 + h * 128, 128)])
    gdn_cols = np.concatenate([r(h * 512, 512), r(1024 + h * 512, 512), r(2048 + h * 512, 512)])
    cw = np.concatenate([cws[:, ssd_cols], cwg[:, gdn_cols]], axis=1)
    cwT = cw.T.reshape(NCM, 128, 5).transpose(1, 0, 2).reshape(128, NCM * 5)
    cbT = inp["conv_b_ssd"][0][ssd_cols].reshape(6, 128).T
    rowp = np.concatenate([
        inp["dt_bias_ssd"][0][0, 8 * h:8 * h + 8], inp["dt_bias_ssd"][0][1, 8 * h:8 * h + 8],
        inp["dt_bias_gdn"][0][0, 4 * h:4 * h + 4], inp["dt_bias_gdn"][0][1, 4 * h:4 * h + 4],
        inp["a_log_ssd"][0][0, 8 * h:8 * h + 8], inp["a_log_ssd"][0][1, 8 * h:8 * h + 8],
        inp["a_log_gdn"][0][0, 4 * h:4 * h + 4], inp["a_log_gdn"][0][1, 4 * h:4 * h + 4],
        inp["d_skip_ssd"][0][8 * h:8 * h + 8],
        inp["norm_w_ssd"][0][h * 512:(h + 1) * 512], inp["norm_w_gdn"][0],
        inp["ln1_g"][0], inp["ln1_b"][0], inp["ln2_g"][0], inp["ln2_b"][0]])[None, :]
    cv = np.stack([inp["c"][b], inp["c_ctx"]])
    cvT = cv.reshape(2, 8, 128).transpose(2, 0, 1).reshape(128, 16)
    wo = inp["w_out"][0]
    own = np.concatenate([r(h * 512, 512), r(1024 + h * 512, 512)])
    oth = np.concatenate([r((1 - h) * 512, 512), r(1024 + (1 - h) * 512, 512)])
    return {
        "xin": f(np.concatenate([ctx[b], x[b]], axis=0)),
        "xhalf": f(x[b, h * HALF:(h + 1) * HALF]),
        "cvecT": f(cvT),
        "w_ada": f(inp["w_ada"][0]), "b_ada": f(inp["b_ada"][0][None, :]),
        "w_cm": f(w_in[:, cols_cm]), "w_tm": f(w_in[:, cols_tm]),
        "convw": f(cwT), "convb": f(cbT), "rowp": f(rowp), "consts": host_consts(),
        "w_out": f(wo[np.concatenate([own, oth])]),
        "w_gate": f(inp["w_ffn_gate"][0]), "w_up": f(inp["w_ffn_up"][0]), "w_down": f(inp["w_ffn_down"][0]),
    }


_CACHE = {}


def run(inputs, SEQ, debug=False, stop_after=99, ncores=8):
    key = (SEQ, debug, stop_after)
    if key not in _CACHE:
        _CACHE[key] = build_program(SEQ, debug, stop_after)
    nc, stats = _CACHE[key]
    in_maps = [prep_core(inputs, c, SEQ) for c in range(ncores)]
    res = run_bass_kernel_spmd(nc, in_maps, core_ids=list(range(ncores)))
    return res.results, stats


def kernel(**inputs):
    SEQ = inputs["x"].shape[1]
    B = inputs["x"].shape[0]
    results, _ = run(inputs, SEQ)
    HALF = SEQ // 2
    out = np.empty((B, SEQ, D), np.float32)
    for c in range(8):
        b, h = c // 2, c % 2
        out[b, h * HALF:(h + 1) * HALF] = results[c]["yout"]
    return out
```

```python
import numpy as np
from contextlib import ExitStack
import concourse.bass as bass
import concourse.mybir as mybir
from concourse.bass_utils import run_bass_kernel_spmd

F32 = mybir.dt.float32
BF16 = mybir.dt.bfloat16
AF = mybir.ActivationFunctionType
ALU = mybir.AluOpType
AX = mybir.AxisListType

D = 1024
CTX = 256
GRID_W = 64
DFF = 2816
NCM = 18
NTMC = 1056
TMW = 2688
SMW = 64
ALPHA = 2.0 ** 0.25
LN_EPS = 1e-5
RMS_EPS = 1e-6
EPOCH = 30000
NCST = 13
PARTS = 3
DBG_NCH = 10 ** 6


class T:
    __slots__ = ("name", "ap", "last_w", "readers", "dma_readers", "sem", "cnt", "last_dma", "excl")

    def __init__(self, name, ap):
        self.excl = False
        self.name = name
        self.ap = ap
        self.last_w = None
        self.readers = {}
        self.dma_readers = []
        self.sem = None
        self.cnt = 0
        self.last_dma = None

    def __getitem__(self, k):
        return self.ap[k]


class Op:
    __slots__ = ("eng", "fn", "deps", "needs_inc", "inc_val", "epoch", "is_dma", "sem", "val", "amt")

    def __init__(self, eng, fn, is_dma=False):
        self.eng = eng
        self.fn = fn
        self.deps = []
        self.needs_inc = False
        self.inc_val = 0
        self.epoch = 0
        self.is_dma = is_dma
        self.sem = None
        self.val = 0
        self.amt = 16


class Prog:
    ENGS = ("sp", "act", "pool", "dve", "pe")

    def __init__(self, nc, stack):
        self.nc = nc
        self.stack = stack
        self.ops = {e: [] for e in self.ENGS}
        self.same_engine_sync = {"sp": False, "act": True, "pool": True, "dve": True, "pe": False}
        self.nsem = 0
        self.dma_open = []
        self.pending = {e: [] for e in self.ENGS}
        self.sem_pool = {}

    def sb(self, st, name, shape, dt):
        return T(name, st.enter_context(self.nc.sbuf_tensor(name, list(shape), dt)))

    def ps(self, st, name, shape, dt):
        t = T(name, st.enter_context(self.nc.psum_tensor(name, list(shape), dt)))
        t.excl = True
        return t

    def new_sem(self, name):
        self.nsem += 1
        return self.stack.enter_context(self.nc.semaphore(name))

    def _track(self, op, R, W, extra=()):
        deps = list(extra)
        for t in R:
            if t.last_w is not None:
                deps.append(t.last_w)
            if t.excl:
                deps.extend(o for en, o in t.readers.items() if en != op.eng)
        for t in W:
            if t.last_w is not None:
                deps.append(t.last_w)
            deps.extend(t.readers.values())
            deps.extend(t.dma_readers)
        deps.extend(self.pending[op.eng])
        self.pending[op.eng] = []
        seen = set()
        for d in deps:
            if d is op or id(d) in seen:
                continue
            seen.add(id(d))
            if (not d.is_dma) and d.eng == op.eng and not op.is_dma and not self.same_engine_sync[op.eng]:
                continue
            if not d.is_dma:
                d.needs_inc = True
            op.deps.append(d)
        for t in R:
            if op.is_dma:
                t.dma_readers.append(op)
            else:
                t.readers[op.eng] = op
        for t in W:
            t.last_w = op
            t.readers = {}
            t.dma_readers = []

    def op(self, eng, fn, R=(), W=()):
        o = Op(eng, fn)
        self._track(o, R, W)
        self.ops[eng].append(o)
        return o

    def pe(self, fn, R=(), W=()):
        return self.op("pe", fn, R, W)

    def act(self, fn, R=(), W=()):
        return self.op("act", fn, R, W)

    def dve(self, fn, R=(), W=()):
        return self.op("dve", fn, R, W)

    def pool(self, fn, R=(), W=()):
        return self.op("pool", fn, R, W)

    def dma(self, eng, out_ap=None, in_ap=None, R=(), W=(), semt=None, fn=None, amt=16, **kw):
        if fn is None:
            fn = lambda e: e.dma_start(out=out_ap, in_=in_ap, **kw)
        o = Op(eng, fn, is_dma=True)
        o.amt = amt
        if semt is None:
            semt = (list(W) + list(R))[0]
        if semt.sem is None:
            key = semt.name
            if key not in self.sem_pool:
                self.sem_pool[key] = [self.new_sem("d_" + key), 0]
            semt.sem = self.sem_pool[key]
        semt.sem[1] += amt
        o.sem = semt.sem[0]
        o.val = semt.sem[1]
        extra = [semt.last_dma] if semt.last_dma is not None else []
        semt.last_dma = o
        self._track(o, R, W, extra)
        self.ops[eng].append(o)
        self.dma_open.append(o)
        return o

    def barrier(self):
        deps = list(self.dma_open)
        for e in self.ENGS:
            for o in reversed(self.ops[e]):
                if not o.is_dma:
                    deps.append(o)
                    break
        self.dma_open = []
        for e in self.ENGS:
            self.pending[e] = list(deps)

    def emit(self, final_ops=()):
        nc = self.nc
        esems = {}
        for e in self.ENGS:
            n = 0
            for o in self.ops[e]:
                if o.is_dma or not o.needs_inc:
                    continue
                o.epoch = n // EPOCH
                o.inc_val = n % EPOCH + 1
                n += 1
            nep = (n + EPOCH - 1) // EPOCH
            esems[e] = [self.new_sem(f"s_{e}{i}") for i in range(max(nep, 1))]
        block = self.stack.enter_context(nc.Block())
        deco = {"sp": block.sync, "act": block.scalar, "pool": block.gpsimd, "dve": block.vector, "pe": block.tensor}
        nwaits = {e: 0 for e in self.ENGS}

        def make(eng):
            def body(e):
                seen = {}
                maxep = {}

                def wait_all(deps):
                    need = {}
                    for d in deps:
                        if d.is_dma:
                            key, val, sem = ("d", id(d.sem)), d.val, d.sem
                        else:
                            key, val, sem = (d.eng, d.epoch), d.inc_val, esems[d.eng][d.epoch]
                        if seen.get(key, 0) >= val:
                            continue
                        if key not in need or need[key][0] < val:
                            need[key] = (val, sem)
                    for key, (val, sem) in need.items():
                        if key[0] != "d":
                            if any(k[0] == key[0] and k[1] > key[1] for k in list(seen) + list(need) if k[0] != "d"):
                                continue
                        seen[key] = val
                        e.wait_ge(sem, val)
                        nwaits[eng] += 1

                def wait_for(d):
                    wait_all([d])

                for o in self.ops[eng]:
                    wait_all(o.deps)
                    ins = o.fn(e)
                    if o.is_dma:
                        ins.then_inc(o.sem, o.amt)
                    elif o.needs_inc:
                        ins.then_inc(esems[eng][o.epoch], 1)
                if eng == "sp":
                    for d in final_ops:
                        wait_for(d)
                        e.nop()
            return body

        for eng in self.ENGS:
            if self.ops[eng] or eng == "sp":
                deco[eng](make(eng))
        self.nwaits = nwaits
        return block


def bc(ap, shape, axis):
    return ap.unsqueeze(axis).to_broadcast(list(shape))


def build_program(SEQ, debug=False, stop_after=99):
    TT = CTX + SEQ
    NTT = TT // 128
    NCT = CTX // 128
    NLT = SEQ // 128
    HALF = SEQ // 2
    NTH = HALF // 128
    nc = bass.Bass("TRN2", target_bir_lowering=False)
    dt = nc.dram_tensor
    xin = dt("xin", [TT, D], F32, kind="ExternalInput").ap()
    xhalf = dt("xhalf", [HALF, D], F32, kind="ExternalInput").ap()
    cvecT = dt("cvecT", [128, 16], F32, kind="ExternalInput").ap()
    w_ada = dt("w_ada", [D, 6 * D], F32, kind="ExternalInput").ap()
    b_ada = dt("b_ada", [1, 6 * D], F32, kind="ExternalInput").ap()
    w_cm = dt("w_cm", [D, NCM * 128], F32, kind="ExternalInput").ap()
    w_tm = dt("w_tm", [D, NTMC], F32, kind="ExternalInput").ap()
    convw = dt("convw", [128, NCM * 5], F32, kind="ExternalInput").ap()
    convb = dt("convb", [128, 6], F32, kind="ExternalInput").ap()
    rowp = dt("rowp", [1, 4792], F32, kind="ExternalInput").ap()
    consts = dt("consts", [128, NCST * 128], F32, kind="ExternalInput").ap()
    w_out = dt("w_out", [2 * D, D], F32, kind="ExternalInput").ap()
    w_gate = dt("w_gate", [D, DFF], F32, kind="ExternalInput").ap()
    w_up = dt("w_up", [D, DFF], F32, kind="ExternalInput").ap()
    w_down = dt("w_down", [DFF, D], F32, kind="ExternalInput").ap()
    yout = dt("yout", [HALF, D], F32, kind="ExternalOutput").ap()
    CMs = dt("CMs", [NTT, 128, 10 * 128], BF16).ap()
    TMs = dt("TMs", [TT, TMW], BF16).ap()
    SMs = dt("SMs", [TT, SMW], F32).ap()
    YFs = dt("YFs", [SEQ, 1024], F32).ap()
    YY = dt("YY", [2, NTH, 128, 1024], BF16)
    TPK = min(NTH, 8)
    NCC = NTH // TPK
    ZO = dt("ZO", [NCC, 2, TPK, 128, 1024], BF16)
    ZIN = dt("ZIN", [NTH, 128, 1024], BF16)
    MINE = dt("MINE", [NTH, 128, 1024], BF16)
    PART = dt("PART", [NTH, 128, 1024], BF16)
    WO = dt("WO", [2, 128, 16 * 512], BF16).ap()
    WGU = dt("WGU", [11, 128, 2 * 8 * 256], BF16).ap()
    WD = dt("WD", [4, 128, 11 * 512], BF16).ap()
    dbg = {}
    if debug:
        dbg["modT"] = dt("dbg_modT", [128, 64], F32, kind="ExternalOutput").ap()
        dbg["CM"] = dt("dbg_CM", [NTT, 128, 10 * 128], BF16, kind="ExternalOutput").ap()
        dbg["TM"] = dt("dbg_TM", [TT, TMW], BF16, kind="ExternalOutput").ap()
        dbg["SM"] = dt("dbg_SM", [TT, SMW], F32, kind="ExternalOutput").ap()
        dbg["YF"] = dt("dbg_YF", [SEQ, 1024], F32, kind="ExternalOutput").ap()
        dbg["YY"] = dt("dbg_YY", [2 * NTH * 128, 1024], BF16, kind="ExternalOutput").ap()
        for nm_, dt_ in (("E", F32), ("Es", F32), ("tks", F32), ("S", F32), ("t3", F32), ("vb0", F32)):
            dbg["s_" + nm_] = dt("dbg_s_" + nm_, [128, 512], dt_, kind="ExternalOutput").ap()
        for nm_ in ("Ei", "Ak0", "Ak1", "Nk0", "Nk1", "Pf0", "qkT0", "kd0", "rv", "vnb", "Sb", "qk", "Pk0", "Pk1", "Afull", "Nfull", "Ym", "Dk0", "Dk1", "Wk0", "Wk1"):
            dbg["s_" + nm_] = dt("dbg_s_" + nm_, [128, 512], BF16, kind="ExternalOutput").ap()
        dbg["s_eg0"] = dt("dbg_s_eg0", [128, 12], F32, kind="ExternalOutput").ap()

    with ExitStack() as top:
        P = Prog(nc, top)
        cst = P.sb(top, "cst", [128, NCST * 128], F32)
        identb = P.sb(top, "identb", [128, 128], BF16)
        modT = P.sb(top, "modT", [128, 64], F32)
        g12row = P.sb(top, "g12row", [128, 2 * D], F32)
        psb = [P.ps(top, f"psb{i}", [128, 512], F32) for i in range(8)]
        epsT = P.sb(top, "epsT", [128, 4], F32)
        P.pool(lambda e: e.memset(epsT[:, 0:1], RMS_EPS), W=[epsT])
        P.pool(lambda e: e.memset(epsT[:, 1:2], LN_EPS), W=[epsT])
        ident = cst[:, 0:128]
        Uincl, Lstrict, Lincl, Ustrict, ones = (cst[:, i * 128:(i + 1) * 128] for i in range(1, 6))
        P.dma("sp", cst[:], consts[:, :], W=[cst])
        P.dve(lambda e: e.tensor_copy(out=identb[:], in_=ident), R=[cst], W=[identb])
        final_ops = []

        with ExitStack() as st:
            ccol = P.sb(st, "ccol", [128, 2, 8], F32)
            csil = P.sb(st, "csil", [128, 2, 8], F32)
            crep = P.sb(st, "crep", [128, 2, 8, 128], F32)
            barow = P.sb(st, "barow", [128, 6 * D], F32)
            modrow = P.sb(st, "modrow", [128, 4 * D], F32)
            cmodrow = P.sb(st, "cmodrow", [128, 2 * D], F32)
            wab = [P.sb(st, f"wab{i}", [128, 8, 512], F32) for i in range(2)]
            P.dma("sp", ccol[:].rearrange("p v k -> p (v k)"), cvecT[:, :], W=[ccol])
            P.dma("sp", barow[:], b_ada.partition_broadcast(128), W=[barow])
            P.act(lambda e: e.activation(out=csil[:], in_=ccol[:], func=AF.Silu), R=[ccol], W=[csil])
            P.dve(lambda e: e.tensor_copy(out=crep[:].rearrange("p v k m -> p (v k) m"),
                                          in_=bc(csil[:].rearrange("p v k -> p (v k)"), [128, 16, 128], 2)),
                  R=[csil], W=[crep])
            w_ada_v = w_ada.rearrange("(kc p) n -> p kc n", p=128)
            for nb in range(12):
                wb = wab[nb % 2]
                P.dma("sp", wb[:], w_ada_v[:, :, nb * 512:(nb + 1) * 512], W=[wb])
                pl, pc = psb[(2 * nb) % 8], psb[(2 * nb + 1) % 8]
                for kc in range(8):
                    P.pe(lambda e, kc=kc, wb=wb, pl=pl: e.matmul(pl[:], lhsT=crep[:, 0, kc, :], rhs=wb[:, kc, :],
                                                                 start=(kc == 0), stop=(kc == 7)), R=[crep, wb], W=[pl])
                if nb < 4:
                    for kc in range(8):
                        P.pe(lambda e, kc=kc, wb=wb, pc=pc: e.matmul(pc[:], lhsT=crep[:, 1, kc, :], rhs=wb[:, kc, :],
                                                                     start=(kc == 0), stop=(kc == 7)), R=[crep, wb], W=[pc])
                seg = nb // 2
                half = nb % 2
                bsl = barow[:, nb * 512:(nb + 1) * 512]
                if seg in (0, 1, 3, 4):
                    mi = {0: 0, 1: 1, 3: 2, 4: 3}[seg]
                    dst = modrow[:, mi * D + half * 512: mi * D + half * 512 + 512]
                    P.dve(lambda e, dst=dst, pl=pl, bsl=bsl: e.tensor_tensor(out=dst, in0=pl[:], in1=bsl, op=ALU.add),
                          R=[pl, barow], W=[modrow])
                    if seg in (1, 4):
                        P.dve(lambda e, dst=dst: e.tensor_scalar_add(out=dst, in0=dst, scalar1=1.0), R=[modrow], W=[modrow])
                else:
                    gi = 0 if seg == 2 else 1
                    dst = g12row[:, gi * D + half * 512: gi * D + half * 512 + 512]
                    P.dve(lambda e, dst=dst, pl=pl, bsl=bsl: e.tensor_tensor(out=dst, in0=pl[:], in1=bsl, op=ALU.add),
                          R=[pl, barow], W=[g12row])
                if nb < 4:
                    dst = cmodrow[:, nb * 512:(nb + 1) * 512]
                    P.dve(lambda e, dst=dst, pc=pc, bsl=bsl: e.tensor_tensor(out=dst, in0=pc[:], in1=bsl, op=ALU.add),
                          R=[pc, barow], W=[cmodrow])
                    if nb >= 2:
                        P.dve(lambda e, dst=dst: e.tensor_scalar_add(out=dst, in0=dst, scalar1=1.0), R=[cmodrow], W=[cmodrow])
            srcs = [(modrow, 0), (modrow, 1), (cmodrow, 0), (cmodrow, 1), (modrow, 2), (modrow, 3)]
            for v, (src, si) in enumerate(srcs):
                for g in range(2):
                    pt = psb[(2 * v + g) % 8]
                    for q in range(4):
                        fc = g * 4 + q
                        P.pe(lambda e, pt=pt, q=q, src=src, off=si * D + fc * 128: e.transpose(
                            out=pt[:, q * 128:(q + 1) * 128], in_=src[:, off:off + 128], identity=ident),
                            R=[src, cst], W=[pt])
                    P.act(lambda e, pt=pt, v=v, g=g: e.copy(
                        out=modT[:, 8 * v + 4 * g: 8 * v + 4 * g + 4],
                        in_=pt[:].rearrange("p (q m) -> p q m", q=4)[:, :, 0]), R=[pt], W=[modT])
            if debug:
                final_ops.append(P.dma("sp", dbg["modT"], modT[:], R=[modT]))

            wst = [P.sb(st, f"wst{i}", [128, 6144], F32) for i in range(2)]
            wsb = [P.sb(st, f"wsb{i}", [128, 6144], BF16) for i in range(2)]
            cnt = [0]

            def convert(src_aps, dst_ap, n):
                i = cnt[0] % 2
                cnt[0] += 1
                s, b = wst[i], wsb[i]
                off = 0
                for sap, shape in src_aps:
                    sz = int(np.prod(shape))
                    view = s[:, off:off + sz]
                    if len(shape) == 2:
                        view = view.rearrange("p (a b) -> p a b", a=shape[0])
                    P.dma("sp", view, sap, W=[s])
                    off += sz
                ei = cnt[0] % 3
                if ei == 2:
                    P.act(lambda e, s=s, b=b: e.copy(out=b[:, 0:n], in_=s[:, 0:n]), R=[s], W=[b])
                else:
                    (P.dve, P.pool)[ei](lambda e, s=s, b=b: e.tensor_copy(out=b[:, 0:n], in_=s[:, 0:n]), R=[s], W=[b])
                P.dma("sp", dst_ap, b[:, 0:n], R=[b])

            if stop_after >= 5:
                wo_v = w_out.rearrange("(kc p) n -> p kc n", p=128)
                for dh in range(2):
                    for kh in range(2):
                        convert([(wo_v[:, kh * 8:(kh + 1) * 8, dh * 512:(dh + 1) * 512], (8, 512))],
                                WO[dh, :, kh * 4096:(kh + 1) * 4096], 4096)
                wg_v = w_gate.rearrange("(kc p) n -> p kc n", p=128)
                wu_v = w_up.rearrange("(kc p) n -> p kc n", p=128)
                for blk in range(11):
                    convert([(wg_v[:, :, blk * 256:(blk + 1) * 256], (8, 256)),
                             (wu_v[:, :, blk * 256:(blk + 1) * 256], (8, 256))], WGU[blk, :, :], 4096)
                wd_v = w_down.rearrange("(j p) n -> p j n", p=128)
                for dh in range(2):
                    for jh in range(2):
                        convert([(wd_v[:, jh * 11:(jh + 1) * 11, dh * 512:(dh + 1) * 512], (11, 512))],
                                WD[dh * 2 + jh, :, :], 5632)
        P.barrier()

        if stop_after >= 1:
            stage1(P, nc, locals())
        P.barrier()
        if stop_after >= 2:
            scan_pass(P, nc, locals(), 0)
            P.barrier()
        if stop_after >= 3:
            scan_pass(P, nc, locals(), 1)
            P.barrier()
        if debug and stop_after >= 1:
            dd = T("dd", None)
            final_ops.append(P.dma("sp", dbg["CM"], CMs, semt=dd))
            final_ops.append(P.dma("sp", dbg["TM"], TMs, semt=dd))
            final_ops.append(P.dma("sp", dbg["SM"], SMs, semt=dd))
            if stop_after >= 2:
                final_ops.append(P.dma("sp", dbg["YF"], YFs, semt=dd))
                final_ops.append(P.dma("sp", dbg["YY"], YY.ap().rearrange("a t p f -> (a t p) f"), semt=dd))
            P.barrier()
        if stop_after >= 4:
            cps = [T(f"cp{i}", None) for i in range(4)]
            CW = 8192
            ccnt = [0]

            def dyn_copy(dst3, src4, sel, fresh):
                nt_ = dst3.shape[0]
                nr = nt_ * 128 * 1024 // CW
                dflat = dst3.rearrange("t p f -> (t p f)").rearrange("(r c) -> r c", c=CW)
                def fn(e, fresh=fresh):
                    if fresh:
                        pid = e.partition_id()
                        P.dyn = {0: e.snap(pid % 2), 1: e.snap(1 - pid % 2)}
                    sflat = src4[bass.ds(P.dyn[sel], 1)].rearrange("a t p f -> (a t p f)").rearrange("(r c) -> r c", c=CW)
                    return e.dma_start(out=dflat, in_=sflat)
                ccnt[0] += 1
                P.dma("sp", semt=cps[ccnt[0] % 4], fn=fn)

            dyn_copy(ZIN.ap(), YY.ap(), 1, True)
            P.barrier()
            cct = T("cc", None)
            for k in range(NCC):
                P.dma("pool", semt=cct, amt=1, fn=lambda e, k=k: e.collective_compute(
                    "AllGather", ALU.bypass, replica_groups=[[0, 1], [2, 3], [4, 5], [6, 7]],
                    ins=[ZIN.ap()[k * TPK:(k + 1) * TPK].rearrange("t p f -> (t p) f").opt()],
                    outs=[ZO.ap()[k].rearrange("a t p f -> (a t p) f").opt()]))
            P.barrier()
            dyn_copy(MINE.ap(), YY.ap(), 0, True)
            for k in range(NCC):
                dyn_copy(PART.ap()[k * TPK:(k + 1) * TPK], ZO.ap()[k], 1, False)
            P.barrier()
        if stop_after >= 5:
            stage5(P, nc, locals())
        else:
            with ExitStack() as st:
                tb = P.sb(st, "tb", [128, D], F32)
                for a in range(HALF // 128):
                    P.dma("sp", tb[:], xhalf[a * 128:(a + 1) * 128, :], W=[tb])
                    final_ops.append(P.dma("sp", yout[a * 128:(a + 1) * 128, :], tb[:], R=[tb]))
        P.emit(final_ops=final_ops + locals().get("_final", []))
        stats = dict(nsem=P.nsem, nops={e: len(P.ops[e]) for e in P.ENGS}, nwaits=P.nwaits)
    return nc, stats


def stage1(P, nc, L):
    TT, NTT, NCT = L["TT"], L["NTT"], L["NCT"]
    xin, w_cm, w_tm, convw, convb, rowp = L["xin"], L["w_cm"], L["w_tm"], L["convw"], L["convb"], L["rowp"]
    CMs, TMs, SMs = L["CMs"], L["TMs"], L["SMs"]
    cst, identb, modT, psb = L["cst"], L["identb"], L["modT"], L["psb"]
    ident = cst[:, 0:128]
    ones = cst[:, 5 * 128:6 * 128]
    with ExitStack() as st:
        wcm = P.sb(st, "wcm", [128, 8, NCM * 128], BF16)
        wtm = P.sb(st, "wtm", [128, 8, NTMC], BF16)
        wld = [P.sb(st, f"wld{i}", [128, 1152], F32) for i in range(2)]
        cw = P.sb(st, "cw", [128, NCM, 5], F32)
        cb = P.sb(st, "cb", [128, 6], F32)
        spb = P.sb(st, "spb", [128, 48], F32)
        amul = P.sb(st, "amul", [128, 24], F32)
        xt = [P.sb(st, f"xt{i}", [128, 4, D], F32) for i in range(2)]
        hT = [P.sb(st, f"hT{i}", [128, 8, 512], BF16) for i in range(2)]
        pad = [P.sb(st, f"pad{i}", [128, 544], F32) for i in range(3)]
        acc = [P.sb(st, f"acc{i}", [128, 512], F32) for i in range(3)]
        ptmp = P.sb(st, "ptmp", [128, 512], F32)
        sv = [P.sb(st, f"sv{i}", [128, 512], F32) for i in range(3)]
        sq = [P.sb(st, f"sq{i}", [128, 512], F32) for i in range(3)]
        rr = [P.sb(st, f"rr{i}", [128, 512], F32) for i in range(3)]
        tbf = [P.sb(st, f"tbf{i}", [128, 512], BF16) for i in range(4)]
        cmst = [P.sb(st, f"cmst{i}", [128, 4, 10, 128], BF16) for i in range(2)]
        tmst = [P.sb(st, f"tmst{i}", [128, 4, TMW], BF16) for i in range(1)]
        smst = [P.sb(st, f"smst{i}", [128, 4, SMW], F32) for i in range(2)]
        smr = [P.sb(st, f"smr{i}", [128, 32], F32) for i in range(2)]
        wcm_v = w_cm.rearrange("(kc p) n -> p kc n", p=128)
        wtm_v = w_tm.rearrange("(kc p) n -> p kc n", p=128)
        for kc in range(8):
            for hf in range(2):
                w = wld[hf]
                P.dma("sp", w[:], wcm_v[:, kc, hf * 1152:(hf + 1) * 1152], W=[w])
                (P.dve if hf == 0 else P.pool)(lambda e, w=w, kc=kc, hf=hf: e.tensor_copy(
                    out=wcm[:, kc, hf * 1152:(hf + 1) * 1152], in_=w[:]), R=[w], W=[wcm])
        for kc in range(8):
            w = wld[kc % 2]
            P.dma("sp", w[:, 0:NTMC], wtm_v[:, kc, :], W=[w])
            (P.dve if kc % 2 == 0 else P.pool)(lambda e, w=w, kc=kc: e.tensor_copy(out=wtm[:, kc, :], in_=w[:, 0:NTMC]), R=[w], W=[wtm])
        P.dma("sp", cw[:].rearrange("p c k -> p (c k)"), convw[:, :], W=[cw])
        P.dma("sp", cb[:], convb[:, :], W=[cb])
        P.dma("sp", spb[:], rowp[:, 0:48].partition_broadcast(128), W=[spb])
        P.act(lambda e: e.activation(out=amul[:], in_=spb[:, 24:48], func=AF.Exp), R=[spb], W=[amul])
        P.dve(lambda e: e.tensor_scalar_mul(out=amul[:], in0=amul[:], scalar1=-1.0), R=[amul], W=[amul])
        for p_ in pad:
            P.pool(lambda e, p_=p_: e.memset(p_[:], 0.0), W=[p_])

        blocks = [(0, CTX, CTX, 2)]
        t0 = CTX
        while t0 < TT:
            blocks.append((t0, 512, GRID_W, 0))
            t0 += 512
        PX, PA, PB, PN, PZ0, PZ1, PSm, PTr = L["psb"]
        ptr_bf = PTr.ap[:].bitcast(BF16)
        for bi, (t0, ntok, rowlen, mv) in enumerate(blocks):
            nt = ntok // 128
            nrow = ntok // rowlen
            x_, h_ = xt[bi % 2], hT[bi % 2]
            if bi == 1:
                for p_ in pad:
                    P.pool(lambda e, p_=p_: e.memset(p_[:], 0.0), W=[p_])
            cm_, tm_, sm_ = cmst[bi % 2], tmst[0], smst[bi % 2]
            P.dma("sp", x_[:, 0:nt, :], xin[t0:t0 + ntok, :].rearrange("(a p) d -> p a d", p=128), W=[x_])
            for fc in range(8):
                for a in range(nt):
                    P.pe(lambda e, a=a, fc=fc, x_=x_: e.transpose(out=PX[:, a * 128:(a + 1) * 128],
                                                                  in_=x_[:, a, fc * 128:(fc + 1) * 128], identity=ident),
                         R=[x_, cst], W=[PX])
                P.act(lambda e, fc=fc, h_=h_, mv=mv, ntok=ntok: e.activation(
                    out=h_[:, fc, 0:ntok], in_=PX[:, 0:ntok], func=AF.Identity,
                    bias=modT[:, 8 * mv + fc: 8 * mv + fc + 1], scale=modT[:, 8 * (mv + 1) + fc: 8 * (mv + 1) + fc + 1]),
                    R=[PX, modT], W=[h_])
            deferred = []

            def flush(upto):
                keep = []
                for due, fn_ in deferred:
                    if due <= upto:
                        fn_()
                    else:
                        keep.append((due, fn_))
                deferred[:] = keep

            def emit_transposes(src_fn, Rt, tmoff, nt=nt, tm_=tm_):
                for a in range(nt):
                    P.pe(lambda e, a=a: e.transpose(out=ptr_bf[:, a * 128:(a + 1) * 128], in_=src_fn(a), identity=identb[:]),
                         R=[Rt, identb], W=[PTr])
                P.dve(lambda e: e.tensor_copy(
                    out=tm_[:, 0:nt, tmoff:tmoff + 128], in_=ptr_bf[:, 0:nt * 128].rearrange("p (a l) -> p a l", a=nt)),
                    R=[PTr], W=[tm_])

            def emit_l2norm(cc, kind, s_, q_, r_, ntok=ntok, nt=nt, cm_=cm_):
                P.pe(lambda e: e.matmul(PN[:, 0:ntok], lhsT=ones, rhs=q_[:, 0:ntok], start=True, stop=True),
                     R=[cst, q_], W=[PN])
                P.act(lambda e: e.activation(out=r_[:, 0:ntok], in_=PN[:, 0:ntok], func=AF.Sqrt,
                                             bias=cst_eps(L), scale=1.0), R=[PN, L["epsT"]], W=[r_])
                P.dve(lambda e: e.reciprocal(out=r_[:, 0:ntok], in_=r_[:, 0:ntok]), R=[r_], W=[r_])
                if kind == "q":
                    dst = cm_[:, 0:nt, cc, :]
                    P.dve(lambda e: e.scalar_tensor_tensor(
                        out=dst, in0=s_[:, 0:ntok].rearrange("p (a l) -> p a l", a=nt), scalar=128.0 ** -0.5,
                        in1=r_[:, 0:ntok].rearrange("p (a l) -> p a l", a=nt), op0=ALU.mult, op1=ALU.mult),
                        R=[s_, r_], W=[cm_])
                else:
                    dst = cm_[:, 0:nt, 2 + (cc - 10), :]
                    P.dve(lambda e: e.tensor_tensor(
                        out=dst, in0=s_[:, 0:ntok].rearrange("p (a l) -> p a l", a=nt),
                        in1=r_[:, 0:ntok].rearrange("p (a l) -> p a l", a=nt), op=ALU.mult), R=[s_, r_], W=[cm_])

            for cc in range(NCM):
                pp = (PA, PB)[cc % 2]
                for kc in range(8):
                    P.pe(lambda e, kc=kc, cc=cc, pp=pp, h_=h_, ntok=ntok: e.matmul(
                        pp[:, 0:ntok], lhsT=wcm[:, kc, cc * 128:(cc + 1) * 128], rhs=h_[:, kc, 0:ntok],
                        start=(kc == 0), stop=(kc == 7)), R=[wcm, h_], W=[pp])
                flush(cc)
                pd, ac = pad[cc % 3], acc[cc % 3]
                pdv = pd[:, 0:nrow * (rowlen + 4)].rearrange("p (r l) -> p r l", r=nrow)
                P.act(lambda e, pdv=pdv, pp=pp, ntok=ntok, nrow=nrow, rowlen=rowlen: e.copy(
                    out=pdv[:, :, 2:2 + rowlen], in_=pp[:, 0:ntok].rearrange("p (r l) -> p r l", r=nrow)), R=[pp], W=[pd])
                acv = ac[:, 0:ntok].rearrange("p (r l) -> p r l", r=nrow)
                P.dve(lambda e, acv=acv, pdv=pdv, cc=cc, rowlen=rowlen: e.tensor_scalar_mul(
                    out=acv, in0=pdv[:, :, 0:rowlen], scalar1=cw[:, cc, 0:1]), R=[pd, cw], W=[ac])
                for k in range(1, 5):
                    P.dve(lambda e, acv=acv, pdv=pdv, cc=cc, k=k, rowlen=rowlen: e.scalar_tensor_tensor(
                        out=acv, in0=pdv[:, :, k:k + rowlen], scalar=cw[:, cc, k:k + 1], in1=acv,
                        op0=ALU.mult, op1=ALU.add), R=[pd, cw, ac], W=[ac])
                kind = ("xs" if cc < 4 else "B" if cc == 4 else "C" if cc == 5 else "q" if cc < 10 else "k" if cc < 14 else "v")
                if kind in ("xs", "v", "B", "C"):
                    tb = tbf[cc % 4]
                    if kind == "C":
                        dst = cm_[:, 0:nt, 0, :]
                    elif kind == "B":
                        dst = cm_[:, 0:nt, 1, :]
                    else:
                        dst = tb[:, 0:ntok].rearrange("p (a l) -> p a l", a=nt)
                    Wt = [cm_] if kind in ("B", "C") else [tb]
                    if cc < 6:
                        P.act(lambda e, dst=dst, ac=ac, cc=cc, ntok=ntok, nt=nt: e.activation(
                            out=dst, in_=ac[:, 0:ntok].rearrange("p (a l) -> p a l", a=nt), func=AF.Silu,
                            bias=cb[:, cc:cc + 1], scale=1.0), R=[ac, cb], W=Wt)
                    else:
                        P.act(lambda e, dst=dst, ac=ac, ntok=ntok, nt=nt: e.activation(
                            out=dst, in_=ac[:, 0:ntok].rearrange("p (a l) -> p a l", a=nt), func=AF.Silu),
                            R=[ac], W=Wt)
                    if kind == "C":
                        continue
                    tmoff = {"xs": cc * 128, "B": 512, "v": 1152 + (cc - 14) * 128}[kind]
                    if kind == "B":
                        deferred.append((cc + 3, lambda cm_=cm_, tmoff=tmoff: emit_transposes(lambda a: cm_[:, a, 1, :], cm_, tmoff)))
                    else:
                        deferred.append((cc + 3, lambda tb=tb, tmoff=tmoff: emit_transposes(lambda a: tb[:, a * 128:(a + 1) * 128], tb, tmoff)))
                else:
                    s_, q_, r_ = sv[cc % 3], sq[cc % 3], rr[cc % 3]
                    P.act(lambda e, s_=s_, ac=ac, ntok=ntok: e.activation(out=s_[:, 0:ntok], in_=ac[:, 0:ntok], func=AF.Silu),
                          R=[ac], W=[s_])
                    P.pool(lambda e, s_=s_, q_=q_, ntok=ntok: e.tensor_tensor(out=q_[:, 0:ntok], in0=s_[:, 0:ntok], in1=s_[:, 0:ntok], op=ALU.mult),
                           R=[s_], W=[q_])
                    deferred.append((cc + 2, lambda cc=cc, kind=kind, s_=s_, q_=q_, r_=r_: emit_l2norm(cc, kind, s_, q_, r_)))
                    if kind == "k":
                        ci = 2 + (cc - 10)
                        tmoff = 640 + (cc - 10) * 128
                        deferred.append((cc + 3, lambda cm_=cm_, ci=ci, tmoff=tmoff: emit_transposes(lambda a: cm_[:, a, ci, :], cm_, tmoff)))
            flush(10 ** 9)
            for a in range(nt):
                for gi, (n0, nn, pz) in enumerate(((0, 512, PZ0), (512, 512, PZ1), (1024, 32, PSm))):
                    for kc in range(8):
                        P.pe(lambda e, kc=kc, a=a, n0=n0, nn=nn, pz=pz, h_=h_: e.matmul(
                            pz[:, 0:nn], lhsT=h_[:, kc, a * 128:(a + 1) * 128], rhs=wtm[:, kc, n0:n0 + nn],
                            start=(kc == 0), stop=(kc == 7)), R=[h_, wtm], W=[pz])
                    if gi < 2:
                        off = 1664 + gi * 512
                        P.act(lambda e, a=a, off=off, pz=pz, tm_=tm_: e.activation(out=tm_[:, a, off:off + 512], in_=pz[:], func=AF.Silu),
                              R=[pz], W=[tm_])
                    else:
                        r_ = smr[a % 2]
                        smv = sm_[:, a, :]
                        P.dve(lambda e, r_=r_: e.tensor_tensor(out=r_[:, 0:24], in0=PSm[:, 0:24], in1=spb[:, 0:24], op=ALU.add),
                              R=[PSm, spb], W=[r_])
                        P.act(lambda e, r_=r_: e.activation(out=r_[:, 0:24], in_=r_[:, 0:24], func=AF.Exp), R=[r_], W=[r_])
                        P.act(lambda e, r_=r_: e.activation(out=r_[:, 0:24], in_=r_[:, 0:24], func=AF.Ln, bias=1.0, scale=1.0), R=[r_], W=[r_])
                        P.act(lambda e, smv=smv: e.activation(out=smv[:, 40:48], in_=PSm[:, 24:32], func=AF.Sigmoid), R=[PSm], W=[sm_])
                        P.dve(lambda e, smv=smv, r_=r_: e.tensor_copy(out=smv[:, 0:16], in_=r_[:, 0:16]), R=[r_], W=[sm_])
                        P.dve(lambda e, smv=smv, r_=r_: e.tensor_tensor(out=smv[:, 16:40], in0=r_[:, 0:24], in1=amul[:], op=ALU.mult),
                              R=[r_, amul], W=[sm_])
                        P.dve(lambda e, smv=smv: e.tensor_scalar_mul(out=smv[:, 48:56], in0=smv[:, 40:48], scalar1=-1.0), R=[sm_], W=[sm_])
                        P.dve(lambda e, smv=smv: e.memset(smv[:, 56:64], 0.0), W=[sm_])
            ti0 = t0 // 128
            P.dma("sp", CMs[ti0:ti0 + nt].rearrange("t p f -> p t f"), cm_[:, 0:nt].rearrange("p t c l -> p t (c l)"), R=[cm_])
            P.dma("sp", TMs[t0:t0 + ntok, :].rearrange("(a p) f -> p a f", p=128), tm_[:, 0:nt, :], R=[tm_])
            P.dma("sp", SMs[t0:t0 + ntok, :].rearrange("(a p) f -> p a f", p=128), sm_[:, 0:nt, :], R=[sm_])


def cst_eps(L):
    return L["epsT"][:, 0:1]


def scan_pass(P, nc, L, d):
    NTT, NCT, NLT, NTH = L["NTT"], L["NCT"], L["NLT"], L["NTH"]
    CMs, TMs, SMs, YFs, YY, rowp = L["CMs"], L["TMs"], L["SMs"], L["YFs"], L["YY"], L["rowp"]
    cst, identb, psb, epsT = L["cst"], L["identb"], L["psb"], L["epsT"]
    Uincl, Lstrict, Lincl, Ustrict, ones = (cst[:, i * 128:(i + 1) * 128] for i in range(1, 6))
    if d == 0:
        Tm, Um, TmT = Uincl, Lstrict, Lincl
        order = list(range(NCT)) + [NCT + i for i in range(NLT)]
    else:
        Tm, Um, TmT = Lincl, Ustrict, Uincl
        order = list(reversed(range(NCT))) + [NCT + i for i in reversed(range(NLT))]
    prep_banks = psb[0:3]
    out_banks = psb[3:5]
    bKS, bSU, bST = psb[5], psb[6], psb[7]
    cnt = {"p": 0, "o": 0}

    def pb():
        cnt["p"] += 1
        return prep_banks[cnt["p"] % 3]

    def ob():
        cnt["o"] += 1
        return out_banks[cnt["o"] % 2]

    with ExitStack() as st:
        sb = lambda n, shp, dt_: P.sb(st, f"{n}_{d}", shp, dt_)
        cmb = [sb(f"cmb{i}", [128, 10, 128], BF16) for i in range(3)]
        tmb = [sb(f"tmb{i}", [128, TMW], BF16) for i in range(3)]
        smb = [sb(f"smb{i}", [128, SMW], F32) for i in range(3)]
        yo = [sb(f"yo{i}", [128, 1024], F32) for i in range(2)]
        rla = sb("rla", [128, 8, 128], F32)
        ET = sb("ET", [128, 8, 128], BF16)
        gmt = sb("gmt", [128, 128], BF16)
        att = [sb(f"att{i}", [128, 8, 128], BF16) for i in range(2)]
        xdt = [sb(f"xdt{i}", [128, 8, 64], BF16) for i in range(2)]
        xdw = [sb(f"xdw{i}", [128, 8, 64], BF16) for i in range(2)]
        ea = [sb(f"ea{i}", [128, 24], F32) for i in range(2)]
        t1 = sb("t1", [128, 8, 64], F32)
        ST = sb("ST", [128, 8, 64], F32)
        STb = sb("STb", [128, 512], BF16)
        rlu = sb("rlu", [128, 4, 128], F32)
        E = sb("E", [128, 4, 128], F32)
        Es = sb("Es", [128, 4, 128], F32)
        Ei = sb("Ei", [128, 4, 128], BF16)
        eg = [sb(f"eg{i}", [128, 12], F32) for i in range(2)]
        eb = [sb(f"eb{i}", [128, 4], F32) for i in range(2)]
        Ak = [sb(f"Ak{i}", [128, 4, 128], BF16) for i in range(2)]
        Nk = [sb(f"Nk{i}", [128, 4, 128], BF16) for i in range(2)]
        Pk = [sb(f"Pk{i}", [128, 4, 128], BF16) for i in range(2)]
        Pf = [sb(f"Pf{i}", [128, 4, 128], BF16) for i in range(2)]
        Wk = [sb(f"Wk{i}", [128, 4, 128], BF16) for i in range(2)]
        Dk = [sb(f"Dk{i}", [128, 4, 128], BF16) for i in range(2)]
        Afull = sb("Afull", [128, 4, 128], BF16)
        Nfull = sb("Nfull", [128, 4, 128], BF16)
        Ym = sb("Ym", [128, 4, 128], BF16)
        qk = sb("qk", [128, 4, 128], BF16)
        qkT = [sb(f"qkT{i}", [128, 4, 128], BF16) for i in range(2)]
        vb = [sb(f"vb{i}", [128, 4, 128], F32) for i in range(2)]
        kd = [sb(f"kd{i}", [128, 4, 128], BF16) for i in range(2)]
        tks = sb("tks", [128, 4, 128], F32)
        rv = sb("rv", [128, 4, 128], BF16)
        vnb = sb("vnb", [128, 4, 128], BF16)
        S = sb("S", [128, 4, 128], F32)
        Sb = sb("Sb", [128, 4, 128], BF16)
        t3 = sb("t3", [128, 4, 128], F32)
        P.pool(lambda e: e.memset(ST[:], 0.0), W=[ST])
        P.pool(lambda e: e.memset(STb[:], 0.0), W=[STb])
        P.pool(lambda e: e.memset(S[:], 0.0), W=[S])
        P.pool(lambda e: e.memset(Sb[:], 0.0), W=[Sb])
        if d == 1:
            yfb = [sb(f"yfb{i}", [128, 1024], F32) for i in range(3)]
            fin = sb("fin", [128, 648], F32)
            P.dma("sp", fin[:], rowp[:, 48:696].partition_broadcast(128), W=[fin])
            u = sb("u", [128, 1024], F32)
            usq = sb("usq", [128, 1024], F32)
            ss = sb("ss", [128, 8], F32)
            yfin = sb("yfin", [128, 1024], BF16)
            yT = [sb(f"yT{i}", [128, 8, 128], BF16) for i in range(2)]

        def load(ci):
            ti = order[ci]
            cm, tm, sm = cmb[ci % 3], tmb[ci % 3], smb[ci % 3]
            P.dma("sp", cm[:].rearrange("p c l -> p (c l)"), CMs[ti], W=[cm])
            P.dma("sp", tm[:], TMs[ti * 128:(ti + 1) * 128, :], W=[tm])
            P.dma("sp", sm[:], SMs[ti * 128:(ti + 1) * 128, :], W=[sm])
            if d == 1 and ti >= NCT:
                yf = yfb[ci % 3]
                P.dma("sp", yf[:], YFs[(ti - NCT) * 128:(ti - NCT + 1) * 128, :], W=[yf])

        def prep(ci):
            cm, tm, sm = cmb[ci % 3], tmb[ci % 3], smb[ci % 3]
            i2 = ci % 2
            CT, BT = cm[:, 0, :], cm[:, 1, :]
            la = sm[:, 16 + 8 * d:24 + 8 * d]
            dtd = sm[:, 8 * d:8 * d + 8]
            lag = sm[:, 32 + 4 * d:36 + 4 * d]
            beta = sm[:, 40 + 4 * d:44 + 4 * d]
            nbeta = sm[:, 48 + 4 * d:52 + 4 * d]
            if PARTS & 4:
                return
            P.pool(lambda e: e.tensor_tensor(out=rla[:], in0=bc(la, [128, 8, 128], 2), in1=bc(Tm, [128, 8, 128], 1), op=ALU.mult),
                   R=[sm, cst], W=[rla])
            for hf in range(2):
                pd = pb()
                P.pe(lambda e, pd=pd, hf=hf: e.matmul(pd[:], lhsT=Um, rhs=rla[:, 4 * hf:4 * hf + 4, :].rearrange("p h l -> p (h l)"),
                                                      start=True, stop=True), R=[cst, rla], W=[pd])
                P.act(lambda e, pd=pd, hf=hf: e.activation(out=ET[:, 4 * hf:4 * hf + 4, :].rearrange("p h l -> p (h l)"), in_=pd[:], func=AF.Exp),
                      R=[pd], W=[ET])
            pg = pb()
            P.pe(lambda e: e.matmul(pg[:, 0:128], lhsT=BT, rhs=CT, start=True, stop=True), R=[cm], W=[pg])
            P.pe(lambda e: e.matmul(pg[:, 128:136], lhsT=Tm, rhs=la, start=True, stop=True), R=[cst, sm], W=[pg])
            P.pe(lambda e: e.matmul(pg[:, 136:144], lhsT=Um, rhs=la, start=True, stop=True), R=[cst, sm], W=[pg])
            P.pe(lambda e: e.matmul(pg[:, 144:152], lhsT=ones, rhs=la, start=True, stop=True), R=[cst, sm], W=[pg])
            P.dve(lambda e: e.tensor_tensor(out=gmt[:], in0=pg[:, 0:128], in1=Tm, op=ALU.mult), R=[pg, cst], W=[gmt])
            ea_ = ea[i2]
            P.act(lambda e: e.activation(out=ea_[:], in_=pg[:, 128:152], func=AF.Exp), R=[pg], W=[ea_])
            att_, xdt_, xdw_ = att[i2], xdt[i2], xdw[i2]
            P.dve(lambda e: e.tensor_tensor(out=att_[:], in0=ET[:], in1=bc(gmt[:], [128, 8, 128], 1), op=ALU.mult),
                  R=[ET, gmt], W=[att_])
            P.pool(lambda e: e.tensor_tensor(out=xdt_[:], in0=tm[:, 0:512].rearrange("p (h q) -> p h q", h=8),
                                             in1=bc(dtd, [128, 8, 64], 2), op=ALU.mult), R=[tm, sm], W=[xdt_])
            P.dve(lambda e: e.tensor_tensor(out=xdw_[:], in0=xdt_[:], in1=bc(ea_[:, 8:16], [128, 8, 64], 2), op=ALU.mult),
                  R=[xdt_, ea_], W=[xdw_])
            if PARTS & 8:
                return
            P.pool(lambda e: e.tensor_tensor(out=rlu[:], in0=bc(lag, [128, 4, 128], 2), in1=bc(Um, [128, 4, 128], 1), op=ALU.mult),
                   R=[sm, cst], W=[rlu])
            pdg = pb()
            P.pe(lambda e: e.matmul(pdg[:], lhsT=Tm, rhs=rlu[:].rearrange("p h l -> p (h l)"), start=True, stop=True),
                 R=[cst, rlu], W=[pdg])
            P.act(lambda e: e.activation(out=E[:].rearrange("p h l -> p (h l)"), in_=pdg[:], func=AF.Exp), R=[pdg], W=[E])
            ps2 = pb()
            P.pe(lambda e: e.matmul(ps2[:, 0:4], lhsT=Tm, rhs=lag, start=True, stop=True), R=[cst, sm], W=[ps2])
            P.pe(lambda e: e.matmul(ps2[:, 4:8], lhsT=Um, rhs=lag, start=True, stop=True), R=[cst, sm], W=[ps2])
            P.pe(lambda e: e.matmul(ps2[:, 8:12], lhsT=ones, rhs=lag, start=True, stop=True), R=[cst, sm], W=[ps2])
            eg_, eb_ = eg[i2], eb[i2]
            P.act(lambda e: e.activation(out=eg_[:], in_=ps2[:, 0:12], func=AF.Exp), R=[ps2], W=[eg_])
            P.dve(lambda e: e.tensor_tensor(out=Es[:], in0=E[:], in1=bc(Um, [128, 4, 128], 1), op=ALU.mult), R=[E, cst], W=[Es])
            P.dve(lambda e: e.tensor_tensor(out=Es[:], in0=Es[:], in1=bc(nbeta, [128, 4, 128], 2), op=ALU.mult), R=[Es, sm], W=[Es])
            P.pool(lambda e: e.tensor_tensor(out=Ei[:], in0=E[:], in1=bc(TmT, [128, 4, 128], 1), op=ALU.mult), R=[E, cst], W=[Ei])
            pkk, pqk = pb(), pb()
            for h in range(4):
                P.pe(lambda e, h=h: e.matmul(pkk[:, h * 128:(h + 1) * 128], lhsT=cm[:, 2 + h, :], rhs=cm[:, 2 + h, :], start=True, stop=True),
                     R=[cm], W=[pkk])
            for h in range(4):
                P.pe(lambda e, h=h: e.matmul(pqk[:, h * 128:(h + 1) * 128], lhsT=cm[:, 6 + h, :], rhs=cm[:, 2 + h, :], start=True, stop=True),
                     R=[cm], W=[pqk])
            BD16 = cst[:, 6 * 128:7 * 128]
            Mb = [cst[:, (7 + i + 3 * d) * 128:(8 + i + 3 * d) * 128] for i in range(3)]
            P.dve(lambda e: e.tensor_tensor(out=Afull[:].rearrange("p h l -> p (h l)"), in0=pkk[:], in1=Es[:].rearrange("p h l -> p (h l)"), op=ALU.mult),
                  R=[pkk, Es], W=[Afull])
            P.dve(lambda e: e.tensor_tensor(out=qk[:].rearrange("p h l -> p (h l)"), in0=pqk[:], in1=Ei[:].rearrange("p h l -> p (h l)"), op=ALU.mult),
                  R=[pqk, Ei], W=[qk])
            pt = pb()
            ptb = pt.ap[:].bitcast(BF16)
            for h in range(4):
                P.pe(lambda e, h=h: e.transpose(out=ptb[:, h * 128:(h + 1) * 128], in_=Afull[:, h, :], identity=identb[:]), R=[Afull, identb], W=[pt])
            for h in range(4):
                P.pe(lambda e, h=h: e.transpose(out=ptb[:, 512 + h * 128:512 + (h + 1) * 128], in_=qk[:, h, :], identity=identb[:]), R=[qk, identb], W=[pt])
            qkT_ = qkT[i2]
            P.act(lambda e: e.copy(out=Nfull[:].rearrange("p h l -> p (h l)"), in_=ptb[:, 0:512]), R=[pt], W=[Nfull])
            P.dve(lambda e: e.tensor_copy(out=qkT_[:].rearrange("p h l -> p (h l)"), in_=ptb[:, 512:1024]), R=[pt], W=[qkT_])
            A, N, Pc = Ak[0], Nk[0], Pk[0]
            P.dve(lambda e, A=A: e.tensor_tensor(out=A[:], in0=Afull[:], in1=bc(BD16, [128, 4, 128], 1), op=ALU.mult), R=[Afull, cst], W=[A])
            P.dve(lambda e, N=N: e.tensor_tensor(out=N[:], in0=Nfull[:], in1=bc(BD16, [128, 4, 128], 1), op=ALU.mult), R=[Nfull, cst], W=[N])
            P.dve(lambda e, N=N, Pc=Pc: e.tensor_tensor(out=Pc[:], in0=N[:], in1=bc(identb[:], [128, 4, 128], 1), op=ALU.add), R=[N, identb], W=[Pc])
            for k in range(1, 4):
                A2, N2 = Ak[k % 2], Nk[k % 2]
                P2 = Pk[k % 2]
                pa = pb()
                for h in range(4):
                    P.pe(lambda e, h=h, N=N, A=A, pa=pa: e.matmul(pa[:, h * 128:(h + 1) * 128], lhsT=N[:, h, :], rhs=A[:, h, :], start=True, stop=True),
                         R=[N, A], W=[pa])
                P.act(lambda e, A2=A2, pa=pa: e.copy(out=A2[:].rearrange("p h l -> p (h l)"), in_=pa[:]), R=[pa], W=[A2])
                if k <= 2:
                    pn = pb()
                    for h in range(4):
                        P.pe(lambda e, h=h, N=N, A=A, pn=pn: e.matmul(pn[:, h * 128:(h + 1) * 128], lhsT=A[:, h, :], rhs=N[:, h, :], start=True, stop=True),
                             R=[N, A], W=[pn])
                    P.act(lambda e, N2=N2, pn=pn: e.copy(out=N2[:].rearrange("p h l -> p (h l)"), in_=pn[:]), R=[pn], W=[N2])
                pm = pb()
                for h in range(4):
                    P.pe(lambda e, h=h, A2=A2, Pc=Pc, pm=pm: e.matmul(pm[:, h * 128:(h + 1) * 128], lhsT=A2[:, h, :], rhs=Pc[:, h, :], start=True, stop=True),
                         R=[A2, Pc], W=[pm])
                P.dve(lambda e, P2=P2, Pc=Pc, pm=pm: e.tensor_tensor(out=P2[:].rearrange("p h l -> p (h l)"), in0=pm[:],
                                                                     in1=Pc[:].rearrange("p h l -> p (h l)"), op=ALU.add), R=[pm, Pc], W=[P2])
                A, N, Pc = A2, N2, P2
            Wc = Pc
            pt2 = pb()
            pt2b = pt2.ap[:].bitcast(BF16)
            for h in range(4):
                P.pe(lambda e, h=h, Wc=Wc: e.transpose(out=pt2b[:, h * 128:(h + 1) * 128], in_=Wc[:, h, :], identity=identb[:]), R=[Wc, identb], W=[pt2])
            Dc = Dk[0]
            P.act(lambda e, Dc=Dc: e.copy(out=Dc[:].rearrange("p h l -> p (h l)"), in_=pt2b[:, 0:512]), R=[pt2], W=[Dc])
            for li in range(3):
                last = li == 2
                py_ = pb()
                for h in range(4):
                    P.pe(lambda e, h=h, Dc=Dc, py_=py_: e.matmul(py_[:, h * 128:(h + 1) * 128], lhsT=Nfull[:, h, :], rhs=Dc[:, h, :], start=True, stop=True),
                         R=[Nfull, Dc], W=[py_])
                P.dve(lambda e, py_=py_, li=li: e.tensor_tensor(out=Ym[:], in0=py_[:].rearrange("p (h l) -> p h l", h=4), in1=bc(Mb[li], [128, 4, 128], 1), op=ALU.mult),
                      R=[py_, cst], W=[Ym])
                if not last:
                    pz = pb()
                    for h in range(4):
                        P.pe(lambda e, h=h, Wc=Wc, pz=pz: e.matmul(pz[:, h * 128:(h + 1) * 128], lhsT=Wc[:, h, :], rhs=Ym[:, h, :], start=True, stop=True),
                             R=[Wc, Ym], W=[pz])
                    D2 = Dk[(li + 1) % 2]
                    P.dve(lambda e, D2=D2, Dc=Dc, pz=pz: e.tensor_tensor(out=D2[:].rearrange("p h l -> p (h l)"), in0=pz[:],
                                                                         in1=Dc[:].rearrange("p h l -> p (h l)"), op=ALU.add), R=[pz, Dc], W=[D2])
                pzt = pb()
                for h in range(4):
                    P.pe(lambda e, h=h, Wc=Wc, pzt=pzt: e.matmul(pzt[:, h * 128:(h + 1) * 128], lhsT=Ym[:, h, :], rhs=Wc[:, h, :], start=True, stop=True),
                         R=[Wc, Ym], W=[pzt])
                W2 = Pf[i2] if last else Wk[li % 2]
                P.dve(lambda e, W2=W2, Wc=Wc, pzt=pzt: e.tensor_tensor(out=W2[:].rearrange("p h l -> p (h l)"), in0=pzt[:],
                                                                        in1=Wc[:].rearrange("p h l -> p (h l)"), op=ALU.add), R=[pzt, Wc], W=[W2])
                Wc = W2
                if not last:
                    Dc = D2
            vb_, kd_ = vb[i2], kd[i2]
            P.pool(lambda e: e.tensor_tensor(out=vb_[:], in0=tm[:, 1152:1664].rearrange("p (h v) -> p h v", h=4), in1=bc(beta, [128, 4, 128], 2), op=ALU.mult),
                   R=[tm, sm], W=[vb_])
            P.dve(lambda e: e.tensor_tensor(out=eb_[:], in0=eg_[:, 0:4], in1=beta, op=ALU.mult), R=[eg_, sm], W=[eb_])
            P.pool(lambda e: e.tensor_tensor(out=kd_[:], in0=tm[:, 640:1152].rearrange("p (h v) -> p h v", h=4), in1=bc(eg_[:, 4:8], [128, 4, 128], 2), op=ALU.mult),
                   R=[tm, eg_], W=[kd_])

        def seq(ci):
            ti = order[ci]
            lat = ti >= NCT
            cm, tm, sm = cmb[ci % 3], tmb[ci % 3], smb[ci % 3]
            i2 = ci % 2
            CT = cm[:, 0, :]
            ea_, att_, xdt_, xdw_ = ea[i2], att[i2], xdt[i2], xdw[i2]
            eg_, eb_, qkT_, vb_, kd_, Pf_ = eg[i2], eb[i2], qkT[i2], vb[i2], kd[i2], Pf[i2]
            yo_ = yo[ci % 2]
            for h in range(4):
                P.pe(lambda e, h=h: e.matmul(bKS[:, h * 128:(h + 1) * 128], lhsT=cm[:, 2 + h, :], rhs=Sb[:, h, :], start=True, stop=True),
                     R=[cm, Sb], W=[bKS])
            P.dve(lambda e: e.tensor_tensor(out=tks[:], in0=bKS[:].rearrange("p (h v) -> p h v", h=4), in1=bc(eb_[:], [128, 4, 128], 2), op=ALU.mult),
                  R=[bKS, eb_], W=[tks])
            P.dve(lambda e: e.tensor_tensor(out=rv[:], in0=vb_[:], in1=tks[:], op=ALU.subtract), R=[vb_, tks], W=[rv])
            if lat:
                pi = ob()
                P.pe(lambda e: e.matmul(pi[:], lhsT=CT, rhs=STb[:], start=True, stop=True), R=[cm, STb], W=[pi])
                pqs = ob()
                for h in range(4):
                    P.pe(lambda e, h=h: e.matmul(pqs[:, h * 128:(h + 1) * 128], lhsT=cm[:, 6 + h, :], rhs=Sb[:, h, :], start=True, stop=True),
                         R=[cm, Sb], W=[pqs])
            P.pe(lambda e: e.matmul(bST[:], lhsT=tm[:, 512:640], rhs=xdw_[:].rearrange("p h q -> p (h q)"), start=True, stop=True),
                 R=[tm, xdw_], W=[bST])
            if lat:
                P.dve(lambda e: e.tensor_tensor(out=t1[:], in0=pi[:].rearrange("p (h q) -> p h q", h=8), in1=bc(ea_[:, 0:8], [128, 8, 64], 2), op=ALU.mult),
                      R=[pi, ea_], W=[t1])
                P.dve(lambda e: e.tensor_tensor(out=t3[:], in0=pqs[:].rearrange("p (h v) -> p h v", h=4), in1=bc(eg_[:, 0:4], [128, 4, 128], 2), op=ALU.mult),
                      R=[pqs, eg_], W=[t3])
            P.dve(lambda e: e.tensor_tensor(out=ST[:], in0=ST[:], in1=bc(ea_[:, 16:24], [128, 8, 64], 2), op=ALU.mult), R=[ST, ea_], W=[ST])
            P.dve(lambda e: e.tensor_tensor(out=ST[:].rearrange("p h q -> p (h q)"), in0=ST[:].rearrange("p h q -> p (h q)"), in1=bST[:], op=ALU.add),
                  R=[ST, bST], W=[ST])
            P.act(lambda e: e.copy(out=STb[:], in_=ST[:].rearrange("p h q -> p (h q)")), R=[ST], W=[STb])
            for h in range(4):
                P.pe(lambda e, h=h: e.matmul(bKS[:, h * 128:(h + 1) * 128], lhsT=Pf_[:, h, :], rhs=rv[:, h, :], start=True, stop=True),
                     R=[Pf_, rv], W=[bKS])
            P.act(lambda e: e.copy(out=vnb[:].rearrange("p h v -> p (h v)"), in_=bKS[:]), R=[bKS], W=[vnb])
            for h in range(4):
                P.pe(lambda e, h=h: e.matmul(bSU[:, h * 128:(h + 1) * 128], lhsT=kd_[:, h, :], rhs=vnb[:, h, :], start=True, stop=True),
                     R=[kd_, vnb], W=[bSU])
            P.dve(lambda e: e.tensor_tensor(out=S[:], in0=S[:], in1=bc(eg_[:, 8:12], [128, 4, 128], 2), op=ALU.mult), R=[S, eg_], W=[S])
            P.dve(lambda e: e.tensor_tensor(out=S[:].rearrange("p h v -> p (h v)"), in0=S[:].rearrange("p h v -> p (h v)"), in1=bSU[:], op=ALU.add),
                  R=[S, bSU], W=[S])
            P.act(lambda e: e.copy(out=Sb[:].rearrange("p h v -> p (h v)"), in_=S[:].rearrange("p h v -> p (h v)")), R=[S], W=[Sb])
            if not lat:
                return
            py = ob()
            for h in range(8):
                P.pe(lambda e, h=h: e.matmul(py[:, h * 64:(h + 1) * 64], lhsT=att_[:, h, :], rhs=xdt_[:, h, :], start=True, stop=True),
                     R=[att_, xdt_], W=[py])
            P.dve(lambda e: e.tensor_tensor(out=yo_[:, 0:512], in0=t1[:].rearrange("p h q -> p (h q)"), in1=py[:], op=ALU.add), R=[t1, py], W=[yo_])
            pqv = ob()
            for h in range(4):
                P.pe(lambda e, h=h: e.matmul(pqv[:, h * 128:(h + 1) * 128], lhsT=qkT_[:, h, :], rhs=vnb[:, h, :], start=True, stop=True),
                     R=[qkT_, vnb], W=[pqv])
            P.dve(lambda e: e.tensor_tensor(out=yo_[:, 512:1024], in0=t3[:].rearrange("p h v -> p (h v)"), in1=pqv[:], op=ALU.add), R=[t3, pqv], W=[yo_])
            li = ti - NCT
            if d == 0:
                P.dma("sp", YFs[li * 128:(li + 1) * 128, :], yo_[:], R=[yo_])
                return
            yf = yfb[ci % 3]
            dsk, nws, nwg = fin[:, 0:8], fin[:, 8:520], fin[:, 520:648]
            P.dve(lambda e: e.tensor_tensor(out=u[:], in0=yo_[:], in1=yf[:], op=ALU.add), R=[yo_, yf], W=[u])
            P.pool(lambda e: e.tensor_tensor(out=usq[:, 0:512].rearrange("p (h q) -> p h q", h=8), in0=tm[:, 0:512].rearrange("p (h q) -> p h q", h=8),
                                             in1=bc(dsk, [128, 8, 64], 2), op=ALU.mult), R=[tm, fin], W=[usq])
            P.dve(lambda e: e.tensor_tensor(out=u[:, 0:512], in0=u[:, 0:512], in1=usq[:, 0:512], op=ALU.add), R=[u, usq], W=[u])
            P.dve(lambda e: e.tensor_tensor(out=u[:, 0:512], in0=u[:, 0:512], in1=tm[:, 1664:2176], op=ALU.mult), R=[u, tm], W=[u])
            P.pool(lambda e: e.memset(ss[:], 0.0), W=[ss])
            P.act(lambda e: e.activation(out=usq[:, 0:512], in_=u[:, 0:512], func=AF.Square, accum_out=ss[:, 0:1]), R=[u, ss], W=[usq, ss])
            for h in range(4):
                P.act(lambda e, h=h: e.activation(out=usq[:, 512 + h * 128:512 + (h + 1) * 128], in_=u[:, 512 + h * 128:512 + (h + 1) * 128],
                                                  func=AF.Square, accum_out=ss[:, 1 + h:2 + h]), R=[u, ss], W=[usq, ss])
            P.act(lambda e: e.activation(out=ss[:, 0:1], in_=ss[:, 0:1], func=AF.Sqrt, bias=epsT[:, 0:1], scale=1.0 / 512), R=[ss, epsT], W=[ss])
            P.act(lambda e: e.activation(out=ss[:, 1:5], in_=ss[:, 1:5], func=AF.Sqrt, bias=epsT[:, 0:1], scale=1.0 / 128), R=[ss, epsT], W=[ss])
            P.dve(lambda e: e.reciprocal(out=ss[:, 0:5], in_=ss[:, 0:5]), R=[ss], W=[ss])
            P.dve(lambda e: e.scalar_tensor_tensor(out=yfin[:, 0:512], in0=u[:, 0:512], scalar=ss[:, 0:1], in1=nws, op0=ALU.mult, op1=ALU.mult),
                  R=[u, ss, fin], W=[yfin])
            u4 = u[:, 512:1024].rearrange("p (h v) -> p h v", h=4)
            P.dve(lambda e: e.tensor_tensor(out=u4, in0=u4, in1=bc(ss[:, 1:5], [128, 4, 128], 2), op=ALU.mult), R=[u, ss], W=[u])
            P.dve(lambda e: e.tensor_tensor(out=u4, in0=u4, in1=bc(nwg, [128, 4, 128], 1), op=ALU.mult), R=[u, fin], W=[u])
            P.dve(lambda e: e.tensor_tensor(out=yfin[:, 512:1024], in0=u[:, 512:1024], in1=tm[:, 2176:2688], op=ALU.mult), R=[u, tm], W=[yfin])
            pt = ob()
            ptb = pt.ap[:].bitcast(BF16)
            for c8 in range(8):
                P.pe(lambda e, c8=c8: e.transpose(out=ptb[:, c8 * 128:(c8 + 1) * 128], in_=yfin[:, c8 * 128:(c8 + 1) * 128], identity=identb[:]),
                     R=[yfin, identb], W=[pt])
            yT_ = yT[ci % 2]
            P.act(lambda e: e.copy(out=yT_[:].rearrange("p c l -> p (c l)"), in_=ptb[:, :]), R=[pt], W=[yT_])
            hc, tl = li // NTH, li % NTH
            P.dma("sp", YY.ap()[hc, tl], yT_[:].rearrange("p c l -> p (c l)"), R=[yT_])

        n = len(order)
        if L["debug"] and d == 0:
            n = min(DBG_NCH, n)
        load(0)
        if n > 1:
            load(1)
        if PARTS & 1:
            prep(0)
        for ci in range(n):
            if ci + 2 < n:
                load(ci + 2)
            if ci + 1 < n and PARTS & 1:
                prep(ci + 1)
            if PARTS & 2:
                seq(ci)
        if L["debug"] and d == 0:
            loc = locals()
            for nm_, ap_ in L["dbg"].items():
                if not nm_.startswith("s_"):
                    continue
                key = nm_[2:]
                t_ = loc[key] if key in loc else loc[key[:-1]][int(key[-1])]
                src = t_[:] if len(t_.ap.shape) == 2 else t_[:].rearrange("p h l -> p (h l)")
                L["final_ops"].append(P.dma("sp", ap_, src, R=[t_]))


def stage5(P, nc, L):
    HALF, NTH = L["HALF"], L["NTH"]
    xhalf, yout, MINE, PART, WO, WGU, WD, rowp = L["xhalf"], L["yout"], L["MINE"], L["PART"], L["WO"], L["WGU"], L["WD"], L["rowp"]
    g12row, modT, cst, psb, epsT, final_ops = L["g12row"], L["modT"], L["cst"], L["psb"], L["epsT"], L["final_ops"]
    ident = cst[:, 0:128]
    with ExitStack() as st:
        lnp = P.sb(st, "lnp", [128, 4 * D], F32)
        P.dma("sp", lnp[:], rowp[:, 696:4792].partition_broadcast(128), W=[lnp])
        xt = [P.sb(st, f"x5_{i}", [128, 4, D], F32) for i in range(2)]
        yT = P.sb(st, "yT5", [128, 2, 4, 1024], BF16)
        h2T = P.sb(st, "h2T", [128, 8, 512], BF16)
        actT = P.sb(st, "actT", [128, 22, 512], BF16)
        wbuf = [P.sb(st, f"wbuf{i}", [128, 8192], BF16) for i in range(3)]
        tmp = [P.sb(st, f"tmp5_{i}", [128, 512], F32) for i in range(2)]
        sgt = [P.sb(st, f"sgt{i}", [128, 512], F32) for i in range(2)]
        junk = P.sb(st, "junk", [128, D], F32)
        stat = [P.sb(st, f"stat{i}", [128, 8], F32) for i in range(2)]
        wcnt = [0]

        def wload(src, n):
            wb = wbuf[wcnt[0] % 3]
            wcnt[0] += 1
            P.dma("sp", wb[:, 0:n], src, W=[wb])
            return wb

        def resid(ps, x_, a, dh, gi, k):
            t = tmp[k % 2]
            P.dve(lambda e: e.tensor_tensor(out=t[:], in0=ps[:], in1=g12row[:, gi * D + dh * 512: gi * D + dh * 512 + 512], op=ALU.mult),
                  R=[ps, g12row], W=[t])
            xs_ = x_[:, a, dh * 512:(dh + 1) * 512]
            P.dve(lambda e: e.scalar_tensor_tensor(out=xs_, in0=xs_, scalar=ALPHA, in1=t[:], op0=ALU.mult, op1=ALU.add),
                  R=[x_, t], W=[x_])

        def layer_norm(x_, a, li, k):
            sx = stat[k % 2]
            xa = x_[:, a, :]
            g_, b_ = lnp[:, (2 * li) * D:(2 * li + 1) * D], lnp[:, (2 * li + 1) * D:(2 * li + 2) * D]
            P.pool(lambda e: e.memset(sx[:], 0.0), W=[sx])
            P.act(lambda e: e.activation(out=junk[:], in_=xa, func=AF.Identity, accum_out=sx[:, 0:1]), R=[x_, sx], W=[junk, sx])
            P.act(lambda e: e.activation(out=junk[:], in_=xa, func=AF.Square, accum_out=sx[:, 1:2]), R=[x_, sx], W=[junk, sx])
            P.dve(lambda e: e.tensor_scalar_mul(out=sx[:, 2:3], in0=sx[:, 0:1], scalar1=1.0 / D), R=[sx], W=[sx])
            P.dve(lambda e: e.tensor_tensor(out=sx[:, 3:4], in0=sx[:, 2:3], in1=sx[:, 2:3], op=ALU.mult), R=[sx], W=[sx])
            P.dve(lambda e: e.scalar_tensor_tensor(out=sx[:, 4:5], in0=sx[:, 1:2], scalar=1.0 / D, in1=sx[:, 3:4], op0=ALU.mult, op1=ALU.subtract),
                  R=[sx], W=[sx])
            P.act(lambda e: e.activation(out=sx[:, 5:6], in_=sx[:, 4:5], func=AF.Sqrt, bias=epsT[:, 1:2], scale=1.0), R=[sx, epsT], W=[sx])
            P.dve(lambda e: e.reciprocal(out=sx[:, 5:6], in_=sx[:, 5:6]), R=[sx], W=[sx])
            P.dve(lambda e: e.scalar_tensor_tensor(out=sx[:, 6:7], in0=sx[:, 2:3], scalar=-1.0, in1=sx[:, 5:6], op0=ALU.mult, op1=ALU.mult),
                  R=[sx], W=[sx])
            P.act(lambda e: e.activation(out=xa, in_=xa, func=AF.Identity, bias=sx[:, 6:7], scale=sx[:, 5:6]), R=[x_, sx], W=[x_])
            P.dve(lambda e: e.tensor_tensor(out=xa, in0=xa, in1=g_, op=ALU.mult), R=[x_, lnp], W=[x_])
            P.dve(lambda e: e.tensor_tensor(out=xa, in0=xa, in1=b_, op=ALU.add), R=[x_, lnp], W=[x_])

        nblk = HALF // 512
        for blk in range(nblk):
            x_ = xt[blk % 2]
            tl0 = blk * 4
            P.dma("sp", x_[:], xhalf[blk * 512:(blk + 1) * 512, :].rearrange("(a p) d -> p a d", p=128), W=[x_])
            yTa = T("yTa", yT.ap)
            yTb = T("yTb", yT.ap)
            yTa.last_w, yTa.readers = yT.last_w, dict(yT.readers)
            P.dma("sp", yT[:, 0], MINE.ap()[tl0:tl0 + 4].rearrange("t p f -> p t f"), W=[yT], semt=yTa)
            P.dma("sp", yT[:, 1], PART.ap()[tl0:tl0 + 4].rearrange("t p f -> p t f"), W=[yT], semt=yTb)
            k = 0
            for dh in range(2):
                wb = wload(WO[dh], 8192)
                wv = wb[:, 0:8192].rearrange("p (k n) -> p k n", k=16)
                for a in range(4):
                    ps = psb[k % 2]
                    for kc in range(16):
                        src, cc = kc // 8, kc % 8
                        P.pe(lambda e, ps=ps, kc=kc, src=src, cc=cc, a=a, wv=wv: e.matmul(
                            ps[:], lhsT=yT[:, src, a, cc * 128:(cc + 1) * 128], rhs=wv[:, kc, :], start=(kc == 0), stop=(kc == 15)),
                            R=[yT, wb], W=[ps])
                    resid(ps, x_, a, dh, 0, k)
                    k += 1
            for a in range(4):
                layer_norm(x_, a, 0, a)
            PX = psb[2]
            for fc in range(8):
                for a in range(4):
                    P.pe(lambda e, a=a, fc=fc, x_=x_: e.transpose(out=PX[:, a * 128:(a + 1) * 128], in_=x_[:, a, fc * 128:(fc + 1) * 128], identity=ident),
                         R=[x_, cst], W=[PX])
                P.act(lambda e, fc=fc: e.activation(out=h2T[:, fc, :], in_=PX[:], func=AF.Identity,
                                                    bias=modT[:, 32 + fc:33 + fc], scale=modT[:, 40 + fc:41 + fc]), R=[PX, modT], W=[h2T])
            for b11 in range(11):
                wb = wload(WGU[b11], 4096)
                wv = wb[:, 0:4096].rearrange("p (g k n) -> p g k n", g=2, k=8)
                for jj in range(2):
                    j = b11 * 2 + jj
                    pg, pu = psb[4 + j % 2], psb[6 + j % 2]
                    for kc in range(8):
                        P.pe(lambda e, pg=pg, kc=kc, jj=jj, wv=wv: e.matmul(pg[:], lhsT=wv[:, 0, kc, jj * 128:(jj + 1) * 128], rhs=h2T[:, kc, :],
                                                                            start=(kc == 0), stop=(kc == 7)), R=[wb, h2T], W=[pg])
                    for kc in range(8):
                        P.pe(lambda e, pu=pu, kc=kc, jj=jj, wv=wv: e.matmul(pu[:], lhsT=wv[:, 1, kc, jj * 128:(jj + 1) * 128], rhs=h2T[:, kc, :],
                                                                            start=(kc == 0), stop=(kc == 7)), R=[wb, h2T], W=[pu])
                    sg = sgt[j % 2]
                    P.act(lambda e, sg=sg, pg=pg: e.activation(out=sg[:], in_=pg[:], func=AF.Silu), R=[pg], W=[sg])
                    P.dve(lambda e, sg=sg, pu=pu, j=j: e.tensor_tensor(out=actT[:, j, :], in0=sg[:], in1=pu[:], op=ALU.mult), R=[sg, pu], W=[actT])
            for dh in range(2):
                accs = [psb[0], psb[1], psb[2], psb[3]]
                for jh in range(2):
                    wb = wload(WD[dh * 2 + jh], 5632)
                    wv = wb[:, 0:5632].rearrange("p (j n) -> p j n", j=11)
                    for a in range(4):
                        for j11 in range(11):
                            j = jh * 11 + j11
                            P.pe(lambda e, a=a, j=j, j11=j11, wv=wv, accs=accs: e.matmul(
                                accs[a][:], lhsT=actT[:, j, a * 128:(a + 1) * 128], rhs=wv[:, j11, :], start=(j == 0), stop=(j == 21)),
                                R=[actT, wb], W=[accs[a]])
                for a in range(4):
                    resid(accs[a], x_, a, dh, 1, a)
            for a in range(4):
                layer_norm(x_, a, 1, a)
            final_ops.append(P.dma("sp", yout[blk * 512:(blk + 1) * 512, :].rearrange("(a p) d -> p a d", p=128), x_[:], R=[x_]))


def host_consts():
    j = np.arange(128)[:, None]
    l = np.arange(128)[None, :]
    mats = [np.eye(128), (j <= l), (j > l), (j >= l), (j < l), np.ones((128, 128)), (j // 16 == l // 16)]
    for b in (16, 32, 64):
        mats.append((j // (2 * b) == l // (2 * b)) & (j % (2 * b) >= b) & (l % (2 * b) < b))
    for b in (16, 32, 64):
        mats.append(((j // (2 * b) == l // (2 * b)) & (j % (2 * b) >= b) & (l % (2 * b) < b)).T)
    return np.concatenate([m.astype(np.float32) for m in mats], axis=1)


def prep_core(inp, core, SEQ):
    b, h = core // 2, core % 2
    HALF = SEQ // 2
    f = lambda a: np.ascontiguousarray(a, dtype=np.float32)
    x, ctx = inp["x"], inp["ctx"]
    w_in = inp["w_in"][0]
    xs0 = 1024
    B0, C0 = 2048, 2304
    dt0 = 2560
    q0, k0, v0 = 2592, 3616, 4640
    g0 = 5664
    b0, a0 = 6688, 6704
    r = lambda s, n: np.arange(s, s + n)
    cols_cm = np.concatenate([r(xs0 + h * 512, 512), r(B0 + h * 128, 128), r(C0 + h * 128, 128),
                              r(q0 + h * 512, 512), r(k0 + h * 512, 512), r(v0 + h * 512, 512)])
    cols_tm = np.concatenate([r(h * 512, 512), r(g0 + h * 512, 512),
                              r(dt0 + 8 * h, 8), r(dt0 + 16 + 8 * h, 8),
                              r(a0 + 4 * h, 4), r(a0 + 8 + 4 * h, 4),
                              r(b0 + 4 * h, 4), r(b0 + 8 + 4 * h, 4)])
    cws, cwg = inp["conv_w_ssd"][0], inp["conv_w_gdn"][0]
    ssd_cols = np.concatenate([r(h * 512, 512), r(1024 + h * 128, 128), r(1280 + h * 128, 128)])
    gdn_cols = np.concatenate([r(h * 512, 512), r(1024 + h * 512, 512), r(2048 + h * 512, 512)])
    cw = np.concatenate([cws[:, ssd_cols], cwg[:, gdn_cols]], axis=1)
    cwT = cw.T.reshape(NCM, 128, 5).transpose(1, 0, 2).reshape(128, NCM * 5)
    cbT = inp["conv_b_ssd"][0][ssd_cols].reshape(6, 128).T
    rowp = np.concatenate([
        inp["dt_bias_ssd"][0][0, 8 * h:8 * h + 8], inp["dt_bias_ssd"][0][1, 8 * h:8 * h + 8],
        inp["dt_bias_gdn"][0][0, 4 * h:4 * h + 4], inp["dt_bias_gdn"][0][1, 4 * h:4 * h + 4],
        inp["a_log_ssd"][0][0, 8 * h:8 * h + 8], inp["a_log_ssd"][0][1, 8 * h:8 * h + 8],
        inp["a_log_gdn"][0][0, 4 * h:4 * h + 4], inp["a_log_gdn"][0][1, 4 * h:4 * h + 4],
        inp["d_skip_ssd"][0][8 * h:8 * h + 8],
        inp["norm_w_ssd"][0][h * 512:(h + 1) * 512], inp["norm_w_gdn"][0],
        inp["ln1_g"][0], inp["ln1_b"][0], inp["ln2_g"][0], inp["ln2_b"][0]])[None, :]
    cv = np.stack([inp["c"][b], inp["c_ctx"]])
    cvT = cv.reshape(2, 8, 128).transpose(2, 0, 1).reshape(128, 16)
    wo = inp["w_out"][0]
    own = np.concatenate([r(h * 512, 512), r(1024 + h * 512, 512)])
    oth = np.concatenate([r((1 - h) * 512, 512), r(1024 + (1 - h) * 512, 512)])
    return {
        "xin": f(np.concatenate([ctx[b], x[b]], axis=0)),
        "xhalf": f(x[b, h * HALF:(h + 1) * HALF]),
        "cvecT": f(cvT),
        "w_ada": f(inp["w_ada"][0]), "b_ada": f(inp["b_ada"][0][None, :]),
        "w_cm": f(w_in[:, cols_cm]), "w_tm": f(w_in[:, cols_tm]),
        "convw": f(cwT), "convb": f(cbT), "rowp": f(rowp), "consts": host_consts(),
        "w_out": f(wo[np.concatenate([own, oth])]),
        "w_gate": f(inp["w_ffn_gate"][0]), "w_up": f(inp["w_ffn_up"][0]), "w_down": f(inp["w_ffn_down"][0]),
    }


_CACHE = {}


def run(inputs, SEQ, debug=False, stop_after=99, ncores=8):
    key = (SEQ, debug, stop_after)
    if key not in _CACHE:
        _CACHE[key] = build_program(SEQ, debug, stop_after)
    nc, stats = _CACHE[key]
    in_maps = [prep_core(inputs, c, SEQ) for c in range(ncores)]
    res = run_bass_kernel_spmd(nc, in_maps, core_ids=list(range(ncores)))
    return res.results, stats


def kernel(**inputs):
    SEQ = inputs["x"].shape[1]
    B = inputs["x"].shape[0]
    results, _ = run(inputs, SEQ)
    HALF = SEQ // 2
    out = np.empty((B, SEQ, D), np.float32)
    for c in range(8):
        b, h = c // 2, c % 2
        out[b, h * HALF:(h + 1) * HALF] = results[c]["yout"]
    return out
```

```python
import numpy as np
from contextlib import ExitStack
import concourse.bass as bass
import concourse.mybir as mybir
from concourse.bass_utils import run_bass_kernel_spmd

F32 = mybir.dt.float32
BF16 = mybir.dt.bfloat16
AF = mybir.ActivationFunctionType
ALU = mybir.AluOpType
AX = mybir.AxisListType

D = 1024
CTX = 256
GRID_W = 64
DFF = 2816
NCM = 18
NTMC = 1056
TMW = 2688
SMW = 64
ALPHA = 2.0 ** 0.25
LN_EPS = 1e-5
RMS_EPS = 1e-6
EPOCH = 30000
NCST = 13
PARTS = 3
DBG_NCH = 10 ** 6


class T:
    __slots__ = ("name", "ap", "last_w", "readers", "dma_readers", "sem", "cnt", "last_dma", "excl")

    def __init__(self, name, ap):
        self.excl = False
        self.name = name
        self.ap = ap
        self.last_w = None
        self.readers = {}
        self.dma_readers = []
        self.sem = None
        self.cnt = 0
        self.last_dma = None

    def __getitem__(self, k):
        return self.ap[k]


class Op:
    __slots__ = ("eng", "fn", "deps", "needs_inc", "inc_val", "epoch", "is_dma", "sem", "val", "amt")

    def __init__(self, eng, fn, is_dma=False):
        self.eng = eng
        self.fn = fn
        self.deps = []
        self.needs_inc = False
        self.inc_val = 0
        self.epoch = 0
        self.is_dma = is_dma
        self.sem = None
        self.val = 0
        self.amt = 16


class Prog:
    ENGS = ("sp", "act", "pool", "dve", "pe")

    def __init__(self, nc, stack):
        self.nc = nc
        self.stack = stack
        self.ops = {e: [] for e in self.ENGS}
        self.same_engine_sync = {"sp": False, "act": True, "pool": True, "dve": True, "pe": False}
        self.nsem = 0
        self.dma_open = []
        self.pending = {e: [] for e in self.ENGS}
        self.sem_pool = {}

    def sb(self, st, name, shape, dt):
        return T(name, st.enter_context(self.nc.sbuf_tensor(name, list(shape), dt)))

    def ps(self, st, name, shape, dt):
        t = T(name, st.enter_context(self.nc.psum_tensor(name, list(shape), dt)))
        t.excl = True
        return t

    def new_sem(self, name):
        self.nsem += 1
        return self.stack.enter_context(self.nc.semaphore(name))

    def _track(self, op, R, W, extra=()):
        deps = list(extra)
        for t in R:
            if t.last_w is not None:
                deps.append(t.last_w)
            if t.excl:
                deps.extend(o for en, o in t.readers.items() if en != op.eng)
        for t in W:
            if t.last_w is not None:
                deps.append(t.last_w)
            deps.extend(t.readers.values())
            deps.extend(t.dma_readers)
        deps.extend(self.pending[op.eng])
        self.pending[op.eng] = []
        seen = set()
        for d in deps:
            if d is op or id(d) in seen:
                continue
            seen.add(id(d))
            if (not d.is_dma) and d.eng == op.eng and not op.is_dma and not self.same_engine_sync[op.eng]:
                continue
            if not d.is_dma:
                d.needs_inc = True
            op.deps.append(d)
        for t in R:
            if op.is_dma:
                t.dma_readers.append(op)
            else:
                t.readers[op.eng] = op
        for t in W:
            t.last_w = op
            t.readers = {}
            t.dma_readers = []

    def op(self, eng, fn, R=(), W=()):
        o = Op(eng, fn)
        self._track(o, R, W)
        self.ops[eng].append(o)
        return o

    def pe(self, fn, R=(), W=()):
        return self.op("pe", fn, R, W)

    def act(self, fn, R=(), W=()):
        return self.op("act", fn, R, W)

    def dve(self, fn, R=(), W=()):
        return self.op("dve", fn, R, W)

    def pool(self, fn, R=(), W=()):
        return self.op("pool", fn, R, W)

    def dma(self, eng, out_ap=None, in_ap=None, R=(), W=(), semt=None, fn=None, amt=16, **kw):
        if fn is None:
            fn = lambda e: e.dma_start(out=out_ap, in_=in_ap, **kw)
        o = Op(eng, fn, is_dma=True)
        o.amt = amt
        if semt is None:
            semt = (list(W) + list(R))[0]
        if semt.sem is None:
            key = semt.name
            if key not in self.sem_pool:
                self.sem_pool[key] = [self.new_sem("d_" + key), 0]
            semt.sem = self.sem_pool[key]
        semt.sem[1] += amt
        o.sem = semt.sem[0]
        o.val = semt.sem[1]
        extra = [semt.last_dma] if semt.last_dma is not None else []
        semt.last_dma = o
        self._track(o, R, W, extra)
        self.ops[eng].append(o)
        self.dma_open.append(o)
        return o

    def barrier(self):
        deps = list(self.dma_open)
        for e in self.ENGS:
            for o in reversed(self.ops[e]):
                if not o.is_dma:
                    deps.append(o)
                    break
        self.dma_open = []
        for e in self.ENGS:
            self.pending[e] = list(deps)

    def emit(self, final_ops=()):
        nc = self.nc
        esems = {}
        for e in self.ENGS:
            n = 0
            for o in self.ops[e]:
                if o.is_dma or not o.needs_inc:
                    continue
                o.epoch = n // EPOCH
                o.inc_val = n % EPOCH + 1
                n += 1
            nep = (n + EPOCH - 1) // EPOCH
            esems[e] = [self.new_sem(f"s_{e}{i}") for i in range(max(nep, 1))]
        block = self.stack.enter_context(nc.Block())
        deco = {"sp": block.sync, "act": block.scalar, "pool": block.gpsimd, "dve": block.vector, "pe": block.tensor}
        nwaits = {e: 0 for e in self.ENGS}

        def make(eng):
            def body(e):
                seen = {}
                maxep = {}

                def wait_all(deps):
                    need = {}
                    for d in deps:
                        if d.is_dma:
                            key, val, sem = ("d", id(d.sem)), d.val, d.sem
                        else:
                            key, val, sem = (d.eng, d.epoch), d.inc_val, esems[d.eng][d.epoch]
                        if seen.get(key, 0) >= val:
                            continue
                        if key not in need or need[key][0] < val:
                            need[key] = (val, sem)
                    for key, (val, sem) in need.items():
                        if key[0] != "d":
                            if any(k[0] == key[0] and k[1] > key[1] for k in list(seen) + list(need) if k[0] != "d"):
                                continue
                        seen[key] = val
                        e.wait_ge(sem, val)
                        nwaits[eng] += 1

                def wait_for(d):
                    wait_all([d])

                for o in self.ops[eng]:
                    wait_all(o.deps)
                    ins = o.fn(e)
                    if o.is_dma:
                        ins.then_inc(o.sem, o.amt)
                    elif o.needs_inc:
                        ins.then_inc(esems[eng][o.epoch], 1)
                if eng == "sp":
                    for d in final_ops:
                        wait_for(d)
                        e.nop()
            return body

        for eng in self.ENGS:
            if self.ops[eng] or eng == "sp":
                deco[eng](make(eng))
        self.nwaits = nwaits
        return block


def bc(ap, shape, axis):
    return ap.unsqueeze(axis).to_broadcast(list(shape))


def build_program(SEQ, debug=False, stop_after=99):
    TT = CTX + SEQ
    NTT = TT // 128
    NCT = CTX // 128
    NLT = SEQ // 128
    HALF = SEQ // 2
    NTH = HALF // 128
    nc = bass.Bass("TRN2", target_bir_lowering=False)
    dt = nc.dram_tensor
    xin = dt("xin", [TT, D], F32, kind="ExternalInput").ap()
    xhalf = dt("xhalf", [HALF, D], F32, kind="ExternalInput").ap()
    cvecT = dt("cvecT", [128, 16], F32, kind="ExternalInput").ap()
    w_ada = dt("w_ada", [D, 6 * D], F32, kind="ExternalInput").ap()
    b_ada = dt("b_ada", [1, 6 * D], F32, kind="ExternalInput").ap()
    w_cm = dt("w_cm", [D, NCM * 128], F32, kind="ExternalInput").ap()
    w_tm = dt("w_tm", [D, NTMC], F32, kind="ExternalInput").ap()
    convw = dt("convw", [128, NCM * 5], F32, kind="ExternalInput").ap()
    convb = dt("convb", [128, 6], F32, kind="ExternalInput").ap()
    rowp = dt("rowp", [1, 4792], F32, kind="ExternalInput").ap()
    consts = dt("consts", [128, NCST * 128], F32, kind="ExternalInput").ap()
    w_out = dt("w_out", [2 * D, D], F32, kind="ExternalInput").ap()
    w_gate = dt("w_gate", [D, DFF], F32, kind="ExternalInput").ap()
    w_up = dt("w_up", [D, DFF], F32, kind="ExternalInput").ap()
    w_down = dt("w_down", [DFF, D], F32, kind="ExternalInput").ap()
    yout = dt("yout", [HALF, D], F32, kind="ExternalOutput").ap()
    CMs = dt("CMs", [NTT, 128, 10 * 128], BF16).ap()
    TMs = dt("TMs", [TT, TMW], BF16).ap()
    SMs = dt("SMs", [TT, SMW], F32).ap()
    YFs = dt("YFs", [SEQ, 1024], F32).ap()
    YY = dt("YY", [2, NTH, 128, 1024], BF16)
    TPK = min(NTH, 8)
    NCC = NTH // TPK
    ZO = dt("ZO", [NCC, 2, TPK, 128, 1024], BF16)
    ZIN = dt("ZIN", [NTH, 128, 1024], BF16)
    MINE = dt("MINE", [NTH, 128, 1024], BF16)
    PART = dt("PART", [NTH, 128, 1024], BF16)
    WO = dt("WO", [2, 128, 16 * 512], BF16).ap()
    WGU = dt("WGU", [11, 128, 2 * 8 * 256], BF16).ap()
    WD = dt("WD", [4, 128, 11 * 512], BF16).ap()
    dbg = {}
    if debug:
        dbg["modT"] = dt("dbg_modT", [128, 64], F32, kind="ExternalOutput").ap()
        dbg["CM"] = dt("dbg_CM", [NTT, 128, 10 * 128], BF16, kind="ExternalOutput").ap()
        dbg["TM"] = dt("dbg_TM", [TT, TMW], BF16, kind="ExternalOutput").ap()
        dbg["SM"] = dt("dbg_SM", [TT, SMW], F32, kind="ExternalOutput").ap()
        dbg["YF"] = dt("dbg_YF", [SEQ, 1024], F32, kind="ExternalOutput").ap()
        dbg["YY"] = dt("dbg_YY", [2 * NTH * 128, 1024], BF16, kind="ExternalOutput").ap()
        for nm_, dt_ in (("E", F32), ("Es", F32), ("tks", F32), ("S", F32), ("t3", F32), ("vb0", F32)):
            dbg["s_" + nm_] = dt("dbg_s_" + nm_, [128, 512], dt_, kind="ExternalOutput").ap()
        for nm_ in ("Ei", "Ak0", "Ak1", "Nk0", "Nk1", "Pf0", "qkT0", "kd0", "rv", "vnb", "Sb", "qk", "Pk0", "Pk1", "Afull", "Nfull", "Ym", "Dk0", "Dk1", "Wk0", "Wk1"):
            dbg["s_" + nm_] = dt("dbg_s_" + nm_, [128, 512], BF16, kind="ExternalOutput").ap()
        dbg["s_eg0"] = dt("dbg_s_eg0", [128, 12], F32, kind="ExternalOutput").ap()

    with ExitStack() as top:
        P = Prog(nc, top)
        cst = P.sb(top, "cst", [128, NCST * 128], F32)
        identb = P.sb(top, "identb", [128, 128], BF16)
        modT = P.sb(top, "modT", [128, 64], F32)
        g12row = P.sb(top, "g12row", [128, 2 * D], F32)
        psb = [P.ps(top, f"psb{i}", [128, 512], F32) for i in range(8)]
        epsT = P.sb(top, "epsT", [128, 4], F32)
        P.pool(lambda e: e.memset(epsT[:, 0:1], RMS_EPS), W=[epsT])
        P.pool(lambda e: e.memset(epsT[:, 1:2], LN_EPS), W=[epsT])
        ident = cst[:, 0:128]
        Uincl, Lstrict, Lincl, Ustrict, ones = (cst[:, i * 128:(i + 1) * 128] for i in range(1, 6))
        P.dma("sp", cst[:], consts[:, :], W=[cst])
        P.dve(lambda e: e.tensor_copy(out=identb[:], in_=ident), R=[cst], W=[identb])
        final_ops = []

        with ExitStack() as st:
            ccol = P.sb(st, "ccol", [128, 2, 8], F32)
            csil = P.sb(st, "csil", [128, 2, 8], F32)
            crep = P.sb(st, "crep", [128, 2, 8, 128], F32)
            barow = P.sb(st, "barow", [128, 6 * D], F32)
            modrow = P.sb(st, "modrow", [128, 4 * D], F32)
            cmodrow = P.sb(st, "cmodrow", [128, 2 * D], F32)
            wab = [P.sb(st, f"wab{i}", [128, 8, 512], F32) for i in range(2)]
            P.dma("sp", ccol[:].rearrange("p v k -> p (v k)"), cvecT[:, :], W=[ccol])
            P.dma("sp", barow[:], b_ada.partition_broadcast(128), W=[barow])
            P.act(lambda e: e.activation(out=csil[:], in_=ccol[:], func=AF.Silu), R=[ccol], W=[csil])
            P.dve(lambda e: e.tensor_copy(out=crep[:].rearrange("p v k m -> p (v k) m"),
                                          in_=bc(csil[:].rearrange("p v k -> p (v k)"), [128, 16, 128], 2)),
                  R=[csil], W=[crep])
            w_ada_v = w_ada.rearrange("(kc p) n -> p kc n", p=128)
            for nb in range(12):
                wb = wab[nb % 2]
                P.dma("sp", wb[:], w_ada_v[:, :, nb * 512:(nb + 1) * 512], W=[wb])
                pl, pc = psb[(2 * nb) % 8], psb[(2 * nb + 1) % 8]
                for kc in range(8):
                    P.pe(lambda e, kc=kc, wb=wb, pl=pl: e.matmul(pl[:], lhsT=crep[:, 0, kc, :], rhs=wb[:, kc, :],
                                                                 start=(kc == 0), stop=(kc == 7)), R=[crep, wb], W=[pl])
                if nb < 4:
                    for kc in range(8):
                        P.pe(lambda e, kc=kc, wb=wb, pc=pc: e.matmul(pc[:], lhsT=crep[:, 1, kc, :], rhs=wb[:, kc, :],
                                                                     start=(kc == 0), stop=(kc == 7)), R=[crep, wb], W=[pc])
                seg = nb // 2
                half = nb % 2
                bsl = barow[:, nb * 512:(nb + 1) * 512]
                if seg in (0, 1, 3, 4):
                    mi = {0: 0, 1: 1, 3: 2, 4: 3}[seg]
                    dst = modrow[:, mi * D + half * 512: mi * D + half * 512 + 512]
                    P.dve(lambda e, dst=dst, pl=pl, bsl=bsl: e.tensor_tensor(out=dst, in0=pl[:], in1=bsl, op=ALU.add),
                          R=[pl, barow], W=[modrow])
                    if seg in (1, 4):
                        P.dve(lambda e, dst=dst: e.tensor_scalar_add(out=dst, in0=dst, scalar1=1.0), R=[modrow], W=[modrow])
                else:
                    gi = 0 if seg == 2 else 1
                    dst = g12row[:, gi * D + half * 512: gi * D + half * 512 + 512]
                    P.dve(lambda e, dst=dst, pl=pl, bsl=bsl: e.tensor_tensor(out=dst, in0=pl[:], in1=bsl, op=ALU.add),
                          R=[pl, barow], W=[g12row])
                if nb < 4:
                    dst = cmodrow[:, nb * 512:(nb + 1) * 512]
                    P.dve(lambda e, dst=dst, pc=pc, bsl=bsl: e.tensor_tensor(out=dst, in0=pc[:], in1=bsl, op=ALU.add),
                          R=[pc, barow], W=[cmodrow])
                    if nb >= 2:
                        P.dve(lambda e, dst=dst: e.tensor_scalar_add(out=dst, in0=dst, scalar1=1.0), R=[cmodrow], W=[cmodrow])
            srcs = [(modrow, 0), (modrow, 1), (cmodrow, 0), (cmodrow, 1), (modrow, 2), (modrow, 3)]
            for v, (src, si) in enumerate(srcs):
                for g in range(2):
                    pt = psb[(2 * v + g) % 8]
                    for q in range(4):
                        fc = g * 4 + q
                        P.pe(lambda e, pt=pt, q=q, src=src, off=si * D + fc * 128: e.transpose(
                            out=pt[:, q * 128:(q + 1) * 128], in_=src[:, off:off + 128], identity=ident),
                            R=[src, cst], W=[pt])
                    P.act(lambda e, pt=pt, v=v, g=g: e.copy(
                        out=modT[:, 8 * v + 4 * g: 8 * v + 4 * g + 4],
                        in_=pt[:].rearrange("p (q m) -> p q m", q=4)[:, :, 0]), R=[pt], W=[modT])
            if debug:
                final_ops.append(P.dma("sp", dbg["modT"], modT[:], R=[modT]))

            conv_jobs = []

            def convert(src_aps, dst_ap, n):
                conv_jobs.append((src_aps, dst_ap, n))

            if stop_after >= 5:
                wo_v = w_out.rearrange("(kc p) n -> p kc n", p=128)
                for dh in range(2):
                    for kh in range(2):
                        convert([(wo_v[:, kh * 8:(kh + 1) * 8, dh * 512:(dh + 1) * 512], (8, 512))],
                                WO[dh, :, kh * 4096:(kh + 1) * 4096], 4096)
                wg_v = w_gate.rearrange("(kc p) n -> p kc n", p=128)
                wu_v = w_up.rearrange("(kc p) n -> p kc n", p=128)
                for blk in range(11):
                    convert([(wg_v[:, :, blk * 256:(blk + 1) * 256], (8, 256)),
                             (wu_v[:, :, blk * 256:(blk + 1) * 256], (8, 256))], WGU[blk, :, :], 4096)
                wd_v = w_down.rearrange("(j p) n -> p j n", p=128)
                for dh in range(2):
                    for jh in range(2):
                        convert([(wd_v[:, jh * 11:(jh + 1) * 11, dh * 512:(dh + 1) * 512], (11, 512))],
                                WD[dh * 2 + jh, :, :], 5632)
        P.barrier()

        if stop_after >= 1:
            stage1(P, nc, locals())
        P.barrier()
        if stop_after >= 2:
            scan_pass(P, nc, locals(), 0)
            P.barrier()
        if stop_after >= 3:
            scan_pass(P, nc, locals(), 1)
            P.barrier()
        if debug and stop_after >= 1:
            dd = T("dd", None)
            final_ops.append(P.dma("sp", dbg["CM"], CMs, semt=dd))
            final_ops.append(P.dma("sp", dbg["TM"], TMs, semt=dd))
            final_ops.append(P.dma("sp", dbg["SM"], SMs, semt=dd))
            if stop_after >= 2:
                final_ops.append(P.dma("sp", dbg["YF"], YFs, semt=dd))
                final_ops.append(P.dma("sp", dbg["YY"], YY.ap().rearrange("a t p f -> (a t p) f"), semt=dd))
            P.barrier()
        if stop_after >= 4:
            cps = [T(f"cp{i}", None) for i in range(4)]
            CW = 8192
            ccnt = [0]

            def dyn_copy(dst3, src4, sel, fresh):
                nt_ = dst3.shape[0]
                nr = nt_ * 128 * 1024 // CW
                dflat = dst3.rearrange("t p f -> (t p f)").rearrange("(r c) -> r c", c=CW)
                def fn(e, fresh=fresh):
                    if fresh:
                        pid = e.partition_id()
                        P.dyn = {0: e.snap(pid % 2), 1: e.snap(1 - pid % 2)}
                    sflat = src4[bass.ds(P.dyn[sel], 1)].rearrange("a t p f -> (a t p f)").rearrange("(r c) -> r c", c=CW)
                    return e.dma_start(out=dflat, in_=sflat)
                ccnt[0] += 1
                P.dma("sp", semt=cps[ccnt[0] % 4], fn=fn)

            dyn_copy(ZIN.ap(), YY.ap(), 1, True)
            P.barrier()
            cct = T("cc", None)
            for k in range(NCC):
                P.dma("pool", semt=cct, amt=1, fn=lambda e, k=k: e.collective_compute(
                    "AllGather", ALU.bypass, replica_groups=[[0, 1], [2, 3], [4, 5], [6, 7]],
                    ins=[ZIN.ap()[k * TPK:(k + 1) * TPK].rearrange("t p f -> (t p) f").opt()],
                    outs=[ZO.ap()[k].rearrange("a t p f -> (a t p) f").opt()]))
            P.barrier()
            dyn_copy(MINE.ap(), YY.ap(), 0, True)
            for k in range(NCC):
                dyn_copy(PART.ap()[k * TPK:(k + 1) * TPK], ZO.ap()[k], 1, False)
            P.barrier()
        if stop_after >= 5:
            stage5(P, nc, locals())
        else:
            with ExitStack() as st:
                tb = P.sb(st, "tb", [128, D], F32)
                for a in range(HALF // 128):
                    P.dma("sp", tb[:], xhalf[a * 128:(a + 1) * 128, :], W=[tb])
                    final_ops.append(P.dma("sp", yout[a * 128:(a + 1) * 128, :], tb[:], R=[tb]))
        P.emit(final_ops=final_ops + locals().get("_final", []))
        stats = dict(nsem=P.nsem, nops={e: len(P.ops[e]) for e in P.ENGS}, nwaits=P.nwaits)
    return nc, stats


def stage1(P, nc, L):
    TT, NTT, NCT = L["TT"], L["NTT"], L["NCT"]
    xin, w_cm, w_tm, convw, convb, rowp = L["xin"], L["w_cm"], L["w_tm"], L["convw"], L["convb"], L["rowp"]
    CMs, TMs, SMs = L["CMs"], L["TMs"], L["SMs"]
    cst, identb, modT, psb = L["cst"], L["identb"], L["modT"], L["psb"]
    ident = cst[:, 0:128]
    ones = cst[:, 5 * 128:6 * 128]
    with ExitStack() as st:
        wcm = P.sb(st, "wcm", [128, 8, NCM * 128], BF16)
        wtm = P.sb(st, "wtm", [128, 8, NTMC], BF16)
        wld = [P.sb(st, f"wld{i}", [128, 1152], F32) for i in range(2)]
        cw = P.sb(st, "cw", [128, NCM, 5], F32)
        cb = P.sb(st, "cb", [128, 6], F32)
        spb = P.sb(st, "spb", [128, 48], F32)
        amul = P.sb(st, "amul", [128, 24], F32)
        xt = [P.sb(st, f"xt{i}", [128, 4, D], F32) for i in range(2)]
        hT = [P.sb(st, f"hT{i}", [128, 8, 512], BF16) for i in range(2)]
        pad = [P.sb(st, f"pad{i}", [128, 544], F32) for i in range(3)]
        acc = [P.sb(st, f"acc{i}", [128, 512], F32) for i in range(3)]
        ptmp = P.sb(st, "ptmp", [128, 512], F32)
        sv = [P.sb(st, f"sv{i}", [128, 512], F32) for i in range(3)]
        sq = [P.sb(st, f"sq{i}", [128, 512], F32) for i in range(3)]
        rr = [P.sb(st, f"rr{i}", [128, 512], F32) for i in range(3)]
        tbf = [P.sb(st, f"tbf{i}", [128, 512], BF16) for i in range(4)]
        cmst = [P.sb(st, f"cmst{i}", [128, 4, 10, 128], BF16) for i in range(2)]
        tmst = [P.sb(st, f"tmst{i}", [128, 4, TMW], BF16) for i in range(1)]
        smst = [P.sb(st, f"smst{i}", [128, 4, SMW], F32) for i in range(2)]
        smr = [P.sb(st, f"smr{i}", [128, 32], F32) for i in range(2)]
        wcm_v = w_cm.rearrange("(kc p) n -> p kc n", p=128)
        wtm_v = w_tm.rearrange("(kc p) n -> p kc n", p=128)
        for kc in range(8):
            for hf in range(2):
                w = wld[hf]
                P.dma("sp", w[:], wcm_v[:, kc, hf * 1152:(hf + 1) * 1152], W=[w])
                (P.dve if hf == 0 else P.pool)(lambda e, w=w, kc=kc, hf=hf: e.tensor_copy(
                    out=wcm[:, kc, hf * 1152:(hf + 1) * 1152], in_=w[:]), R=[w], W=[wcm])
        for kc in range(8):
            w = wld[kc % 2]
            P.dma("sp", w[:, 0:NTMC], wtm_v[:, kc, :], W=[w])
            (P.dve if kc % 2 == 0 else P.pool)(lambda e, w=w, kc=kc: e.tensor_copy(out=wtm[:, kc, :], in_=w[:, 0:NTMC]), R=[w], W=[wtm])
        P.dma("sp", cw[:].rearrange("p c k -> p (c k)"), convw[:, :], W=[cw])
        P.dma("sp", cb[:], convb[:, :], W=[cb])
        P.dma("sp", spb[:], rowp[:, 0:48].partition_broadcast(128), W=[spb])
        P.act(lambda e: e.activation(out=amul[:], in_=spb[:, 24:48], func=AF.Exp), R=[spb], W=[amul])
        P.dve(lambda e: e.tensor_scalar_mul(out=amul[:], in0=amul[:], scalar1=-1.0), R=[amul], W=[amul])
        for p_ in pad:
            P.pool(lambda e, p_=p_: e.memset(p_[:], 0.0), W=[p_])

        blocks = [(0, CTX, CTX, 2)]
        t0 = CTX
        while t0 < TT:
            blocks.append((t0, 512, GRID_W, 0))
            t0 += 512
        PX, PA, PB, PN, PZ0, PZ1, PSm, PTr = L["psb"]
        ptr_bf = PTr.ap[:].bitcast(BF16)
        for bi, (t0, ntok, rowlen, mv) in enumerate(blocks):
            nt = ntok // 128
            nrow = ntok // rowlen
            x_, h_ = xt[bi % 2], hT[bi % 2]
            if bi == 1:
                for p_ in pad:
                    P.pool(lambda e, p_=p_: e.memset(p_[:], 0.0), W=[p_])
            cm_, tm_, sm_ = cmst[bi % 2], tmst[0], smst[bi % 2]
            P.dma("sp", x_[:, 0:nt, :], xin[t0:t0 + ntok, :].rearrange("(a p) d -> p a d", p=128), W=[x_])
            for fc in range(8):
                for a in range(nt):
                    P.pe(lambda e, a=a, fc=fc, x_=x_: e.transpose(out=PX[:, a * 128:(a + 1) * 128],
                                                                  in_=x_[:, a, fc * 128:(fc + 1) * 128], identity=ident),
                         R=[x_, cst], W=[PX])
                P.act(lambda e, fc=fc, h_=h_, mv=mv, ntok=ntok: e.activation(
                    out=h_[:, fc, 0:ntok], in_=PX[:, 0:ntok], func=AF.Identity,
                    bias=modT[:, 8 * mv + fc: 8 * mv + fc + 1], scale=modT[:, 8 * (mv + 1) + fc: 8 * (mv + 1) + fc + 1]),
                    R=[PX, modT], W=[h_])
            deferred = []

            def flush(upto):
                keep = []
                for due, fn_ in deferred:
                    if due <= upto:
                        fn_()
                    else:
                        keep.append((due, fn_))
                deferred[:] = keep

            def emit_transposes(src_fn, Rt, tmoff, nt=nt, tm_=tm_):
                for a in range(nt):
                    P.pe(lambda e, a=a: e.transpose(out=ptr_bf[:, a * 128:(a + 1) * 128], in_=src_fn(a), identity=identb[:]),
                         R=[Rt, identb], W=[PTr])
                P.dve(lambda e: e.tensor_copy(
                    out=tm_[:, 0:nt, tmoff:tmoff + 128], in_=ptr_bf[:, 0:nt * 128].rearrange("p (a l) -> p a l", a=nt)),
                    R=[PTr], W=[tm_])

            def emit_l2norm(cc, kind, s_, q_, r_, ntok=ntok, nt=nt, cm_=cm_):
                P.pe(lambda e: e.matmul(PN[:, 0:ntok], lhsT=ones, rhs=q_[:, 0:ntok], start=True, stop=True),
                     R=[cst, q_], W=[PN])
                P.act(lambda e: e.activation(out=r_[:, 0:ntok], in_=PN[:, 0:ntok], func=AF.Sqrt,
                                             bias=cst_eps(L), scale=1.0), R=[PN, L["epsT"]], W=[r_])
                P.dve(lambda e: e.reciprocal(out=r_[:, 0:ntok], in_=r_[:, 0:ntok]), R=[r_], W=[r_])
                if kind == "q":
                    dst = cm_[:, 0:nt, cc, :]
                    P.dve(lambda e: e.scalar_tensor_tensor(
                        out=dst, in0=s_[:, 0:ntok].rearrange("p (a l) -> p a l", a=nt), scalar=128.0 ** -0.5,
                        in1=r_[:, 0:ntok].rearrange("p (a l) -> p a l", a=nt), op0=ALU.mult, op1=ALU.mult),
                        R=[s_, r_], W=[cm_])
                else:
                    dst = cm_[:, 0:nt, 2 + (cc - 10), :]
                    P.dve(lambda e: e.tensor_tensor(
                        out=dst, in0=s_[:, 0:ntok].rearrange("p (a l) -> p a l", a=nt),
                        in1=r_[:, 0:ntok].rearrange("p (a l) -> p a l", a=nt), op=ALU.mult), R=[s_, r_], W=[cm_])

            for cc in range(NCM):
                pp = (PA, PB)[cc % 2]
                for kc in range(8):
                    P.pe(lambda e, kc=kc, cc=cc, pp=pp, h_=h_, ntok=ntok: e.matmul(
                        pp[:, 0:ntok], lhsT=wcm[:, kc, cc * 128:(cc + 1) * 128], rhs=h_[:, kc, 0:ntok],
                        start=(kc == 0), stop=(kc == 7)), R=[wcm, h_], W=[pp])
                flush(cc)
                pd, ac = pad[cc % 3], acc[cc % 3]
                pdv = pd[:, 0:nrow * (rowlen + 4)].rearrange("p (r l) -> p r l", r=nrow)
                P.act(lambda e, pdv=pdv, pp=pp, ntok=ntok, nrow=nrow, rowlen=rowlen: e.copy(
                    out=pdv[:, :, 2:2 + rowlen], in_=pp[:, 0:ntok].rearrange("p (r l) -> p r l", r=nrow)), R=[pp], W=[pd])
                acv = ac[:, 0:ntok].rearrange("p (r l) -> p r l", r=nrow)
                P.dve(lambda e, acv=acv, pdv=pdv, cc=cc, rowlen=rowlen: e.tensor_scalar_mul(
                    out=acv, in0=pdv[:, :, 0:rowlen], scalar1=cw[:, cc, 0:1]), R=[pd, cw], W=[ac])
                for k in range(1, 5):
                    P.dve(lambda e, acv=acv, pdv=pdv, cc=cc, k=k, rowlen=rowlen: e.scalar_tensor_tensor(
                        out=acv, in0=pdv[:, :, k:k + rowlen], scalar=cw[:, cc, k:k + 1], in1=acv,
                        op0=ALU.mult, op1=ALU.add), R=[pd, cw, ac], W=[ac])
                kind = ("xs" if cc < 4 else "B" if cc == 4 else "C" if cc == 5 else "q" if cc < 10 else "k" if cc < 14 else "v")
                if kind in ("xs", "v", "B", "C"):
                    tb = tbf[cc % 4]
                    if kind == "C":
                        dst = cm_[:, 0:nt, 0, :]
                    elif kind == "B":
                        dst = cm_[:, 0:nt, 1, :]
                    else:
                        dst = tb[:, 0:ntok].rearrange("p (a l) -> p a l", a=nt)
                    Wt = [cm_] if kind in ("B", "C") else [tb]
                    if cc < 6:
                        P.act(lambda e, dst=dst, ac=ac, cc=cc, ntok=ntok, nt=nt: e.activation(
                            out=dst, in_=ac[:, 0:ntok].rearrange("p (a l) -> p a l", a=nt), func=AF.Silu,
                            bias=cb[:, cc:cc + 1], scale=1.0), R=[ac, cb], W=Wt)
                    else:
                        P.act(lambda e, dst=dst, ac=ac, ntok=ntok, nt=nt: e.activation(
                            out=dst, in_=ac[:, 0:ntok].rearrange("p (a l) -> p a l", a=nt), func=AF.Silu),
                            R=[ac], W=Wt)
                    if kind == "C":
                        continue
                    tmoff = {"xs": cc * 128, "B": 512, "v": 1152 + (cc - 14) * 128}[kind]
                    if kind == "B":
                        deferred.append((cc + 3, lambda cm_=cm_, tmoff=tmoff: emit_transposes(lambda a: cm_[:, a, 1, :], cm_, tmoff)))
                    else:
                        deferred.append((cc + 3, lambda tb=tb, tmoff=tmoff: emit_transposes(lambda a: tb[:, a * 128:(a + 1) * 128], tb, tmoff)))
                else:
                    s_, q_, r_ = sv[cc % 3], sq[cc % 3], rr[cc % 3]
                    P.act(lambda e, s_=s_, ac=ac, ntok=ntok: e.activation(out=s_[:, 0:ntok], in_=ac[:, 0:ntok], func=AF.Silu),
                          R=[ac], W=[s_])
                    P.pool(lambda e, s_=s_, q_=q_, ntok=ntok: e.tensor_tensor(out=q_[:, 0:ntok], in0=s_[:, 0:ntok], in1=s_[:, 0:ntok], op=ALU.mult),
                           R=[s_], W=[q_])
                    deferred.append((cc + 2, lambda cc=cc, kind=kind, s_=s_, q_=q_, r_=r_: emit_l2norm(cc, kind, s_, q_, r_)))
                    if kind == "k":
                        ci = 2 + (cc - 10)
                        tmoff = 640 + (cc - 10) * 128
                        deferred.append((cc + 3, lambda cm_=cm_, ci=ci, tmoff=tmoff: emit_transposes(lambda a: cm_[:, a, ci, :], cm_, tmoff)))
            flush(10 ** 9)
            for a in range(nt):
                for gi, (n0, nn, pz) in enumerate(((0, 512, PZ0), (512, 512, PZ1), (1024, 32, PSm))):
                    for kc in range(8):
                        P.pe(lambda e, kc=kc, a=a, n0=n0, nn=nn, pz=pz, h_=h_: e.matmul(
                            pz[:, 0:nn], lhsT=h_[:, kc, a * 128:(a + 1) * 128], rhs=wtm[:, kc, n0:n0 + nn],
                            start=(kc == 0), stop=(kc == 7)), R=[h_, wtm], W=[pz])
                    if gi < 2:
                        off = 1664 + gi * 512
                        P.act(lambda e, a=a, off=off, pz=pz, tm_=tm_: e.activation(out=tm_[:, a, off:off + 512], in_=pz[:], func=AF.Silu),
                              R=[pz], W=[tm_])
                    else:
                        r_ = smr[a % 2]
                        smv = sm_[:, a, :]
                        P.dve(lambda e, r_=r_: e.tensor_tensor(out=r_[:, 0:24], in0=PSm[:, 0:24], in1=spb[:, 0:24], op=ALU.add),
                              R=[PSm, spb], W=[r_])
                        P.act(lambda e, r_=r_: e.activation(out=r_[:, 0:24], in_=r_[:, 0:24], func=AF.Exp), R=[r_], W=[r_])
                        P.act(lambda e, r_=r_: e.activation(out=r_[:, 0:24], in_=r_[:, 0:24], func=AF.Ln, bias=1.0, scale=1.0), R=[r_], W=[r_])
                        P.act(lambda e, smv=smv: e.activation(out=smv[:, 40:48], in_=PSm[:, 24:32], func=AF.Sigmoid), R=[PSm], W=[sm_])
                        P.dve(lambda e, smv=smv, r_=r_: e.tensor_copy(out=smv[:, 0:16], in_=r_[:, 0:16]), R=[r_], W=[sm_])
                        P.dve(lambda e, smv=smv, r_=r_: e.tensor_tensor(out=smv[:, 16:40], in0=r_[:, 0:24], in1=amul[:], op=ALU.mult),
                              R=[r_, amul], W=[sm_])
                        P.dve(lambda e, smv=smv: e.tensor_scalar_mul(out=smv[:, 48:56], in0=smv[:, 40:48], scalar1=-1.0), R=[sm_], W=[sm_])
                        P.dve(lambda e, smv=smv: e.memset(smv[:, 56:64], 0.0), W=[sm_])
            ti0 = t0 // 128
            P.dma("sp", CMs[ti0:ti0 + nt].rearrange("t p f -> p t f"), cm_[:, 0:nt].rearrange("p t c l -> p t (c l)"), R=[cm_])
            P.dma("sp", TMs[t0:t0 + ntok, :].rearrange("(a p) f -> p a f", p=128), tm_[:, 0:nt, :], R=[tm_])
            P.dma("sp", SMs[t0:t0 + ntok, :].rearrange("(a p) f -> p a f", p=128), sm_[:, 0:nt, :], R=[sm_])


def cst_eps(L):
    return L["epsT"][:, 0:1]


def scan_pass(P, nc, L, d):
    NTT, NCT, NLT, NTH = L["NTT"], L["NCT"], L["NLT"], L["NTH"]
    CMs, TMs, SMs, YFs, YY, rowp = L["CMs"], L["TMs"], L["SMs"], L["YFs"], L["YY"], L["rowp"]
    cst, identb, psb, epsT = L["cst"], L["identb"], L["psb"], L["epsT"]
    Uincl, Lstrict, Lincl, Ustrict, ones = (cst[:, i * 128:(i + 1) * 128] for i in range(1, 6))
    if d == 0:
        Tm, Um, TmT = Uincl, Lstrict, Lincl
        order = list(range(NCT)) + [NCT + i for i in range(NLT)]
    else:
        Tm, Um, TmT = Lincl, Ustrict, Uincl
        order = list(reversed(range(NCT))) + [NCT + i for i in reversed(range(NLT))]
    prep_banks = psb[0:3]
    out_banks = psb[3:5]
    bKS, bSU, bST = psb[5], psb[6], psb[7]
    cnt = {"p": 0, "o": 0}

    def pb():
        cnt["p"] += 1
        return prep_banks[cnt["p"] % 3]

    def ob():
        cnt["o"] += 1
        return out_banks[cnt["o"] % 2]

    with ExitStack() as st:
        sb = lambda n, shp, dt_: P.sb(st, f"{n}_{d}", shp, dt_)
        cmb = [sb(f"cmb{i}", [128, 10, 128], BF16) for i in range(3)]
        tmb = [sb(f"tmb{i}", [128, TMW], BF16) for i in range(3)]
        smb = [sb(f"smb{i}", [128, SMW], F32) for i in range(3)]
        yo = [sb(f"yo{i}", [128, 1024], F32) for i in range(2)]
        rla = sb("rla", [128, 8, 128], F32)
        ET = sb("ET", [128, 8, 128], BF16)
        gmt = sb("gmt", [128, 128], BF16)
        att = [sb(f"att{i}", [128, 8, 128], BF16) for i in range(2)]
        xdt = [sb(f"xdt{i}", [128, 8, 64], BF16) for i in range(2)]
        xdw = [sb(f"xdw{i}", [128, 8, 64], BF16) for i in range(2)]
        ea = [sb(f"ea{i}", [128, 24], F32) for i in range(2)]
        t1 = sb("t1", [128, 8, 64], F32)
        ST = sb("ST", [128, 8, 64], F32)
        STb = sb("STb", [128, 512], BF16)
        rlu = sb("rlu", [128, 4, 128], F32)
        E = sb("E", [128, 4, 128], F32)
        Es = sb("Es", [128, 4, 128], F32)
        Ei = sb("Ei", [128, 4, 128], BF16)
        eg = [sb(f"eg{i}", [128, 12], F32) for i in range(2)]
        eb = [sb(f"eb{i}", [128, 4], F32) for i in range(2)]
        Ak = [sb(f"Ak{i}", [128, 4, 128], BF16) for i in range(2)]
        Nk = [sb(f"Nk{i}", [128, 4, 128], BF16) for i in range(2)]
        Pk = [sb(f"Pk{i}", [128, 4, 128], BF16) for i in range(2)]
        Pf = [sb(f"Pf{i}", [128, 4, 128], BF16) for i in range(2)]
        Wk = [sb(f"Wk{i}", [128, 4, 128], BF16) for i in range(2)]
        Dk = [sb(f"Dk{i}", [128, 4, 128], BF16) for i in range(2)]
        Afull = sb("Afull", [128, 4, 128], BF16)
        Nfull = sb("Nfull", [128, 4, 128], BF16)
        Ym = sb("Ym", [128, 4, 128], BF16)
        qk = sb("qk", [128, 4, 128], BF16)
        qkT = [sb(f"qkT{i}", [128, 4, 128], BF16) for i in range(2)]
        vb = [sb(f"vb{i}", [128, 4, 128], F32) for i in range(2)]
        kd = [sb(f"kd{i}", [128, 4, 128], BF16) for i in range(2)]
        tks = sb("tks", [128, 4, 128], F32)
        rv = sb("rv", [128, 4, 128], BF16)
        vnb = sb("vnb", [128, 4, 128], BF16)
        S = sb("S", [128, 4, 128], F32)
        Sb = sb("Sb", [128, 4, 128], BF16)
        t3 = sb("t3", [128, 4, 128], F32)
        P.pool(lambda e: e.memset(ST[:], 0.0), W=[ST])
        P.pool(lambda e: e.memset(STb[:], 0.0), W=[STb])
        P.pool(lambda e: e.memset(S[:], 0.0), W=[S])
        P.pool(lambda e: e.memset(Sb[:], 0.0), W=[Sb])
        if d == 1:
            yfb = [sb(f"yfb{i}", [128, 1024], F32) for i in range(3)]
            fin = sb("fin", [128, 648], F32)
            P.dma("sp", fin[:], rowp[:, 48:696].partition_broadcast(128), W=[fin])
            u = sb("u", [128, 1024], F32)
            usq = sb("usq", [128, 1024], F32)
            ss = sb("ss", [128, 8], F32)
            yfin = sb("yfin", [128, 1024], BF16)
            yT = [sb(f"yT{i}", [128, 8, 128], BF16) for i in range(2)]

        def load(ci):
            ti = order[ci]
            cm, tm, sm = cmb[ci % 3], tmb[ci % 3], smb[ci % 3]
            P.dma("sp", cm[:].rearrange("p c l -> p (c l)"), CMs[ti], W=[cm])
            P.dma("sp", tm[:], TMs[ti * 128:(ti + 1) * 128, :], W=[tm])
            P.dma("sp", sm[:], SMs[ti * 128:(ti + 1) * 128, :], W=[sm])
            if d == 1 and ti >= NCT:
                yf = yfb[ci % 3]
                P.dma("sp", yf[:], YFs[(ti - NCT) * 128:(ti - NCT + 1) * 128, :], W=[yf])

        def prep(ci):
            cm, tm, sm = cmb[ci % 3], tmb[ci % 3], smb[ci % 3]
            i2 = ci % 2
            CT, BT = cm[:, 0, :], cm[:, 1, :]
            la = sm[:, 16 + 8 * d:24 + 8 * d]
            dtd = sm[:, 8 * d:8 * d + 8]
            lag = sm[:, 32 + 4 * d:36 + 4 * d]
            beta = sm[:, 40 + 4 * d:44 + 4 * d]
            nbeta = sm[:, 48 + 4 * d:52 + 4 * d]
            if PARTS & 4:
                return
            P.pool(lambda e: e.tensor_tensor(out=rla[:], in0=bc(la, [128, 8, 128], 2), in1=bc(Tm, [128, 8, 128], 1), op=ALU.mult),
                   R=[sm, cst], W=[rla])
            for hf in range(2):
                pd = pb()
                P.pe(lambda e, pd=pd, hf=hf: e.matmul(pd[:], lhsT=Um, rhs=rla[:, 4 * hf:4 * hf + 4, :].rearrange("p h l -> p (h l)"),
                                                      start=True, stop=True), R=[cst, rla], W=[pd])
                P.act(lambda e, pd=pd, hf=hf: e.activation(out=ET[:, 4 * hf:4 * hf + 4, :].rearrange("p h l -> p (h l)"), in_=pd[:], func=AF.Exp),
                      R=[pd], W=[ET])
            pg = pb()
            P.pe(lambda e: e.matmul(pg[:, 0:128], lhsT=BT, rhs=CT, start=True, stop=True), R=[cm], W=[pg])
            P.pe(lambda e: e.matmul(pg[:, 128:136], lhsT=Tm, rhs=la, start=True, stop=True), R=[cst, sm], W=[pg])
            P.pe(lambda e: e.matmul(pg[:, 136:144], lhsT=Um, rhs=la, start=True, stop=True), R=[cst, sm], W=[pg])
            P.pe(lambda e: e.matmul(pg[:, 144:152], lhsT=ones, rhs=la, start=True, stop=True), R=[cst, sm], W=[pg])
            P.dve(lambda e: e.tensor_tensor(out=gmt[:], in0=pg[:, 0:128], in1=Tm, op=ALU.mult), R=[pg, cst], W=[gmt])
            ea_ = ea[i2]
            P.act(lambda e: e.activation(out=ea_[:], in_=pg[:, 128:152], func=AF.Exp), R=[pg], W=[ea_])
            att_, xdt_, xdw_ = att[i2], xdt[i2], xdw[i2]
            P.dve(lambda e: e.tensor_tensor(out=att_[:], in0=ET[:], in1=bc(gmt[:], [128, 8, 128], 1), op=ALU.mult),
                  R=[ET, gmt], W=[att_])
            P.pool(lambda e: e.tensor_tensor(out=xdt_[:], in0=tm[:, 0:512].rearrange("p (h q) -> p h q", h=8),
                                             in1=bc(dtd, [128, 8, 64], 2), op=ALU.mult), R=[tm, sm], W=[xdt_])
            P.dve(lambda e: e.tensor_tensor(out=xdw_[:], in0=xdt_[:], in1=bc(ea_[:, 8:16], [128, 8, 64], 2), op=ALU.mult),
                  R=[xdt_, ea_], W=[xdw_])
            if PARTS & 8:
                return
            P.pool(lambda e: e.tensor_tensor(out=rlu[:], in0=bc(lag, [128, 4, 128], 2), in1=bc(Um, [128, 4, 128], 1), op=ALU.mult),
                   R=[sm, cst], W=[rlu])
            pdg = pb()
            P.pe(lambda e: e.matmul(pdg[:], lhsT=Tm, rhs=rlu[:].rearrange("p h l -> p (h l)"), start=True, stop=True),
                 R=[cst, rlu], W=[pdg])
            P.act(lambda e: e.activation(out=E[:].rearrange("p h l -> p (h l)"), in_=pdg[:], func=AF.Exp), R=[pdg], W=[E])
            ps2 = pb()
            P.pe(lambda e: e.matmul(ps2[:, 0:4], lhsT=Tm, rhs=lag, start=True, stop=True), R=[cst, sm], W=[ps2])
            P.pe(lambda e: e.matmul(ps2[:, 4:8], lhsT=Um, rhs=lag, start=True, stop=True), R=[cst, sm], W=[ps2])
            P.pe(lambda e: e.matmul(ps2[:, 8:12], lhsT=ones, rhs=lag, start=True, stop=True), R=[cst, sm], W=[ps2])
            eg_, eb_ = eg[i2], eb[i2]
            P.act(lambda e: e.activation(out=eg_[:], in_=ps2[:, 0:12], func=AF.Exp), R=[ps2], W=[eg_])
            P.dve(lambda e: e.tensor_tensor(out=Es[:], in0=E[:], in1=bc(Um, [128, 4, 128], 1), op=ALU.mult), R=[E, cst], W=[Es])
            P.dve(lambda e: e.tensor_tensor(out=Es[:], in0=Es[:], in1=bc(nbeta, [128, 4, 128], 2), op=ALU.mult), R=[Es, sm], W=[Es])
            P.pool(lambda e: e.tensor_tensor(out=Ei[:], in0=E[:], in1=bc(TmT, [128, 4, 128], 1), op=ALU.mult), R=[E, cst], W=[Ei])
            pkk, pqk = pb(), pb()
            for h in range(4):
                P.pe(lambda e, h=h: e.matmul(pkk[:, h * 128:(h + 1) * 128], lhsT=cm[:, 2 + h, :], rhs=cm[:, 2 + h, :], start=True, stop=True),
                     R=[cm], W=[pkk])
            for h in range(4):
                P.pe(lambda e, h=h: e.matmul(pqk[:, h * 128:(h + 1) * 128], lhsT=cm[:, 6 + h, :], rhs=cm[:, 2 + h, :], start=True, stop=True),
                     R=[cm], W=[pqk])
            BD16 = cst[:, 6 * 128:7 * 128]
            Mb = [cst[:, (7 + i + 3 * d) * 128:(8 + i + 3 * d) * 128] for i in range(3)]
            P.dve(lambda e: e.tensor_tensor(out=Afull[:].rearrange("p h l -> p (h l)"), in0=pkk[:], in1=Es[:].rearrange("p h l -> p (h l)"), op=ALU.mult),
                  R=[pkk, Es], W=[Afull])
            P.dve(lambda e: e.tensor_tensor(out=qk[:].rearrange("p h l -> p (h l)"), in0=pqk[:], in1=Ei[:].rearrange("p h l -> p (h l)"), op=ALU.mult),
                  R=[pqk, Ei], W=[qk])
            pt = pb()
            ptb = pt.ap[:].bitcast(BF16)
            for h in range(4):
                P.pe(lambda e, h=h: e.transpose(out=ptb[:, h * 128:(h + 1) * 128], in_=Afull[:, h, :], identity=identb[:]), R=[Afull, identb], W=[pt])
            for h in range(4):
                P.pe(lambda e, h=h: e.transpose(out=ptb[:, 512 + h * 128:512 + (h + 1) * 128], in_=qk[:, h, :], identity=identb[:]), R=[qk, identb], W=[pt])
            qkT_ = qkT[i2]
            P.act(lambda e: e.copy(out=Nfull[:].rearrange("p h l -> p (h l)"), in_=ptb[:, 0:512]), R=[pt], W=[Nfull])
            P.dve(lambda e: e.tensor_copy(out=qkT_[:].rearrange("p h l -> p (h l)"), in_=ptb[:, 512:1024]), R=[pt], W=[qkT_])
            A, N, Pc = Ak[0], Nk[0], Pk[0]
            P.dve(lambda e, A=A: e.tensor_tensor(out=A[:], in0=Afull[:], in1=bc(BD16, [128, 4, 128], 1), op=ALU.mult), R=[Afull, cst], W=[A])
            P.dve(lambda e, N=N: e.tensor_tensor(out=N[:], in0=Nfull[:], in1=bc(BD16, [128, 4, 128], 1), op=ALU.mult), R=[Nfull, cst], W=[N])
            P.dve(lambda e, N=N, Pc=Pc: e.tensor_tensor(out=Pc[:], in0=N[:], in1=bc(identb[:], [128, 4, 128], 1), op=ALU.add), R=[N, identb], W=[Pc])
            for k in range(1, 4):
                A2, N2 = Ak[k % 2], Nk[k % 2]
                P2 = Pk[k % 2]
                pa = pb()
                for h in range(4):
                    P.pe(lambda e, h=h, N=N, A=A, pa=pa: e.matmul(pa[:, h * 128:(h + 1) * 128], lhsT=N[:, h, :], rhs=A[:, h, :], start=True, stop=True),
                         R=[N, A], W=[pa])
                P.act(lambda e, A2=A2, pa=pa: e.copy(out=A2[:].rearrange("p h l -> p (h l)"), in_=pa[:]), R=[pa], W=[A2])
                if k <= 2:
                    pn = pb()
                    for h in range(4):
                        P.pe(lambda e, h=h, N=N, A=A, pn=pn: e.matmul(pn[:, h * 128:(h + 1) * 128], lhsT=A[:, h, :], rhs=N[:, h, :], start=True, stop=True),
                             R=[N, A], W=[pn])
                    P.act(lambda e, N2=N2, pn=pn: e.copy(out=N2[:].rearrange("p h l -> p (h l)"), in_=pn[:]), R=[pn], W=[N2])
                pm = pb()
                for h in range(4):
                    P.pe(lambda e, h=h, A2=A2, Pc=Pc, pm=pm: e.matmul(pm[:, h * 128:(h + 1) * 128], lhsT=A2[:, h, :], rhs=Pc[:, h, :], start=True, stop=True),
                         R=[A2, Pc], W=[pm])
                P.dve(lambda e, P2=P2, Pc=Pc, pm=pm: e.tensor_tensor(out=P2[:].rearrange("p h l -> p (h l)"), in0=pm[:],
                                                                     in1=Pc[:].rearrange("p h l -> p (h l)"), op=ALU.add), R=[pm, Pc], W=[P2])
                A, N, Pc = A2, N2, P2
            Wc = Pc
            pt2 = pb()
            pt2b = pt2.ap[:].bitcast(BF16)
            for h in range(4):
                P.pe(lambda e, h=h, Wc=Wc: e.transpose(out=pt2b[:, h * 128:(h + 1) * 128], in_=Wc[:, h, :], identity=identb[:]), R=[Wc, identb], W=[pt2])
            Dc = Dk[0]
            P.act(lambda e, Dc=Dc: e.copy(out=Dc[:].rearrange("p h l -> p (h l)"), in_=pt2b[:, 0:512]), R=[pt2], W=[Dc])
            for li in range(3):
                last = li == 2
                py_ = pb()
                for h in range(4):
                    P.pe(lambda e, h=h, Dc=Dc, py_=py_: e.matmul(py_[:, h * 128:(h + 1) * 128], lhsT=Nfull[:, h, :], rhs=Dc[:, h, :], start=True, stop=True),
                         R=[Nfull, Dc], W=[py_])
                P.dve(lambda e, py_=py_, li=li: e.tensor_tensor(out=Ym[:], in0=py_[:].rearrange("p (h l) -> p h l", h=4), in1=bc(Mb[li], [128, 4, 128], 1), op=ALU.mult),
                      R=[py_, cst], W=[Ym])
                if not last:
                    pz = pb()
                    for h in range(4):
                        P.pe(lambda e, h=h, Wc=Wc, pz=pz: e.matmul(pz[:, h * 128:(h + 1) * 128], lhsT=Wc[:, h, :], rhs=Ym[:, h, :], start=True, stop=True),
                             R=[Wc, Ym], W=[pz])
                    D2 = Dk[(li + 1) % 2]
                    P.dve(lambda e, D2=D2, Dc=Dc, pz=pz: e.tensor_tensor(out=D2[:].rearrange("p h l -> p (h l)"), in0=pz[:],
                                                                         in1=Dc[:].rearrange("p h l -> p (h l)"), op=ALU.add), R=[pz, Dc], W=[D2])
                pzt = pb()
                for h in range(4):
                    P.pe(lambda e, h=h, Wc=Wc, pzt=pzt: e.matmul(pzt[:, h * 128:(h + 1) * 128], lhsT=Ym[:, h, :], rhs=Wc[:, h, :], start=True, stop=True),
                         R=[Wc, Ym], W=[pzt])
                W2 = Pf[i2] if last else Wk[li % 2]
                P.dve(lambda e, W2=W2, Wc=Wc, pzt=pzt: e.tensor_tensor(out=W2[:].rearrange("p h l -> p (h l)"), in0=pzt[:],
                                                                        in1=Wc[:].rearrange("p h l -> p (h l)"), op=ALU.add), R=[pzt, Wc], W=[W2])
                Wc = W2
                if not last:
                    Dc = D2
            vb_, kd_ = vb[i2], kd[i2]
            P.pool(lambda e: e.tensor_tensor(out=vb_[:], in0=tm[:, 1152:1664].rearrange("p (h v) -> p h v", h=4), in1=bc(beta, [128, 4, 128], 2), op=ALU.mult),
                   R=[tm, sm], W=[vb_])
            P.dve(lambda e: e.tensor_tensor(out=eb_[:], in0=eg_[:, 0:4], in1=beta, op=ALU.mult), R=[eg_, sm], W=[eb_])
            P.pool(lambda e: e.tensor_tensor(out=kd_[:], in0=tm[:, 640:1152].rearrange("p (h v) -> p h v", h=4), in1=bc(eg_[:, 4:8], [128, 4, 128], 2), op=ALU.mult),
                   R=[tm, eg_], W=[kd_])

        def seq(ci):
            ti = order[ci]
            lat = ti >= NCT
            cm, tm, sm = cmb[ci % 3], tmb[ci % 3], smb[ci % 3]
            i2 = ci % 2
            CT = cm[:, 0, :]
            ea_, att_, xdt_, xdw_ = ea[i2], att[i2], xdt[i2], xdw[i2]
            eg_, eb_, qkT_, vb_, kd_, Pf_ = eg[i2], eb[i2], qkT[i2], vb[i2], kd[i2], Pf[i2]
            yo_ = yo[ci % 2]
            for h in range(4):
                P.pe(lambda e, h=h: e.matmul(bKS[:, h * 128:(h + 1) * 128], lhsT=cm[:, 2 + h, :], rhs=Sb[:, h, :], start=True, stop=True),
                     R=[cm, Sb], W=[bKS])
            P.dve(lambda e: e.tensor_tensor(out=tks[:], in0=bKS[:].rearrange("p (h v) -> p h v", h=4), in1=bc(eb_[:], [128, 4, 128], 2), op=ALU.mult),
                  R=[bKS, eb_], W=[tks])
            P.dve(lambda e: e.tensor_tensor(out=rv[:], in0=vb_[:], in1=tks[:], op=ALU.subtract), R=[vb_, tks], W=[rv])
            if lat:
                pi = ob()
                P.pe(lambda e: e.matmul(pi[:], lhsT=CT, rhs=STb[:], start=True, stop=True), R=[cm, STb], W=[pi])
                pqs = ob()
                for h in range(4):
                    P.pe(lambda e, h=h: e.matmul(pqs[:, h * 128:(h + 1) * 128], lhsT=cm[:, 6 + h, :], rhs=Sb[:, h, :], start=True, stop=True),
                         R=[cm, Sb], W=[pqs])
            P.pe(lambda e: e.matmul(bST[:], lhsT=tm[:, 512:640], rhs=xdw_[:].rearrange("p h q -> p (h q)"), start=True, stop=True),
                 R=[tm, xdw_], W=[bST])
            if lat:
                P.dve(lambda e: e.tensor_tensor(out=t1[:], in0=pi[:].rearrange("p (h q) -> p h q", h=8), in1=bc(ea_[:, 0:8], [128, 8, 64], 2), op=ALU.mult),
                      R=[pi, ea_], W=[t1])
                P.dve(lambda e: e.tensor_tensor(out=t3[:], in0=pqs[:].rearrange("p (h v) -> p h v", h=4), in1=bc(eg_[:, 0:4], [128, 4, 128], 2), op=ALU.mult),
                      R=[pqs, eg_], W=[t3])
            P.dve(lambda e: e.tensor_tensor(out=ST[:], in0=ST[:], in1=bc(ea_[:, 16:24], [128, 8, 64], 2), op=ALU.mult), R=[ST, ea_], W=[ST])
            P.dve(lambda e: e.tensor_tensor(out=ST[:].rearrange("p h q -> p (h q)"), in0=ST[:].rearrange("p h q -> p (h q)"), in1=bST[:], op=ALU.add),
                  R=[ST, bST], W=[ST])
            P.act(lambda e: e.copy(out=STb[:], in_=ST[:].rearrange("p h q -> p (h q)")), R=[ST], W=[STb])
            for h in range(4):
                P.pe(lambda e, h=h: e.matmul(bKS[:, h * 128:(h + 1) * 128], lhsT=Pf_[:, h, :], rhs=rv[:, h, :], start=True, stop=True),
                     R=[Pf_, rv], W=[bKS])
            P.act(lambda e: e.copy(out=vnb[:].rearrange("p h v -> p (h v)"), in_=bKS[:]), R=[bKS], W=[vnb])
            for h in range(4):
                P.pe(lambda e, h=h: e.matmul(bSU[:, h * 128:(h + 1) * 128], lhsT=kd_[:, h, :], rhs=vnb[:, h, :], start=True, stop=True),
                     R=[kd_, vnb], W=[bSU])
            P.dve(lambda e: e.tensor_tensor(out=S[:], in0=S[:], in1=bc(eg_[:, 8:12], [128, 4, 128], 2), op=ALU.mult), R=[S, eg_], W=[S])
            P.dve(lambda e: e.tensor_tensor(out=S[:].rearrange("p h v -> p (h v)"), in0=S[:].rearrange("p h v -> p (h v)"), in1=bSU[:], op=ALU.add),
                  R=[S, bSU], W=[S])
            P.act(lambda e: e.copy(out=Sb[:].rearrange("p h v -> p (h v)"), in_=S[:].rearrange("p h v -> p (h v)")), R=[S], W=[Sb])
            if not lat:
                return
            py = ob()
            for h in range(8):
                P.pe(lambda e, h=h: e.matmul(py[:, h * 64:(h + 1) * 64], lhsT=att_[:, h, :], rhs=xdt_[:, h, :], start=True, stop=True),
                     R=[att_, xdt_], W=[py])
            P.dve(lambda e: e.tensor_tensor(out=yo_[:, 0:512], in0=t1[:].rearrange("p h q -> p (h q)"), in1=py[:], op=ALU.add), R=[t1, py], W=[yo_])
            pqv = ob()
            for h in range(4):
                P.pe(lambda e, h=h: e.matmul(pqv[:, h * 128:(h + 1) * 128], lhsT=qkT_[:, h, :], rhs=vnb[:, h, :], start=True, stop=True),
                     R=[qkT_, vnb], W=[pqv])
            P.dve(lambda e: e.tensor_tensor(out=yo_[:, 512:1024], in0=t3[:].rearrange("p h v -> p (h v)"), in1=pqv[:], op=ALU.add), R=[t3, pqv], W=[yo_])
            li = ti - NCT
            if d == 0:
                P.dma("sp", YFs[li * 128:(li + 1) * 128, :], yo_[:], R=[yo_])
                return
            yf = yfb[ci % 3]
            dsk, nws, nwg = fin[:, 0:8], fin[:, 8:520], fin[:, 520:648]
            P.dve(lambda e: e.tensor_tensor(out=u[:], in0=yo_[:], in1=yf[:], op=ALU.add), R=[yo_, yf], W=[u])
            P.pool(lambda e: e.tensor_tensor(out=usq[:, 0:512].rearrange("p (h q) -> p h q", h=8), in0=tm[:, 0:512].rearrange("p (h q) -> p h q", h=8),
                                             in1=bc(dsk, [128, 8, 64], 2), op=ALU.mult), R=[tm, fin], W=[usq])
            P.dve(lambda e: e.tensor_tensor(out=u[:, 0:512], in0=u[:, 0:512], in1=usq[:, 0:512], op=ALU.add), R=[u, usq], W=[u])
            P.dve(lambda e: e.tensor_tensor(out=u[:, 0:512], in0=u[:, 0:512], in1=tm[:, 1664:2176], op=ALU.mult), R=[u, tm], W=[u])
            P.pool(lambda e: e.memset(ss[:], 0.0), W=[ss])
            P.act(lambda e: e.activation(out=usq[:, 0:512], in_=u[:, 0:512], func=AF.Square, accum_out=ss[:, 0:1]), R=[u, ss], W=[usq, ss])
            for h in range(4):
                P.act(lambda e, h=h: e.activation(out=usq[:, 512 + h * 128:512 + (h + 1) * 128], in_=u[:, 512 + h * 128:512 + (h + 1) * 128],
                                                  func=AF.Square, accum_out=ss[:, 1 + h:2 + h]), R=[u, ss], W=[usq, ss])
            P.act(lambda e: e.activation(out=ss[:, 0:1], in_=ss[:, 0:1], func=AF.Sqrt, bias=epsT[:, 0:1], scale=1.0 / 512), R=[ss, epsT], W=[ss])
            P.act(lambda e: e.activation(out=ss[:, 1:5], in_=ss[:, 1:5], func=AF.Sqrt, bias=epsT[:, 0:1], scale=1.0 / 128), R=[ss, epsT], W=[ss])
            P.dve(lambda e: e.reciprocal(out=ss[:, 0:5], in_=ss[:, 0:5]), R=[ss], W=[ss])
            P.dve(lambda e: e.scalar_tensor_tensor(out=yfin[:, 0:512], in0=u[:, 0:512], scalar=ss[:, 0:1], in1=nws, op0=ALU.mult, op1=ALU.mult),
                  R=[u, ss, fin], W=[yfin])
            u4 = u[:, 512:1024].rearrange("p (h v) -> p h v", h=4)
            P.dve(lambda e: e.tensor_tensor(out=u4, in0=u4, in1=bc(ss[:, 1:5], [128, 4, 128], 2), op=ALU.mult), R=[u, ss], W=[u])
            P.dve(lambda e: e.tensor_tensor(out=u4, in0=u4, in1=bc(nwg, [128, 4, 128], 1), op=ALU.mult), R=[u, fin], W=[u])
            P.dve(lambda e: e.tensor_tensor(out=yfin[:, 512:1024], in0=u[:, 512:1024], in1=tm[:, 2176:2688], op=ALU.mult), R=[u, tm], W=[yfin])
            pt = ob()
            ptb = pt.ap[:].bitcast(BF16)
            for c8 in range(8):
                P.pe(lambda e, c8=c8: e.transpose(out=ptb[:, c8 * 128:(c8 + 1) * 128], in_=yfin[:, c8 * 128:(c8 + 1) * 128], identity=identb[:]),
                     R=[yfin, identb], W=[pt])
            yT_ = yT[ci % 2]
            P.act(lambda e: e.copy(out=yT_[:].rearrange("p c l -> p (c l)"), in_=ptb[:, :]), R=[pt], W=[yT_])
            hc, tl = li // NTH, li % NTH
            P.dma("sp", YY.ap()[hc, tl], yT_[:].rearrange("p c l -> p (c l)"), R=[yT_])

        n = len(order)
        if L["debug"] and d == 0:
            n = min(DBG_NCH, n)
        jobs = list(L["conv_jobs"]) if d == 0 else []
        if jobs:
            wst = [sb(f"wst{i}", [128, 6144], F32) for i in range(2)]
            wsb = [sb(f"wsb{i}", [128, 6144], BF16) for i in range(2)]
        jcnt = [0]

        def run_job():
            if not jobs:
                return
            src_aps, dst_ap, nel = jobs.pop(0)
            s_, b_ = wst[jcnt[0] % 2], wsb[jcnt[0] % 2]
            jcnt[0] += 1
            off = 0
            for sap, shape in src_aps:
                sz = int(np.prod(shape))
                view = s_[:, off:off + sz].rearrange("p (a b) -> p a b", a=shape[0])
                P.dma("sp", view, sap, W=[s_])
                off += sz
            P.act(lambda e: e.copy(out=b_[:, 0:nel], in_=s_[:, 0:nel]), R=[s_], W=[b_])
            P.dma("sp", dst_ap, b_[:, 0:nel], R=[b_])

        load(0)
        if n > 1:
            load(1)
        if PARTS & 1:
            prep(0)
        for ci in range(n):
            if ci + 2 < n:
                load(ci + 2)
            if ci + 1 < n and PARTS & 1:
                prep(ci + 1)
            if PARTS & 2:
                seq(ci)
            run_job()
            if ci == n - 1:
                while jobs:
                    run_job()
        if L["debug"] and d == 0:
            loc = locals()
            for nm_, ap_ in L["dbg"].items():
                if not nm_.startswith("s_"):
                    continue
                key = nm_[2:]
                t_ = loc[key] if key in loc else loc[key[:-1]][int(key[-1])]
                src = t_[:] if len(t_.ap.shape) == 2 else t_[:].rearrange("p h l -> p (h l)")
                L["final_ops"].append(P.dma("sp", ap_, src, R=[t_]))


def stage5(P, nc, L):
    HALF, NTH = L["HALF"], L["NTH"]
    xhalf, yout, MINE, PART, WO, WGU, WD, rowp = L["xhalf"], L["yout"], L["MINE"], L["PART"], L["WO"], L["WGU"], L["WD"], L["rowp"]
    g12row, modT, cst, psb, epsT, final_ops = L["g12row"], L["modT"], L["cst"], L["psb"], L["epsT"], L["final_ops"]
    ident = cst[:, 0:128]
    with ExitStack() as st:
        lnp = P.sb(st, "lnp", [128, 4 * D], F32)
        P.dma("sp", lnp[:], rowp[:, 696:4792].partition_broadcast(128), W=[lnp])
        xt = [P.sb(st, f"x5_{i}", [128, 4, D], F32) for i in range(2)]
        yT = P.sb(st, "yT5", [128, 2, 4, 1024], BF16)
        h2T = P.sb(st, "h2T", [128, 8, 512], BF16)
        actT = P.sb(st, "actT", [128, 22, 512], BF16)
        wbuf = [P.sb(st, f"wbuf{i}", [128, 8192], BF16) for i in range(3)]
        tmp = [P.sb(st, f"tmp5_{i}", [128, 512], F32) for i in range(2)]
        sgt = [P.sb(st, f"sgt{i}", [128, 512], F32) for i in range(2)]
        junk = P.sb(st, "junk", [128, D], F32)
        stat = [P.sb(st, f"stat{i}", [128, 8], F32) for i in range(2)]
        wcnt = [0]

        def wload(src, n):
            wb = wbuf[wcnt[0] % 3]
            wcnt[0] += 1
            P.dma("sp", wb[:, 0:n], src, W=[wb])
            return wb

        def resid(ps, x_, a, dh, gi, k):
            t = tmp[k % 2]
            P.dve(lambda e: e.tensor_tensor(out=t[:], in0=ps[:], in1=g12row[:, gi * D + dh * 512: gi * D + dh * 512 + 512], op=ALU.mult),
                  R=[ps, g12row], W=[t])
            xs_ = x_[:, a, dh * 512:(dh + 1) * 512]
            P.dve(lambda e: e.scalar_tensor_tensor(out=xs_, in0=xs_, scalar=ALPHA, in1=t[:], op0=ALU.mult, op1=ALU.add),
                  R=[x_, t], W=[x_])

        def layer_norm(x_, a, li, k):
            sx = stat[k % 2]
            xa = x_[:, a, :]
            g_, b_ = lnp[:, (2 * li) * D:(2 * li + 1) * D], lnp[:, (2 * li + 1) * D:(2 * li + 2) * D]
            P.pool(lambda e: e.memset(sx[:], 0.0), W=[sx])
            P.act(lambda e: e.activation(out=junk[:], in_=xa, func=AF.Identity, accum_out=sx[:, 0:1]), R=[x_, sx], W=[junk, sx])
            P.act(lambda e: e.activation(out=junk[:], in_=xa, func=AF.Square, accum_out=sx[:, 1:2]), R=[x_, sx], W=[junk, sx])
            P.dve(lambda e: e.tensor_scalar_mul(out=sx[:, 2:3], in0=sx[:, 0:1], scalar1=1.0 / D), R=[sx], W=[sx])
            P.dve(lambda e: e.tensor_tensor(out=sx[:, 3:4], in0=sx[:, 2:3], in1=sx[:, 2:3], op=ALU.mult), R=[sx], W=[sx])
            P.dve(lambda e: e.scalar_tensor_tensor(out=sx[:, 4:5], in0=sx[:, 1:2], scalar=1.0 / D, in1=sx[:, 3:4], op0=ALU.mult, op1=ALU.subtract),
                  R=[sx], W=[sx])
            P.act(lambda e: e.activation(out=sx[:, 5:6], in_=sx[:, 4:5], func=AF.Sqrt, bias=epsT[:, 1:2], scale=1.0), R=[sx, epsT], W=[sx])
            P.dve(lambda e: e.reciprocal(out=sx[:, 5:6], in_=sx[:, 5:6]), R=[sx], W=[sx])
            P.dve(lambda e: e.scalar_tensor_tensor(out=sx[:, 6:7], in0=sx[:, 2:3], scalar=-1.0, in1=sx[:, 5:6], op0=ALU.mult, op1=ALU.mult),
                  R=[sx], W=[sx])
            P.act(lambda e: e.activation(out=xa, in_=xa, func=AF.Identity, bias=sx[:, 6:7], scale=sx[:, 5:6]), R=[x_, sx], W=[x_])
            P.dve(lambda e: e.tensor_tensor(out=xa, in0=xa, in1=g_, op=ALU.mult), R=[x_, lnp], W=[x_])
            P.dve(lambda e: e.tensor_tensor(out=xa, in0=xa, in1=b_, op=ALU.add), R=[x_, lnp], W=[x_])

        nblk = HALF // 512
        for blk in range(nblk):
            x_ = xt[blk % 2]
            tl0 = blk * 4
            P.dma("sp", x_[:], xhalf[blk * 512:(blk + 1) * 512, :].rearrange("(a p) d -> p a d", p=128), W=[x_])
            yTa = T("yTa", yT.ap)
            yTb = T("yTb", yT.ap)
            yTa.last_w, yTa.readers = yT.last_w, dict(yT.readers)
            P.dma("sp", yT[:, 0], MINE.ap()[tl0:tl0 + 4].rearrange("t p f -> p t f"), W=[yT], semt=yTa)
            P.dma("sp", yT[:, 1], PART.ap()[tl0:tl0 + 4].rearrange("t p f -> p t f"), W=[yT], semt=yTb)
            k = 0
            for dh in range(2):
                wb = wload(WO[dh], 8192)
                wv = wb[:, 0:8192].rearrange("p (k n) -> p k n", k=16)
                for a in range(4):
                    ps = psb[k % 2]
                    for kc in range(16):
                        src, cc = kc // 8, kc % 8
                        P.pe(lambda e, ps=ps, kc=kc, src=src, cc=cc, a=a, wv=wv: e.matmul(
                            ps[:], lhsT=yT[:, src, a, cc * 128:(cc + 1) * 128], rhs=wv[:, kc, :], start=(kc == 0), stop=(kc == 15)),
                            R=[yT, wb], W=[ps])
                    resid(ps, x_, a, dh, 0, k)
                    k += 1
            for a in range(4):
                layer_norm(x_, a, 0, a)
            PX = psb[2]
            for fc in range(8):
                for a in range(4):
                    P.pe(lambda e, a=a, fc=fc, x_=x_: e.transpose(out=PX[:, a * 128:(a + 1) * 128], in_=x_[:, a, fc * 128:(fc + 1) * 128], identity=ident),
                         R=[x_, cst], W=[PX])
                P.act(lambda e, fc=fc: e.activation(out=h2T[:, fc, :], in_=PX[:], func=AF.Identity,
                                                    bias=modT[:, 32 + fc:33 + fc], scale=modT[:, 40 + fc:41 + fc]), R=[PX, modT], W=[h2T])
            for b11 in range(11):
                wb = wload(WGU[b11], 4096)
                wv = wb[:, 0:4096].rearrange("p (g k n) -> p g k n", g=2, k=8)
                for jj in range(2):
                    j = b11 * 2 + jj
                    pg, pu = psb[4 + j % 2], psb[6 + j % 2]
                    for kc in range(8):
                        P.pe(lambda e, pg=pg, kc=kc, jj=jj, wv=wv: e.matmul(pg[:], lhsT=wv[:, 0, kc, jj * 128:(jj + 1) * 128], rhs=h2T[:, kc, :],
                                                                            start=(kc == 0), stop=(kc == 7)), R=[wb, h2T], W=[pg])
                    for kc in range(8):
                        P.pe(lambda e, pu=pu, kc=kc, jj=jj, wv=wv: e.matmul(pu[:], lhsT=wv[:, 1, kc, jj * 128:(jj + 1) * 128], rhs=h2T[:, kc, :],
                                                                            start=(kc == 0), stop=(kc == 7)), R=[wb, h2T], W=[pu])
                    sg = sgt[j % 2]
                    P.act(lambda e, sg=sg, pg=pg: e.activation(out=sg[:], in_=pg[:], func=AF.Silu), R=[pg], W=[sg])
                    P.dve(lambda e, sg=sg, pu=pu, j=j: e.tensor_tensor(out=actT[:, j, :], in0=sg[:], in1=pu[:], op=ALU.mult), R=[sg, pu], W=[actT])
            for dh in range(2):
                accs = [psb[0], psb[1], psb[2], psb[3]]
                for jh in range(2):
                    wb = wload(WD[dh * 2 + jh], 5632)
                    wv = wb[:, 0:5632].rearrange("p (j n) -> p j n", j=11)
                    for a in range(4):
                        for j11 in range(11):
                            j = jh * 11 + j11
                            P.pe(lambda e, a=a, j=j, j11=j11, wv=wv, accs=accs: e.matmul(
                                accs[a][:], lhsT=actT[:, j, a * 128:(a + 1) * 128], rhs=wv[:, j11, :], start=(j == 0), stop=(j == 21)),
                                R=[actT, wb], W=[accs[a]])
                for a in range(4):
                    resid(accs[a], x_, a, dh, 1, a)
            for a in range(4):
                layer_norm(x_, a, 1, a)
            final_ops.append(P.dma("sp", yout[blk * 512:(blk + 1) * 512, :].rearrange("(a p) d -> p a d", p=128), x_[:], R=[x_]))


def host_consts():
    j = np.arange(128)[:, None]
    l = np.arange(128)[None, :]
    mats = [np.eye(128), (j <= l), (j > l), (j >= l), (j < l), np.ones((128, 128)), (j // 16 == l // 16)]
    for b in (16, 32, 64):
        mats.append((j // (2 * b) == l // (2 * b)) & (j % (2 * b) >= b) & (l % (2 * b) < b))
    for b in (16, 32, 64):
        mats.append(((j // (2 * b) == l // (2 * b)) & (j % (2 * b) >= b) & (l % (2 * b) < b)).T)
    return np.concatenate([m.astype(np.float32) for m in mats], axis=1)


def prep_core(inp, core, SEQ):
    b, h = core // 2, core % 2
    HALF = SEQ // 2
    f = lambda a: np.ascontiguousarray(a, dtype=np.float32)
    x, ctx = inp["x"], inp["ctx"]
    w_in = inp["w_in"][0]
    xs0 = 1024
    B0, C0 = 2048, 2304
    dt0 = 2560
    q0, k0, v0 = 2592, 3616, 4640
    g0 = 5664
    b0, a0 = 6688, 6704
    r = lambda s, n: np.arange(s, s + n)
    cols_cm = np.concatenate([r(xs0 + h * 512, 512), r(B0 + h * 128, 128), r(C0 + h * 128, 128),
                              r(q0 + h * 512, 512), r(k0 + h * 512, 512), r(v0 + h * 512, 512)])
    cols_tm = np.concatenate([r(h * 512, 512), r(g0 + h * 512, 512),
                              r(dt0 + 8 * h, 8), r(dt0 + 16 + 8 * h, 8),
                              r(a0 + 4 * h, 4), r(a0 + 8 + 4 * h, 4),
                              r(b0 + 4 * h, 4), r(b0 + 8 + 4 * h, 4)])
    cws, cwg = inp["conv_w_ssd"][0], inp["conv_w_gdn"][0]
    ssd_cols = np.concatenate([r(h * 512, 512), r(1024 + h * 128, 128), r(1280 + h * 128, 128)])
    gdn_cols = np.concatenate([r(h * 512, 512), r(1024 + h * 512, 512), r(2048 + h * 512, 512)])
    cw = np.concatenate([cws[:, ssd_cols], cwg[:, gdn_cols]], axis=1)
    cwT = cw.T.reshape(NCM, 128, 5).transpose(1, 0, 2).reshape(128, NCM * 5)
    cbT = inp["conv_b_ssd"][0][ssd_cols].reshape(6, 128).T
    rowp = np.concatenate([
        inp["dt_bias_ssd"][0][0, 8 * h:8 * h + 8], inp["dt_bias_ssd"][0][1, 8 * h:8 * h + 8],
        inp["dt_bias_gdn"][0][0, 4 * h:4 * h + 4], inp["dt_bias_gdn"][0][1, 4 * h:4 * h + 4],
        inp["a_log_ssd"][0][0, 8 * h:8 * h + 8], inp["a_log_ssd"][0][1, 8 * h:8 * h + 8],
        inp["a_log_gdn"][0][0, 4 * h:4 * h + 4], inp["a_log_gdn"][0][1, 4 * h:4 * h + 4],
        inp["d_skip_ssd"][0][8 * h:8 * h + 8],
        inp["norm_w_ssd"][0][h * 512:(h + 1) * 512], inp["norm_w_gdn"][0],
        inp["ln1_g"][0], inp["ln1_b"][0], inp["ln2_g"][0], inp["ln2_b"][0]])[None, :]
    cv = np.stack([inp["c"][b], inp["c_ctx"]])
    cvT = cv.reshape(2, 8, 128).transpose(2, 0, 1).reshape(128, 16)
    wo = inp["w_out"][0]
    own = np.concatenate([r(h * 512, 512), r(1024 + h * 512, 512)])
    oth = np.concatenate([r((1 - h) * 512, 512), r(1024 + (1 - h) * 512, 512)])
    return {
        "xin": f(np.concatenate([ctx[b], x[b]], axis=0)),
        "xhalf": f(x[b, h * HALF:(h + 1) * HALF]),
        "cvecT": f(cvT),
        "w_ada": f(inp["w_ada"][0]), "b_ada": f(inp["b_ada"][0][None, :]),
        "w_cm": f(w_in[:, cols_cm]), "w_tm": f(w_in[:, cols_tm]),
        "convw": f(cwT), "convb": f(cbT), "rowp": f(rowp), "consts": host_consts(),
        "w_out": f(wo[np.concatenate([own, oth])]),
        "w_gate": f(inp["w_ffn_gate"][0]), "w_up": f(inp["w_ffn_up"][0]), "w_down": f(inp["w_ffn_down"][0]),
    }


_CACHE = {}


def run(inputs, SEQ, debug=False, stop_after=99, ncores=8):
    key = (SEQ, debug, stop_after)
    if key not in _CACHE:
        _CACHE[key] = build_program(SEQ, debug, stop_after)
    nc, stats = _CACHE[key]
    in_maps = [prep_core(inputs, c, SEQ) for c in range(ncores)]
    res = run_bass_kernel_spmd(nc, in_maps, core_ids=list(range(ncores)))
    return res.results, stats


def kernel(**inputs):
    SEQ = inputs["x"].shape[1]
    B = inputs["x"].shape[0]
    results, _ = run(inputs, SEQ)
    HALF = SEQ // 2
    out = np.empty((B, SEQ, D), np.float32)
    for c in range(8):
        b, h = c // 2, c % 2
        out[b, h * HALF:(h + 1) * HALF] = results[c]["yout"]
    return out
```

```python
import numpy as np
from contextlib import ExitStack
import concourse.bass as bass
import concourse.mybir as mybir
from concourse.bass_utils import run_bass_kernel_spmd

F32 = mybir.dt.float32
BF16 = mybir.dt.bfloat16
AF = mybir.ActivationFunctionType
ALU = mybir.AluOpType
AX = mybir.AxisListType

D = 1024
CTX = 256
GRID_W = 64
DFF = 2816
NCM = 18
NTMC = 1056
TMW = 2688
SMW = 64
ALPHA = 2.0 ** 0.25
LN_EPS = 1e-5
RMS_EPS = 1e-6
EPOCH = 30000
NCST = 13
PARTS = 3
DBG_NCH = 10 ** 6


class T:
    __slots__ = ("name", "ap", "last_w", "readers", "dma_readers", "sem", "cnt", "last_dma", "excl")

    def __init__(self, name, ap):
        self.excl = False
        self.name = name
        self.ap = ap
        self.last_w = None
        self.readers = {}
        self.dma_readers = []
        self.sem = None
        self.cnt = 0
        self.last_dma = None

    def __getitem__(self, k):
        return self.ap[k]


class Op:
    __slots__ = ("eng", "fn", "deps", "needs_inc", "inc_val", "epoch", "is_dma", "sem", "val", "amt")

    def __init__(self, eng, fn, is_dma=False):
        self.eng = eng
        self.fn = fn
        self.deps = []
        self.needs_inc = False
        self.inc_val = 0
        self.epoch = 0
        self.is_dma = is_dma
        self.sem = None
        self.val = 0
        self.amt = 16


class Prog:
    ENGS = ("sp", "act", "pool", "dve", "pe")

    def __init__(self, nc, stack):
        self.nc = nc
        self.stack = stack
        self.ops = {e: [] for e in self.ENGS}
        self.same_engine_sync = {"sp": False, "act": True, "pool": True, "dve": True, "pe": False}
        self.nsem = 0
        self.dma_open = []
        self.pending = {e: [] for e in self.ENGS}
        self.sem_pool = {}

    def sb(self, st, name, shape, dt):
        return T(name, st.enter_context(self.nc.sbuf_tensor(name, list(shape), dt)))

    def ps(self, st, name, shape, dt):
        t = T(name, st.enter_context(self.nc.psum_tensor(name, list(shape), dt)))
        t.excl = True
        return t

    def new_sem(self, name):
        self.nsem += 1
        return self.stack.enter_context(self.nc.semaphore(name))

    def _track(self, op, R, W, extra=()):
        deps = list(extra)
        for t in R:
            if t.last_w is not None:
                deps.append(t.last_w)
            if t.excl:
                deps.extend(o for en, o in t.readers.items() if en != op.eng)
        for t in W:
            if t.last_w is not None:
                deps.append(t.last_w)
            deps.extend(t.readers.values())
            deps.extend(t.dma_readers)
        deps.extend(self.pending[op.eng])
        self.pending[op.eng] = []
        seen = set()
        for d in deps:
            if d is op or id(d) in seen:
                continue
            seen.add(id(d))
            if (not d.is_dma) and d.eng == op.eng and not op.is_dma and not self.same_engine_sync[op.eng]:
                continue
            if not d.is_dma:
                d.needs_inc = True
            op.deps.append(d)
        for t in R:
            if op.is_dma:
                t.dma_readers.append(op)
            else:
                t.readers[op.eng] = op
        for t in W:
            t.last_w = op
            t.readers = {}
            t.dma_readers = []

    def op(self, eng, fn, R=(), W=()):
        o = Op(eng, fn)
        self._track(o, R, W)
        self.ops[eng].append(o)
        return o

    def pe(self, fn, R=(), W=()):
        return self.op("pe", fn, R, W)

    def act(self, fn, R=(), W=()):
        return self.op("act", fn, R, W)

    def dve(self, fn, R=(), W=()):
        return self.op("dve", fn, R, W)

    def pool(self, fn, R=(), W=()):
        return self.op("pool", fn, R, W)

    def dma(self, eng, out_ap=None, in_ap=None, R=(), W=(), semt=None, fn=None, amt=16, **kw):
        if fn is None:
            fn = lambda e: e.dma_start(out=out_ap, in_=in_ap, **kw)
        o = Op(eng, fn, is_dma=True)
        o.amt = amt
        if semt is None:
            semt = (list(W) + list(R))[0]
        if semt.sem is None:
            key = semt.name
            if key not in self.sem_pool:
                self.sem_pool[key] = [self.new_sem("d_" + key), 0]
            semt.sem = self.sem_pool[key]
        semt.sem[1] += amt
        o.sem = semt.sem[0]
        o.val = semt.sem[1]
        extra = [semt.last_dma] if semt.last_dma is not None else []
        semt.last_dma = o
        self._track(o, R, W, extra)
        self.ops[eng].append(o)
        self.dma_open.append(o)
        return o

    def barrier(self):
        deps = list(self.dma_open)
        for e in self.ENGS:
            for o in reversed(self.ops[e]):
                if not o.is_dma:
                    deps.append(o)
                    break
        self.dma_open = []
        for e in self.ENGS:
            self.pending[e] = list(deps)

    def emit(self, final_ops=()):
        nc = self.nc
        esems = {}
        for e in self.ENGS:
            n = 0
            for o in self.ops[e]:
                if o.is_dma or not o.needs_inc:
                    continue
                o.epoch = n // EPOCH
                o.inc_val = n % EPOCH + 1
                n += 1
            nep = (n + EPOCH - 1) // EPOCH
            esems[e] = [self.new_sem(f"s_{e}{i}") for i in range(max(nep, 1))]
        block = self.stack.enter_context(nc.Block())
        deco = {"sp": block.sync, "act": block.scalar, "pool": block.gpsimd, "dve": block.vector, "pe": block.tensor}
        nwaits = {e: 0 for e in self.ENGS}

        def make(eng):
            def body(e):
                seen = {}
                maxep = {}

                def wait_all(deps):
                    need = {}
                    for d in deps:
                        if d.is_dma:
                            key, val, sem = ("d", id(d.sem)), d.val, d.sem
                        else:
                            key, val, sem = (d.eng, d.epoch), d.inc_val, esems[d.eng][d.epoch]
                        if seen.get(key, 0) >= val:
                            continue
                        if key not in need or need[key][0] < val:
                            need[key] = (val, sem)
                    for key, (val, sem) in need.items():
                        if key[0] != "d":
                            if any(k[0] == key[0] and k[1] > key[1] for k in list(seen) + list(need) if k[0] != "d"):
                                continue
                        seen[key] = val
                        e.wait_ge(sem, val)
                        nwaits[eng] += 1

                def wait_for(d):
                    wait_all([d])

                for o in self.ops[eng]:
                    wait_all(o.deps)
                    ins = o.fn(e)
                    if o.is_dma:
                        ins.then_inc(o.sem, o.amt)
                    elif o.needs_inc:
                        ins.then_inc(esems[eng][o.epoch], 1)
                if eng == "sp":
                    for d in final_ops:
                        wait_for(d)
                        e.nop()
            return body

        for eng in self.ENGS:
            if self.ops[eng] or eng == "sp":
                deco[eng](make(eng))
        self.nwaits = nwaits
        return block


def bc(ap, shape, axis):
    return ap.unsqueeze(axis).to_broadcast(list(shape))


def build_program(SEQ, debug=False, stop_after=99):
    TT = CTX + SEQ
    NTT = TT // 128
    NCT = CTX // 128
    NLT = SEQ // 128
    HALF = SEQ // 2
    NTH = HALF // 128
    nc = bass.Bass("TRN2", target_bir_lowering=False)
    dt = nc.dram_tensor
    xin = dt("xin", [TT, D], F32, kind="ExternalInput").ap()
    xhalf = dt("xhalf", [HALF, D], F32, kind="ExternalInput").ap()
    cvecT = dt("cvecT", [128, 16], F32, kind="ExternalInput").ap()
    w_ada = dt("w_ada", [D, 6 * D], F32, kind="ExternalInput").ap()
    b_ada = dt("b_ada", [1, 6 * D], F32, kind="ExternalInput").ap()
    w_cm = dt("w_cm", [D, NCM * 128], F32, kind="ExternalInput").ap()
    w_tm = dt("w_tm", [D, NTMC], F32, kind="ExternalInput").ap()
    convw = dt("convw", [128, NCM * 5], F32, kind="ExternalInput").ap()
    convb = dt("convb", [128, 6], F32, kind="ExternalInput").ap()
    rowp = dt("rowp", [1, 4792], F32, kind="ExternalInput").ap()
    consts = dt("consts", [128, NCST * 128], F32, kind="ExternalInput").ap()
    w_out = dt("w_out", [2 * D, D], F32, kind="ExternalInput").ap()
    w_gate = dt("w_gate", [D, DFF], F32, kind="ExternalInput").ap()
    w_up = dt("w_up", [D, DFF], F32, kind="ExternalInput").ap()
    w_down = dt("w_down", [DFF, D], F32, kind="ExternalInput").ap()
    yout = dt("yout", [HALF, D], F32, kind="ExternalOutput").ap()
    CMs = dt("CMs", [NTT, 128, 10 * 128], BF16).ap()
    TMs = dt("TMs", [TT, TMW], BF16).ap()
    SMs = dt("SMs", [TT, SMW], F32).ap()
    YFs = dt("YFs", [SEQ, 1024], F32).ap()
    YY = dt("YY", [2, NTH, 128, 1024], BF16)
    TPK = min(NTH, 8)
    NCC = NTH // TPK
    ZO = dt("ZO", [NCC, 2, TPK, 128, 1024], BF16)
    ZIN = dt("ZIN", [NTH, 128, 1024], BF16)
    MINE = dt("MINE", [NTH, 128, 1024], BF16)
    PART = dt("PART", [NTH, 128, 1024], BF16)
    WO = dt("WO", [2, 128, 16 * 512], BF16).ap()
    WGU = dt("WGU", [11, 128, 2 * 8 * 256], BF16).ap()
    WD = dt("WD", [4, 128, 11 * 512], BF16).ap()
    dbg = {}
    if debug:
        dbg["modT"] = dt("dbg_modT", [128, 64], F32, kind="ExternalOutput").ap()
        dbg["CM"] = dt("dbg_CM", [NTT, 128, 10 * 128], BF16, kind="ExternalOutput").ap()
        dbg["TM"] = dt("dbg_TM", [TT, TMW], BF16, kind="ExternalOutput").ap()
        dbg["SM"] = dt("dbg_SM", [TT, SMW], F32, kind="ExternalOutput").ap()
        dbg["YF"] = dt("dbg_YF", [SEQ, 1024], F32, kind="ExternalOutput").ap()
        dbg["YY"] = dt("dbg_YY", [2 * NTH * 128, 1024], BF16, kind="ExternalOutput").ap()
        for nm_, dt_ in (("E", F32), ("Es", F32), ("tks", F32), ("S", F32), ("t3", F32), ("vb0", F32)):
            dbg["s_" + nm_] = dt("dbg_s_" + nm_, [128, 512], dt_, kind="ExternalOutput").ap()
        for nm_ in ("Ei", "Ak0", "Ak1", "Nk0", "Nk1", "Pf0", "qkT0", "kd0", "rv", "vnb", "Sb", "qk", "Pk0", "Pk1", "Afull", "Nfull", "Ym", "Dk0", "Dk1", "Wk0", "Wk1"):
            dbg["s_" + nm_] = dt("dbg_s_" + nm_, [128, 512], BF16, kind="ExternalOutput").ap()
        dbg["s_eg0"] = dt("dbg_s_eg0", [128, 12], F32, kind="ExternalOutput").ap()

    with ExitStack() as top:
        P = Prog(nc, top)
        cst = P.sb(top, "cst", [128, NCST * 128], F32)
        identb = P.sb(top, "identb", [128, 128], BF16)
        modT = P.sb(top, "modT", [128, 64], F32)
        g12row = P.sb(top, "g12row", [128, 2 * D], F32)
        psb = [P.ps(top, f"psb{i}", [128, 512], F32) for i in range(8)]
        epsT = P.sb(top, "epsT", [128, 4], F32)
        P.pool(lambda e: e.memset(epsT[:, 0:1], RMS_EPS), W=[epsT])
        P.pool(lambda e: e.memset(epsT[:, 1:2], LN_EPS), W=[epsT])
        ident = cst[:, 0:128]
        Uincl, Lstrict, Lincl, Ustrict, ones = (cst[:, i * 128:(i + 1) * 128] for i in range(1, 6))
        P.dma("sp", cst[:], consts[:, :], W=[cst])
        P.dve(lambda e: e.tensor_copy(out=identb[:], in_=ident), R=[cst], W=[identb])
        final_ops = []

        with ExitStack() as st:
            ccol = P.sb(st, "ccol", [128, 2, 8], F32)
            csil = P.sb(st, "csil", [128, 2, 8], F32)
            crep = P.sb(st, "crep", [128, 2, 8, 128], F32)
            barow = P.sb(st, "barow", [128, 6 * D], F32)
            modrow = P.sb(st, "modrow", [128, 4 * D], F32)
            cmodrow = P.sb(st, "cmodrow", [128, 2 * D], F32)
            wab = [P.sb(st, f"wab{i}", [128, 8, 512], F32) for i in range(2)]
            P.dma("sp", ccol[:].rearrange("p v k -> p (v k)"), cvecT[:, :], W=[ccol])
            P.dma("sp", barow[:], b_ada.partition_broadcast(128), W=[barow])
            P.act(lambda e: e.activation(out=csil[:], in_=ccol[:], func=AF.Silu), R=[ccol], W=[csil])
            P.dve(lambda e: e.tensor_copy(out=crep[:].rearrange("p v k m -> p (v k) m"),
                                          in_=bc(csil[:].rearrange("p v k -> p (v k)"), [128, 16, 128], 2)),
                  R=[csil], W=[crep])
            w_ada_v = w_ada.rearrange("(kc p) n -> p kc n", p=128)
            for nb in range(12):
                wb = wab[nb % 2]
                P.dma("sp", wb[:], w_ada_v[:, :, nb * 512:(nb + 1) * 512], W=[wb])
                pl, pc = psb[(2 * nb) % 8], psb[(2 * nb + 1) % 8]
                for kc in range(8):
                    P.pe(lambda e, kc=kc, wb=wb, pl=pl: e.matmul(pl[:], lhsT=crep[:, 0, kc, :], rhs=wb[:, kc, :],
                                                                 start=(kc == 0), stop=(kc == 7)), R=[crep, wb], W=[pl])
                if nb < 4:
                    for kc in range(8):
                        P.pe(lambda e, kc=kc, wb=wb, pc=pc: e.matmul(pc[:], lhsT=crep[:, 1, kc, :], rhs=wb[:, kc, :],
                                                                     start=(kc == 0), stop=(kc == 7)), R=[crep, wb], W=[pc])
                seg = nb // 2
                half = nb % 2
                bsl = barow[:, nb * 512:(nb + 1) * 512]
                if seg in (0, 1, 3, 4):
                    mi = {0: 0, 1: 1, 3: 2, 4: 3}[seg]
                    dst = modrow[:, mi * D + half * 512: mi * D + half * 512 + 512]
                    P.dve(lambda e, dst=dst, pl=pl, bsl=bsl: e.tensor_tensor(out=dst, in0=pl[:], in1=bsl, op=ALU.add),
                          R=[pl, barow], W=[modrow])
                    if seg in (1, 4):
                        P.dve(lambda e, dst=dst: e.tensor_scalar_add(out=dst, in0=dst, scalar1=1.0), R=[modrow], W=[modrow])
                else:
                    gi = 0 if seg == 2 else 1
                    dst = g12row[:, gi * D + half * 512: gi * D + half * 512 + 512]
                    P.dve(lambda e, dst=dst, pl=pl, bsl=bsl: e.tensor_tensor(out=dst, in0=pl[:], in1=bsl, op=ALU.add),
                          R=[pl, barow], W=[g12row])
                if nb < 4:
                    dst = cmodrow[:, nb * 512:(nb + 1) * 512]
                    P.dve(lambda e, dst=dst, pc=pc, bsl=bsl: e.tensor_tensor(out=dst, in0=pc[:], in1=bsl, op=ALU.add),
                          R=[pc, barow], W=[cmodrow])
                    if nb >= 2:
                        P.dve(lambda e, dst=dst: e.tensor_scalar_add(out=dst, in0=dst, scalar1=1.0), R=[cmodrow], W=[cmodrow])
            srcs = [(modrow, 0), (modrow, 1), (cmodrow, 0), (cmodrow, 1), (modrow, 2), (modrow, 3)]
            for v, (src, si) in enumerate(srcs):
                for g in range(2):
                    pt = psb[(2 * v + g) % 8]
                    for q in range(4):
                        fc = g * 4 + q
                        P.pe(lambda e, pt=pt, q=q, src=src, off=si * D + fc * 128: e.transpose(
                            out=pt[:, q * 128:(q + 1) * 128], in_=src[:, off:off + 128], identity=ident),
                            R=[src, cst], W=[pt])
                    P.act(lambda e, pt=pt, v=v, g=g: e.copy(
                        out=modT[:, 8 * v + 4 * g: 8 * v + 4 * g + 4],
                        in_=pt[:].rearrange("p (q m) -> p q m", q=4)[:, :, 0]), R=[pt], W=[modT])
            if debug:
                final_ops.append(P.dma("sp", dbg["modT"], modT[:], R=[modT]))

            conv_jobs = []

            def convert(src_aps, dst_ap, n):
                conv_jobs.append((src_aps, dst_ap, n))

            if stop_after >= 5:
                wo_v = w_out.rearrange("(kc p) n -> p kc n", p=128)
                for dh in range(2):
                    for kh in range(2):
                        convert([(wo_v[:, kh * 8:(kh + 1) * 8, dh * 512:(dh + 1) * 512], (8, 512))],
                                WO[dh, :, kh * 4096:(kh + 1) * 4096], 4096)
                wg_v = w_gate.rearrange("(kc p) n -> p kc n", p=128)
                wu_v = w_up.rearrange("(kc p) n -> p kc n", p=128)
                for blk in range(11):
                    convert([(wg_v[:, :, blk * 256:(blk + 1) * 256], (8, 256)),
                             (wu_v[:, :, blk * 256:(blk + 1) * 256], (8, 256))], WGU[blk, :, :], 4096)
                wd_v = w_down.rearrange("(j p) n -> p j n", p=128)
                for dh in range(2):
                    for jh in range(2):
                        convert([(wd_v[:, jh * 11:(jh + 1) * 11, dh * 512:(dh + 1) * 512], (11, 512))],
                                WD[dh * 2 + jh, :, :], 5632)
        P.barrier()

        if stop_after >= 1:
            stage1(P, nc, locals())
        P.barrier()
        if stop_after >= 2:
            scan_pass(P, nc, locals(), 0)
            P.barrier()
        if stop_after >= 3:
            scan_pass(P, nc, locals(), 1)
            P.barrier()
        if debug and stop_after >= 1:
            dd = T("dd", None)
            final_ops.append(P.dma("sp", dbg["CM"], CMs, semt=dd))
            final_ops.append(P.dma("sp", dbg["TM"], TMs, semt=dd))
            final_ops.append(P.dma("sp", dbg["SM"], SMs, semt=dd))
            if stop_after >= 2:
                final_ops.append(P.dma("sp", dbg["YF"], YFs, semt=dd))
                final_ops.append(P.dma("sp", dbg["YY"], YY.ap().rearrange("a t p f -> (a t p) f"), semt=dd))
            P.barrier()
        if stop_after >= 4:
            cps = [T(f"cp{i}", None) for i in range(4)]
            CW = 8192
            ccnt = [0]

            def dyn_copy(dst3, src4, sel, fresh):
                nt_ = dst3.shape[0]
                nr = nt_ * 128 * 1024 // CW
                dflat = dst3.rearrange("t p f -> (t p f)").rearrange("(r c) -> r c", c=CW)
                def fn(e, fresh=fresh):
                    if fresh:
                        pid = e.partition_id()
                        P.dyn = {0: e.snap(pid % 2), 1: e.snap(1 - pid % 2)}
                    sflat = src4[bass.ds(P.dyn[sel], 1)].rearrange("a t p f -> (a t p f)").rearrange("(r c) -> r c", c=CW)
                    return e.dma_start(out=dflat, in_=sflat)
                ccnt[0] += 1
                P.dma("sp", semt=cps[ccnt[0] % 4], fn=fn)

            dyn_copy(ZIN.ap(), YY.ap(), 1, True)
            P.barrier()
            cct = T("cc", None)
            for k in range(NCC):
                P.dma("pool", semt=cct, amt=1, fn=lambda e, k=k: e.collective_compute(
                    "AllGather", ALU.bypass, replica_groups=[[0, 1], [2, 3], [4, 5], [6, 7]],
                    ins=[ZIN.ap()[k * TPK:(k + 1) * TPK].rearrange("t p f -> (t p) f").opt()],
                    outs=[ZO.ap()[k].rearrange("a t p f -> (a t p) f").opt()]))
            P.barrier()
            dyn_copy(MINE.ap(), YY.ap(), 0, True)
            for k in range(NCC):
                dyn_copy(PART.ap()[k * TPK:(k + 1) * TPK], ZO.ap()[k], 1, False)
            P.barrier()
        if stop_after >= 5:
            stage5(P, nc, locals())
        else:
            with ExitStack() as st:
                tb = P.sb(st, "tb", [128, D], F32)
                for a in range(HALF // 128):
                    P.dma("sp", tb[:], xhalf[a * 128:(a + 1) * 128, :], W=[tb])
                    final_ops.append(P.dma("sp", yout[a * 128:(a + 1) * 128, :], tb[:], R=[tb]))
        P.emit(final_ops=final_ops + locals().get("_final", []))
        stats = dict(nsem=P.nsem, nops={e: len(P.ops[e]) for e in P.ENGS}, nwaits=P.nwaits)
    return nc, stats


def stage1(P, nc, L):
    TT, NTT, NCT = L["TT"], L["NTT"], L["NCT"]
    xin, w_cm, w_tm, convw, convb, rowp = L["xin"], L["w_cm"], L["w_tm"], L["convw"], L["convb"], L["rowp"]
    CMs, TMs, SMs = L["CMs"], L["TMs"], L["SMs"]
    cst, identb, modT, psb = L["cst"], L["identb"], L["modT"], L["psb"]
    ident = cst[:, 0:128]
    ones = cst[:, 5 * 128:6 * 128]
    with ExitStack() as st:
        wcm = P.sb(st, "wcm", [128, 8, NCM * 128], BF16)
        wtm = P.sb(st, "wtm", [128, 8, NTMC], BF16)
        wld = [P.sb(st, f"wld{i}", [128, 1152], F32) for i in range(2)]
        cw = P.sb(st, "cw", [128, NCM, 5], F32)
        cb = P.sb(st, "cb", [128, 6], F32)
        spb = P.sb(st, "spb", [128, 48], F32)
        amul = P.sb(st, "amul", [128, 24], F32)
        xt = [P.sb(st, f"xt{i}", [128, 4, D], F32) for i in range(2)]
        hT = [P.sb(st, f"hT{i}", [128, 8, 512], BF16) for i in range(2)]
        pad = [P.sb(st, f"pad{i}", [128, 544], F32) for i in range(3)]
        acc = [P.sb(st, f"acc{i}", [128, 512], F32) for i in range(3)]
        ptmp = P.sb(st, "ptmp", [128, 512], F32)
        sv = [P.sb(st, f"sv{i}", [128, 512], F32) for i in range(3)]
        sq = [P.sb(st, f"sq{i}", [128, 512], F32) for i in range(3)]
        rr = [P.sb(st, f"rr{i}", [128, 512], F32) for i in range(3)]
        tbf = [P.sb(st, f"tbf{i}", [128, 512], BF16) for i in range(4)]
        cmst = [P.sb(st, f"cmst{i}", [128, 4, 10, 128], BF16) for i in range(2)]
        tmst = [P.sb(st, f"tmst{i}", [128, 4, TMW], BF16) for i in range(1)]
        smst = [P.sb(st, f"smst{i}", [128, 4, SMW], F32) for i in range(2)]
        smr = [P.sb(st, f"smr{i}", [128, 32], F32) for i in range(2)]
        wcm_v = w_cm.rearrange("(kc p) n -> p kc n", p=128)
        wtm_v = w_tm.rearrange("(kc p) n -> p kc n", p=128)
        for kc in range(8):
            for hf in range(2):
                w = wld[hf]
                P.dma("sp", w[:], wcm_v[:, kc, hf * 1152:(hf + 1) * 1152], W=[w])
                (P.dve if hf == 0 else P.pool)(lambda e, w=w, kc=kc, hf=hf: e.tensor_copy(
                    out=wcm[:, kc, hf * 1152:(hf + 1) * 1152], in_=w[:]), R=[w], W=[wcm])
        for kc in range(8):
            w = wld[kc % 2]
            P.dma("sp", w[:, 0:NTMC], wtm_v[:, kc, :], W=[w])
            (P.dve if kc % 2 == 0 else P.pool)(lambda e, w=w, kc=kc: e.tensor_copy(out=wtm[:, kc, :], in_=w[:, 0:NTMC]), R=[w], W=[wtm])
        P.dma("sp", cw[:].rearrange("p c k -> p (c k)"), convw[:, :], W=[cw])
        P.dma("sp", cb[:], convb[:, :], W=[cb])
        P.dma("sp", spb[:], rowp[:, 0:48].partition_broadcast(128), W=[spb])
        P.act(lambda e: e.activation(out=amul[:], in_=spb[:, 24:48], func=AF.Exp), R=[spb], W=[amul])
        P.dve(lambda e: e.tensor_scalar_mul(out=amul[:], in0=amul[:], scalar1=-1.0), R=[amul], W=[amul])
        for p_ in pad:
            P.pool(lambda e, p_=p_: e.memset(p_[:], 0.0), W=[p_])

        blocks = [(0, CTX, CTX, 2)]
        t0 = CTX
        while t0 < TT:
            blocks.append((t0, 512, GRID_W, 0))
            t0 += 512
        PX, PA, PB, PN, PZ0, PZ1, PSm, PTr = L["psb"]
        ptr_bf = PTr.ap[:].bitcast(BF16)
        for bi, (t0, ntok, rowlen, mv) in enumerate(blocks):
            nt = ntok // 128
            nrow = ntok // rowlen
            x_, h_ = xt[bi % 2], hT[bi % 2]
            if bi == 1:
                for p_ in pad:
                    P.pool(lambda e, p_=p_: e.memset(p_[:], 0.0), W=[p_])
            cm_, tm_, sm_ = cmst[bi % 2], tmst[0], smst[bi % 2]
            P.dma("sp", x_[:, 0:nt, :], xin[t0:t0 + ntok, :].rearrange("(a p) d -> p a d", p=128), W=[x_])
            for fc in range(8):
                for a in range(nt):
                    P.pe(lambda e, a=a, fc=fc, x_=x_: e.transpose(out=PX[:, a * 128:(a + 1) * 128],
                                                                  in_=x_[:, a, fc * 128:(fc + 1) * 128], identity=ident),
                         R=[x_, cst], W=[PX])
                P.act(lambda e, fc=fc, h_=h_, mv=mv, ntok=ntok: e.activation(
                    out=h_[:, fc, 0:ntok], in_=PX[:, 0:ntok], func=AF.Identity,
                    bias=modT[:, 8 * mv + fc: 8 * mv + fc + 1], scale=modT[:, 8 * (mv + 1) + fc: 8 * (mv + 1) + fc + 1]),
                    R=[PX, modT], W=[h_])
            deferred = []

            def flush(upto):
                keep = []
                for due, fn_ in deferred:
                    if due <= upto:
                        fn_()
                    else:
                        keep.append((due, fn_))
                deferred[:] = keep

            def emit_transposes(src_fn, Rt, tmoff, nt=nt, tm_=tm_):
                for a in range(nt):
                    P.pe(lambda e, a=a: e.transpose(out=ptr_bf[:, a * 128:(a + 1) * 128], in_=src_fn(a), identity=identb[:]),
                         R=[Rt, identb], W=[PTr])
                P.dve(lambda e: e.tensor_copy(
                    out=tm_[:, 0:nt, tmoff:tmoff + 128], in_=ptr_bf[:, 0:nt * 128].rearrange("p (a l) -> p a l", a=nt)),
                    R=[PTr], W=[tm_])

            def emit_l2norm(cc, kind, s_, q_, r_, ntok=ntok, nt=nt, cm_=cm_):
                P.pe(lambda e: e.matmul(PN[:, 0:ntok], lhsT=ones, rhs=q_[:, 0:ntok], start=True, stop=True),
                     R=[cst, q_], W=[PN])
                P.act(lambda e: e.activation(out=r_[:, 0:ntok], in_=PN[:, 0:ntok], func=AF.Sqrt,
                                             bias=cst_eps(L), scale=1.0), R=[PN, L["epsT"]], W=[r_])
                P.dve(lambda e: e.reciprocal(out=r_[:, 0:ntok], in_=r_[:, 0:ntok]), R=[r_], W=[r_])
                if kind == "q":
                    dst = cm_[:, 0:nt, cc, :]
                    P.dve(lambda e: e.scalar_tensor_tensor(
                        out=dst, in0=s_[:, 0:ntok].rearrange("p (a l) -> p a l", a=nt), scalar=128.0 ** -0.5,
                        in1=r_[:, 0:ntok].rearrange("p (a l) -> p a l", a=nt), op0=ALU.mult, op1=ALU.mult),
                        R=[s_, r_], W=[cm_])
                else:
                    dst = cm_[:, 0:nt, 2 + (cc - 10), :]
                    P.dve(lambda e: e.tensor_tensor(
                        out=dst, in0=s_[:, 0:ntok].rearrange("p (a l) -> p a l", a=nt),
                        in1=r_[:, 0:ntok].rearrange("p (a l) -> p a l", a=nt), op=ALU.mult), R=[s_, r_], W=[cm_])

            for cc in range(NCM):
                pp = (PA, PB)[cc % 2]
                for kc in range(8):
                    P.pe(lambda e, kc=kc, cc=cc, pp=pp, h_=h_, ntok=ntok: e.matmul(
                        pp[:, 0:ntok], lhsT=wcm[:, kc, cc * 128:(cc + 1) * 128], rhs=h_[:, kc, 0:ntok],
                        start=(kc == 0), stop=(kc == 7)), R=[wcm, h_], W=[pp])
                flush(cc)
                pd, ac = pad[cc % 3], acc[cc % 3]
                pdv = pd[:, 0:nrow * (rowlen + 4)].rearrange("p (r l) -> p r l", r=nrow)
                P.act(lambda e, pdv=pdv, pp=pp, ntok=ntok, nrow=nrow, rowlen=rowlen: e.copy(
                    out=pdv[:, :, 2:2 + rowlen], in_=pp[:, 0:ntok].rearrange("p (r l) -> p r l", r=nrow)), R=[pp], W=[pd])
                acv = ac[:, 0:ntok].rearrange("p (r l) -> p r l", r=nrow)
                P.dve(lambda e, acv=acv, pdv=pdv, cc=cc, rowlen=rowlen: e.tensor_scalar_mul(
                    out=acv, in0=pdv[:, :, 0:rowlen], scalar1=cw[:, cc, 0:1]), R=[pd, cw], W=[ac])
                for k in range(1, 5):
                    P.dve(lambda e, acv=acv, pdv=pdv, cc=cc, k=k, rowlen=rowlen: e.scalar_tensor_tensor(
                        out=acv, in0=pdv[:, :, k:k + rowlen], scalar=cw[:, cc, k:k + 1], in1=acv,
                        op0=ALU.mult, op1=ALU.add), R=[pd, cw, ac], W=[ac])
                kind = ("xs" if cc < 4 else "B" if cc == 4 else "C" if cc == 5 else "q" if cc < 10 else "k" if cc < 14 else "v")
                if kind in ("xs", "v", "B", "C"):
                    tb = tbf[cc % 4]
                    if kind == "C":
                        dst = cm_[:, 0:nt, 0, :]
                    elif kind == "B":
                        dst = cm_[:, 0:nt, 1, :]
                    else:
                        dst = tb[:, 0:ntok].rearrange("p (a l) -> p a l", a=nt)
                    Wt = [cm_] if kind in ("B", "C") else [tb]
                    if cc < 6:
                        P.act(lambda e, dst=dst, ac=ac, cc=cc, ntok=ntok, nt=nt: e.activation(
                            out=dst, in_=ac[:, 0:ntok].rearrange("p (a l) -> p a l", a=nt), func=AF.Silu,
                            bias=cb[:, cc:cc + 1], scale=1.0), R=[ac, cb], W=Wt)
                    else:
                        P.act(lambda e, dst=dst, ac=ac, ntok=ntok, nt=nt: e.activation(
                            out=dst, in_=ac[:, 0:ntok].rearrange("p (a l) -> p a l", a=nt), func=AF.Silu),
                            R=[ac], W=Wt)
                    if kind == "C":
                        continue
                    tmoff = {"xs": cc * 128, "B": 512, "v": 1152 + (cc - 14) * 128}[kind]
                    if kind == "B":
                        deferred.append((cc + 3, lambda cm_=cm_, tmoff=tmoff: emit_transposes(lambda a: cm_[:, a, 1, :], cm_, tmoff)))
                    else:
                        deferred.append((cc + 3, lambda tb=tb, tmoff=tmoff: emit_transposes(lambda a: tb[:, a * 128:(a + 1) * 128], tb, tmoff)))
                else:
                    s_, q_, r_ = sv[cc % 3], sq[cc % 3], rr[cc % 3]
                    P.act(lambda e, s_=s_, ac=ac, ntok=ntok: e.activation(out=s_[:, 0:ntok], in_=ac[:, 0:ntok], func=AF.Silu),
                          R=[ac], W=[s_])
                    P.pool(lambda e, s_=s_, q_=q_, ntok=ntok: e.tensor_tensor(out=q_[:, 0:ntok], in0=s_[:, 0:ntok], in1=s_[:, 0:ntok], op=ALU.mult),
                           R=[s_], W=[q_])
                    deferred.append((cc + 2, lambda cc=cc, kind=kind, s_=s_, q_=q_, r_=r_: emit_l2norm(cc, kind, s_, q_, r_)))
                    if kind == "k":
                        ci = 2 + (cc - 10)
                        tmoff = 640 + (cc - 10) * 128
                        deferred.append((cc + 3, lambda cm_=cm_, ci=ci, tmoff=tmoff: emit_transposes(lambda a: cm_[:, a, ci, :], cm_, tmoff)))
            flush(10 ** 9)
            for a in range(nt):
                for gi, (n0, nn, pz) in enumerate(((0, 512, PZ0), (512, 512, PZ1), (1024, 32, PSm))):
                    for kc in range(8):
                        P.pe(lambda e, kc=kc, a=a, n0=n0, nn=nn, pz=pz, h_=h_: e.matmul(
                            pz[:, 0:nn], lhsT=h_[:, kc, a * 128:(a + 1) * 128], rhs=wtm[:, kc, n0:n0 + nn],
                            start=(kc == 0), stop=(kc == 7)), R=[h_, wtm], W=[pz])
                    if gi < 2:
                        off = 1664 + gi * 512
                        P.act(lambda e, a=a, off=off, pz=pz, tm_=tm_: e.activation(out=tm_[:, a, off:off + 512], in_=pz[:], func=AF.Silu),
                              R=[pz], W=[tm_])
                    else:
                        r_ = smr[a % 2]
                        smv = sm_[:, a, :]
                        P.dve(lambda e, r_=r_: e.tensor_tensor(out=r_[:, 0:24], in0=PSm[:, 0:24], in1=spb[:, 0:24], op=ALU.add),
                              R=[PSm, spb], W=[r_])
                        P.act(lambda e, r_=r_: e.activation(out=r_[:, 0:24], in_=r_[:, 0:24], func=AF.Exp), R=[r_], W=[r_])
                        P.act(lambda e, r_=r_: e.activation(out=r_[:, 0:24], in_=r_[:, 0:24], func=AF.Ln, bias=1.0, scale=1.0), R=[r_], W=[r_])
                        P.act(lambda e, smv=smv: e.activation(out=smv[:, 40:48], in_=PSm[:, 24:32], func=AF.Sigmoid), R=[PSm], W=[sm_])
                        P.dve(lambda e, smv=smv, r_=r_: e.tensor_copy(out=smv[:, 0:16], in_=r_[:, 0:16]), R=[r_], W=[sm_])
                        P.dve(lambda e, smv=smv, r_=r_: e.tensor_tensor(out=smv[:, 16:40], in0=r_[:, 0:24], in1=amul[:], op=ALU.mult),
                              R=[r_, amul], W=[sm_])
                        P.dve(lambda e, smv=smv: e.tensor_scalar_mul(out=smv[:, 48:56], in0=smv[:, 40:48], scalar1=-1.0), R=[sm_], W=[sm_])
                        P.dve(lambda e, smv=smv: e.memset(smv[:, 56:64], 0.0), W=[sm_])
            ti0 = t0 // 128
            P.dma("sp", CMs[ti0:ti0 + nt].rearrange("t p f -> p t f"), cm_[:, 0:nt].rearrange("p t c l -> p t (c l)"), R=[cm_])
            P.dma("sp", TMs[t0:t0 + ntok, :].rearrange("(a p) f -> p a f", p=128), tm_[:, 0:nt, :], R=[tm_])
            P.dma("sp", SMs[t0:t0 + ntok, :].rearrange("(a p) f -> p a f", p=128), sm_[:, 0:nt, :], R=[sm_])


def cst_eps(L):
    return L["epsT"][:, 0:1]


def scan_pass(P, nc, L, d):
    NTT, NCT, NLT, NTH = L["NTT"], L["NCT"], L["NLT"], L["NTH"]
    CMs, TMs, SMs, YFs, YY, rowp = L["CMs"], L["TMs"], L["SMs"], L["YFs"], L["YY"], L["rowp"]
    cst, identb, psb, epsT = L["cst"], L["identb"], L["psb"], L["epsT"]
    Uincl, Lstrict, Lincl, Ustrict, ones = (cst[:, i * 128:(i + 1) * 128] for i in range(1, 6))
    if d == 0:
        Tm, Um, TmT = Uincl, Lstrict, Lincl
        order = list(range(NCT)) + [NCT + i for i in range(NLT)]
    else:
        Tm, Um, TmT = Lincl, Ustrict, Uincl
        order = list(reversed(range(NCT))) + [NCT + i for i in reversed(range(NLT))]
    prep_banks = psb[0:3]
    out_banks = psb[3:5]
    bKS, bSU, bST = psb[5], psb[6], psb[7]
    cnt = {"p": 0, "o": 0}

    def pb():
        cnt["p"] += 1
        return prep_banks[cnt["p"] % 3]

    def ob():
        cnt["o"] += 1
        return out_banks[cnt["o"] % 2]

    with ExitStack() as st:
        sb = lambda n, shp, dt_: P.sb(st, f"{n}_{d}", shp, dt_)
        cmb = [sb(f"cmb{i}", [128, 10, 128], BF16) for i in range(3)]
        tmb = [sb(f"tmb{i}", [128, TMW], BF16) for i in range(3)]
        smb = [sb(f"smb{i}", [128, SMW], F32) for i in range(3)]
        yo = [sb(f"yo{i}", [128, 1024], F32) for i in range(2)]
        rla = sb("rla", [128, 8, 128], F32)
        ET = sb("ET", [128, 8, 128], BF16)
        gmt = sb("gmt", [128, 128], BF16)
        att = [sb(f"att{i}", [128, 8, 128], BF16) for i in range(2)]
        xdt = [sb(f"xdt{i}", [128, 8, 64], BF16) for i in range(2)]
        xdw = [sb(f"xdw{i}", [128, 8, 64], BF16) for i in range(2)]
        ea = [sb(f"ea{i}", [128, 24], F32) for i in range(2)]
        t1 = sb("t1", [128, 8, 64], F32)
        ST = sb("ST", [128, 8, 64], F32)
        STb = sb("STb", [128, 512], BF16)
        rlu = sb("rlu", [128, 4, 128], F32)
        E = sb("E", [128, 4, 128], F32)
        Es = sb("Es", [128, 4, 128], F32)
        Ei = sb("Ei", [128, 4, 128], BF16)
        eg = [sb(f"eg{i}", [128, 12], F32) for i in range(2)]
        eb = [sb(f"eb{i}", [128, 4], F32) for i in range(2)]
        Ak = [sb(f"Ak{i}", [128, 4, 128], BF16) for i in range(2)]
        Nk = [sb(f"Nk{i}", [128, 4, 128], BF16) for i in range(2)]
        Pk = [sb(f"Pk{i}", [128, 4, 128], BF16) for i in range(2)]
        Pf = [sb(f"Pf{i}", [128, 4, 128], BF16) for i in range(2)]
        Wk = [sb(f"Wk{i}", [128, 4, 128], BF16) for i in range(2)]
        Dk = [sb(f"Dk{i}", [128, 4, 128], BF16) for i in range(2)]
        Afull = sb("Afull", [128, 4, 128], BF16)
        Nfull = sb("Nfull", [128, 4, 128], BF16)
        Ym = sb("Ym", [128, 4, 128], BF16)
        qk = sb("qk", [128, 4, 128], BF16)
        qkT = [sb(f"qkT{i}", [128, 4, 128], BF16) for i in range(2)]
        vb = [sb(f"vb{i}", [128, 4, 128], F32) for i in range(2)]
        kd = [sb(f"kd{i}", [128, 4, 128], BF16) for i in range(2)]
        tks = sb("tks", [128, 4, 128], F32)
        rv = sb("rv", [128, 4, 128], BF16)
        vnb = sb("vnb", [128, 4, 128], BF16)
        S = sb("S", [128, 4, 128], F32)
        Sb = sb("Sb", [128, 4, 128], BF16)
        t3 = sb("t3", [128, 4, 128], F32)
        P.pool(lambda e: e.memset(ST[:], 0.0), W=[ST])
        P.pool(lambda e: e.memset(STb[:], 0.0), W=[STb])
        P.pool(lambda e: e.memset(S[:], 0.0), W=[S])
        P.pool(lambda e: e.memset(Sb[:], 0.0), W=[Sb])
        if d == 1:
            yfb = [sb(f"yfb{i}", [128, 1024], F32) for i in range(3)]
            fin = sb("fin", [128, 648], F32)
            P.dma("sp", fin[:], rowp[:, 48:696].partition_broadcast(128), W=[fin])
            u = sb("u", [128, 1024], F32)
            usq = sb("usq", [128, 1024], F32)
            ss = sb("ss", [128, 8], F32)
            yfin = sb("yfin", [128, 1024], BF16)
            yT = [sb(f"yT{i}", [128, 8, 128], BF16) for i in range(2)]

        def load(ci):
            ti = order[ci]
            cm, tm, sm = cmb[ci % 3], tmb[ci % 3], smb[ci % 3]
            P.dma("sp", cm[:].rearrange("p c l -> p (c l)"), CMs[ti], W=[cm])
            P.dma("sp", tm[:], TMs[ti * 128:(ti + 1) * 128, :], W=[tm])
            P.dma("sp", sm[:], SMs[ti * 128:(ti + 1) * 128, :], W=[sm])
            if d == 1 and ti >= NCT:
                yf = yfb[ci % 3]
                P.dma("sp", yf[:], YFs[(ti - NCT) * 128:(ti - NCT + 1) * 128, :], W=[yf])

        def prep(ci):
            cm, tm, sm = cmb[ci % 3], tmb[ci % 3], smb[ci % 3]
            i2 = ci % 2
            CT, BT = cm[:, 0, :], cm[:, 1, :]
            la = sm[:, 16 + 8 * d:24 + 8 * d]
            dtd = sm[:, 8 * d:8 * d + 8]
            lag = sm[:, 32 + 4 * d:36 + 4 * d]
            beta = sm[:, 40 + 4 * d:44 + 4 * d]
            nbeta = sm[:, 48 + 4 * d:52 + 4 * d]
            if PARTS & 4:
                return
            P.pool(lambda e: e.tensor_tensor(out=rla[:], in0=bc(la, [128, 8, 128], 2), in1=bc(Tm, [128, 8, 128], 1), op=ALU.mult),
                   R=[sm, cst], W=[rla])
            for hf in range(2):
                pd = pb()
                P.pe(lambda e, pd=pd, hf=hf: e.matmul(pd[:], lhsT=Um, rhs=rla[:, 4 * hf:4 * hf + 4, :].rearrange("p h l -> p (h l)"),
                                                      start=True, stop=True), R=[cst, rla], W=[pd])
                P.act(lambda e, pd=pd, hf=hf: e.activation(out=ET[:, 4 * hf:4 * hf + 4, :].rearrange("p h l -> p (h l)"), in_=pd[:], func=AF.Exp),
                      R=[pd], W=[ET])
            pg = pb()
            P.pe(lambda e: e.matmul(pg[:, 0:128], lhsT=BT, rhs=CT, start=True, stop=True), R=[cm], W=[pg])
            P.pe(lambda e: e.matmul(pg[:, 128:136], lhsT=Tm, rhs=la, start=True, stop=True), R=[cst, sm], W=[pg])
            P.pe(lambda e: e.matmul(pg[:, 136:144], lhsT=Um, rhs=la, start=True, stop=True), R=[cst, sm], W=[pg])
            P.pe(lambda e: e.matmul(pg[:, 144:152], lhsT=ones, rhs=la, start=True, stop=True), R=[cst, sm], W=[pg])
            P.dve(lambda e: e.tensor_tensor(out=gmt[:], in0=pg[:, 0:128], in1=Tm, op=ALU.mult), R=[pg, cst], W=[gmt])
            ea_ = ea[i2]
            P.act(lambda e: e.activation(out=ea_[:], in_=pg[:, 128:152], func=AF.Exp), R=[pg], W=[ea_])
            att_, xdt_, xdw_ = att[i2], xdt[i2], xdw[i2]
            P.dve(lambda e: e.tensor_tensor(out=att_[:], in0=ET[:], in1=bc(gmt[:], [128, 8, 128], 1), op=ALU.mult),
                  R=[ET, gmt], W=[att_])
            P.pool(lambda e: e.tensor_tensor(out=xdt_[:], in0=tm[:, 0:512].rearrange("p (h q) -> p h q", h=8),
                                             in1=bc(dtd, [128, 8, 64], 2), op=ALU.mult), R=[tm, sm], W=[xdt_])
            P.dve(lambda e: e.tensor_tensor(out=xdw_[:], in0=xdt_[:], in1=bc(ea_[:, 8:16], [128, 8, 64], 2), op=ALU.mult),
                  R=[xdt_, ea_], W=[xdw_])
            if PARTS & 8:
                return
            P.pool(lambda e: e.tensor_tensor(out=rlu[:], in0=bc(lag, [128, 4, 128], 2), in1=bc(Um, [128, 4, 128], 1), op=ALU.mult),
                   R=[sm, cst], W=[rlu])
            pdg = pb()
            P.pe(lambda e: e.matmul(pdg[:], lhsT=Tm, rhs=rlu[:].rearrange("p h l -> p (h l)"), start=True, stop=True),
                 R=[cst, rlu], W=[pdg])
            P.act(lambda e: e.activation(out=E[:].rearrange("p h l -> p (h l)"), in_=pdg[:], func=AF.Exp), R=[pdg], W=[E])
            ps2 = pb()
            P.pe(lambda e: e.matmul(ps2[:, 0:4], lhsT=Tm, rhs=lag, start=True, stop=True), R=[cst, sm], W=[ps2])
            P.pe(lambda e: e.matmul(ps2[:, 4:8], lhsT=Um, rhs=lag, start=True, stop=True), R=[cst, sm], W=[ps2])
            P.pe(lambda e: e.matmul(ps2[:, 8:12], lhsT=ones, rhs=lag, start=True, stop=True), R=[cst, sm], W=[ps2])
            eg_, eb_ = eg[i2], eb[i2]
            P.act(lambda e: e.activation(out=eg_[:], in_=ps2[:, 0:12], func=AF.Exp), R=[ps2], W=[eg_])
            P.dve(lambda e: e.tensor_tensor(out=Es[:], in0=E[:], in1=bc(Um, [128, 4, 128], 1), op=ALU.mult), R=[E, cst], W=[Es])
            P.dve(lambda e: e.tensor_tensor(out=Es[:], in0=Es[:], in1=bc(nbeta, [128, 4, 128], 2), op=ALU.mult), R=[Es, sm], W=[Es])
            P.pool(lambda e: e.tensor_tensor(out=Ei[:], in0=E[:], in1=bc(TmT, [128, 4, 128], 1), op=ALU.mult), R=[E, cst], W=[Ei])
            pkk, pqk = pb(), pb()
            for h in range(4):
                P.pe(lambda e, h=h: e.matmul(pkk[:, h * 128:(h + 1) * 128], lhsT=cm[:, 2 + h, :], rhs=cm[:, 2 + h, :], start=True, stop=True),
                     R=[cm], W=[pkk])
            for h in range(4):
                P.pe(lambda e, h=h: e.matmul(pqk[:, h * 128:(h + 1) * 128], lhsT=cm[:, 6 + h, :], rhs=cm[:, 2 + h, :], start=True, stop=True),
                     R=[cm], W=[pqk])
            BD16 = cst[:, 6 * 128:7 * 128]
            Mb = [cst[:, (7 + i + 3 * d) * 128:(8 + i + 3 * d) * 128] for i in range(3)]
            P.dve(lambda e: e.tensor_tensor(out=Afull[:].rearrange("p h l -> p (h l)"), in0=pkk[:], in1=Es[:].rearrange("p h l -> p (h l)"), op=ALU.mult),
                  R=[pkk, Es], W=[Afull])
            P.dve(lambda e: e.tensor_tensor(out=qk[:].rearrange("p h l -> p (h l)"), in0=pqk[:], in1=Ei[:].rearrange("p h l -> p (h l)"), op=ALU.mult),
                  R=[pqk, Ei], W=[qk])
            pt = pb()
            ptb = pt.ap[:].bitcast(BF16)
            for h in range(4):
                P.pe(lambda e, h=h: e.transpose(out=ptb[:, h * 128:(h + 1) * 128], in_=Afull[:, h, :], identity=identb[:]), R=[Afull, identb], W=[pt])
            for h in range(4):
                P.pe(lambda e, h=h: e.transpose(out=ptb[:, 512 + h * 128:512 + (h + 1) * 128], in_=qk[:, h, :], identity=identb[:]), R=[qk, identb], W=[pt])
            qkT_ = qkT[i2]
            P.act(lambda e: e.copy(out=Nfull[:].rearrange("p h l -> p (h l)"), in_=ptb[:, 0:512]), R=[pt], W=[Nfull])
            P.dve(lambda e: e.tensor_copy(out=qkT_[:].rearrange("p h l -> p (h l)"), in_=ptb[:, 512:1024]), R=[pt], W=[qkT_])
            A, N, Pc = Ak[0], Nk[0], Pk[0]
            P.dve(lambda e, A=A: e.tensor_tensor(out=A[:], in0=Afull[:], in1=bc(BD16, [128, 4, 128], 1), op=ALU.mult), R=[Afull, cst], W=[A])
            P.dve(lambda e, N=N: e.tensor_tensor(out=N[:], in0=Nfull[:], in1=bc(BD16, [128, 4, 128], 1), op=ALU.mult), R=[Nfull, cst], W=[N])
            P.dve(lambda e, N=N, Pc=Pc: e.tensor_tensor(out=Pc[:], in0=N[:], in1=bc(identb[:], [128, 4, 128], 1), op=ALU.add), R=[N, identb], W=[Pc])
            for k in range(1, 4):
                A2, N2 = Ak[k % 2], Nk[k % 2]
                P2 = Pk[k % 2]
                pa = pb()
                for h in range(4):
                    P.pe(lambda e, h=h, N=N, A=A, pa=pa: e.matmul(pa[:, h * 128:(h + 1) * 128], lhsT=N[:, h, :], rhs=A[:, h, :], start=True, stop=True),
                         R=[N, A], W=[pa])
                P.act(lambda e, A2=A2, pa=pa: e.copy(out=A2[:].rearrange("p h l -> p (h l)"), in_=pa[:]), R=[pa], W=[A2])
                if k <= 2:
                    pn = pb()
                    for h in range(4):
                        P.pe(lambda e, h=h, N=N, A=A, pn=pn: e.matmul(pn[:, h * 128:(h + 1) * 128], lhsT=A[:, h, :], rhs=N[:, h, :], start=True, stop=True),
                             R=[N, A], W=[pn])
                    P.act(lambda e, N2=N2, pn=pn: e.copy(out=N2[:].rearrange("p h l -> p (h l)"), in_=pn[:]), R=[pn], W=[N2])
                pm = pb()
                for h in range(4):
                    P.pe(lambda e, h=h, A2=A2, Pc=Pc, pm=pm: e.matmul(pm[:, h * 128:(h + 1) * 128], lhsT=A2[:, h, :], rhs=Pc[:, h, :], start=True, stop=True),
                         R=[A2, Pc], W=[pm])
                P.dve(lambda e, P2=P2, Pc=Pc, pm=pm: e.tensor_tensor(out=P2[:].rearrange("p h l -> p (h l)"), in0=pm[:],
                                                                     in1=Pc[:].rearrange("p h l -> p (h l)"), op=ALU.add), R=[pm, Pc], W=[P2])
                A, N, Pc = A2, N2, P2
            Wc = Pc
            pt2 = pb()
            pt2b = pt2.ap[:].bitcast(BF16)
            for h in range(4):
                P.pe(lambda e, h=h, Wc=Wc: e.transpose(out=pt2b[:, h * 128:(h + 1) * 128], in_=Wc[:, h, :], identity=identb[:]), R=[Wc, identb], W=[pt2])
            Dc = Dk[0]
            P.act(lambda e, Dc=Dc: e.copy(out=Dc[:].rearrange("p h l -> p (h l)"), in_=pt2b[:, 0:512]), R=[pt2], W=[Dc])
            for li in range(3):
                last = li == 2
                py_ = pb()
                for h in range(4):
                    P.pe(lambda e, h=h, Dc=Dc, py_=py_: e.matmul(py_[:, h * 128:(h + 1) * 128], lhsT=Nfull[:, h, :], rhs=Dc[:, h, :], start=True, stop=True),
                         R=[Nfull, Dc], W=[py_])
                P.dve(lambda e, py_=py_, li=li: e.tensor_tensor(out=Ym[:], in0=py_[:].rearrange("p (h l) -> p h l", h=4), in1=bc(Mb[li], [128, 4, 128], 1), op=ALU.mult),
                      R=[py_, cst], W=[Ym])
                if not last:
                    pz = pb()
                    for h in range(4):
                        P.pe(lambda e, h=h, Wc=Wc, pz=pz: e.matmul(pz[:, h * 128:(h + 1) * 128], lhsT=Wc[:, h, :], rhs=Ym[:, h, :], start=True, stop=True),
                             R=[Wc, Ym], W=[pz])
                    D2 = Dk[(li + 1) % 2]
                    P.dve(lambda e, D2=D2, Dc=Dc, pz=pz: e.tensor_tensor(out=D2[:].rearrange("p h l -> p (h l)"), in0=pz[:],
                                                                         in1=Dc[:].rearrange("p h l -> p (h l)"), op=ALU.add), R=[pz, Dc], W=[D2])
                pzt = pb()
                for h in range(4):
                    P.pe(lambda e, h=h, Wc=Wc, pzt=pzt: e.matmul(pzt[:, h * 128:(h + 1) * 128], lhsT=Ym[:, h, :], rhs=Wc[:, h, :], start=True, stop=True),
                         R=[Wc, Ym], W=[pzt])
                W2 = Pf[i2] if last else Wk[li % 2]
                P.dve(lambda e, W2=W2, Wc=Wc, pzt=pzt: e.tensor_tensor(out=W2[:].rearrange("p h l -> p (h l)"), in0=pzt[:],
                                                                        in1=Wc[:].rearrange("p h l -> p (h l)"), op=ALU.add), R=[pzt, Wc], W=[W2])
                Wc = W2
                if not last:
                    Dc = D2
            vb_, kd_ = vb[i2], kd[i2]
            P.pool(lambda e: e.tensor_tensor(out=vb_[:], in0=tm[:, 1152:1664].rearrange("p (h v) -> p h v", h=4), in1=bc(beta, [128, 4, 128], 2), op=ALU.mult),
                   R=[tm, sm], W=[vb_])
            P.dve(lambda e: e.tensor_tensor(out=eb_[:], in0=eg_[:, 0:4], in1=beta, op=ALU.mult), R=[eg_, sm], W=[eb_])
            P.pool(lambda e: e.tensor_tensor(out=kd_[:], in0=tm[:, 640:1152].rearrange("p (h v) -> p h v", h=4), in1=bc(eg_[:, 4:8], [128, 4, 128], 2), op=ALU.mult),
                   R=[tm, eg_], W=[kd_])

        def seq(ci):
            ti = order[ci]
            lat = ti >= NCT
            cm, tm, sm = cmb[ci % 3], tmb[ci % 3], smb[ci % 3]
            i2 = ci % 2
            CT = cm[:, 0, :]
            ea_, att_, xdt_, xdw_ = ea[i2], att[i2], xdt[i2], xdw[i2]
            eg_, eb_, qkT_, vb_, kd_, Pf_ = eg[i2], eb[i2], qkT[i2], vb[i2], kd[i2], Pf[i2]
            yo_ = yo[ci % 2]
            for h in range(4):
                P.pe(lambda e, h=h: e.matmul(bKS[:, h * 128:(h + 1) * 128], lhsT=cm[:, 2 + h, :], rhs=Sb[:, h, :], start=True, stop=True),
                     R=[cm, Sb], W=[bKS])
            P.dve(lambda e: e.tensor_tensor(out=tks[:], in0=bKS[:].rearrange("p (h v) -> p h v", h=4), in1=bc(eb_[:], [128, 4, 128], 2), op=ALU.mult),
                  R=[bKS, eb_], W=[tks])
            P.dve(lambda e: e.tensor_tensor(out=rv[:], in0=vb_[:], in1=tks[:], op=ALU.subtract), R=[vb_, tks], W=[rv])
            if lat:
                pi = ob()
                P.pe(lambda e: e.matmul(pi[:], lhsT=CT, rhs=STb[:], start=True, stop=True), R=[cm, STb], W=[pi])
                pqs = ob()
                for h in range(4):
                    P.pe(lambda e, h=h: e.matmul(pqs[:, h * 128:(h + 1) * 128], lhsT=cm[:, 6 + h, :], rhs=Sb[:, h, :], start=True, stop=True),
                         R=[cm, Sb], W=[pqs])
            P.pe(lambda e: e.matmul(bST[:], lhsT=tm[:, 512:640], rhs=xdw_[:].rearrange("p h q -> p (h q)"), start=True, stop=True),
                 R=[tm, xdw_], W=[bST])
            if lat:
                P.dve(lambda e: e.tensor_tensor(out=t1[:], in0=pi[:].rearrange("p (h q) -> p h q", h=8), in1=bc(ea_[:, 0:8], [128, 8, 64], 2), op=ALU.mult),
                      R=[pi, ea_], W=[t1])
                P.dve(lambda e: e.tensor_tensor(out=t3[:], in0=pqs[:].rearrange("p (h v) -> p h v", h=4), in1=bc(eg_[:, 0:4], [128, 4, 128], 2), op=ALU.mult),
                      R=[pqs, eg_], W=[t3])
            P.dve(lambda e: e.tensor_tensor(out=ST[:], in0=ST[:], in1=bc(ea_[:, 16:24], [128, 8, 64], 2), op=ALU.mult), R=[ST, ea_], W=[ST])
            P.dve(lambda e: e.tensor_tensor(out=ST[:].rearrange("p h q -> p (h q)"), in0=ST[:].rearrange("p h q -> p (h q)"), in1=bST[:], op=ALU.add),
                  R=[ST, bST], W=[ST])
            P.act(lambda e: e.copy(out=STb[:], in_=ST[:].rearrange("p h q -> p (h q)")), R=[ST], W=[STb])
            for h in range(4):
                P.pe(lambda e, h=h: e.matmul(bKS[:, h * 128:(h + 1) * 128], lhsT=Pf_[:, h, :], rhs=rv[:, h, :], start=True, stop=True),
                     R=[Pf_, rv], W=[bKS])
            P.act(lambda e: e.copy(out=vnb[:].rearrange("p h v -> p (h v)"), in_=bKS[:]), R=[bKS], W=[vnb])
            for h in range(4):
                P.pe(lambda e, h=h: e.matmul(bSU[:, h * 128:(h + 1) * 128], lhsT=kd_[:, h, :], rhs=vnb[:, h, :], start=True, stop=True),
                     R=[kd_, vnb], W=[bSU])
            P.dve(lambda e: e.tensor_tensor(out=S[:], in0=S[:], in1=bc(eg_[:, 8:12], [128, 4, 128], 2), op=ALU.mult), R=[S, eg_], W=[S])
            P.dve(lambda e: e.tensor_tensor(out=S[:].rearrange("p h v -> p (h v)"), in0=S[:].rearrange("p h v -> p (h v)"), in1=bSU[:], op=ALU.add),
                  R=[S, bSU], W=[S])
            P.act(lambda e: e.copy(out=Sb[:].rearrange("p h v -> p (h v)"), in_=S[:].rearrange("p h v -> p (h v)")), R=[S], W=[Sb])
            if not lat:
                return
            py = ob()
            for h in range(8):
                P.pe(lambda e, h=h: e.matmul(py[:, h * 64:(h + 1) * 64], lhsT=att_[:, h, :], rhs=xdt_[:, h, :], start=True, stop=True),
                     R=[att_, xdt_], W=[py])
            P.dve(lambda e: e.tensor_tensor(out=yo_[:, 0:512], in0=t1[:].rearrange("p h q -> p (h q)"), in1=py[:], op=ALU.add), R=[t1, py], W=[yo_])
            pqv = ob()
            for h in range(4):
                P.pe(lambda e, h=h: e.matmul(pqv[:, h * 128:(h + 1) * 128], lhsT=qkT_[:, h, :], rhs=vnb[:, h, :], start=True, stop=True),
                     R=[qkT_, vnb], W=[pqv])
            P.dve(lambda e: e.tensor_tensor(out=yo_[:, 512:1024], in0=t3[:].rearrange("p h v -> p (h v)"), in1=pqv[:], op=ALU.add), R=[t3, pqv], W=[yo_])
            li = ti - NCT
            if d == 0:
                P.dma("sp", YFs[li * 128:(li + 1) * 128, :], yo_[:], R=[yo_])
                return
            yf = yfb[ci % 3]
            dsk, nws, nwg = fin[:, 0:8], fin[:, 8:520], fin[:, 520:648]
            P.dve(lambda e: e.tensor_tensor(out=u[:], in0=yo_[:], in1=yf[:], op=ALU.add), R=[yo_, yf], W=[u])
            P.pool(lambda e: e.tensor_tensor(out=usq[:, 0:512].rearrange("p (h q) -> p h q", h=8), in0=tm[:, 0:512].rearrange("p (h q) -> p h q", h=8),
                                             in1=bc(dsk, [128, 8, 64], 2), op=ALU.mult), R=[tm, fin], W=[usq])
            P.dve(lambda e: e.tensor_tensor(out=u[:, 0:512], in0=u[:, 0:512], in1=usq[:, 0:512], op=ALU.add), R=[u, usq], W=[u])
            P.dve(lambda e: e.tensor_tensor(out=u[:, 0:512], in0=u[:, 0:512], in1=tm[:, 1664:2176], op=ALU.mult), R=[u, tm], W=[u])
            P.pool(lambda e: e.memset(ss[:], 0.0), W=[ss])
            P.act(lambda e: e.activation(out=usq[:, 0:512], in_=u[:, 0:512], func=AF.Square, accum_out=ss[:, 0:1]), R=[u, ss], W=[usq, ss])
            for h in range(4):
                P.act(lambda e, h=h: e.activation(out=usq[:, 512 + h * 128:512 + (h + 1) * 128], in_=u[:, 512 + h * 128:512 + (h + 1) * 128],
                                                  func=AF.Square, accum_out=ss[:, 1 + h:2 + h]), R=[u, ss], W=[usq, ss])
            P.act(lambda e: e.activation(out=ss[:, 0:1], in_=ss[:, 0:1], func=AF.Sqrt, bias=epsT[:, 0:1], scale=1.0 / 512), R=[ss, epsT], W=[ss])
            P.act(lambda e: e.activation(out=ss[:, 1:5], in_=ss[:, 1:5], func=AF.Sqrt, bias=epsT[:, 0:1], scale=1.0 / 128), R=[ss, epsT], W=[ss])
            P.dve(lambda e: e.reciprocal(out=ss[:, 0:5], in_=ss[:, 0:5]), R=[ss], W=[ss])
            P.dve(lambda e: e.scalar_tensor_tensor(out=yfin[:, 0:512], in0=u[:, 0:512], scalar=ss[:, 0:1], in1=nws, op0=ALU.mult, op1=ALU.mult),
                  R=[u, ss, fin], W=[yfin])
            u4 = u[:, 512:1024].rearrange("p (h v) -> p h v", h=4)
            P.dve(lambda e: e.tensor_tensor(out=u4, in0=u4, in1=bc(ss[:, 1:5], [128, 4, 128], 2), op=ALU.mult), R=[u, ss], W=[u])
            P.dve(lambda e: e.tensor_tensor(out=u4, in0=u4, in1=bc(nwg, [128, 4, 128], 1), op=ALU.mult), R=[u, fin], W=[u])
            P.dve(lambda e: e.tensor_tensor(out=yfin[:, 512:1024], in0=u[:, 512:1024], in1=tm[:, 2176:2688], op=ALU.mult), R=[u, tm], W=[yfin])
            pt = ob()
            ptb = pt.ap[:].bitcast(BF16)
            for c8 in range(8):
                P.pe(lambda e, c8=c8: e.transpose(out=ptb[:, c8 * 128:(c8 + 1) * 128], in_=yfin[:, c8 * 128:(c8 + 1) * 128], identity=identb[:]),
                     R=[yfin, identb], W=[pt])
            yT_ = yT[ci % 2]
            P.act(lambda e: e.copy(out=yT_[:].rearrange("p c l -> p (c l)"), in_=ptb[:, :]), R=[pt], W=[yT_])
            hc, tl = li // NTH, li % NTH
            P.dma("sp", YY.ap()[hc, tl], yT_[:].rearrange("p c l -> p (c l)"), R=[yT_])

        n = len(order)
        if L["debug"] and d == 0:
            n = min(DBG_NCH, n)
        jobs = list(L["conv_jobs"]) if d == 0 else []
        if jobs:
            wst = [sb(f"wst{i}", [128, 6144], F32) for i in range(2)]
            wsb = [sb(f"wsb{i}", [128, 6144], BF16) for i in range(2)]
        jcnt = [0]

        def run_job():
            if not jobs:
                return
            src_aps, dst_ap, nel = jobs.pop(0)
            s_, b_ = wst[jcnt[0] % 2], wsb[jcnt[0] % 2]
            jcnt[0] += 1
            off = 0
            for sap, shape in src_aps:
                sz = int(np.prod(shape))
                view = s_[:, off:off + sz].rearrange("p (a b) -> p a b", a=shape[0])
                P.dma("sp", view, sap, W=[s_])
                off += sz
            P.act(lambda e: e.copy(out=b_[:, 0:nel], in_=s_[:, 0:nel]), R=[s_], W=[b_])
            P.dma("sp", dst_ap, b_[:, 0:nel], R=[b_])

        load(0)
        if n > 1:
            load(1)
        if PARTS & 1:
            prep(0)
        for ci in range(n):
            if ci + 2 < n:
                load(ci + 2)
            if ci + 1 < n and PARTS & 1:
                prep(ci + 1)
            if PARTS & 2:
                seq(ci)
            run_job()
            if ci == n - 1:
                while jobs:
                    run_job()
        if L["debug"] and d == 0:
            loc = locals()
            for nm_, ap_ in L["dbg"].items():
                if not nm_.startswith("s_"):
                    continue
                key = nm_[2:]
                t_ = loc[key] if key in loc else loc[key[:-1]][int(key[-1])]
                src = t_[:] if len(t_.ap.shape) == 2 else t_[:].rearrange("p h l -> p (h l)")
                L["final_ops"].append(P.dma("sp", ap_, src, R=[t_]))


def stage5(P, nc, L):
    HALF, NTH = L["HALF"], L["NTH"]
    xhalf, yout, MINE, PART, WO, WGU, WD, rowp = L["xhalf"], L["yout"], L["MINE"], L["PART"], L["WO"], L["WGU"], L["WD"], L["rowp"]
    g12row, modT, cst, psb, epsT, final_ops = L["g12row"], L["modT"], L["cst"], L["psb"], L["epsT"], L["final_ops"]
    ident = cst[:, 0:128]
    with ExitStack() as st:
        lnp = P.sb(st, "lnp", [128, 4 * D], F32)
        P.dma("sp", lnp[:], rowp[:, 696:4792].partition_broadcast(128), W=[lnp])
        xt = [P.sb(st, f"x5_{i}", [128, 4, D], F32) for i in range(2)]
        yTs = [P.sb(st, f"yT5_{i}", [128, 2, 4, 1024], BF16) for i in range(2)]
        ysem = [[T(f"yTa{i}", None), T(f"yTb{i}", None)] for i in range(2)]
        h2T = P.sb(st, "h2T", [128, 8, 512], BF16)
        actT = P.sb(st, "actT", [128, 22, 512], BF16)
        wbuf = [P.sb(st, f"wbuf{i}", [128, 8192], BF16) for i in range(3)]
        tmp = [P.sb(st, f"tmp5_{i}", [128, 512], F32) for i in range(2)]
        sgt = [P.sb(st, f"sgt{i}", [128, 512], F32) for i in range(2)]
        junk = P.sb(st, "junk", [128, D], F32)
        stat = [P.sb(st, f"stat{i}", [128, 8], F32) for i in range(2)]
        wcnt = [0]

        def wload(src, n):
            wb = wbuf[wcnt[0] % 3]
            wcnt[0] += 1
            P.dma("sp", wb[:, 0:n], src, W=[wb])
            return wb

        def resid(ps, x_, a, dh, gi, k):
            t = tmp[k % 2]
            P.dve(lambda e: e.tensor_tensor(out=t[:], in0=ps[:], in1=g12row[:, gi * D + dh * 512: gi * D + dh * 512 + 512], op=ALU.mult),
                  R=[ps, g12row], W=[t])
            xs_ = x_[:, a, dh * 512:(dh + 1) * 512]
            P.dve(lambda e: e.scalar_tensor_tensor(out=xs_, in0=xs_, scalar=ALPHA, in1=t[:], op0=ALU.mult, op1=ALU.add),
                  R=[x_, t], W=[x_])

        def layer_norm(x_, a, li, k):
            sx = stat[k % 2]
            xa = x_[:, a, :]
            g_, b_ = lnp[:, (2 * li) * D:(2 * li + 1) * D], lnp[:, (2 * li + 1) * D:(2 * li + 2) * D]
            P.pool(lambda e: e.memset(sx[:], 0.0), W=[sx])
            P.act(lambda e: e.activation(out=junk[:], in_=xa, func=AF.Identity, accum_out=sx[:, 0:1]), R=[x_, sx], W=[junk, sx])
            P.act(lambda e: e.activation(out=junk[:], in_=xa, func=AF.Square, accum_out=sx[:, 1:2]), R=[x_, sx], W=[junk, sx])
            P.dve(lambda e: e.tensor_scalar_mul(out=sx[:, 2:3], in0=sx[:, 0:1], scalar1=1.0 / D), R=[sx], W=[sx])
            P.dve(lambda e: e.tensor_tensor(out=sx[:, 3:4], in0=sx[:, 2:3], in1=sx[:, 2:3], op=ALU.mult), R=[sx], W=[sx])
            P.dve(lambda e: e.scalar_tensor_tensor(out=sx[:, 4:5], in0=sx[:, 1:2], scalar=1.0 / D, in1=sx[:, 3:4], op0=ALU.mult, op1=ALU.subtract),
                  R=[sx], W=[sx])
            P.act(lambda e: e.activation(out=sx[:, 5:6], in_=sx[:, 4:5], func=AF.Sqrt, bias=epsT[:, 1:2], scale=1.0), R=[sx, epsT], W=[sx])
            P.dve(lambda e: e.reciprocal(out=sx[:, 5:6], in_=sx[:, 5:6]), R=[sx], W=[sx])
            P.dve(lambda e: e.scalar_tensor_tensor(out=sx[:, 6:7], in0=sx[:, 2:3], scalar=-1.0, in1=sx[:, 5:6], op0=ALU.mult, op1=ALU.mult),
                  R=[sx], W=[sx])
            P.act(lambda e: e.activation(out=xa, in_=xa, func=AF.Identity, bias=sx[:, 6:7], scale=sx[:, 5:6]), R=[x_, sx], W=[x_])
            P.dve(lambda e: e.tensor_tensor(out=xa, in0=xa, in1=g_, op=ALU.mult), R=[x_, lnp], W=[x_])
            P.dve(lambda e: e.tensor_tensor(out=xa, in0=xa, in1=b_, op=ALU.add), R=[x_, lnp], W=[x_])

        nblk = HALF // 512
        for blk in range(nblk):
            x_ = xt[blk % 2]
            yT = yTs[blk % 2]

            def issue_loads(b):
                xb, yb, tlb = xt[b % 2], yTs[b % 2], b * 4
                P.dma("sp", xb[:], xhalf[b * 512:(b + 1) * 512, :].rearrange("(a p) d -> p a d", p=128), W=[xb])
                P.dma("sp", yb[:, 0], MINE.ap()[tlb:tlb + 4].rearrange("t p f -> p t f"), W=[yb], semt=ysem[b % 2][0])
                P.dma("sp", yb[:, 1], PART.ap()[tlb:tlb + 4].rearrange("t p f -> p t f"), W=[yb], semt=ysem[b % 2][1])

            if blk == 0:
                issue_loads(0)
            k = 0
            for dh in range(2):
                wb = wload(WO[dh], 8192)
                wv = wb[:, 0:8192].rearrange("p (k n) -> p k n", k=16)
                for a in range(4):
                    ps = psb[k % 2]
                    for kc in range(16):
                        src, cc = kc // 8, kc % 8
                        P.pe(lambda e, ps=ps, kc=kc, src=src, cc=cc, a=a, wv=wv, yT=yT: e.matmul(
                            ps[:], lhsT=yT[:, src, a, cc * 128:(cc + 1) * 128], rhs=wv[:, kc, :], start=(kc == 0), stop=(kc == 15)),
                            R=[yT, wb], W=[ps])
                    resid(ps, x_, a, dh, 0, k)
                    k += 1
            if blk + 1 < nblk:
                issue_loads(blk + 1)
            for a in range(4):
                layer_norm(x_, a, 0, a)
            PX = psb[2]
            for fc in range(8):
                for a in range(4):
                    P.pe(lambda e, a=a, fc=fc, x_=x_: e.transpose(out=PX[:, a * 128:(a + 1) * 128], in_=x_[:, a, fc * 128:(fc + 1) * 128], identity=ident),
                         R=[x_, cst], W=[PX])
                P.act(lambda e, fc=fc: e.activation(out=h2T[:, fc, :], in_=PX[:], func=AF.Identity,
                                                    bias=modT[:, 32 + fc:33 + fc], scale=modT[:, 40 + fc:41 + fc]), R=[PX, modT], W=[h2T])
            for b11 in range(11):
                wb = wload(WGU[b11], 4096)
                wv = wb[:, 0:4096].rearrange("p (g k n) -> p g k n", g=2, k=8)
                for jj in range(2):
                    j = b11 * 2 + jj
                    pg, pu = psb[4 + j % 2], psb[6 + j % 2]
                    for kc in range(8):
                        P.pe(lambda e, pg=pg, kc=kc, jj=jj, wv=wv: e.matmul(pg[:], lhsT=wv[:, 0, kc, jj * 128:(jj + 1) * 128], rhs=h2T[:, kc, :],
                                                                            start=(kc == 0), stop=(kc == 7)), R=[wb, h2T], W=[pg])
                    for kc in range(8):
                        P.pe(lambda e, pu=pu, kc=kc, jj=jj, wv=wv: e.matmul(pu[:], lhsT=wv[:, 1, kc, jj * 128:(jj + 1) * 128], rhs=h2T[:, kc, :],
                                                                            start=(kc == 0), stop=(kc == 7)), R=[wb, h2T], W=[pu])
                    sg = sgt[j % 2]
                    P.act(lambda e, sg=sg, pg=pg: e.activation(out=sg[:], in_=pg[:], func=AF.Silu), R=[pg], W=[sg])
                    P.dve(lambda e, sg=sg, pu=pu, j=j: e.tensor_tensor(out=actT[:, j, :], in0=sg[:], in1=pu[:], op=ALU.mult), R=[sg, pu], W=[actT])
            for dh in range(2):
                accs = [psb[0], psb[1], psb[2], psb[3]]
                for jh in range(2):
                    wb = wload(WD[dh * 2 + jh], 5632)
                    wv = wb[:, 0:5632].rearrange("p (j n) -> p j n", j=11)
                    for a in range(4):
                        for j11 in range(11):
                            j = jh * 11 + j11
                            P.pe(lambda e, a=a, j=j, j11=j11, wv=wv, accs=accs: e.matmul(
                                accs[a][:], lhsT=actT[:, j, a * 128:(a + 1) * 128], rhs=wv[:, j11, :], start=(j == 0), stop=(j == 21)),
                                R=[actT, wb], W=[accs[a]])
                for a in range(4):
                    resid(accs[a], x_, a, dh, 1, a)
            for a in range(4):
                layer_norm(x_, a, 1, a)
            final_ops.append(P.dma("sp", yout[blk * 512:(blk + 1) * 512, :].rearrange("(a p) d -> p a d", p=128), x_[:], R=[x_]))


def host_consts():
    j = np.arange(128)[:, None]
    l = np.arange(128)[None, :]
    mats = [np.eye(128), (j <= l), (j > l), (j >= l), (j < l), np.ones((128, 128)), (j // 16 == l // 16)]
    for b in (16, 32, 64):
        mats.append((j // (2 * b) == l // (2 * b)) & (j % (2 * b) >= b) & (l % (2 * b) < b))
    for b in (16, 32, 64):
        mats.append(((j // (2 * b) == l // (2 * b)) & (j % (2 * b) >= b) & (l % (2 * b) < b)).T)
    return np.concatenate([m.astype(np.float32) for m in mats], axis=1)


def prep_core(inp, core, SEQ):
    b, h = core // 2, core % 2
    HALF = SEQ // 2
    f = lambda a: np.ascontiguousarray(a, dtype=np.float32)
    x, ctx = inp["x"], inp["ctx"]
    w_in = inp["w_in"][0]
    xs0 = 1024
    B0, C0 = 2048, 2304
    dt0 = 2560
    q0, k0, v0 = 2592, 3616, 4640
    g0 = 5664
    b0, a0 = 6688, 6704
    r = lambda s, n: np.arange(s, s + n)
    cols_cm = np.concatenate([r(xs0 + h * 512, 512), r(B0 + h * 128, 128), r(C0 + h * 128, 128),
                              r(q0 + h * 512, 512), r(k0 + h * 512, 512), r(v0 + h * 512, 512)])
    cols_tm = np.concatenate([r(h * 512, 512), r(g0 + h * 512, 512),
                              r(dt0 + 8 * h, 8), r(dt0 + 16 + 8 * h, 8),
                              r(a0 + 4 * h, 4), r(a0 + 8 + 4 * h, 4),
                              r(b0 + 4 * h, 4), r(b0 + 8 + 4 * h, 4)])
    cws, cwg = inp["conv_w_ssd"][0], inp["conv_w_gdn"][0]
    ssd_cols = np.concatenate([r(h * 512, 512), r(1024 + h * 128, 128), r(1280 + h * 128, 128)])
    gdn_cols = np.concatenate([r(h * 512, 512), r(1024 + h * 512, 512), r(2048 + h * 512, 512)])
    cw = np.concatenate([cws[:, ssd_cols], cwg[:, gdn_cols]], axis=1)
    cwT = cw.T.reshape(NCM, 128, 5).transpose(1, 0, 2).reshape(128, NCM * 5)
    cbT = inp["conv_b_ssd"][0][ssd_cols].reshape(6, 128).T
    rowp = np.concatenate([
        inp["dt_bias_ssd"][0][0, 8 * h:8 * h + 8], inp["dt_bias_ssd"][0][1, 8 * h:8 * h + 8],
        inp["dt_bias_gdn"][0][0, 4 * h:4 * h + 4], inp["dt_bias_gdn"][0][1, 4 * h:4 * h + 4],
        inp["a_log_ssd"][0][0, 8 * h:8 * h + 8], inp["a_log_ssd"][0][1, 8 * h:8 * h + 8],
        inp["a_log_gdn"][0][0, 4 * h:4 * h + 4], inp["a_log_gdn"][0][1, 4 * h:4 * h + 4],
        inp["d_skip_ssd"][0][8 * h:8 * h + 8],
        inp["norm_w_ssd"][0][h * 512:(h + 1) * 512], inp["norm_w_gdn"][0],
        inp["ln1_g"][0], inp["ln1_b"][0], inp["ln2_g"][0], inp["ln2_b"][0]])[None, :]
    cv = np.stack([inp["c"][b], inp["c_ctx"]])
    cvT = cv.reshape(2, 8, 128).transpose(2, 0, 1).reshape(128, 16)
    wo = inp["w_out"][0]
    own = np.concatenate([r(h * 512, 512), r(1024 + h * 512, 512)])
    oth = np.concatenate([r((1 - h) * 512, 512), r(1024 + (1 - h) * 512, 512)])
    return {
        "xin": f(np.concatenate([ctx[b], x[b]], axis=0)),
        "xhalf": f(x[b, h * HALF:(h + 1) * HALF]),
        "cvecT": f(cvT),
        "w_ada": f(inp["w_ada"][0]), "b_ada": f(inp["b_ada"][0][None, :]),
        "w_cm": f(w_in[:, cols_cm]), "w_tm": f(w_in[:, cols_tm]),
        "convw": f(cwT), "convb": f(cbT), "rowp": f(rowp), "consts": host_consts(),
        "w_out": f(wo[np.concatenate([own, oth])]),
        "w_gate": f(inp["w_ffn_gate"][0]), "w_up": f(inp["w_ffn_up"][0]), "w_down": f(inp["w_ffn_down"][0]),
    }


_CACHE = {}


def run(inputs, SEQ, debug=False, stop_after=99, ncores=8):
    key = (SEQ, debug, stop_after)
    if key not in _CACHE:
        _CACHE[key] = build_program(SEQ, debug, stop_after)
    nc, stats = _CACHE[key]
    in_maps = [prep_core(inputs, c, SEQ) for c in range(ncores)]
    res = run_bass_kernel_spmd(nc, in_maps, core_ids=list(range(ncores)))
    return res.results, stats


def kernel(**inputs):
    SEQ = inputs["x"].shape[1]
    B = inputs["x"].shape[0]
    results, _ = run(inputs, SEQ)
    HALF = SEQ // 2
    out = np.empty((B, SEQ, D), np.float32)
    for c in range(8):
        b, h = c // 2, c % 2
        out[b, h * HALF:(h + 1) * HALF] = results[c]["yout"]
    return out
```

```python
import numpy as np
from contextlib import ExitStack
import concourse.bass as bass
import concourse.mybir as mybir
from concourse.bass_utils import run_bass_kernel_spmd

F32 = mybir.dt.float32
BF16 = mybir.dt.bfloat16
AF = mybir.ActivationFunctionType
ALU = mybir.AluOpType
AX = mybir.AxisListType

D = 1024
CTX = 256
GRID_W = 64
DFF = 2816
NCM = 18
NTMC = 1056
TMW = 2688
SMW = 64
ALPHA = 2.0 ** 0.25
LN_EPS = 1e-5
RMS_EPS = 1e-6
EPOCH = 30000
NCST = 13
PARTS = 3
DBG_NCH = 10 ** 6


class T:
    __slots__ = ("name", "ap", "last_w", "readers", "dma_readers", "sem", "cnt", "last_dma", "excl")

    def __init__(self, name, ap):
        self.excl = False
        self.name = name
        self.ap = ap
        self.last_w = None
        self.readers = {}
        self.dma_readers = []
        self.sem = None
        self.cnt = 0
        self.last_dma = None

    def __getitem__(self, k):
        return self.ap[k]


class Op:
    __slots__ = ("eng", "fn", "deps", "needs_inc", "inc_val", "epoch", "is_dma", "sem", "val", "amt")

    def __init__(self, eng, fn, is_dma=False):
        self.eng = eng
        self.fn = fn
        self.deps = []
        self.needs_inc = False
        self.inc_val = 0
        self.epoch = 0
        self.is_dma = is_dma
        self.sem = None
        self.val = 0
        self.amt = 16


class Prog:
    ENGS = ("sp", "act", "pool", "dve", "pe")

    def __init__(self, nc, stack):
        self.nc = nc
        self.stack = stack
        self.ops = {e: [] for e in self.ENGS}
        self.same_engine_sync = {"sp": False, "act": True, "pool": True, "dve": True, "pe": False}
        self.nsem = 0
        self.dma_open = []
        self.pending = {e: [] for e in self.ENGS}
        self.sem_pool = {}

    def sb(self, st, name, shape, dt):
        return T(name, st.enter_context(self.nc.sbuf_tensor(name, list(shape), dt)))

    def ps(self, st, name, shape, dt):
        t = T(name, st.enter_context(self.nc.psum_tensor(name, list(shape), dt)))
        t.excl = True
        return t

    def new_sem(self, name):
        self.nsem += 1
        return self.stack.enter_context(self.nc.semaphore(name))

    def _track(self, op, R, W, extra=()):
        deps = list(extra)
        for t in R:
            if t.last_w is not None:
                deps.append(t.last_w)
            if t.excl:
                deps.extend(o for en, o in t.readers.items() if en != op.eng)
        for t in W:
            if t.last_w is not None:
                deps.append(t.last_w)
            deps.extend(t.readers.values())
            deps.extend(t.dma_readers)
        deps.extend(self.pending[op.eng])
        self.pending[op.eng] = []
        seen = set()
        for d in deps:
            if d is op or id(d) in seen:
                continue
            seen.add(id(d))
            if (not d.is_dma) and d.eng == op.eng and not op.is_dma and not self.same_engine_sync[op.eng]:
                continue
            if not d.is_dma:
                d.needs_inc = True
            op.deps.append(d)
        for t in R:
            if op.is_dma:
                t.dma_readers.append(op)
            else:
                t.readers[op.eng] = op
        for t in W:
            t.last_w = op
            t.readers = {}
            t.dma_readers = []

    def op(self, eng, fn, R=(), W=()):
        o = Op(eng, fn)
        self._track(o, R, W)
        self.ops[eng].append(o)
        return o

    def pe(self, fn, R=(), W=()):
        return self.op("pe", fn, R, W)

    def act(self, fn, R=(), W=()):
        return self.op("act", fn, R, W)

    def dve(self, fn, R=(), W=()):
        return self.op("dve", fn, R, W)

    def pool(self, fn, R=(), W=()):
        return self.op("pool", fn, R, W)

    def dma(self, eng, out_ap=None, in_ap=None, R=(), W=(), semt=None, fn=None, amt=16, **kw):
        if fn is None:
            fn = lambda e: e.dma_start(out=out_ap, in_=in_ap, **kw)
        o = Op(eng, fn, is_dma=True)
        o.amt = amt
        if semt is None:
            semt = (list(W) + list(R))[0]
        if semt.sem is None:
            key = semt.name
            if key not in self.sem_pool:
                self.sem_pool[key] = [self.new_sem("d_" + key), 0]
            semt.sem = self.sem_pool[key]
        semt.sem[1] += amt
        o.sem = semt.sem[0]
        o.val = semt.sem[1]
        extra = [semt.last_dma] if semt.last_dma is not None else []
        semt.last_dma = o
        self._track(o, R, W, extra)
        self.ops[eng].append(o)
        self.dma_open.append(o)
        return o

    def barrier(self):
        deps = list(self.dma_open)
        for e in self.ENGS:
            for o in reversed(self.ops[e]):
                if not o.is_dma:
                    deps.append(o)
                    break
        self.dma_open = []
        for e in self.ENGS:
            self.pending[e] = list(deps)

    def emit(self, final_ops=()):
        nc = self.nc
        esems = {}
        for e in self.ENGS:
            n = 0
            for o in self.ops[e]:
                if o.is_dma or not o.needs_inc:
                    continue
                o.epoch = n // EPOCH
                o.inc_val = n % EPOCH + 1
                n += 1
            nep = (n + EPOCH - 1) // EPOCH
            esems[e] = [self.new_sem(f"s_{e}{i}") for i in range(max(nep, 1))]
        block = self.stack.enter_context(nc.Block())
        deco = {"sp": block.sync, "act": block.scalar, "pool": block.gpsimd, "dve": block.vector, "pe": block.tensor}
        nwaits = {e: 0 for e in self.ENGS}

        def make(eng):
            def body(e):
                seen = {}
                maxep = {}

                def wait_all(deps):
                    need = {}
                    for d in deps:
                        if d.is_dma:
                            key, val, sem = ("d", id(d.sem)), d.val, d.sem
                        else:
                            key, val, sem = (d.eng, d.epoch), d.inc_val, esems[d.eng][d.epoch]
                        if seen.get(key, 0) >= val:
                            continue
                        if key not in need or need[key][0] < val:
                            need[key] = (val, sem)
                    for key, (val, sem) in need.items():
                        if key[0] != "d":
                            if any(k[0] == key[0] and k[1] > key[1] for k in list(seen) + list(need) if k[0] != "d"):
                                continue
                        seen[key] = val
                        e.wait_ge(sem, val)
                        nwaits[eng] += 1

                def wait_for(d):
                    wait_all([d])

                for o in self.ops[eng]:
                    wait_all(o.deps)
                    ins = o.fn(e)
                    if o.is_dma:
                        ins.then_inc(o.sem, o.amt)
                    elif o.needs_inc:
                        ins.then_inc(esems[eng][o.epoch], 1)
                if eng == "sp":
                    for d in final_ops:
                        wait_for(d)
                        e.nop()
            return body

        for eng in self.ENGS:
            if self.ops[eng] or eng == "sp":
                deco[eng](make(eng))
        self.nwaits = nwaits
        return block


def bc(ap, shape, axis):
    return ap.unsqueeze(axis).to_broadcast(list(shape))


def build_program(SEQ, debug=False, stop_after=99):
    TT = CTX + SEQ
    NTT = TT // 128
    NCT = CTX // 128
    NLT = SEQ // 128
    HALF = SEQ // 2
    NTH = HALF // 128
    nc = bass.Bass("TRN2", target_bir_lowering=False)
    dt = nc.dram_tensor
    xin = dt("xin", [TT, D], F32, kind="ExternalInput").ap()
    xhalf = dt("xhalf", [HALF, D], F32, kind="ExternalInput").ap()
    cvecT = dt("cvecT", [128, 16], F32, kind="ExternalInput").ap()
    w_ada = dt("w_ada", [D, 6 * D], F32, kind="ExternalInput").ap()
    b_ada = dt("b_ada", [1, 6 * D], F32, kind="ExternalInput").ap()
    w_cm = dt("w_cm", [D, NCM * 128], F32, kind="ExternalInput").ap()
    w_tm = dt("w_tm", [D, NTMC], F32, kind="ExternalInput").ap()
    convw = dt("convw", [128, NCM * 5], F32, kind="ExternalInput").ap()
    convb = dt("convb", [128, 6], F32, kind="ExternalInput").ap()
    rowp = dt("rowp", [1, 4792], F32, kind="ExternalInput").ap()
    consts = dt("consts", [128, NCST * 128], F32, kind="ExternalInput").ap()
    w_out = dt("w_out", [2 * D, D], F32, kind="ExternalInput").ap()
    w_gate = dt("w_gate", [D, DFF], F32, kind="ExternalInput").ap()
    w_up = dt("w_up", [D, DFF], F32, kind="ExternalInput").ap()
    w_down = dt("w_down", [DFF, D], F32, kind="ExternalInput").ap()
    yout = dt("yout", [HALF, D], F32, kind="ExternalOutput").ap()
    CMs = dt("CMs", [NTT, 128, 10 * 128], BF16).ap()
    TMs = dt("TMs", [TT, TMW], BF16).ap()
    SMs = dt("SMs", [TT, SMW], F32).ap()
    YFs = dt("YFs", [SEQ, 1024], F32).ap()
    YY = dt("YY", [2, NTH, 128, 1024], BF16)
    TPK = min(NTH, 8)
    NCC = NTH // TPK
    ZO = dt("ZO", [NCC, 2, TPK, 128, 1024], BF16)
    ZIN = dt("ZIN", [NTH, 128, 1024], BF16)
    MINE = dt("MINE", [NTH, 128, 1024], BF16)
    PART = dt("PART", [NTH, 128, 1024], BF16)
    WO = dt("WO", [2, 128, 16 * 512], BF16).ap()
    WGU = dt("WGU", [11, 128, 2 * 8 * 256], BF16).ap()
    WD = dt("WD", [4, 128, 11 * 512], BF16).ap()
    dbg = {}
    if debug:
        dbg["modT"] = dt("dbg_modT", [128, 64], F32, kind="ExternalOutput").ap()
        dbg["CM"] = dt("dbg_CM", [NTT, 128, 10 * 128], BF16, kind="ExternalOutput").ap()
        dbg["TM"] = dt("dbg_TM", [TT, TMW], BF16, kind="ExternalOutput").ap()
        dbg["SM"] = dt("dbg_SM", [TT, SMW], F32, kind="ExternalOutput").ap()
        dbg["YF"] = dt("dbg_YF", [SEQ, 1024], F32, kind="ExternalOutput").ap()
        dbg["YY"] = dt("dbg_YY", [2 * NTH * 128, 1024], BF16, kind="ExternalOutput").ap()
        for nm_, dt_ in (("E", F32), ("Es", F32), ("tks", F32), ("S", F32), ("t3", F32), ("vb0", F32)):
            dbg["s_" + nm_] = dt("dbg_s_" + nm_, [128, 512], dt_, kind="ExternalOutput").ap()
        for nm_ in ("Ei", "Ak0", "Ak1", "Nk0", "Nk1", "Pf0", "qkT0", "kd0", "rv", "vnb", "Sb", "qk", "Pk0", "Pk1", "Afull", "Nfull", "Ym", "Dk0", "Dk1", "Wk0", "Wk1"):
            dbg["s_" + nm_] = dt("dbg_s_" + nm_, [128, 512], BF16, kind="ExternalOutput").ap()
        dbg["s_eg0"] = dt("dbg_s_eg0", [128, 12], F32, kind="ExternalOutput").ap()

    with ExitStack() as top:
        P = Prog(nc, top)
        cst = P.sb(top, "cst", [128, NCST * 128], F32)
        identb = P.sb(top, "identb", [128, 128], BF16)
        modT = P.sb(top, "modT", [128, 64], F32)
        g12row = P.sb(top, "g12row", [128, 2 * D], F32)
        psb = [P.ps(top, f"psb{i}", [128, 512], F32) for i in range(8)]
        epsT = P.sb(top, "epsT", [128, 4], F32)
        P.pool(lambda e: e.memset(epsT[:, 0:1], RMS_EPS), W=[epsT])
        P.pool(lambda e: e.memset(epsT[:, 1:2], LN_EPS), W=[epsT])
        ident = cst[:, 0:128]
        Uincl, Lstrict, Lincl, Ustrict, ones = (cst[:, i * 128:(i + 1) * 128] for i in range(1, 6))
        P.dma("sp", cst[:], consts[:, :], W=[cst])
        P.dve(lambda e: e.tensor_copy(out=identb[:], in_=ident), R=[cst], W=[identb])
        final_ops = []

        with ExitStack() as st:
            ccol = P.sb(st, "ccol", [128, 2, 8], F32)
            csil = P.sb(st, "csil", [128, 2, 8], F32)
            crep = P.sb(st, "crep", [128, 2, 8, 128], F32)
            barow = P.sb(st, "barow", [128, 6 * D], F32)
            modrow = P.sb(st, "modrow", [128, 4 * D], F32)
            cmodrow = P.sb(st, "cmodrow", [128, 2 * D], F32)
            wab = [P.sb(st, f"wab{i}", [128, 8, 512], F32) for i in range(2)]
            P.dma("sp", ccol[:].rearrange("p v k -> p (v k)"), cvecT[:, :], W=[ccol])
            P.dma("sp", barow[:], b_ada.partition_broadcast(128), W=[barow])
            P.act(lambda e: e.activation(out=csil[:], in_=ccol[:], func=AF.Silu), R=[ccol], W=[csil])
            P.dve(lambda e: e.tensor_copy(out=crep[:].rearrange("p v k m -> p (v k) m"),
                                          in_=bc(csil[:].rearrange("p v k -> p (v k)"), [128, 16, 128], 2)),
                  R=[csil], W=[crep])
            w_ada_v = w_ada.rearrange("(kc p) n -> p kc n", p=128)
            for nb in range(12):
                wb = wab[nb % 2]
                P.dma("sp", wb[:], w_ada_v[:, :, nb * 512:(nb + 1) * 512], W=[wb])
                pl, pc = psb[(2 * nb) % 8], psb[(2 * nb + 1) % 8]
                for kc in range(8):
                    P.pe(lambda e, kc=kc, wb=wb, pl=pl: e.matmul(pl[:], lhsT=crep[:, 0, kc, :], rhs=wb[:, kc, :],
                                                                 start=(kc == 0), stop=(kc == 7)), R=[crep, wb], W=[pl])
                if nb < 4:
                    for kc in range(8):
                        P.pe(lambda e, kc=kc, wb=wb, pc=pc: e.matmul(pc[:], lhsT=crep[:, 1, kc, :], rhs=wb[:, kc, :],
                                                                     start=(kc == 0), stop=(kc == 7)), R=[crep, wb], W=[pc])
                seg = nb // 2
                half = nb % 2
                bsl = barow[:, nb * 512:(nb + 1) * 512]
                if seg in (0, 1, 3, 4):
                    mi = {0: 0, 1: 1, 3: 2, 4: 3}[seg]
                    dst = modrow[:, mi * D + half * 512: mi * D + half * 512 + 512]
                    P.dve(lambda e, dst=dst, pl=pl, bsl=bsl: e.tensor_tensor(out=dst, in0=pl[:], in1=bsl, op=ALU.add),
                          R=[pl, barow], W=[modrow])
                    if seg in (1, 4):
                        P.dve(lambda e, dst=dst: e.tensor_scalar_add(out=dst, in0=dst, scalar1=1.0), R=[modrow], W=[modrow])
                else:
                    gi = 0 if seg == 2 else 1
                    dst = g12row[:, gi * D + half * 512: gi * D + half * 512 + 512]
                    P.dve(lambda e, dst=dst, pl=pl, bsl=bsl: e.tensor_tensor(out=dst, in0=pl[:], in1=bsl, op=ALU.add),
                          R=[pl, barow], W=[g12row])
                if nb < 4:
                    dst = cmodrow[:, nb * 512:(nb + 1) * 512]
                    P.dve(lambda e, dst=dst, pc=pc, bsl=bsl: e.tensor_tensor(out=dst, in0=pc[:], in1=bsl, op=ALU.add),
                          R=[pc, barow], W=[cmodrow])
                    if nb >= 2:
                        P.dve(lambda e, dst=dst: e.tensor_scalar_add(out=dst, in0=dst, scalar1=1.0), R=[cmodrow], W=[cmodrow])
            srcs = [(modrow, 0), (modrow, 1), (cmodrow, 0), (cmodrow, 1), (modrow, 2), (modrow, 3)]
            for v, (src, si) in enumerate(srcs):
                for g in range(2):
                    pt = psb[(2 * v + g) % 8]
                    for q in range(4):
                        fc = g * 4 + q
                        P.pe(lambda e, pt=pt, q=q, src=src, off=si * D + fc * 128: e.transpose(
                            out=pt[:, q * 128:(q + 1) * 128], in_=src[:, off:off + 128], identity=ident),
                            R=[src, cst], W=[pt])
                    P.act(lambda e, pt=pt, v=v, g=g: e.copy(
                        out=modT[:, 8 * v + 4 * g: 8 * v + 4 * g + 4],
                        in_=pt[:].rearrange("p (q m) -> p q m", q=4)[:, :, 0]), R=[pt], W=[modT])
            if debug:
                final_ops.append(P.dma("sp", dbg["modT"], modT[:], R=[modT]))

            conv_jobs = []

            def convert(src_aps, dst_ap, n):
                conv_jobs.append((src_aps, dst_ap, n))

            if stop_after >= 5:
                wo_v = w_out.rearrange("(kc p) n -> p kc n", p=128)
                for dh in range(2):
                    for kh in range(2):
                        convert([(wo_v[:, kh * 8:(kh + 1) * 8, dh * 512:(dh + 1) * 512], (8, 512))],
                                WO[dh, :, kh * 4096:(kh + 1) * 4096], 4096)
                wg_v = w_gate.rearrange("(kc p) n -> p kc n", p=128)
                wu_v = w_up.rearrange("(kc p) n -> p kc n", p=128)
                for blk in range(11):
                    convert([(wg_v[:, :, blk * 256:(blk + 1) * 256], (8, 256)),
                             (wu_v[:, :, blk * 256:(blk + 1) * 256], (8, 256))], WGU[blk, :, :], 4096)
                wd_v = w_down.rearrange("(j p) n -> p j n", p=128)
                for dh in range(2):
                    for jh in range(2):
                        convert([(wd_v[:, jh * 11:(jh + 1) * 11, dh * 512:(dh + 1) * 512], (11, 512))],
                                WD[dh * 2 + jh, :, :], 5632)
        P.barrier()

        if stop_after >= 1:
            stage1(P, nc, locals())
        P.barrier()
        if stop_after >= 2:
            scan_pass(P, nc, locals(), 0)
            P.barrier()
        if stop_after >= 3:
            scan_pass(P, nc, locals(), 1)
            P.barrier()
        if debug and stop_after >= 1:
            dd = T("dd", None)
            final_ops.append(P.dma("sp", dbg["CM"], CMs, semt=dd))
            final_ops.append(P.dma("sp", dbg["TM"], TMs, semt=dd))
            final_ops.append(P.dma("sp", dbg["SM"], SMs, semt=dd))
            if stop_after >= 2:
                final_ops.append(P.dma("sp", dbg["YF"], YFs, semt=dd))
                final_ops.append(P.dma("sp", dbg["YY"], YY.ap().rearrange("a t p f -> (a t p) f"), semt=dd))
            P.barrier()
        if stop_after >= 4:
            cps = [T(f"cp{i}", None) for i in range(4)]
            CW = 8192
            ccnt = [0]

            def dyn_copy(dst3, src4, sel, fresh):
                nt_ = dst3.shape[0]
                nr = nt_ * 128 * 1024 // CW
                dflat = dst3.rearrange("t p f -> (t p f)").rearrange("(r c) -> r c", c=CW)
                def fn(e, fresh=fresh):
                    if fresh:
                        pid = e.partition_id()
                        P.dyn = {0: e.snap(pid % 2), 1: e.snap(1 - pid % 2)}
                    sflat = src4[bass.ds(P.dyn[sel], 1)].rearrange("a t p f -> (a t p f)").rearrange("(r c) -> r c", c=CW)
                    return e.dma_start(out=dflat, in_=sflat)
                ccnt[0] += 1
                P.dma("sp", semt=cps[ccnt[0] % 4], fn=fn)

            dyn_copy(ZIN.ap(), YY.ap(), 1, True)
            P.barrier()
            cct = T("cc", None)
            for k in range(NCC):
                P.dma("pool", semt=cct, amt=1, fn=lambda e, k=k: e.collective_compute(
                    "AllGather", ALU.bypass, replica_groups=[[0, 1], [2, 3], [4, 5], [6, 7]],
                    ins=[ZIN.ap()[k * TPK:(k + 1) * TPK].rearrange("t p f -> (t p) f").opt()],
                    outs=[ZO.ap()[k].rearrange("a t p f -> (a t p) f").opt()]))
            P.barrier()
            dyn_copy(MINE.ap(), YY.ap(), 0, True)
            for k in range(NCC):
                dyn_copy(PART.ap()[k * TPK:(k + 1) * TPK], ZO.ap()[k], 1, False)
            P.barrier()
        if stop_after >= 5:
            stage5(P, nc, locals())
        else:
            with ExitStack() as st:
                tb = P.sb(st, "tb", [128, D], F32)
                for a in range(HALF // 128):
                    P.dma("sp", tb[:], xhalf[a * 128:(a + 1) * 128, :], W=[tb])
                    final_ops.append(P.dma("sp", yout[a * 128:(a + 1) * 128, :], tb[:], R=[tb]))
        P.emit(final_ops=final_ops + locals().get("_final", []))
        stats = dict(nsem=P.nsem, nops={e: len(P.ops[e]) for e in P.ENGS}, nwaits=P.nwaits)
    return nc, stats


def stage1(P, nc, L):
    TT, NTT, NCT = L["TT"], L["NTT"], L["NCT"]
    xin, w_cm, w_tm, convw, convb, rowp = L["xin"], L["w_cm"], L["w_tm"], L["convw"], L["convb"], L["rowp"]
    CMs, TMs, SMs = L["CMs"], L["TMs"], L["SMs"]
    cst, identb, modT, psb = L["cst"], L["identb"], L["modT"], L["psb"]
    ident = cst[:, 0:128]
    ones = cst[:, 5 * 128:6 * 128]
    with ExitStack() as st:
        wcm = P.sb(st, "wcm", [128, 8, NCM * 128], BF16)
        wtm = P.sb(st, "wtm", [128, 8, NTMC], BF16)
        wld = [P.sb(st, f"wld{i}", [128, 1152], F32) for i in range(2)]
        cw = P.sb(st, "cw", [128, NCM, 5], F32)
        cb = P.sb(st, "cb", [128, 6], F32)
        spb = P.sb(st, "spb", [128, 48], F32)
        amul = P.sb(st, "amul", [128, 24], F32)
        xt = [P.sb(st, f"xt{i}", [128, 4, D], F32) for i in range(2)]
        hT = [P.sb(st, f"hT{i}", [128, 8, 512], BF16) for i in range(2)]
        pad = [P.sb(st, f"pad{i}", [128, 544], F32) for i in range(3)]
        acc = [P.sb(st, f"acc{i}", [128, 512], F32) for i in range(3)]
        ptmp = P.sb(st, "ptmp", [128, 512], F32)
        sv = [P.sb(st, f"sv{i}", [128, 512], F32) for i in range(3)]
        sq = [P.sb(st, f"sq{i}", [128, 512], F32) for i in range(3)]
        rr = [P.sb(st, f"rr{i}", [128, 512], F32) for i in range(3)]
        tbf = [P.sb(st, f"tbf{i}", [128, 512], BF16) for i in range(4)]
        cmst = [P.sb(st, f"cmst{i}", [128, 4, 10, 128], BF16) for i in range(2)]
        tmst = [P.sb(st, f"tmst{i}", [128, 4, TMW], BF16) for i in range(1)]
        smst = [P.sb(st, f"smst{i}", [128, 4, SMW], F32) for i in range(2)]
        smr = [P.sb(st, f"smr{i}", [128, 32], F32) for i in range(2)]
        wcm_v = w_cm.rearrange("(kc p) n -> p kc n", p=128)
        wtm_v = w_tm.rearrange("(kc p) n -> p kc n", p=128)
        for kc in range(8):
            for hf in range(2):
                w = wld[hf]
                P.dma("sp", w[:], wcm_v[:, kc, hf * 1152:(hf + 1) * 1152], W=[w])
                (P.dve if hf == 0 else P.pool)(lambda e, w=w, kc=kc, hf=hf: e.tensor_copy(
                    out=wcm[:, kc, hf * 1152:(hf + 1) * 1152], in_=w[:]), R=[w], W=[wcm])
        for kc in range(8):
            w = wld[kc % 2]
            P.dma("sp", w[:, 0:NTMC], wtm_v[:, kc, :], W=[w])
            (P.dve if kc % 2 == 0 else P.pool)(lambda e, w=w, kc=kc: e.tensor_copy(out=wtm[:, kc, :], in_=w[:, 0:NTMC]), R=[w], W=[wtm])
        P.dma("sp", cw[:].rearrange("p c k -> p (c k)"), convw[:, :], W=[cw])
        P.dma("sp", cb[:], convb[:, :], W=[cb])
        P.dma("sp", spb[:], rowp[:, 0:48].partition_broadcast(128), W=[spb])
        P.act(lambda e: e.activation(out=amul[:], in_=spb[:, 24:48], func=AF.Exp), R=[spb], W=[amul])
        P.dve(lambda e: e.tensor_scalar_mul(out=amul[:], in0=amul[:], scalar1=-1.0), R=[amul], W=[amul])
        for p_ in pad:
            P.pool(lambda e, p_=p_: e.memset(p_[:], 0.0), W=[p_])

        blocks = [(0, CTX, CTX, 2)]
        t0 = CTX
        while t0 < TT:
            blocks.append((t0, 512, GRID_W, 0))
            t0 += 512
        PX, PA, PB, PN, PZ0, PZ1, PSm, PTr = L["psb"]
        ptr_bf = PTr.ap[:].bitcast(BF16)
        for bi, (t0, ntok, rowlen, mv) in enumerate(blocks):
            nt = ntok // 128
            nrow = ntok // rowlen
            x_, h_ = xt[bi % 2], hT[bi % 2]
            if bi == 1:
                for p_ in pad:
                    P.pool(lambda e, p_=p_: e.memset(p_[:], 0.0), W=[p_])
            cm_, tm_, sm_ = cmst[bi % 2], tmst[0], smst[bi % 2]
            def issue_x(b):
                t0b, ntokb = blocks[b][0], blocks[b][1]
                xb = xt[b % 2]
                P.dma("sp", xb[:, 0:ntokb // 128, :], xin[t0b:t0b + ntokb, :].rearrange("(a p) d -> p a d", p=128), W=[xb])

            if bi == 0:
                issue_x(0)
            for fc in range(8):
                for a in range(nt):
                    P.pe(lambda e, a=a, fc=fc, x_=x_: e.transpose(out=PX[:, a * 128:(a + 1) * 128],
                                                                  in_=x_[:, a, fc * 128:(fc + 1) * 128], identity=ident),
                         R=[x_, cst], W=[PX])
                P.act(lambda e, fc=fc, h_=h_, mv=mv, ntok=ntok: e.activation(
                    out=h_[:, fc, 0:ntok], in_=PX[:, 0:ntok], func=AF.Identity,
                    bias=modT[:, 8 * mv + fc: 8 * mv + fc + 1], scale=modT[:, 8 * (mv + 1) + fc: 8 * (mv + 1) + fc + 1]),
                    R=[PX, modT], W=[h_])
            if bi + 1 < len(blocks):
                issue_x(bi + 1)
            deferred = []

            def flush(upto):
                keep = []
                for due, fn_ in deferred:
                    if due <= upto:
                        fn_()
                    else:
                        keep.append((due, fn_))
                deferred[:] = keep

            def emit_transposes(src_fn, Rt, tmoff, nt=nt, tm_=tm_):
                for a in range(nt):
                    P.pe(lambda e, a=a: e.transpose(out=ptr_bf[:, a * 128:(a + 1) * 128], in_=src_fn(a), identity=identb[:]),
                         R=[Rt, identb], W=[PTr])
                P.dve(lambda e: e.tensor_copy(
                    out=tm_[:, 0:nt, tmoff:tmoff + 128], in_=ptr_bf[:, 0:nt * 128].rearrange("p (a l) -> p a l", a=nt)),
                    R=[PTr], W=[tm_])

            def emit_l2norm(cc, kind, s_, q_, r_, ntok=ntok, nt=nt, cm_=cm_):
                P.pe(lambda e: e.matmul(PN[:, 0:ntok], lhsT=ones, rhs=q_[:, 0:ntok], start=True, stop=True),
                     R=[cst, q_], W=[PN])
                P.act(lambda e: e.activation(out=r_[:, 0:ntok], in_=PN[:, 0:ntok], func=AF.Sqrt,
                                             bias=cst_eps(L), scale=1.0), R=[PN, L["epsT"]], W=[r_])
                P.dve(lambda e: e.reciprocal(out=r_[:, 0:ntok], in_=r_[:, 0:ntok]), R=[r_], W=[r_])
                if kind == "q":
                    dst = cm_[:, 0:nt, cc, :]
                    P.dve(lambda e: e.scalar_tensor_tensor(
                        out=dst, in0=s_[:, 0:ntok].rearrange("p (a l) -> p a l", a=nt), scalar=128.0 ** -0.5,
                        in1=r_[:, 0:ntok].rearrange("p (a l) -> p a l", a=nt), op0=ALU.mult, op1=ALU.mult),
                        R=[s_, r_], W=[cm_])
                else:
                    dst = cm_[:, 0:nt, 2 + (cc - 10), :]
                    P.dve(lambda e: e.tensor_tensor(
                        out=dst, in0=s_[:, 0:ntok].rearrange("p (a l) -> p a l", a=nt),
                        in1=r_[:, 0:ntok].rearrange("p (a l) -> p a l", a=nt), op=ALU.mult), R=[s_, r_], W=[cm_])

            for cc in range(NCM):
                pp = (PA, PB)[cc % 2]
                for kc in range(8):
                    P.pe(lambda e, kc=kc, cc=cc, pp=pp, h_=h_, ntok=ntok: e.matmul(
                        pp[:, 0:ntok], lhsT=wcm[:, kc, cc * 128:(cc + 1) * 128], rhs=h_[:, kc, 0:ntok],
                        start=(kc == 0), stop=(kc == 7)), R=[wcm, h_], W=[pp])
                flush(cc)
                pd, ac = pad[cc % 3], acc[cc % 3]
                pdv = pd[:, 0:nrow * (rowlen + 4)].rearrange("p (r l) -> p r l", r=nrow)
                P.act(lambda e, pdv=pdv, pp=pp, ntok=ntok, nrow=nrow, rowlen=rowlen: e.copy(
                    out=pdv[:, :, 2:2 + rowlen], in_=pp[:, 0:ntok].rearrange("p (r l) -> p r l", r=nrow)), R=[pp], W=[pd])
                acv = ac[:, 0:ntok].rearrange("p (r l) -> p r l", r=nrow)
                P.dve(lambda e, acv=acv, pdv=pdv, cc=cc, rowlen=rowlen: e.tensor_scalar_mul(
                    out=acv, in0=pdv[:, :, 0:rowlen], scalar1=cw[:, cc, 0:1]), R=[pd, cw], W=[ac])
                for k in range(1, 5):
                    P.dve(lambda e, acv=acv, pdv=pdv, cc=cc, k=k, rowlen=rowlen: e.scalar_tensor_tensor(
                        out=acv, in0=pdv[:, :, k:k + rowlen], scalar=cw[:, cc, k:k + 1], in1=acv,
                        op0=ALU.mult, op1=ALU.add), R=[pd, cw, ac], W=[ac])
                kind = ("xs" if cc < 4 else "B" if cc == 4 else "C" if cc == 5 else "q" if cc < 10 else "k" if cc < 14 else "v")
                if kind in ("xs", "v", "B", "C"):
                    tb = tbf[cc % 4]
                    if kind == "C":
                        dst = cm_[:, 0:nt, 0, :]
                    elif kind == "B":
                        dst = cm_[:, 0:nt, 1, :]
                    else:
                        dst = tb[:, 0:ntok].rearrange("p (a l) -> p a l", a=nt)
                    Wt = [cm_] if kind in ("B", "C") else [tb]
                    if cc < 6:
                        P.act(lambda e, dst=dst, ac=ac, cc=cc, ntok=ntok, nt=nt: e.activation(
                            out=dst, in_=ac[:, 0:ntok].rearrange("p (a l) -> p a l", a=nt), func=AF.Silu,
                            bias=cb[:, cc:cc + 1], scale=1.0), R=[ac, cb], W=Wt)
                    else:
                        P.act(lambda e, dst=dst, ac=ac, ntok=ntok, nt=nt: e.activation(
                            out=dst, in_=ac[:, 0:ntok].rearrange("p (a l) -> p a l", a=nt), func=AF.Silu),
                            R=[ac], W=Wt)
                    if kind == "C":
                        continue
                    tmoff = {"xs": cc * 128, "B": 512, "v": 1152 + (cc - 14) * 128}[kind]
                    if kind == "B":
                        deferred.append((cc + 3, lambda cm_=cm_, tmoff=tmoff: emit_transposes(lambda a: cm_[:, a, 1, :], cm_, tmoff)))
                    else:
                        deferred.append((cc + 3, lambda tb=tb, tmoff=tmoff: emit_transposes(lambda a: tb[:, a * 128:(a + 1) * 128], tb, tmoff)))
                else:
                    s_, q_, r_ = sv[cc % 3], sq[cc % 3], rr[cc % 3]
                    P.act(lambda e, s_=s_, ac=ac, ntok=ntok: e.activation(out=s_[:, 0:ntok], in_=ac[:, 0:ntok], func=AF.Silu),
                          R=[ac], W=[s_])
                    P.pool(lambda e, s_=s_, q_=q_, ntok=ntok: e.tensor_tensor(out=q_[:, 0:ntok], in0=s_[:, 0:ntok], in1=s_[:, 0:ntok], op=ALU.mult),
                           R=[s_], W=[q_])
                    deferred.append((cc + 2, lambda cc=cc, kind=kind, s_=s_, q_=q_, r_=r_: emit_l2norm(cc, kind, s_, q_, r_)))
                    if kind == "k":
                        ci = 2 + (cc - 10)
                        tmoff = 640 + (cc - 10) * 128
                        deferred.append((cc + 3, lambda cm_=cm_, ci=ci, tmoff=tmoff: emit_transposes(lambda a: cm_[:, a, ci, :], cm_, tmoff)))
            flush(10 ** 9)
            for a in range(nt):
                for gi, (n0, nn, pz) in enumerate(((0, 512, PZ0), (512, 512, PZ1), (1024, 32, PSm))):
                    for kc in range(8):
                        P.pe(lambda e, kc=kc, a=a, n0=n0, nn=nn, pz=pz, h_=h_: e.matmul(
                            pz[:, 0:nn], lhsT=h_[:, kc, a * 128:(a + 1) * 128], rhs=wtm[:, kc, n0:n0 + nn],
                            start=(kc == 0), stop=(kc == 7)), R=[h_, wtm], W=[pz])
                    if gi < 2:
                        off = 1664 + gi * 512
                        P.act(lambda e, a=a, off=off, pz=pz, tm_=tm_: e.activation(out=tm_[:, a, off:off + 512], in_=pz[:], func=AF.Silu),
                              R=[pz], W=[tm_])
                    else:
                        r_ = smr[a % 2]
                        smv = sm_[:, a, :]
                        P.dve(lambda e, r_=r_: e.tensor_tensor(out=r_[:, 0:24], in0=PSm[:, 0:24], in1=spb[:, 0:24], op=ALU.add),
                              R=[PSm, spb], W=[r_])
                        P.act(lambda e, r_=r_: e.activation(out=r_[:, 0:24], in_=r_[:, 0:24], func=AF.Exp), R=[r_], W=[r_])
                        P.act(lambda e, r_=r_: e.activation(out=r_[:, 0:24], in_=r_[:, 0:24], func=AF.Ln, bias=1.0, scale=1.0), R=[r_], W=[r_])
                        P.act(lambda e, smv=smv: e.activation(out=smv[:, 40:48], in_=PSm[:, 24:32], func=AF.Sigmoid), R=[PSm], W=[sm_])
                        P.dve(lambda e, smv=smv, r_=r_: e.tensor_copy(out=smv[:, 0:16], in_=r_[:, 0:16]), R=[r_], W=[sm_])
                        P.dve(lambda e, smv=smv, r_=r_: e.tensor_tensor(out=smv[:, 16:40], in0=r_[:, 0:24], in1=amul[:], op=ALU.mult),
                              R=[r_, amul], W=[sm_])
                        P.dve(lambda e, smv=smv: e.tensor_scalar_mul(out=smv[:, 48:56], in0=smv[:, 40:48], scalar1=-1.0), R=[sm_], W=[sm_])
                        P.dve(lambda e, smv=smv: e.memset(smv[:, 56:64], 0.0), W=[sm_])
            ti0 = t0 // 128
            P.dma("sp", CMs[ti0:ti0 + nt].rearrange("t p f -> p t f"), cm_[:, 0:nt].rearrange("p t c l -> p t (c l)"), R=[cm_])
            P.dma("sp", TMs[t0:t0 + ntok, :].rearrange("(a p) f -> p a f", p=128), tm_[:, 0:nt, :], R=[tm_])
            P.dma("sp", SMs[t0:t0 + ntok, :].rearrange("(a p) f -> p a f", p=128), sm_[:, 0:nt, :], R=[sm_])


def cst_eps(L):
    return L["epsT"][:, 0:1]


def scan_pass(P, nc, L, d):
    NTT, NCT, NLT, NTH = L["NTT"], L["NCT"], L["NLT"], L["NTH"]
    CMs, TMs, SMs, YFs, YY, rowp = L["CMs"], L["TMs"], L["SMs"], L["YFs"], L["YY"], L["rowp"]
    cst, identb, psb, epsT = L["cst"], L["identb"], L["psb"], L["epsT"]
    Uincl, Lstrict, Lincl, Ustrict, ones = (cst[:, i * 128:(i + 1) * 128] for i in range(1, 6))
    if d == 0:
        Tm, Um, TmT = Uincl, Lstrict, Lincl
        order = list(range(NCT)) + [NCT + i for i in range(NLT)]
    else:
        Tm, Um, TmT = Lincl, Ustrict, Uincl
        order = list(reversed(range(NCT))) + [NCT + i for i in reversed(range(NLT))]
    prep_banks = psb[0:3]
    out_banks = psb[3:5]
    bKS, bSU, bST = psb[5], psb[6], psb[7]
    cnt = {"p": 0, "o": 0}

    def pb():
        cnt["p"] += 1
        return prep_banks[cnt["p"] % 3]

    def ob():
        cnt["o"] += 1
        return out_banks[cnt["o"] % 2]

    with ExitStack() as st:
        sb = lambda n, shp, dt_: P.sb(st, f"{n}_{d}", shp, dt_)
        cmb = [sb(f"cmb{i}", [128, 10, 128], BF16) for i in range(3)]
        tmb = [sb(f"tmb{i}", [128, TMW], BF16) for i in range(3)]
        smb = [sb(f"smb{i}", [128, SMW], F32) for i in range(3)]
        yo = [sb(f"yo{i}", [128, 1024], F32) for i in range(2)]
        rla = sb("rla", [128, 8, 128], F32)
        ET = sb("ET", [128, 8, 128], BF16)
        gmt = sb("gmt", [128, 128], BF16)
        att = [sb(f"att{i}", [128, 8, 128], BF16) for i in range(2)]
        xdt = [sb(f"xdt{i}", [128, 8, 64], BF16) for i in range(2)]
        xdw = [sb(f"xdw{i}", [128, 8, 64], BF16) for i in range(2)]
        ea = [sb(f"ea{i}", [128, 24], F32) for i in range(2)]
        t1 = sb("t1", [128, 8, 64], F32)
        ST = sb("ST", [128, 8, 64], F32)
        STb = sb("STb", [128, 512], BF16)
        rlu = sb("rlu", [128, 4, 128], F32)
        E = sb("E", [128, 4, 128], F32)
        Es = sb("Es", [128, 4, 128], F32)
        Ei = sb("Ei", [128, 4, 128], BF16)
        eg = [sb(f"eg{i}", [128, 12], F32) for i in range(2)]
        eb = [sb(f"eb{i}", [128, 4], F32) for i in range(2)]
        Ak = [sb(f"Ak{i}", [128, 4, 128], BF16) for i in range(2)]
        Nk = [sb(f"Nk{i}", [128, 4, 128], BF16) for i in range(2)]
        Pk = [sb(f"Pk{i}", [128, 4, 128], BF16) for i in range(2)]
        Pf = [sb(f"Pf{i}", [128, 4, 128], BF16) for i in range(2)]
        Wk = [sb(f"Wk{i}", [128, 4, 128], BF16) for i in range(2)]
        Dk = [sb(f"Dk{i}", [128, 4, 128], BF16) for i in range(2)]
        Afull = sb("Afull", [128, 4, 128], BF16)
        Nfull = sb("Nfull", [128, 4, 128], BF16)
        Ym = sb("Ym", [128, 4, 128], BF16)
        qk = sb("qk", [128, 4, 128], BF16)
        qkT = [sb(f"qkT{i}", [128, 4, 128], BF16) for i in range(2)]
        vb = [sb(f"vb{i}", [128, 4, 128], F32) for i in range(2)]
        kd = [sb(f"kd{i}", [128, 4, 128], BF16) for i in range(2)]
        tks = sb("tks", [128, 4, 128], F32)
        rv = sb("rv", [128, 4, 128], BF16)
        vnb = sb("vnb", [128, 4, 128], BF16)
        S = sb("S", [128, 4, 128], F32)
        Sb = sb("Sb", [128, 4, 128], BF16)
        t3 = sb("t3", [128, 4, 128], F32)
        P.pool(lambda e: e.memset(ST[:], 0.0), W=[ST])
        P.pool(lambda e: e.memset(STb[:], 0.0), W=[STb])
        P.pool(lambda e: e.memset(S[:], 0.0), W=[S])
        P.pool(lambda e: e.memset(Sb[:], 0.0), W=[Sb])
        if d == 1:
            yfb = [sb(f"yfb{i}", [128, 1024], F32) for i in range(3)]
            fin = sb("fin", [128, 648], F32)
            P.dma("sp", fin[:], rowp[:, 48:696].partition_broadcast(128), W=[fin])
            u = sb("u", [128, 1024], F32)
            usq = sb("usq", [128, 1024], F32)
            ss = sb("ss", [128, 8], F32)
            yfin = sb("yfin", [128, 1024], BF16)
            yT = [sb(f"yT{i}", [128, 8, 128], BF16) for i in range(2)]

        def load(ci):
            ti = order[ci]
            cm, tm, sm = cmb[ci % 3], tmb[ci % 3], smb[ci % 3]
            P.dma("sp", cm[:].rearrange("p c l -> p (c l)"), CMs[ti], W=[cm])
            P.dma("sp", tm[:], TMs[ti * 128:(ti + 1) * 128, :], W=[tm])
            P.dma("sp", sm[:], SMs[ti * 128:(ti + 1) * 128, :], W=[sm])
            if d == 1 and ti >= NCT:
                yf = yfb[ci % 3]
                P.dma("sp", yf[:], YFs[(ti - NCT) * 128:(ti - NCT + 1) * 128, :], W=[yf])

        def prep(ci):
            cm, tm, sm = cmb[ci % 3], tmb[ci % 3], smb[ci % 3]
            i2 = ci % 2
            CT, BT = cm[:, 0, :], cm[:, 1, :]
            la = sm[:, 16 + 8 * d:24 + 8 * d]
            dtd = sm[:, 8 * d:8 * d + 8]
            lag = sm[:, 32 + 4 * d:36 + 4 * d]
            beta = sm[:, 40 + 4 * d:44 + 4 * d]
            nbeta = sm[:, 48 + 4 * d:52 + 4 * d]
            if PARTS & 4:
                return
            P.pool(lambda e: e.tensor_tensor(out=rla[:], in0=bc(la, [128, 8, 128], 2), in1=bc(Tm, [128, 8, 128], 1), op=ALU.mult),
                   R=[sm, cst], W=[rla])
            for hf in range(2):
                pd = pb()
                P.pe(lambda e, pd=pd, hf=hf: e.matmul(pd[:], lhsT=Um, rhs=rla[:, 4 * hf:4 * hf + 4, :].rearrange("p h l -> p (h l)"),
                                                      start=True, stop=True), R=[cst, rla], W=[pd])
                P.act(lambda e, pd=pd, hf=hf: e.activation(out=ET[:, 4 * hf:4 * hf + 4, :].rearrange("p h l -> p (h l)"), in_=pd[:], func=AF.Exp),
                      R=[pd], W=[ET])
            pg = pb()
            P.pe(lambda e: e.matmul(pg[:, 0:128], lhsT=BT, rhs=CT, start=True, stop=True), R=[cm], W=[pg])
            P.pe(lambda e: e.matmul(pg[:, 128:136], lhsT=Tm, rhs=la, start=True, stop=True), R=[cst, sm], W=[pg])
            P.pe(lambda e: e.matmul(pg[:, 136:144], lhsT=Um, rhs=la, start=True, stop=True), R=[cst, sm], W=[pg])
            P.pe(lambda e: e.matmul(pg[:, 144:152], lhsT=ones, rhs=la, start=True, stop=True), R=[cst, sm], W=[pg])
            P.dve(lambda e: e.tensor_tensor(out=gmt[:], in0=pg[:, 0:128], in1=Tm, op=ALU.mult), R=[pg, cst], W=[gmt])
            ea_ = ea[i2]
            P.act(lambda e: e.activation(out=ea_[:], in_=pg[:, 128:152], func=AF.Exp), R=[pg], W=[ea_])
            att_, xdt_, xdw_ = att[i2], xdt[i2], xdw[i2]
            P.dve(lambda e: e.tensor_tensor(out=att_[:], in0=ET[:], in1=bc(gmt[:], [128, 8, 128], 1), op=ALU.mult),
                  R=[ET, gmt], W=[att_])
            P.pool(lambda e: e.tensor_tensor(out=xdt_[:], in0=tm[:, 0:512].rearrange("p (h q) -> p h q", h=8),
                                             in1=bc(dtd, [128, 8, 64], 2), op=ALU.mult), R=[tm, sm], W=[xdt_])
            P.dve(lambda e: e.tensor_tensor(out=xdw_[:], in0=xdt_[:], in1=bc(ea_[:, 8:16], [128, 8, 64], 2), op=ALU.mult),
                  R=[xdt_, ea_], W=[xdw_])
            if PARTS & 8:
                return
            P.pool(lambda e: e.tensor_tensor(out=rlu[:], in0=bc(lag, [128, 4, 128], 2), in1=bc(Um, [128, 4, 128], 1), op=ALU.mult),
                   R=[sm, cst], W=[rlu])
            pdg = pb()
            P.pe(lambda e: e.matmul(pdg[:], lhsT=Tm, rhs=rlu[:].rearrange("p h l -> p (h l)"), start=True, stop=True),
                 R=[cst, rlu], W=[pdg])
            P.act(lambda e: e.activation(out=E[:].rearrange("p h l -> p (h l)"), in_=pdg[:], func=AF.Exp), R=[pdg], W=[E])
            ps2 = pb()
            P.pe(lambda e: e.matmul(ps2[:, 0:4], lhsT=Tm, rhs=lag, start=True, stop=True), R=[cst, sm], W=[ps2])
            P.pe(lambda e: e.matmul(ps2[:, 4:8], lhsT=Um, rhs=lag, start=True, stop=True), R=[cst, sm], W=[ps2])
            P.pe(lambda e: e.matmul(ps2[:, 8:12], lhsT=ones, rhs=lag, start=True, stop=True), R=[cst, sm], W=[ps2])
            eg_, eb_ = eg[i2], eb[i2]
            P.act(lambda e: e.activation(out=eg_[:], in_=ps2[:, 0:12], func=AF.Exp), R=[ps2], W=[eg_])
            P.dve(lambda e: e.tensor_tensor(out=Es[:], in0=E[:], in1=bc(Um, [128, 4, 128], 1), op=ALU.mult), R=[E, cst], W=[Es])
            P.dve(lambda e: e.tensor_tensor(out=Es[:], in0=Es[:], in1=bc(nbeta, [128, 4, 128], 2), op=ALU.mult), R=[Es, sm], W=[Es])
            P.pool(lambda e: e.tensor_tensor(out=Ei[:], in0=E[:], in1=bc(TmT, [128, 4, 128], 1), op=ALU.mult), R=[E, cst], W=[Ei])
            pkk, pqk = pb(), pb()
            for h in range(4):
                P.pe(lambda e, h=h: e.matmul(pkk[:, h * 128:(h + 1) * 128], lhsT=cm[:, 2 + h, :], rhs=cm[:, 2 + h, :], start=True, stop=True),
                     R=[cm], W=[pkk])
            for h in range(4):
                P.pe(lambda e, h=h: e.matmul(pqk[:, h * 128:(h + 1) * 128], lhsT=cm[:, 6 + h, :], rhs=cm[:, 2 + h, :], start=True, stop=True),
                     R=[cm], W=[pqk])
            BD16 = cst[:, 6 * 128:7 * 128]
            Mb = [cst[:, (7 + i + 3 * d) * 128:(8 + i + 3 * d) * 128] for i in range(3)]
            P.dve(lambda e: e.tensor_tensor(out=Afull[:].rearrange("p h l -> p (h l)"), in0=pkk[:], in1=Es[:].rearrange("p h l -> p (h l)"), op=ALU.mult),
                  R=[pkk, Es], W=[Afull])
            P.dve(lambda e: e.tensor_tensor(out=qk[:].rearrange("p h l -> p (h l)"), in0=pqk[:], in1=Ei[:].rearrange("p h l -> p (h l)"), op=ALU.mult),
                  R=[pqk, Ei], W=[qk])
            pt = pb()
            ptb = pt.ap[:].bitcast(BF16)
            for h in range(4):
                P.pe(lambda e, h=h: e.transpose(out=ptb[:, h * 128:(h + 1) * 128], in_=Afull[:, h, :], identity=identb[:]), R=[Afull, identb], W=[pt])
            for h in range(4):
                P.pe(lambda e, h=h: e.transpose(out=ptb[:, 512 + h * 128:512 + (h + 1) * 128], in_=qk[:, h, :], identity=identb[:]), R=[qk, identb], W=[pt])
            qkT_ = qkT[i2]
            P.act(lambda e: e.copy(out=Nfull[:].rearrange("p h l -> p (h l)"), in_=ptb[:, 0:512]), R=[pt], W=[Nfull])
            P.dve(lambda e: e.tensor_copy(out=qkT_[:].rearrange("p h l -> p (h l)"), in_=ptb[:, 512:1024]), R=[pt], W=[qkT_])
            A, N, Pc = Ak[0], Nk[0], Pk[0]
            P.dve(lambda e, A=A: e.tensor_tensor(out=A[:], in0=Afull[:], in1=bc(BD16, [128, 4, 128], 1), op=ALU.mult), R=[Afull, cst], W=[A])
            P.dve(lambda e, N=N: e.tensor_tensor(out=N[:], in0=Nfull[:], in1=bc(BD16, [128, 4, 128], 1), op=ALU.mult), R=[Nfull, cst], W=[N])
            P.dve(lambda e, N=N, Pc=Pc: e.tensor_tensor(out=Pc[:], in0=N[:], in1=bc(identb[:], [128, 4, 128], 1), op=ALU.add), R=[N, identb], W=[Pc])
            for k in range(1, 4):
                A2, N2 = Ak[k % 2], Nk[k % 2]
                P2 = Pk[k % 2]
                pa = pb()
                for h in range(4):
                    P.pe(lambda e, h=h, N=N, A=A, pa=pa: e.matmul(pa[:, h * 128:(h + 1) * 128], lhsT=N[:, h, :], rhs=A[:, h, :], start=True, stop=True),
                         R=[N, A], W=[pa])
                P.act(lambda e, A2=A2, pa=pa: e.copy(out=A2[:].rearrange("p h l -> p (h l)"), in_=pa[:]), R=[pa], W=[A2])
                if k <= 2:
                    pn = pb()
                    for h in range(4):
                        P.pe(lambda e, h=h, N=N, A=A, pn=pn: e.matmul(pn[:, h * 128:(h + 1) * 128], lhsT=A[:, h, :], rhs=N[:, h, :], start=True, stop=True),
                             R=[N, A], W=[pn])
                    P.act(lambda e, N2=N2, pn=pn: e.copy(out=N2[:].rearrange("p h l -> p (h l)"), in_=pn[:]), R=[pn], W=[N2])
                pm = pb()
                for h in range(4):
                    P.pe(lambda e, h=h, A2=A2, Pc=Pc, pm=pm: e.matmul(pm[:, h * 128:(h + 1) * 128], lhsT=A2[:, h, :], rhs=Pc[:, h, :], start=True, stop=True),
                         R=[A2, Pc], W=[pm])
                P.dve(lambda e, P2=P2, Pc=Pc, pm=pm: e.tensor_tensor(out=P2[:].rearrange("p h l -> p (h l)"), in0=pm[:],
                                                                     in1=Pc[:].rearrange("p h l -> p (h l)"), op=ALU.add), R=[pm, Pc], W=[P2])
                A, N, Pc = A2, N2, P2
            Wc = Pc
            pt2 = pb()
            pt2b = pt2.ap[:].bitcast(BF16)
            for h in range(4):
                P.pe(lambda e, h=h, Wc=Wc: e.transpose(out=pt2b[:, h * 128:(h + 1) * 128], in_=Wc[:, h, :], identity=identb[:]), R=[Wc, identb], W=[pt2])
            Dc = Dk[0]
            P.act(lambda e, Dc=Dc: e.copy(out=Dc[:].rearrange("p h l -> p (h l)"), in_=pt2b[:, 0:512]), R=[pt2], W=[Dc])
            for li in range(3):
                last = li == 2
                py_ = pb()
                for h in range(4):
                    P.pe(lambda e, h=h, Dc=Dc, py_=py_: e.matmul(py_[:, h * 128:(h + 1) * 128], lhsT=Nfull[:, h, :], rhs=Dc[:, h, :], start=True, stop=True),
                         R=[Nfull, Dc], W=[py_])
                P.dve(lambda e, py_=py_, li=li: e.tensor_tensor(out=Ym[:], in0=py_[:].rearrange("p (h l) -> p h l", h=4), in1=bc(Mb[li], [128, 4, 128], 1), op=ALU.mult),
                      R=[py_, cst], W=[Ym])
                if not last:
                    pz = pb()
                    for h in range(4):
                        P.pe(lambda e, h=h, Wc=Wc, pz=pz: e.matmul(pz[:, h * 128:(h + 1) * 128], lhsT=Wc[:, h, :], rhs=Ym[:, h, :], start=True, stop=True),
                             R=[Wc, Ym], W=[pz])
                    D2 = Dk[(li + 1) % 2]
                    P.dve(lambda e, D2=D2, Dc=Dc, pz=pz: e.tensor_tensor(out=D2[:].rearrange("p h l -> p (h l)"), in0=pz[:],
                                                                         in1=Dc[:].rearrange("p h l -> p (h l)"), op=ALU.add), R=[pz, Dc], W=[D2])
                pzt = pb()
                for h in range(4):
                    P.pe(lambda e, h=h, Wc=Wc, pzt=pzt: e.matmul(pzt[:, h * 128:(h + 1) * 128], lhsT=Ym[:, h, :], rhs=Wc[:, h, :], start=True, stop=True),
                         R=[Wc, Ym], W=[pzt])
                W2 = Pf[i2] if last else Wk[li % 2]
                P.dve(lambda e, W2=W2, Wc=Wc, pzt=pzt: e.tensor_tensor(out=W2[:].rearrange("p h l -> p (h l)"), in0=pzt[:],
                                                                        in1=Wc[:].rearrange("p h l -> p (h l)"), op=ALU.add), R=[pzt, Wc], W=[W2])
                Wc = W2
                if not last:
                    Dc = D2
            vb_, kd_ = vb[i2], kd[i2]
            P.pool(lambda e: e.tensor_tensor(out=vb_[:], in0=tm[:, 1152:1664].rearrange("p (h v) -> p h v", h=4), in1=bc(beta, [128, 4, 128], 2), op=ALU.mult),
                   R=[tm, sm], W=[vb_])
            P.dve(lambda e: e.tensor_tensor(out=eb_[:], in0=eg_[:, 0:4], in1=beta, op=ALU.mult), R=[eg_, sm], W=[eb_])
            P.pool(lambda e: e.tensor_tensor(out=kd_[:], in0=tm[:, 640:1152].rearrange("p (h v) -> p h v", h=4), in1=bc(eg_[:, 4:8], [128, 4, 128], 2), op=ALU.mult),
                   R=[tm, eg_], W=[kd_])

        def seq(ci):
            ti = order[ci]
            lat = ti >= NCT
            cm, tm, sm = cmb[ci % 3], tmb[ci % 3], smb[ci % 3]
            i2 = ci % 2
            CT = cm[:, 0, :]
            ea_, att_, xdt_, xdw_ = ea[i2], att[i2], xdt[i2], xdw[i2]
            eg_, eb_, qkT_, vb_, kd_, Pf_ = eg[i2], eb[i2], qkT[i2], vb[i2], kd[i2], Pf[i2]
            yo_ = yo[ci % 2]
            for h in range(4):
                P.pe(lambda e, h=h: e.matmul(bKS[:, h * 128:(h + 1) * 128], lhsT=cm[:, 2 + h, :], rhs=Sb[:, h, :], start=True, stop=True),
                     R=[cm, Sb], W=[bKS])
            P.dve(lambda e: e.tensor_tensor(out=tks[:], in0=bKS[:].rearrange("p (h v) -> p h v", h=4), in1=bc(eb_[:], [128, 4, 128], 2), op=ALU.mult),
                  R=[bKS, eb_], W=[tks])
            P.dve(lambda e: e.tensor_tensor(out=rv[:], in0=vb_[:], in1=tks[:], op=ALU.subtract), R=[vb_, tks], W=[rv])
            if lat:
                pi = ob()
                P.pe(lambda e: e.matmul(pi[:], lhsT=CT, rhs=STb[:], start=True, stop=True), R=[cm, STb], W=[pi])
                pqs = ob()
                for h in range(4):
                    P.pe(lambda e, h=h: e.matmul(pqs[:, h * 128:(h + 1) * 128], lhsT=cm[:, 6 + h, :], rhs=Sb[:, h, :], start=True, stop=True),
                         R=[cm, Sb], W=[pqs])
            P.pe(lambda e: e.matmul(bST[:], lhsT=tm[:, 512:640], rhs=xdw_[:].rearrange("p h q -> p (h q)"), start=True, stop=True),
                 R=[tm, xdw_], W=[bST])
            if lat:
                P.dve(lambda e: e.tensor_tensor(out=t1[:], in0=pi[:].rearrange("p (h q) -> p h q", h=8), in1=bc(ea_[:, 0:8], [128, 8, 64], 2), op=ALU.mult),
                      R=[pi, ea_], W=[t1])
                P.dve(lambda e: e.tensor_tensor(out=t3[:], in0=pqs[:].rearrange("p (h v) -> p h v", h=4), in1=bc(eg_[:, 0:4], [128, 4, 128], 2), op=ALU.mult),
                      R=[pqs, eg_], W=[t3])
            P.dve(lambda e: e.tensor_tensor(out=ST[:], in0=ST[:], in1=bc(ea_[:, 16:24], [128, 8, 64], 2), op=ALU.mult), R=[ST, ea_], W=[ST])
            P.dve(lambda e: e.tensor_tensor(out=ST[:].rearrange("p h q -> p (h q)"), in0=ST[:].rearrange("p h q -> p (h q)"), in1=bST[:], op=ALU.add),
                  R=[ST, bST], W=[ST])
            P.act(lambda e: e.copy(out=STb[:], in_=ST[:].rearrange("p h q -> p (h q)")), R=[ST], W=[STb])
            for h in range(4):
                P.pe(lambda e, h=h: e.matmul(bKS[:, h * 128:(h + 1) * 128], lhsT=Pf_[:, h, :], rhs=rv[:, h, :], start=True, stop=True),
                     R=[Pf_, rv], W=[bKS])
            P.act(lambda e: e.copy(out=vnb[:].rearrange("p h v -> p (h v)"), in_=bKS[:]), R=[bKS], W=[vnb])
            for h in range(4):
                P.pe(lambda e, h=h: e.matmul(bSU[:, h * 128:(h + 1) * 128], lhsT=kd_[:, h, :], rhs=vnb[:, h, :], start=True, stop=True),
                     R=[kd_, vnb], W=[bSU])
            P.dve(lambda e: e.tensor_tensor(out=S[:], in0=S[:], in1=bc(eg_[:, 8:12], [128, 4, 128], 2), op=ALU.mult), R=[S, eg_], W=[S])
            P.dve(lambda e: e.tensor_tensor(out=S[:].rearrange("p h v -> p (h v)"), in0=S[:].rearrange("p h v -> p (h v)"), in1=bSU[:], op=ALU.add),
                  R=[S, bSU], W=[S])
            P.act(lambda e: e.copy(out=Sb[:].rearrange("p h v -> p (h v)"), in_=S[:].rearrange("p h v -> p (h v)")), R=[S], W=[Sb])
            if not lat:
                return
            py = ob()
            for h in range(8):
                P.pe(lambda e, h=h: e.matmul(py[:, h * 64:(h + 1) * 64], lhsT=att_[:, h, :], rhs=xdt_[:, h, :], start=True, stop=True),
                     R=[att_, xdt_], W=[py])
            P.dve(lambda e: e.tensor_tensor(out=yo_[:, 0:512], in0=t1[:].rearrange("p h q -> p (h q)"), in1=py[:], op=ALU.add), R=[t1, py], W=[yo_])
            pqv = ob()
            for h in range(4):
                P.pe(lambda e, h=h: e.matmul(pqv[:, h * 128:(h + 1) * 128], lhsT=qkT_[:, h, :], rhs=vnb[:, h, :], start=True, stop=True),
                     R=[qkT_, vnb], W=[pqv])
            P.dve(lambda e: e.tensor_tensor(out=yo_[:, 512:1024], in0=t3[:].rearrange("p h v -> p (h v)"), in1=pqv[:], op=ALU.add), R=[t3, pqv], W=[yo_])
            li = ti - NCT
            if d == 0:
                P.dma("sp", YFs[li * 128:(li + 1) * 128, :], yo_[:], R=[yo_])
                return
            yf = yfb[ci % 3]
            dsk, nws, nwg = fin[:, 0:8], fin[:, 8:520], fin[:, 520:648]
            P.dve(lambda e: e.tensor_tensor(out=u[:], in0=yo_[:], in1=yf[:], op=ALU.add), R=[yo_, yf], W=[u])
            P.pool(lambda e: e.tensor_tensor(out=usq[:, 0:512].rearrange("p (h q) -> p h q", h=8), in0=tm[:, 0:512].rearrange("p (h q) -> p h q", h=8),
                                             in1=bc(dsk, [128, 8, 64], 2), op=ALU.mult), R=[tm, fin], W=[usq])
            P.dve(lambda e: e.tensor_tensor(out=u[:, 0:512], in0=u[:, 0:512], in1=usq[:, 0:512], op=ALU.add), R=[u, usq], W=[u])
            P.dve(lambda e: e.tensor_tensor(out=u[:, 0:512], in0=u[:, 0:512], in1=tm[:, 1664:2176], op=ALU.mult), R=[u, tm], W=[u])
            P.pool(lambda e: e.memset(ss[:], 0.0), W=[ss])
            P.act(lambda e: e.activation(out=usq[:, 0:512], in_=u[:, 0:512], func=AF.Square, accum_out=ss[:, 0:1]), R=[u, ss], W=[usq, ss])
            for h in range(4):
                P.act(lambda e, h=h: e.activation(out=usq[:, 512 + h * 128:512 + (h + 1) * 128], in_=u[:, 512 + h * 128:512 + (h + 1) * 128],
                                                  func=AF.Square, accum_out=ss[:, 1 + h:2 + h]), R=[u, ss], W=[usq, ss])
            P.act(lambda e: e.activation(out=ss[:, 0:1], in_=ss[:, 0:1], func=AF.Sqrt, bias=epsT[:, 0:1], scale=1.0 / 512), R=[ss, epsT], W=[ss])
            P.act(lambda e: e.activation(out=ss[:, 1:5], in_=ss[:, 1:5], func=AF.Sqrt, bias=epsT[:, 0:1], scale=1.0 / 128), R=[ss, epsT], W=[ss])
            P.dve(lambda e: e.reciprocal(out=ss[:, 0:5], in_=ss[:, 0:5]), R=[ss], W=[ss])
            P.dve(lambda e: e.scalar_tensor_tensor(out=yfin[:, 0:512], in0=u[:, 0:512], scalar=ss[:, 0:1], in1=nws, op0=ALU.mult, op1=ALU.mult),
                  R=[u, ss, fin], W=[yfin])
            u4 = u[:, 512:1024].rearrange("p (h v) -> p h v", h=4)
            P.dve(lambda e: e.tensor_tensor(out=u4, in0=u4, in1=bc(ss[:, 1:5], [128, 4, 128], 2), op=ALU.mult), R=[u, ss], W=[u])
            P.dve(lambda e: e.tensor_tensor(out=u4, in0=u4, in1=bc(nwg, [128, 4, 128], 1), op=ALU.mult), R=[u, fin], W=[u])
            P.dve(lambda e: e.tensor_tensor(out=yfin[:, 512:1024], in0=u[:, 512:1024], in1=tm[:, 2176:2688], op=ALU.mult), R=[u, tm], W=[yfin])
            pt = ob()
            ptb = pt.ap[:].bitcast(BF16)
            for c8 in range(8):
                P.pe(lambda e, c8=c8: e.transpose(out=ptb[:, c8 * 128:(c8 + 1) * 128], in_=yfin[:, c8 * 128:(c8 + 1) * 128], identity=identb[:]),
                     R=[yfin, identb], W=[pt])
            yT_ = yT[ci % 2]
            P.act(lambda e: e.copy(out=yT_[:].rearrange("p c l -> p (c l)"), in_=ptb[:, :]), R=[pt], W=[yT_])
            hc, tl = li // NTH, li % NTH
            P.dma("sp", YY.ap()[hc, tl], yT_[:].rearrange("p c l -> p (c l)"), R=[yT_])

        n = len(order)
        if L["debug"] and d == 0:
            n = min(DBG_NCH, n)
        jobs = list(L["conv_jobs"]) if d == 0 else []
        if jobs:
            wst = [sb(f"wst{i}", [128, 6144], F32) for i in range(2)]
            wsb = [sb(f"wsb{i}", [128, 6144], BF16) for i in range(2)]
        jcnt = [0]

        def run_job():
            if not jobs:
                return
            src_aps, dst_ap, nel = jobs.pop(0)
            s_, b_ = wst[jcnt[0] % 2], wsb[jcnt[0] % 2]
            jcnt[0] += 1
            off = 0
            for sap, shape in src_aps:
                sz = int(np.prod(shape))
                view = s_[:, off:off + sz].rearrange("p (a b) -> p a b", a=shape[0])
                P.dma("sp", view, sap, W=[s_])
                off += sz
            P.act(lambda e: e.copy(out=b_[:, 0:nel], in_=s_[:, 0:nel]), R=[s_], W=[b_])
            P.dma("sp", dst_ap, b_[:, 0:nel], R=[b_])

        load(0)
        if n > 1:
            load(1)
        if PARTS & 1:
            prep(0)
        for ci in range(n):
            if ci + 2 < n:
                load(ci + 2)
            if ci + 1 < n and PARTS & 1:
                prep(ci + 1)
            if PARTS & 2:
                seq(ci)
            run_job()
            if ci == n - 1:
                while jobs:
                    run_job()
        if L["debug"] and d == 0:
            loc = locals()
            for nm_, ap_ in L["dbg"].items():
                if not nm_.startswith("s_"):
                    continue
                key = nm_[2:]
                t_ = loc[key] if key in loc else loc[key[:-1]][int(key[-1])]
                src = t_[:] if len(t_.ap.shape) == 2 else t_[:].rearrange("p h l -> p (h l)")
                L["final_ops"].append(P.dma("sp", ap_, src, R=[t_]))


def stage5(P, nc, L):
    HALF, NTH = L["HALF"], L["NTH"]
    xhalf, yout, MINE, PART, WO, WGU, WD, rowp = L["xhalf"], L["yout"], L["MINE"], L["PART"], L["WO"], L["WGU"], L["WD"], L["rowp"]
    g12row, modT, cst, psb, epsT, final_ops = L["g12row"], L["modT"], L["cst"], L["psb"], L["epsT"], L["final_ops"]
    ident = cst[:, 0:128]
    with ExitStack() as st:
        lnp = P.sb(st, "lnp", [128, 4 * D], F32)
        P.dma("sp", lnp[:], rowp[:, 696:4792].partition_broadcast(128), W=[lnp])
        xt = [P.sb(st, f"x5_{i}", [128, 4, D], F32) for i in range(2)]
        yTs = [P.sb(st, f"yT5_{i}", [128, 2, 4, 1024], BF16) for i in range(2)]
        ysem = [[T(f"yTa{i}", None), T(f"yTb{i}", None)] for i in range(2)]
        h2T = P.sb(st, "h2T", [128, 8, 512], BF16)
        actT = P.sb(st, "actT", [128, 22, 512], BF16)
        wbuf = [P.sb(st, f"wbuf{i}", [128, 8192], BF16) for i in range(3)]
        tmp = [P.sb(st, f"tmp5_{i}", [128, 512], F32) for i in range(2)]
        sgt = [P.sb(st, f"sgt{i}", [128, 512], F32) for i in range(2)]
        junk = P.sb(st, "junk", [128, D], F32)
        stat = [P.sb(st, f"stat{i}", [128, 8], F32) for i in range(2)]
        wcnt = [0]

        def wload(src, n):
            wb = wbuf[wcnt[0] % 3]
            wcnt[0] += 1
            P.dma("sp", wb[:, 0:n], src, W=[wb])
            return wb

        def resid(ps, x_, a, dh, gi, k):
            t = tmp[k % 2]
            P.dve(lambda e: e.tensor_tensor(out=t[:], in0=ps[:], in1=g12row[:, gi * D + dh * 512: gi * D + dh * 512 + 512], op=ALU.mult),
                  R=[ps, g12row], W=[t])
            xs_ = x_[:, a, dh * 512:(dh + 1) * 512]
            P.dve(lambda e: e.scalar_tensor_tensor(out=xs_, in0=xs_, scalar=ALPHA, in1=t[:], op0=ALU.mult, op1=ALU.add),
                  R=[x_, t], W=[x_])

        def layer_norm(x_, a, li, k):
            sx = stat[k % 2]
            xa = x_[:, a, :]
            g_, b_ = lnp[:, (2 * li) * D:(2 * li + 1) * D], lnp[:, (2 * li + 1) * D:(2 * li + 2) * D]
            P.pool(lambda e: e.memset(sx[:], 0.0), W=[sx])
            P.act(lambda e: e.activation(out=junk[:], in_=xa, func=AF.Identity, accum_out=sx[:, 0:1]), R=[x_, sx], W=[junk, sx])
            P.act(lambda e: e.activation(out=junk[:], in_=xa, func=AF.Square, accum_out=sx[:, 1:2]), R=[x_, sx], W=[junk, sx])
            P.dve(lambda e: e.tensor_scalar_mul(out=sx[:, 2:3], in0=sx[:, 0:1], scalar1=1.0 / D), R=[sx], W=[sx])
            P.dve(lambda e: e.tensor_tensor(out=sx[:, 3:4], in0=sx[:, 2:3], in1=sx[:, 2:3], op=ALU.mult), R=[sx], W=[sx])
            P.dve(lambda e: e.scalar_tensor_tensor(out=sx[:, 4:5], in0=sx[:, 1:2], scalar=1.0 / D, in1=sx[:, 3:4], op0=ALU.mult, op1=ALU.subtract),
                  R=[sx], W=[sx])
            P.act(lambda e: e.activation(out=sx[:, 5:6], in_=sx[:, 4:5], func=AF.Sqrt, bias=epsT[:, 1:2], scale=1.0), R=[sx, epsT], W=[sx])
            P.dve(lambda e: e.reciprocal(out=sx[:, 5:6], in_=sx[:, 5:6]), R=[sx], W=[sx])
            P.dve(lambda e: e.scalar_tensor_tensor(out=sx[:, 6:7], in0=sx[:, 2:3], scalar=-1.0, in1=sx[:, 5:6], op0=ALU.mult, op1=ALU.mult),
                  R=[sx], W=[sx])
            P.act(lambda e: e.activation(out=xa, in_=xa, func=AF.Identity, bias=sx[:, 6:7], scale=sx[:, 5:6]), R=[x_, sx], W=[x_])
            P.dve(lambda e: e.tensor_tensor(out=xa, in0=xa, in1=g_, op=ALU.mult), R=[x_, lnp], W=[x_])
            P.dve(lambda e: e.tensor_tensor(out=xa, in0=xa, in1=b_, op=ALU.add), R=[x_, lnp], W=[x_])

        nblk = HALF // 512
        for blk in range(nblk):
            x_ = xt[blk % 2]
            yT = yTs[blk % 2]

            def issue_loads(b):
                xb, yb, tlb = xt[b % 2], yTs[b % 2], b * 4
                P.dma("sp", xb[:], xhalf[b * 512:(b + 1) * 512, :].rearrange("(a p) d -> p a d", p=128), W=[xb])
                P.dma("sp", yb[:, 0], MINE.ap()[tlb:tlb + 4].rearrange("t p f -> p t f"), W=[yb], semt=ysem[b % 2][0])
                P.dma("sp", yb[:, 1], PART.ap()[tlb:tlb + 4].rearrange("t p f -> p t f"), W=[yb], semt=ysem[b % 2][1])

            if blk == 0:
                issue_loads(0)
            k = 0
            for dh in range(2):
                wb = wload(WO[dh], 8192)
                wv = wb[:, 0:8192].rearrange("p (k n) -> p k n", k=16)
                for a in range(4):
                    ps = psb[k % 2]
                    for kc in range(16):
                        src, cc = kc // 8, kc % 8
                        P.pe(lambda e, ps=ps, kc=kc, src=src, cc=cc, a=a, wv=wv, yT=yT: e.matmul(
                            ps[:], lhsT=yT[:, src, a, cc * 128:(cc + 1) * 128], rhs=wv[:, kc, :], start=(kc == 0), stop=(kc == 15)),
                            R=[yT, wb], W=[ps])
                    resid(ps, x_, a, dh, 0, k)
                    k += 1
            if blk + 1 < nblk:
                issue_loads(blk + 1)
            for a in range(4):
                layer_norm(x_, a, 0, a)
            PX = psb[2]
            for fc in range(8):
                for a in range(4):
                    P.pe(lambda e, a=a, fc=fc, x_=x_: e.transpose(out=PX[:, a * 128:(a + 1) * 128], in_=x_[:, a, fc * 128:(fc + 1) * 128], identity=ident),
                         R=[x_, cst], W=[PX])
                P.act(lambda e, fc=fc: e.activation(out=h2T[:, fc, :], in_=PX[:], func=AF.Identity,
                                                    bias=modT[:, 32 + fc:33 + fc], scale=modT[:, 40 + fc:41 + fc]), R=[PX, modT], W=[h2T])
            for b11 in range(11):
                wb = wload(WGU[b11], 4096)
                wv = wb[:, 0:4096].rearrange("p (g k n) -> p g k n", g=2, k=8)
                for jj in range(2):
                    j = b11 * 2 + jj
                    pg, pu = psb[4 + j % 2], psb[6 + j % 2]
                    for kc in range(8):
                        P.pe(lambda e, pg=pg, kc=kc, jj=jj, wv=wv: e.matmul(pg[:], lhsT=wv[:, 0, kc, jj * 128:(jj + 1) * 128], rhs=h2T[:, kc, :],
                                                                            start=(kc == 0), stop=(kc == 7)), R=[wb, h2T], W=[pg])
                    for kc in range(8):
                        P.pe(lambda e, pu=pu, kc=kc, jj=jj, wv=wv: e.matmul(pu[:], lhsT=wv[:, 1, kc, jj * 128:(jj + 1) * 128], rhs=h2T[:, kc, :],
                                                                            start=(kc == 0), stop=(kc == 7)), R=[wb, h2T], W=[pu])
                    sg = sgt[j % 2]
                    P.act(lambda e, sg=sg, pg=pg: e.activation(out=sg[:], in_=pg[:], func=AF.Silu), R=[pg], W=[sg])
                    P.dve(lambda e, sg=sg, pu=pu, j=j: e.tensor_tensor(out=actT[:, j, :], in0=sg[:], in1=pu[:], op=ALU.mult), R=[sg, pu], W=[actT])
            for dh in range(2):
                accs = [psb[0], psb[1], psb[2], psb[3]]
                for jh in range(2):
                    wb = wload(WD[dh * 2 + jh], 5632)
                    wv = wb[:, 0:5632].rearrange("p (j n) -> p j n", j=11)
                    for a in range(4):
                        for j11 in range(11):
                            j = jh * 11 + j11
                            P.pe(lambda e, a=a, j=j, j11=j11, wv=wv, accs=accs: e.matmul(
                                accs[a][:], lhsT=actT[:, j, a * 128:(a + 1) * 128], rhs=wv[:, j11, :], start=(j == 0), stop=(j == 21)),
                                R=[actT, wb], W=[accs[a]])
                for a in range(4):
                    resid(accs[a], x_, a, dh, 1, a)
            for a in range(4):
                layer_norm(x_, a, 1, a)
            final_ops.append(P.dma("sp", yout[blk * 512:(blk + 1) * 512, :].rearrange("(a p) d -> p a d", p=128), x_[:], R=[x_]))


def host_consts():
    j = np.arange(128)[:, None]
    l = np.arange(128)[None, :]
    mats = [np.eye(128), (j <= l), (j > l), (j >= l), (j < l), np.ones((128, 128)), (j // 16 == l // 16)]
    for b in (16, 32, 64):
        mats.append((j // (2 * b) == l // (2 * b)) & (j % (2 * b) >= b) & (l % (2 * b) < b))
    for b in (16, 32, 64):
        mats.append(((j // (2 * b) == l // (2 * b)) & (j % (2 * b) >= b) & (l % (2 * b) < b)).T)
    return np.concatenate([m.astype(np.float32) for m in mats], axis=1)


def prep_core(inp, core, SEQ):
    b, h = core // 2, core % 2
    HALF = SEQ // 2
    f = lambda a: np.ascontiguousarray(a, dtype=np.float32)
    x, ctx = inp["x"], inp["ctx"]
    w_in = inp["w_in"][0]
    xs0 = 1024
    B0, C0 = 2048, 2304
    dt0 = 2560
    q0, k0, v0 = 2592, 3616, 4640
    g0 = 5664
    b0, a0 = 6688, 6704
    r = lambda s, n: np.arange(s, s + n)
    cols_cm = np.concatenate([r(xs0 + h * 512, 512), r(B0 + h * 128, 128), r(C0 + h * 128, 128),
                              r(q0 + h * 512, 512), r(k0 + h * 512, 512), r(v0 + h * 512, 512)])
    cols_tm = np.concatenate([r(h * 512, 512), r(g0 + h * 512, 512),
                              r(dt0 + 8 * h, 8), r(dt0 + 16 + 8 * h, 8),
                              r(a0 + 4 * h, 4), r(a0 + 8 + 4 * h, 4),
                              r(b0 + 4 * h, 4), r(b0 + 8 + 4 * h, 4)])
    cws, cwg = inp["conv_w_ssd"][0], inp["conv_w_gdn"][0]
    ssd_cols = np.concatenate([r(h * 512, 512), r(1024 + h * 128, 128), r(1280 + h * 128, 128)])
    gdn_cols = np.concatenate([r(h * 512, 512), r(1024 + h * 512, 512), r(2048 + h * 512, 512)])
    cw = np.concatenate([cws[:, ssd_cols], cwg[:, gdn_cols]], axis=1)
    cwT = cw.T.reshape(NCM, 128, 5).transpose(1, 0, 2).reshape(128, NCM * 5)
    cbT = inp["conv_b_ssd"][0][ssd_cols].reshape(6, 128).T
    rowp = np.concatenate([
        inp["dt_bias_ssd"][0][0, 8 * h:8 * h + 8], inp["dt_bias_ssd"][0][1, 8 * h:8 * h + 8],
        inp["dt_bias_gdn"][0][0, 4 * h:4 * h + 4], inp["dt_bias_gdn"][0][1, 4 * h:4 * h + 4],
        inp["a_log_ssd"][0][0, 8 * h:8 * h + 8], inp["a_log_ssd"][0][1, 8 * h:8 * h + 8],
        inp["a_log_gdn"][0][0, 4 * h:4 * h + 4], inp["a_log_gdn"][0][1, 4 * h:4 * h + 4],
        inp["d_skip_ssd"][0][8 * h:8 * h + 8],
        inp["norm_w_ssd"][0][h * 512:(h + 1) * 512], inp["norm_w_gdn"][0],
        inp["ln1_g"][0], inp["ln1_b"][0], inp["ln2_g"][0], inp["ln2_b"][0]])[None, :]
    cv = np.stack([inp["c"][b], inp["c_ctx"]])
    cvT = cv.reshape(2, 8, 128).transpose(2, 0, 1).reshape(128, 16)
    wo = inp["w_out"][0]
    own = np.concatenate([r(h * 512, 512), r(1024 + h * 512, 512)])
    oth = np.concatenate([r((1 - h) * 512, 512), r(1024 + (1 - h) * 512, 512)])
    return {
        "xin": f(np.concatenate([ctx[b], x[b]], axis=0)),
        "xhalf": f(x[b, h * HALF:(h + 1) * HALF]),
        "cvecT": f(cvT),
        "w_ada": f(inp["w_ada"][0]), "b_ada": f(inp["b_ada"][0][None, :]),
        "w_cm": f(w_in[:, cols_cm]), "w_tm": f(w_in[:, cols_tm]),
        "convw": f(cwT), "convb": f(cbT), "rowp": f(rowp), "consts": host_consts(),
        "w_out": f(wo[np.concatenate([own, oth])]),
        "w_gate": f(inp["w_ffn_gate"][0]), "w_up": f(inp["w_ffn_up"][0]), "w_down": f(inp["w_ffn_down"][0]),
    }


_CACHE = {}


def run(inputs, SEQ, debug=False, stop_after=99, ncores=8):
    key = (SEQ, debug, stop_after)
    if key not in _CACHE:
        _CACHE[key] = build_program(SEQ, debug, stop_after)
    nc, stats = _CACHE[key]
    in_maps = [prep_core(inputs, c, SEQ) for c in range(ncores)]
    res = run_bass_kernel_spmd(nc, in_maps, core_ids=list(range(ncores)))
    return res.results, stats


def kernel(**inputs):
    SEQ = inputs["x"].shape[1]
    B = inputs["x"].shape[0]
    results, _ = run(inputs, SEQ)
    HALF = SEQ // 2
    out = np.empty((B, SEQ, D), np.float32)
    for c in range(8):
        b, h = c // 2, c % 2
        out[b, h * HALF:(h + 1) * HALF] = results[c]["yout"]
    return out
```
